# Optimizing a Trainium2 kernel written in Bass

```python
import math
import jax, jax.numpy as jnp
from jax import lax
import numpy as np

D_MODEL = 1024
BATCH = 4
SEQ = 4096
DEPTH = 4
DEC_BATCH = 128
DEC_SEQ = 1
PAST_LEN = 2048
PAGE_SIZE = 128

RET_HEADS = 4
RET_DIM = 128
RET_W = RET_HEADS * RET_DIM
ATT_HEADS = 4
ATT_DIM = 128
ATT_W = ATT_HEADS * ATT_DIM
IDX_HEADS = 4
IDX_DIM = 64
TOPK_MAX = 256
Q_BLOCK = 128
RET_CHUNK = 128
GDN_HEADS = 8
GDN_DIM = 128
GDN_W = GDN_HEADS * GDN_DIM
CONV_W = 4
GDN_CHUNK = 64
D_FF = 4 * D_MODEL
ROPE_THETA = 10000.0
KV_ROW = 2 * ATT_DIM + IDX_DIM
EVEN_SPLITS = (RET_W, RET_W, RET_W, RET_W, ATT_W, ATT_DIM, ATT_DIM, IDX_HEADS * IDX_DIM, IDX_DIM, IDX_HEADS)
EVEN_IN = sum(EVEN_SPLITS)
ODD_SPLITS = (3 * GDN_W, GDN_W, GDN_HEADS, GDN_HEADS)
ODD_IN = sum(ODD_SPLITS)
N_EVEN = (DEPTH + 1) // 2
N_ODD = DEPTH // 2
DEEPNORM_ALPHA = (2 * DEPTH) ** 0.25
DEEPNORM_BETA = (8 * DEPTH) ** -0.25
NEG_INF = -1e30
LN_EPS = 1e-5

kernel_name = 'hybrid_retention_dsa_gdn_decoder_step'


def split_cols(a, sizes):
    out, start = [], 0
    for s in sizes:
        out.append(a[..., start:start + s])
        start += s
    return out


def pick_chunk(length, chunk):
    return chunk if length % chunk == 0 else length


def layer_norm(x, g, b):
    xf = x.astype(jnp.float32)
    mu = xf.mean(-1, keepdims=True)
    var = jnp.square(xf - mu).mean(-1, keepdims=True)
    return ((xf - mu) * lax.rsqrt(var + LN_EPS) * g.astype(jnp.float32) + b.astype(jnp.float32)).astype(x.dtype)


def head_norm(x, g):
    xf = x.astype(jnp.float32)
    mu = xf.mean(-1, keepdims=True)
    var = jnp.square(xf - mu).mean(-1, keepdims=True)
    return ((xf - mu) * lax.rsqrt(var + LN_EPS) * g.astype(jnp.float32)).astype(x.dtype)


def rms_norm(x, g):
    xf = x.astype(jnp.float32)
    return xf * lax.rsqrt(jnp.square(xf).mean(-1, keepdims=True) + LN_EPS) * g.astype(jnp.float32)


def l2_norm(x):
    xf = x.astype(jnp.float32)
    return xf * lax.rsqrt(jnp.square(xf).sum(-1, keepdims=True) + 1e-6)


def rope(x, pos):
    d = x.shape[-1]
    inv = ROPE_THETA ** (-jnp.arange(0, d, 2, dtype=jnp.float32) / d)
    ang = pos.astype(jnp.float32)[:, None] * inv[None, :]
    shape = (pos.shape[0],) + (1,) * (x.ndim - 3) + (d // 2,)
    cos, sin = jnp.cos(ang).reshape(shape), jnp.sin(ang).reshape(shape)
    xf = x.astype(jnp.float32)
    x1, x2 = xf[..., : d // 2], xf[..., d // 2:]
    return jnp.concatenate([x1 * cos - x2 * sin, x2 * cos + x1 * sin], -1).astype(x.dtype)


def retention_log_decay():
    return jnp.log(1.0 - 2.0 ** (-5.0 - jnp.arange(RET_HEADS, dtype=jnp.float32)))


def retention_chunked(q, k, v, s0, chunk):
    f32 = jnp.float32
    B, L, H, _ = q.shape
    Dv = v.shape[-1]
    n = L // chunk
    blk = lambda a: jnp.moveaxis(a.astype(f32).reshape(B, n, chunk, H, a.shape[-1]), 1, 0)
    qc, kc, vc = blk(q), blk(k), blk(v)
    lg = retention_log_decay()
    i = jnp.arange(chunk, dtype=f32)
    rel = i[:, None] - i[None, :]
    causal = rel >= 0
    dmat = jnp.where(causal, jnp.exp(lg[:, None, None] * jnp.where(causal, rel, 0.0)), 0.0)
    qdec = jnp.exp(lg[None, :] * (i[:, None] + 1.0))
    kdec = jnp.exp(lg[None, :] * (chunk - 1.0 - i[:, None]))
    cdec = jnp.exp(lg * chunk)
    o_intra = jnp.einsum('nbhij,nbjhe->nbihe', jnp.einsum('nbihd,nbjhd->nbhij', qc, kc) * dmat, vc)

    def step(s, inp):
        qi, ki, vi = inp
        o = jnp.einsum('bihd,bhde->bihe', qi * qdec[None, :, :, None], s)
        s = s * cdec[None, :, None, None] + jnp.einsum('bjhd,bjhe->bhde', ki * kdec[None, :, :, None], vi)
        return s, o

    s, o_inter = lax.scan(step, s0.astype(f32), (qc, kc, vc))
    o = jnp.moveaxis(o_intra + o_inter, 0, 1).reshape(B, L, H, Dv)
    return o, s


def gated_delta_chunked(q, k, v, g, beta, s0, chunk):
    f32 = jnp.float32
    B, L, H, _ = q.shape
    Dv = v.shape[-1]
    n = L // chunk
    blk4 = lambda a: a.astype(f32).reshape(B, n, chunk, H, a.shape[-1]).transpose(1, 0, 3, 2, 4)
    blk3 = lambda a: a.astype(f32).reshape(B, n, chunk, H).transpose(1, 0, 3, 2)
    qc, kc, vc = blk4(q), blk4(k), blk4(v)
    gc = jnp.cumsum(blk3(g), axis=-1)
    bc = blk3(beta)
    i = jnp.arange(chunk)
    tri = i[:, None] >= i[None, :]
    strict = i[:, None] > i[None, :]
    diff = gc[..., :, None] - gc[..., None, :]
    gam = jnp.where(tri, jnp.exp(jnp.where(tri, diff, 0.0)), 0.0)
    kb = kc * bc[..., None]
    m = jnp.where(strict, jnp.einsum('nbhid,nbhjd->nbhij', kb, kc) * gam, 0.0)
    eye = jnp.eye(chunk, dtype=f32)
    t_inv = lax.linalg.triangular_solve(eye + m, jnp.broadcast_to(eye, m.shape), left_side=True, lower=True)
    u = jnp.einsum('nbhij,nbhje->nbhie', t_inv, vc * bc[..., None])
    w = jnp.einsum('nbhij,nbhjd->nbhid', t_inv, kb * jnp.exp(gc)[..., None])
    aqk = jnp.where(tri, jnp.einsum('nbhid,nbhjd->nbhij', qc, kc) * gam, 0.0)
    qg = qc * jnp.exp(gc)[..., None]
    kg = kc * jnp.exp(gc[..., -1:] - gc)[..., None]
    glast = jnp.exp(gc[..., -1])

    def step(s, inp):
        u_i, w_i, qg_i, kg_i, a_i, gl_i = inp
        v_new = u_i - jnp.einsum('bhcd,bhde->bhce', w_i, s)
        o = jnp.einsum('bhcd,bhde->bhce', qg_i, s) + jnp.einsum('bhij,bhje->bhie', a_i, v_new)
        s = s * gl_i[..., None, None] + jnp.einsum('bhcd,bhce->bhde', kg_i, v_new)
        return s, o

    s, o = lax.scan(step, s0.astype(f32), (u, w, qg, kg, aqk, glast))
    return o.transpose(1, 0, 3, 2, 4).reshape(B, L, H, Dv), s


def causal_conv_silu(x, w, buf):
    L = x.shape[1]
    xp = jnp.concatenate([buf.astype(x.dtype), x], axis=1)
    y = sum(xp[:, j:j + L] * w[j] for j in range(CONV_W))
    return jax.nn.silu(y), xp[:, L:]


def dsa_attend(q, qi, wi, qpos, k_all, v_all, ki_all, top_k):
    kpos = jnp.arange(k_all.shape[1])
    causal = kpos[None, :] <= qpos[:, None]
    rel = jax.nn.relu(jnp.einsum('bqhd,bkd->bqhk', qi, ki_all).astype(jnp.float32))
    score = jnp.einsum('bqhk,bqh->bqk', rel, wi.astype(jnp.float32)) * IDX_DIM ** -0.5
    score = jnp.where(causal[None], score, NEG_INF)
    _, idx = lax.top_k(score, top_k)
    valid = idx <= qpos[None, :, None]
    bidx = jnp.arange(k_all.shape[0])[:, None, None]
    k_sel, v_sel = k_all[bidx, idx], v_all[bidx, idx]
    logits = jnp.einsum('bqhd,bqkd->bqhk', q, k_sel).astype(jnp.float32) * ATT_DIM ** -0.5
    prob = jax.nn.softmax(jnp.where(valid[:, :, None, :], logits, NEG_INF), axis=-1)
    return jnp.einsum('bqhk,bqkd->bqhd', prob.astype(v_sel.dtype), v_sel)


def even_mixer(h, pos, kv_past, ret_s0, w_in, w_out, ret_gn, n_q_blocks):
    B, L, _ = h.shape
    rq, rk, rv, rg, aq, ak, av, iq, ik, iw = split_cols(h @ w_in, EVEN_SPLITS)
    rq = rope(rq.reshape(B, L, RET_HEADS, RET_DIM), pos)
    rk = rope(rk.reshape(B, L, RET_HEADS, RET_DIM), pos) * RET_DIM ** -0.5
    rv = rv.reshape(B, L, RET_HEADS, RET_DIM)
    ret_o, ret_s = retention_chunked(rq, rk, rv, ret_s0, pick_chunk(L, RET_CHUNK))
    ret_o = head_norm(ret_o.astype(h.dtype), ret_gn).reshape(B, L, RET_W) * jax.nn.silu(rg)
    aq = rope(aq.reshape(B, L, ATT_HEADS, ATT_DIM), pos)
    ak = rope(ak, pos)
    iq = rope(iq.reshape(B, L, IDX_HEADS, IDX_DIM), pos)
    ik = rope(ik, pos)
    iw = iw * IDX_HEADS ** -0.5
    new_kv = jnp.concatenate([ak, av, ik], -1)
    keys = new_kv if kv_past is None else jnp.concatenate([kv_past.astype(new_kv.dtype), new_kv], 1)
    k_all, v_all, ki_all = split_cols(keys, (ATT_DIM, ATT_DIM, IDX_DIM))
    top_k = min(TOPK_MAX, keys.shape[1] // 4)
    bs = L // n_q_blocks
    blocks = lambda a: jnp.moveaxis(a.reshape((B, n_q_blocks, bs) + a.shape[2:]), 1, 0)
    att = lax.map(lambda a: dsa_attend(a[0], a[1], a[2], a[3], k_all, v_all, ki_all, top_k),
                  (blocks(aq), blocks(iq), blocks(iw), pos.reshape(n_q_blocks, bs)))
    att = jnp.moveaxis(att, 0, 1).reshape(B, L, ATT_W)
    y = jnp.concatenate([ret_o, att], -1) @ w_out
    return y, ret_s, new_kv


def odd_mixer(h, ssm_s0, conv_buf, w_in, w_out, conv_w, a_log, dt_bias, gn):
    B, L, _ = h.shape
    qkv, z, a, b = split_cols(h @ w_in, ODD_SPLITS)
    qkv, new_buf = causal_conv_silu(qkv, conv_w, conv_buf)
    q, k, v = [t.reshape(B, L, GDN_HEADS, GDN_DIM) for t in split_cols(qkv, (GDN_W, GDN_W, GDN_W))]
    q = l2_norm(q) * GDN_DIM ** -0.5
    k = l2_norm(k)
    g = -jnp.exp(a_log.astype(jnp.float32)) * jax.nn.softplus(a.astype(jnp.float32) + dt_bias.astype(jnp.float32))
    beta = jax.nn.sigmoid(b.astype(jnp.float32))
    o, s = gated_delta_chunked(q, k, v, g, beta, ssm_s0, pick_chunk(L, GDN_CHUNK))
    o = rms_norm(o, gn) * jax.nn.silu(z.reshape(B, L, GDN_HEADS, GDN_DIM).astype(jnp.float32))
    y = o.reshape(B, L, GDN_W).astype(h.dtype) @ w_out
    return y, s, new_buf


def run_trunk(x, c, pos, kv_past, ret_s0, ssm_s0, conv_s0, n_q_blocks, p):
    kv_new, ret_new, ssm_new, conv_new = [], [], [], []
    sc = jax.nn.silu(c)
    for l in range(DEPTH):
        mod = (sc @ p['ada_w'][l] + p['ada_b'][l])[:, None, :]
        sh_m, sc_m, gt_m, sh_f, sc_f, gt_f = jnp.split(mod, 6, axis=-1)
        hm = x * (1.0 + sc_m) + sh_m
        if l % 2 == 0:
            i = l // 2
            y, s, kv = even_mixer(hm, pos, kv_past[i], ret_s0[i], p['even_w_in'][i], p['even_w_out'][i],
                                  p['ret_gn'][i], n_q_blocks)
            ret_new.append(s)
            kv_new.append(kv)
        else:
            j = l // 2
            y, s, buf = odd_mixer(hm, ssm_s0[j], conv_s0[j], p['odd_w_in'][j], p['odd_w_out'][j], p['odd_conv'][j],
                                  p['odd_a_log'][j], p['odd_dt_bias'][j], p['odd_gn'][j])
            ssm_new.append(s)
            conv_new.append(buf)
        x = layer_norm(DEEPNORM_ALPHA * x + (1.0 + gt_m) * y, p['ln_g'][l, 0], p['ln_b'][l, 0])
        hf = x * (1.0 + sc_f) + sh_f
        f = jnp.square(jax.nn.relu(hf @ p['mlp_up'][l])) @ p['mlp_down'][l]
        x = layer_norm(DEEPNORM_ALPHA * x + (1.0 + gt_f) * f, p['ln_g'][l, 1], p['ln_b'][l, 1])
    return x, jnp.stack(kv_new), jnp.stack(ret_new), jnp.stack(ssm_new), jnp.stack(conv_new)


def setup_inputs(seed: int = 0) -> dict:
    key = jax.random.key(seed)
    ks = jax.random.split(key, 26)
    f32 = jnp.float32
    n_pages = PAST_LEN // PAGE_SIZE
    n_pool = (DEC_BATCH * n_pages * 5) // 4
    nrm = lambda k, shape, s=1.0: s * jax.random.normal(k, shape, f32)
    perm = jax.random.permutation(ks[5], n_pool)[: DEC_BATCH * n_pages]
    dt = jnp.exp(jax.random.uniform(ks[20], (N_ODD, GDN_HEADS), f32, math.log(1e-3), math.log(1e-1)))
    return {
        'x_prompt': nrm(ks[0], (BATCH, SEQ, D_MODEL)),
        'x_sample': nrm(ks[1], (DEC_BATCH, DEC_SEQ, D_MODEL)),
        'c_prompt': nrm(ks[2], (BATCH, D_MODEL)),
        'c_sample': nrm(ks[3], (DEC_BATCH, D_MODEL)),
        'cache_kv': nrm(ks[4], (N_EVEN, n_pool, PAGE_SIZE, KV_ROW)),
        'page_table': perm.reshape(DEC_BATCH, n_pages).astype(jnp.int32),
        'state_ret': nrm(ks[6], (N_EVEN, DEC_BATCH, RET_HEADS, RET_DIM, RET_DIM), 0.5),
        'state_ssm': nrm(ks[7], (N_ODD, DEC_BATCH, GDN_HEADS, GDN_DIM, GDN_DIM), 0.1),
        'state_conv': nrm(ks[8], (N_ODD, DEC_BATCH, CONV_W - 1, 3 * GDN_W)),
        'ada_w': nrm(ks[9], (DEPTH, D_MODEL, 6 * D_MODEL), 0.2 * D_MODEL ** -0.5),
        'ada_b': nrm(ks[10], (DEPTH, 6 * D_MODEL), 0.02),
        'ln_g': 1.0 + nrm(ks[11], (DEPTH, 2, D_MODEL), 0.02),
        'ln_b': nrm(ks[12], (DEPTH, 2, D_MODEL), 0.02),
        'mlp_up': nrm(ks[13], (DEPTH, D_MODEL, D_FF), D_MODEL ** -0.5),
        'mlp_down': nrm(ks[14], (DEPTH, D_FF, D_MODEL), DEEPNORM_BETA * D_FF ** -0.5),
        'even_w_in': nrm(ks[15], (N_EVEN, D_MODEL, EVEN_IN), D_MODEL ** -0.5),
        'even_w_out': nrm(ks[16], (N_EVEN, RET_W + ATT_W, D_MODEL), DEEPNORM_BETA * (RET_W + ATT_W) ** -0.5),
        'ret_gn': 1.0 + nrm(ks[17], (N_EVEN, RET_HEADS, RET_DIM), 0.02),
        'odd_w_in': nrm(ks[18], (N_ODD, D_MODEL, ODD_IN), D_MODEL ** -0.5),
        'odd_w_out': nrm(ks[23], (N_ODD, GDN_W, D_MODEL), DEEPNORM_BETA * GDN_W ** -0.5),
        'odd_conv': nrm(ks[19], (N_ODD, CONV_W, 3 * GDN_W), CONV_W ** -0.5),
        'odd_a_log': jnp.log(jax.random.uniform(ks[21], (N_ODD, GDN_HEADS), f32, 1.0, 16.0)),
        'odd_dt_bias': dt + jnp.log(-jnp.expm1(-dt)),
        'odd_gn': 1.0 + nrm(ks[22], (N_ODD, GDN_DIM), 0.02),
    }


def reference(x_prompt, x_sample, c_prompt, c_sample, cache_kv, page_table, state_ret, state_ssm, state_conv,
              ada_w, ada_b, ln_g, ln_b, mlp_up, mlp_down, even_w_in, even_w_out, ret_gn,
              odd_w_in, odd_w_out, odd_conv, odd_a_log, odd_dt_bias, odd_gn):
    p = {'ada_w': ada_w, 'ada_b': ada_b, 'ln_g': ln_g, 'ln_b': ln_b, 'mlp_up': mlp_up, 'mlp_down': mlp_down,
         'even_w_in': even_w_in, 'even_w_out': even_w_out, 'ret_gn': ret_gn,
         'odd_w_in': odd_w_in, 'odd_w_out': odd_w_out, 'odd_conv': odd_conv,
         'odd_a_log': odd_a_log, 'odd_dt_bias': odd_dt_bias, 'odd_gn': odd_gn}
    B, T, _ = x_prompt.shape
    Bd, Td, _ = x_sample.shape
    past_len = page_table.shape[1] * cache_kv.shape[2]
    pos_p = jnp.arange(T)
    ret0 = [jnp.zeros((B, RET_HEADS, RET_DIM, RET_DIM), jnp.float32)] * N_EVEN
    ssm0 = [jnp.zeros((B, GDN_HEADS, GDN_DIM, GDN_DIM), jnp.float32)] * N_ODD
    conv0 = [jnp.zeros((B, CONV_W - 1, 3 * GDN_W), x_prompt.dtype)] * N_ODD
    y_prompt, kv_p, ret_p, ssm_p, conv_p = run_trunk(x_prompt, c_prompt, pos_p, [None] * N_EVEN, ret0, ssm0, conv0,
                                                     T // Q_BLOCK, p)
    pos_s = past_len + jnp.arange(Td)
    kv_past = [cache_kv[i][page_table].reshape(Bd, past_len, KV_ROW) for i in range(N_EVEN)]
    y_sample, kv_s, ret_s, ssm_s, conv_s = run_trunk(x_sample, c_sample, pos_s, kv_past,
                                                     [state_ret[i] for i in range(N_EVEN)],
                                                     [state_ssm[j] for j in range(N_ODD)],
                                                     [state_conv[j] for j in range(N_ODD)], 1, p)
    return (y_prompt, y_sample, kv_p, kv_s, ret_p, ret_s, ssm_p, ssm_s, conv_p, conv_s)
```

```python
import math
import numpy as np
from contextlib import ExitStack
import concourse.bass as bass
import concourse.mybir as mybir
from concourse.bass_utils import run_bass_kernel_spmd

F32 = mybir.dt.float32
BF16 = mybir.dt.bfloat16
I32 = mybir.dt.int32
ALU = mybir.AluOpType
AF = mybir.ActivationFunctionType
AX = mybir.AxisListType

ENGS = ("pe", "act", "dve", "pool", "sp")
NDMA = 6

D = 1024
DFF = 4096
EVEN_IN = 3140
ODD_IN = 4112
NS = 16
ALPHA = 8.0 ** 0.25
LN_EPS = 1e-5
NEG = -1e30


class Buf:
    __slots__ = ("w", "r")

    def __init__(self):
        self.w = None
        self.r = {}


class TL:
    def __init__(self, t, nb=1):
        self.t = t
        self.b = [Buf() for _ in range(nb)]

    def __getitem__(self, k):
        return self.t[k]


def _bufs(lst):
    out = []
    for x in lst:
        if isinstance(x, Buf):
            out.append(x)
        elif isinstance(x, TL):
            out.extend(x.b)
        elif isinstance(x, tuple):
            out.append(x[0].b[x[1]])
        else:
            raise TypeError(x)
    return out


class Sched:
    def __init__(self, nc, es):
        self.nc = nc
        self.ops = {e: [] for e in ENGS}
        self.cnt = {e: 0 for e in ENGS}
        self.seen = {e: {} for e in ENGS}
        self.sem = {e: es.enter_context(nc.semaphore("s_" + e)) for e in ENGS}
        self.dsem = {}
        self.dcnt = {}
        self.pend = {e: [] for e in ENGS}
        for q in ("sp", "act", "pool"):
            for j in range(NDMA):
                k = "d_%s%d" % (q, j)
                self.dsem[k] = es.enter_context(nc.semaphore(k))
            self.dcnt[q] = 0

    def barrier(self):
        tgt = {e: self.cnt[e] for e in ENGS if self.cnt[e] > 0}
        for q in ("sp", "act", "pool"):
            for j in range(NDMA):
                if self.dcnt[q] > j:
                    tgt["d_%s%d" % (q, j)] = 16 * ((self.dcnt[q] - j + NDMA - 1) // NDMA)
        for e in ENGS:
            for k, v in tgt.items():
                if self.seen[e].get(k, 0) < v:
                    self.seen[e][k] = v
                    self.pend[e].append((k, v))

    def _semobj(self, key):
        return self.sem[key] if key in self.sem else self.dsem[key]

    def _collect(self, eng, r, w):
        deps = []
        for b in r:
            if b.w is not None:
                deps.append(b.w)
        for b in w:
            if b.w is not None:
                deps.append(b.w)
            for k, v in b.r.items():
                deps.append((k, v))
        waits = []
        sn = self.seen[eng]
        for k, v in deps:
            if eng == "pe" and k == "pe":
                continue
            if sn.get(k, 0) >= v:
                continue
            sn[k] = v
            waits.append((k, v))
        return waits

    def op(self, eng, fn, r=(), w=()):
        r = _bufs(r)
        w = _bufs(w)
        waits = self.pend[eng] + self._collect(eng, r, w)
        self.pend[eng] = []
        self.cnt[eng] += 1
        tok = (eng, self.cnt[eng])
        self.ops[eng].append((waits, fn, None))
        for b in r:
            if b.r.get(eng, 0) < tok[1]:
                b.r[eng] = tok[1]
        for b in w:
            b.w = tok
            b.r = {}

    def dma(self, q, fn, r=(), w=()):
        r = _bufs(r)
        w = _bufs(w)
        i = self.dcnt[q]
        self.dcnt[q] += 1
        key = "d_%s%d" % (q, i % NDMA)
        prev = 16 * (i // NDMA)
        waits = self.pend[q] + self._collect(q, r, w)
        self.pend[q] = []
        sn = self.seen[q]
        if prev > 0 and sn.get(key, 0) < prev:
            sn[key] = prev
            waits.append((key, prev))
        tok = (key, prev + 16)
        self.ops[q].append((waits, fn, key))
        for b in r:
            if b.r.get(key, 0) < tok[1]:
                b.r[key] = tok[1]
        for b in w:
            b.w = tok
            b.r = {}

    def finish(self):
        final = {}
        for q in ("sp", "act", "pool"):
            for j in range(NDMA):
                if self.dcnt[q] > j:
                    final["d_%s%d" % (q, j)] = 16 * ((self.dcnt[q] - j + NDMA - 1) // NDMA)
        nc = self.nc
        with nc.Block() as block:
            def replay(ename, eng):
                for waits, fn, key in self.ops[ename]:
                    for k, v in waits:
                        eng.wait_ge(self._semobj(k), v)
                    inst = fn(eng)
                    if key is None:
                        inst.then_inc(self.sem[ename], 1)
                    else:
                        inst.then_inc(self.dsem[key], 16)
                if ename == "sp":
                    for k, v in final.items():
                        eng.wait_ge(self._semobj(k), v)
                    for e in ("pe", "act", "dve", "pool"):
                        if self.cnt[e] > 0:
                            eng.wait_ge(self.sem[e], self.cnt[e])

            @block.tensor
            def _(e):
                replay("pe", e)

            @block.scalar
            def _(e):
                replay("act", e)

            @block.vector
            def _(e):
                replay("dve", e)

            @block.gpsimd
            def _(e):
                replay("pool", e)

            @block.sync
            def _(e):
                replay("sp", e)


def _rope_rows(pos):
    def tab(d):
        inv = (10000.0 ** (-np.arange(0, d, 2, dtype=np.float32) / d)).astype(np.float32)
        ang = (pos.astype(np.float32)[:, None] * inv[None, :]).astype(np.float32)
        return np.cos(ang.astype(np.float64)), np.sin(ang.astype(np.float64))
    c128, s128 = tab(128)
    c64, s64 = tab(64)
    return np.concatenate([c128, s128, -s128, c64, s64, -s64], 1).astype(np.float32)


def make_consts(NT, past_len):
    c = {}
    p = np.arange(128)
    c["ident"] = np.eye(128, dtype=np.float32)
    c["ones"] = np.ones((128, 128), np.float32)
    c["triu"] = (p[:, None] <= p[None, :]).astype(np.float32)
    c["mvinc"] = np.where(p[None, :] >= p[:, None], 0.0, -1e4).astype(np.float32)
    c["strict"] = (p[None, :] > p[:, None]).astype(np.float32)
    c["cmask"] = np.where(p[None, :] <= p[:, None], 0.0, NEG).astype(np.float32)
    lg = np.log(1.0 - 2.0 ** (-5.0 - np.arange(4, dtype=np.float64)))
    i = p.astype(np.float64)
    rel = i[None, :] - i[:, None]
    dm = np.where(rel >= 0, np.exp(lg[:, None, None] * np.maximum(rel, 0)[None]), 0.0)
    c["dmatT"] = (np.transpose(dm, (1, 0, 2)) * 128 ** -0.5).astype(np.float32).reshape(128, 512)
    qd = np.exp(lg[:, None] * (i[None, :] + 1.0))
    c["qdecB"] = np.broadcast_to(qd.reshape(1, 512), (128, 512)).astype(np.float32).copy()
    kd = np.exp(lg[None, :] * (127.0 - i[:, None])) * 128 ** -0.5
    c["kdecC"] = kd.astype(np.float32)
    cd = np.exp(lg * 128.0)
    c["cdecB"] = np.broadcast_to(np.repeat(cd, 128).reshape(1, 512), (128, 512)).astype(np.float32).copy()
    g1 = np.exp(lg)
    c["gam1B"] = np.broadcast_to(np.repeat(g1, 128).reshape(1, 512), (128, 512)).astype(np.float32).copy()
    c["ropeP"] = _rope_rows(np.arange(NT * 128)).reshape(NT, 128, 288)
    c["iotap"] = np.arange(128, dtype=np.float32).reshape(128, 1)
    c["ropeS"] = np.broadcast_to(_rope_rows(np.array([past_len])), (NS, 288)).copy()
    d16 = np.broadcast_to(np.eye(16, dtype=np.float32).reshape(1, 256), (128, 256)).copy()
    c["delta16"] = d16
    return c


class KB:
    def __init__(self, cfg):
        self.cfg = cfg
        self.nc = bass.Bass("TRN2", target_bir_lowering=False)
        self.es = ExitStack()
        self.rr = 0

    def scope(self):
        kb = self

        class _Sc(ExitStack):
            def __exit__(self_, *a):
                kb.S.barrier()
                return super().__exit__(*a)
        return _Sc()

    def dram(self, name, shape, dt, kind):
        return TL(self.nc.dram_tensor(name, list(shape), dt, kind=kind).ap())

    def sb(self, es, name, shape, dt, nb=1):
        self.uid = getattr(self, "uid", 0) + 1
        return TL(es.enter_context(self.nc.sbuf_tensor("%s_%d" % (name, self.uid), list(shape), dt)), nb)

    def tt(self, eng, out, a, b, op, r, w):
        self.S.op(eng, lambda e: e.tensor_tensor(out=out, in0=a, in1=b, op=op), r=r, w=w)

    def ts(self, eng, out, a, s1, op0, r, w, s2=None, op1=None, acc=None):
        if op1 is None:
            self.S.op(eng, lambda e: e.tensor_scalar(out=out, in0=a, scalar1=s1, scalar2=None, op0=op0), r=r, w=w)
        elif acc is None:
            self.S.op(eng, lambda e: e.tensor_scalar(out=out, in0=a, scalar1=s1, scalar2=s2, op0=op0, op1=op1), r=r, w=w)
        else:
            self.S.op(eng, lambda e: e.tensor_scalar(out=out, in0=a, scalar1=s1, scalar2=s2, op0=op0, op1=op1, accum_out=acc), r=r, w=w)

    def stt(self, eng, out, a, sc, b, op0, op1, r, w):
        self.S.op(eng, lambda e: e.scalar_tensor_tensor(out=out, in0=a, scalar=sc, in1=b, op0=op0, op1=op1), r=r, w=w)

    def act(self, out, in_, func, r, w, bias=None, scale=None, acc=None):
        kw = {}
        if bias is not None:
            kw["bias"] = bias
        if scale is not None:
            kw["scale"] = scale
        if acc is not None:
            kw["accum_out"] = acc
        self.S.op("act", lambda e: e.activation(out=out, in_=in_, func=func, **kw), r=r, w=w)

    def cp(self, eng, out, in_, r, w):
        if eng == "act":
            self.S.op("act", lambda e: e.copy(out=out, in_=in_), r=r, w=w)
        else:
            self.S.op(eng, lambda e: e.tensor_copy(out=out, in_=in_), r=r, w=w)

    def red(self, eng, out, in_, r, w, op=None):
        if op is None:
            self.S.op(eng, lambda e: e.reduce_sum(out=out, in_=in_, axis=AX.X), r=r, w=w)
        else:
            self.S.op(eng, lambda e: e.tensor_reduce(out=out, in_=in_, axis=AX.X, op=op), r=r, w=w)

    def mm(self, out, lhsT, rhs, start, stop, r, w):
        self.S.op("pe", lambda e: e.matmul(out, lhsT=lhsT, rhs=rhs, start=start, stop=stop), r=r, w=w)

    def tr(self, out, in_, ident, r, w):
        self.S.op("pe", lambda e: e.transpose(out=out, in_=in_, identity=ident), r=r, w=w)

    def dma(self, q, out, in_, r, w):
        self.S.dma(q, lambda e: e.dma_start(out=out, in_=in_), r=r, w=w)

    def memset(self, eng, ap, val, w):
        self.S.op(eng, lambda e: e.memset(ap, val), w=w)

    def evq(self):
        self.rr += 1
        return "act" if self.rr % 2 else "dve"

    def ps(self):
        self.prr = (getattr(self, "prr", -1) + 1) % len(self.PSG)
        return self.PSG[self.prr]

    def rsqrt(self, eng, out, in_, r, w, mult, add):
        self.act(out, in_, AF.Sqrt, r=r, w=w, scale=mult, bias=add)
        self.S.op("dve", lambda e: e.reciprocal(out=out, in_=out), r=w, w=w)


def build(cfg):
    NT = cfg["NT"]
    NPG = cfg["NPG"]
    NPOOL = cfg["NPOOL"]
    DEPTH = cfg["DEPTH"]
    TOPK_P = min(256, (NT * 128) // 4)
    TOPK_S = min(256, (NPG * 128 + 1) // 4)
    NTOK = NT * 128
    NROW = NTOK + NS
    K = KB(cfg)
    nc = K.nc
    es = K.es
    with es:
        S = K.S = Sched(nc, es)
        EI, EO, IN = "ExternalInput", "ExternalOutput", "Internal"
        d = {}
        for nm, sh, dt in [
            ("xp", [NTOK, D], F32), ("xs", [NS, D], F32), ("c17", [17, D], F32),
            ("cache_kv", [2 * NPOOL * 128, 320], F32), ("ptab", [NS, NPG], I32),
            ("st_ret", [2, NS, 4, 128, 128], F32), ("st_ssm", [2, NS, 8, 128, 128], F32),
            ("st_conv", [2, NS, 3, 3072], F32),
            ("ada_w", [4, D, 6 * D], F32), ("ada_b", [4, 6 * D], F32), ("ln_g", [4, 2, D], F32),
            ("ln_b", [4, 2, D], F32), ("mlp_up", [4, D, DFF], F32), ("mlp_down", [4, DFF, D], F32),
            ("even_w_in", [2, D, EVEN_IN], F32), ("even_w_out", [2, D, D], F32), ("ret_gn", [2, 512], F32),
            ("odd_w_in", [2, D, ODD_IN], F32), ("odd_w_out", [2, D, D], F32), ("odd_conv", [2, 4, 3072], F32),
            ("odd_a_log", [2, 8], F32), ("odd_dt_bias", [2, 8], F32), ("odd_gn", [2, 128], F32),
            ("ident", [128, 128], F32), ("ones", [128, 128], F32), ("triu", [128, 128], F32),
            ("mvinc", [128, 128], F32), ("strict", [128, 128], F32), ("cmask", [128, 128], F32),
            ("dmatT", [128, 512], F32), ("qdecB", [128, 512], F32), ("kdecC", [128, 4], F32),
            ("cdecB", [128, 512], F32), ("gam1B", [128, 512], F32), ("ropeP", [NT, 128, 288], F32),
            ("ropeS", [NS, 288], F32), ("delta16", [128, 256], F32), ("iotap", [128, 1], F32),
        ]:
            d[nm] = K.dram(nm, sh, dt, EI)
        o = {}
        for nm, sh in [
            ("y_p", [NTOK, D]), ("y_s", [NS, D]), ("kv_p", [2, NTOK, 320]), ("kv_s", [2, NS, 320]),
            ("ret_p", [2, 4, 128, 128]), ("ret_s", [2, NS, 4, 128, 128]), ("ssm_p", [2, 8, 128, 128]),
            ("ssm_s", [2, NS, 8, 128, 128]), ("conv_p", [2, 3, 3072]), ("conv_s", [2, NS, 3, 3072]),
        ]:
            o[nm] = K.dram(nm, sh, F32, EO)
        x_a = K.dram("x_a", [NROW, D], F32, IN)
        x_b = K.dram("x_b", [NROW, D], F32, IN)
        mod_d = K.dram("mod_d", [4, 17, 6 * D], F32, IN)
        proj_d = K.dram("proj_d", [NROW + 3, ODD_IN], F32, IN)
        tb_xa = [Buf() for _ in range(NT + 1)]
        tb_xb = [Buf() for _ in range(NT + 1)]
        tb_pj = [Buf() for _ in range(NT + 2)]

        def rows(t):
            return (t * 128, 128) if t < NT else (NTOK, NS)

        cst = {}
        for nm, sh in [("ident", [128, 128]), ("ones", [128, 128]), ("triu", [128, 128]), ("mvinc", [128, 128]),
                       ("strict", [128, 128]), ("cmask", [128, 128]), ("delta16", [128, 256]), ("iotap", [128, 1])]:
            cst[nm] = K.sb(es, "c_" + nm, sh, F32)
            K.dma("sp", cst[nm][:], d[nm][:], r=[], w=[cst[nm]])
        identb = K.sb(es, "identb", [128, 128], BF16)
        K.cp("dve", identb[:], cst["ident"][:], r=[cst["ident"]], w=[identb])
        ident = cst["ident"]
        PS = [TL(es.enter_context(nc.psum_tensor("ps%d" % i, [128, 1024], F32)), 2) for i in range(4)]
        K.PSG = PS[:2]
        PSO = [PS[2], PS[3]]
        lnG = K.sb(es, "lnG", [128, D], F32)
        lnB = K.sb(es, "lnB", [128, D], F32)
        modP = K.sb(es, "modP", [128, 3 * D], F32)

        def load_phase_consts(l, which):
            c0 = which * 3 * D
            K.dma("sp", modP[:], mod_d[l, 16:17, c0:c0 + 3 * D].partition_broadcast(128), r=[mod_d], w=[modP])
            K.dma("sp", lnG[:], d["ln_g"][l, which:which + 1, :].partition_broadcast(128), r=[], w=[lnG])
            K.dma("sp", lnB[:], d["ln_b"][l, which:which + 1, :].partition_broadcast(128), r=[], w=[lnB])
            K.ts("pool", modP[:, D:3 * D], modP[:, D:3 * D], 1.0, ALU.add, r=[modP], w=[modP])

        def load_sample_mod(l, which):
            c0 = which * 3 * D
            K.dma("sp", modP[0:NS, :], mod_d[l, 0:NS, c0:c0 + 3 * D], r=[mod_d], w=[modP])
            K.ts("pool", modP[0:NS, D:3 * D], modP[0:NS, D:3 * D], 1.0, ALU.add, r=[modP], w=[modP])

        def mod_of(t):
            return modP

        with K.scope() as p0:
            c17t = K.sb(p0, "c17t", [17, D], F32)
            scT = K.sb(p0, "scT", [128, 8, 17], BF16)
            K.dma("sp", c17t[:], d["c17"][:], r=[], w=[c17t])
            K.act(c17t[:], c17t[:], AF.Silu, r=[c17t], w=[c17t])
            pt = K.ps()
            for k in range(8):
                K.tr(pt[0:128, k * 32:k * 32 + 17], c17t[:, k * 128:(k + 1) * 128], ident[0:17, 0:17], r=[c17t, ident], w=[(pt, 0)])
            K.cp("dve", scT[:], pt[:, 0:256].rearrange("p (k c) -> p k c", c=32)[:, :, 0:17], r=[(pt, 0)], w=[scT])
            aw = [K.sb(p0, "aw%d" % i, [128, 8, 512], BF16) for i in range(2)]
            ab = [K.sb(p0, "ab%d" % i, [17, 512], F32) for i in range(2)]
            mo = [K.sb(p0, "mo%d" % i, [17, 512], F32) for i in range(2)]
            it = 0
            for l in range(DEPTH):
                for cb in range(12):
                    a, b_, m_ = aw[it % 2], ab[it % 2], mo[it % 2]
                    cs = slice(cb * 512, (cb + 1) * 512)
                    K.dma("pool", a[:], d["ada_w"][l, :, cs].rearrange("(k p) n -> p k n", p=128), r=[], w=[a])
                    K.dma("sp", b_[:], d["ada_b"][l:l + 1, cs].partition_broadcast(17), r=[], w=[b_])
                    pt = K.ps()
                    for k in range(8):
                        K.mm(pt[0:17, 0:512], scT[:, k, :], a[:, k, :], k == 0, k == 7, r=[scT, a], w=[(pt, 0)])
                    K.tt("dve", m_[:], pt[0:17, 0:512], b_[:], ALU.add, r=[(pt, 0), b_], w=[m_])
                    K.dma("sp", mod_d[l, :, cs], m_[:], r=[m_], w=[mod_d])
                    it += 1

        def load_weights_bf16(wt, src, ncols):
            nk = src.shape[0] // 128
            for k in range(nk):
                K.dma("pool", wt[:, k, 0:ncols], src[k * 128:(k + 1) * 128, :], r=[], w=[wt])

        def modulate_T(pp, xt, t, n, tmp, hm, hmT):
            md = mod_of(t)
            K.tt("dve", tmp[0:n, :], xt[0:n, :], md[0:n, D:2 * D], ALU.mult, r=[xt, md], w=[tmp])
            K.tt("pool", hm[0:n, :], tmp[0:n, :], md[0:n, 0:D], ALU.add, r=[tmp, md], w=[hm])
            to_T(hm, n, hmT)

        def to_T(hm, n, hmT):
            pt = K.ps()
            pb = pt[:].bitcast(BF16)
            for k in range(8):
                K.tr(pb[:, k * 128:k * 128 + n], hm[0:n, k * 128:(k + 1) * 128], identb[0:n, 0:n], r=[hm, identb], w=[(pt, 0)])
            K.cp("act", hmT[:, :, 0:n], pb[:, 0:1024].rearrange("p (k c) -> p k c", c=128)[:, :, 0:n], r=[(pt, 0)], w=[hmT])

        def layer_norm_out(z, n, xo, scr, st):
            K.memset("pool", st[0:n, :], 0.0, w=[st])
            K.red("dve", st[0:n, 0:1], z[0:n, :], r=[z], w=[st])
            K.ts("dve", st[0:n, 1:2], st[0:n, 0:1], -1.0 / D, ALU.mult, r=[st], w=[st])
            K.ts("dve", scr[0:n, :], z[0:n, :], st[0:n, 1:2], ALU.add, r=[z, st], w=[scr])
            K.act(xo[0:n, :], scr[0:n, :], AF.Square, r=[scr, st], w=[xo, st], acc=st[0:n, 2:3])
            K.rsqrt("dve", st[0:n, 3:4], st[0:n, 2:3], r=[st], w=[st], mult=1.0 / D, add=LN_EPS)
            K.stt("dve", xo[0:n, :], scr[0:n, :], st[0:n, 3:4], lnG[0:n, :], ALU.mult, ALU.mult, r=[scr, st, lnG], w=[xo])
            K.tt("pool", xo[0:n, :], xo[0:n, :], lnB[0:n, :], ALU.add, r=[xo, lnB], w=[xo])

        def residual_ln(py, t, n, xt, z, xo, scr, st):
            md = mod_of(t)
            K.tt("dve", z[0:n, :], py[0:n, :], md[0:n, 2 * D:3 * D], ALU.mult, r=[py, md], w=[z])
            K.stt("dve", z[0:n, :], xt[0:n, :], ALPHA, z[0:n, :], ALU.mult, ALU.add, r=[xt, z], w=[z])
            layer_norm_out(z, n, xo, scr, st)

        K.fn = dict(rows=rows, load_phase_consts=load_phase_consts, load_sample_mod=load_sample_mod, mod_of=mod_of, load_weights_bf16=load_weights_bf16,
                    modulate_T=modulate_T, to_T=to_T, residual_ln=residual_ln)
        K.ctx = dict(d=d, o=o, x_a=x_a, x_b=x_b, proj_d=proj_d, tb_xa=tb_xa, tb_xb=tb_xb, tb_pj=tb_pj, cst=cst,
                     ident=ident, identb=identb, PS=PS, PSO=PSO, NT=NT, NPG=NPG, NTOK=NTOK, NROW=NROW,
                     TOPK_P=TOPK_P, TOPK_S=TOPK_S, modP=modP, mod_d=mod_d, cfg=cfg)

        with K.scope() as pz:
            zt = K.sb(pz, "zt", [3, ODD_IN], F32)
            K.memset("pool", zt[:], 0.0, w=[zt])
            K.dma("sp", proj_d[0:3, :], zt[:], r=[zt], w=[tb_pj[NT + 1]])

        x_in = None
        for l in range(DEPTH):
            even = (l % 2 == 0)
            NIN = EVEN_IN if even else ODD_IN
            w_in_d = d["even_w_in"][l // 2] if even else d["odd_w_in"][l // 2]
            load_phase_consts(l, 0)
            with K.scope() as pa:
                wt = K.sb(pa, "w_in", [128, 8, ODD_IN], BF16)
                load_weights_bf16(wt, w_in_d, NIN)
                xts = [K.sb(pa, "a_x%d" % i, [128, D], F32) for i in range(2)]
                tmps = [K.sb(pa, "a_t%d" % i, [128, D], F32) for i in range(2)]
                hms = [K.sb(pa, "a_h%d" % i, [128, D], BF16) for i in range(2)]
                hmTs = [K.sb(pa, "a_hT%d" % i, [128, 8, 128], BF16) for i in range(2)]
                pjs = [K.sb(pa, "a_pj%d" % i, [128, ODD_IN], F32) for i in range(1)] * 2
                for t in range(NT + 1):
                    r0, n = rows(t)
                    xt, tmp, hm, hmT, pj = xts[t % 2], tmps[t % 2], hms[t % 2], hmTs[t % 2], pjs[t % 2]
                    if t == NT:
                        load_sample_mod(l, 0)
                    if l == 0:
                        src = d["xp"][r0:r0 + n, :] if t < NT else d["xs"][:, :]
                        K.dma("sp", xt[0:n, :], src, r=[], w=[xt])
                    else:
                        K.dma("sp", xt[0:n, :], x_a[r0:r0 + n, :], r=[tb_xa[t]], w=[xt])
                    modulate_T(pa, xt, t, n, tmp, hm, hmT)
                    nb = (NIN + 511) // 512
                    for cb in range(nb):
                        c0 = cb * 512
                        wd = min(512, NIN - c0)
                        pt = K.ps()
                        hb = cb % 2
                        for k in range(8):
                            K.mm(pt[0:n, hb * 512:hb * 512 + wd], hmT[:, k, 0:n], wt[:, k, c0:c0 + wd], k == 0, k == 7, r=[hmT, wt], w=[(pt, hb)])
                        K.cp(K.evq(), pj[0:n, c0:c0 + wd], pt[0:n, hb * 512:hb * 512 + wd], r=[(pt, hb)], w=[pj])
                    K.dma("sp", proj_d[3 + r0:3 + r0 + n, 0:NIN], pj[0:n, 0:NIN], r=[pj], w=[tb_pj[t]])
            if cfg.get("mix", "real") == "stub":
                phase_b_stub(K, l)
            elif even:
                phase_b_even(K, l)
            else:
                phase_b_odd(K, l)
            load_phase_consts(l, 1)
            with K.scope() as pf:
                wu = K.sb(pf, "w_up", [128, 8, DFF], BF16)
                wd_ = K.sb(pf, "w_dn", [128, 32, D], BF16)
                load_weights_bf16(wu, d["mlp_up"][l], DFF)
                load_weights_bf16(wd_, d["mlp_down"][l], D)
                xts = [K.sb(pf, "f_x%d" % i, [128, D], F32) for i in range(2)]
                tmps = [K.sb(pf, "f_t%d" % i, [128, D], F32) for i in range(1)] * 2
                hms = [K.sb(pf, "f_h%d" % i, [128, D], BF16) for i in range(1)] * 2
                hmTs = [K.sb(pf, "f_hT%d" % i, [128, 8, 128], BF16) for i in range(2)]
                uTs = [K.sb(pf, "f_uT%d" % i, [128, 32, 128], BF16) for i in range(1)] * 2
                rl = [K.sb(pf, "f_rl%d" % i, [128, 1024], F32) for i in range(1)] * 2
                xos = [K.sb(pf, "f_xo%d" % i, [128, D], F32) for i in range(1)] * 2
                sts = [K.sb(pf, "f_st%d" % i, [128, 4], F32) for i in range(2)]
                last = (l == DEPTH - 1)
                for t in range(NT + 1):
                    r0, n = rows(t)
                    xt, tmp, hm, hmT, uT, xo, st = xts[t % 2], tmps[t % 2], hms[t % 2], hmTs[t % 2], uTs[t % 2], xos[t % 2], sts[t % 2]
                    if t == NT:
                        load_sample_mod(l, 1)
                    K.dma("sp", xt[0:n, :], x_b[r0:r0 + n, :], r=[tb_xb[t]], w=[xt])
                    modulate_T(pf, xt, t, n, tmp, hm, hmT)
                    for g in range(4):
                        pt = K.ps()
                        for j in range(8):
                            fc = g * 8 + j
                            for k in range(8):
                                K.mm(pt[:, j * 128:j * 128 + n], wu[:, k, fc * 128:(fc + 1) * 128], hmT[:, k, 0:n], k == 0, k == 7, r=[wu, hmT], w=[(pt, j // 4)])
                        r_ = rl[g % 2]
                        pv = pt[:].rearrange("p (j c) -> p j c", c=128)[:, :, 0:n]
                        rv = r_[:].rearrange("p (j c) -> p j c", c=128)[:, :, 0:n]
                        K.act(rv, pv, AF.Relu, r=[pt], w=[r_])
                        K.tt("pool" if g % 2 else "dve", uT[:, g * 8:(g + 1) * 8, 0:n], rv, rv, ALU.mult, r=[r_], w=[uT])
                    py = K.ps()
                    for hb in range(2):
                        for k in range(32):
                            K.mm(py[0:n, hb * 512:(hb + 1) * 512], uT[:, k, 0:n], wd_[:, k, hb * 512:(hb + 1) * 512], k == 0, k == 31, r=[uT, wd_], w=[(py, hb)])
                    residual_ln(py, t, n, xt, tmp, xo, tmp, st)
                    if last:
                        dst = o["y_p"][r0:r0 + n, :] if t < NT else o["y_s"][:, :]
                        K.dma("sp", dst, xo[0:n, :], r=[xo], w=[])
                    else:
                        K.dma("sp", x_a[r0:r0 + n, :], xo[0:n, :], r=[xo], w=[tb_xa[t]])
        S.finish()
    return nc


def phase_b_stub(K, l):
    c = K.ctx
    f = K.fn
    f["load_phase_consts"](l, 0)
    d, NT = c["d"], c["NT"]
    even = (l % 2 == 0)
    with K.scope() as pb:
        wo = K.sb(pb, "w_out", [128, 8, D], BF16)
        f["load_weights_bf16"](wo, (d["even_w_out"] if even else d["odd_w_out"])[l // 2], D)
        xt = K.sb(pb, "b_x", [128, D], F32)
        pj = K.sb(pb, "b_pj", [128, D], F32)
        cat = K.sb(pb, "b_cat", [128, D], BF16)
        catT = K.sb(pb, "b_catT", [128, 8, 128], BF16)
        z = K.sb(pb, "b_z", [128, D], F32)
        xo = K.sb(pb, "b_xo", [128, D], F32)
        st = K.sb(pb, "b_st", [128, 4], F32)
        for t in range(NT + 1):
            r0, n = f["rows"](t)
            if l == 0:
                src = d["xp"][r0:r0 + n, :] if t < NT else d["xs"][:, :]
                K.dma("sp", xt[0:n, :], src, r=[], w=[xt])
            else:
                K.dma("sp", xt[0:n, :], c["x_a"][r0:r0 + n, :], r=[c["tb_xa"][t]], w=[xt])
            if t == NT:
                f["load_sample_mod"](l, 0)
            K.dma("sp", pj[0:n, :], c["proj_d"][3 + r0:3 + r0 + n, 0:D], r=[c["tb_pj"][t]], w=[pj])
            K.cp("dve", cat[0:n, :], pj[0:n, :], r=[pj], w=[cat])
            out_proj_ln(K, t, n, cat, catT, wo, xt, z, xo, st)


def out_proj_ln(K, t, n, cat, catT, wo, xt, z, xo, st):
    c = K.ctx
    f = K.fn
    f["to_T"](cat, n, catT)
    py = K.ps()
    for hb in range(2):
        for k in range(8):
            K.mm(py[0:n, hb * 512:(hb + 1) * 512], catT[:, k, 0:n], wo[:, k, hb * 512:(hb + 1) * 512], k == 0, k == 7, r=[catT, wo], w=[(py, hb)])
    f["residual_ln"](py, t, n, xt, z, xo, z, st)
    r0, _ = f["rows"](t)
    K.dma("sp", c["x_b"][r0:r0 + n, :], xo[0:n, :], r=[xo], w=[c["tb_xb"][t]])


def rope_ops(K, n, src, H, half, rp, c0, t1, t2, dsts, r_src):
    cosb = rp[0:n, c0:c0 + half].unsqueeze(1).unsqueeze(1).to_broadcast([n, H, 2, half])
    sinb = rp[0:n, c0 + half:c0 + 2 * half].unsqueeze(1).to_broadcast([n, H, half])
    nsinb = rp[0:n, c0 + 2 * half:c0 + 3 * half].unsqueeze(1).to_broadcast([n, H, half])
    t1v = t1[0:n, 0:H * 2 * half].rearrange("p (h a c) -> p h a c", h=H, a=2)
    t2v = t2[0:n, 0:H * 2 * half].rearrange("p (h a c) -> p h a c", h=H, a=2)
    K.tt("dve", t1v, src, cosb, ALU.mult, r=r_src + [rp], w=[t1])
    K.tt("pool", t2v[:, :, 0, :], src[:, :, 1, :], nsinb, ALU.mult, r=r_src + [rp], w=[t2])
    K.tt("pool", t2v[:, :, 1, :], src[:, :, 0, :], sinb, ALU.mult, r=r_src + [rp], w=[t2])
    for h0, h1, dap, dt_ in dsts:
        K.tt("dve", dap, t1v[:, h0:h1], t2v[:, h0:h1], ALU.add, r=[t1, t2], w=[dt_])


def phase_b_even(K, l):
    c = K.ctx
    f = K.fn
    d, o, NT, NPG = c["d"], c["o"], c["NT"], c["NPG"]
    PSO, cst, ident, identb = c["PSO"], c["cst"], c["ident"], c["identb"]
    i = l // 2
    TOPK = c["TOPK_P"]
    NIT = 18
    SC_ATT = 128 ** -0.5
    f["load_phase_consts"](l, 0)
    with K.scope() as pb:
        sb = lambda nm, sh, dt=F32: K.sb(pb, "e_" + nm, sh, dt)
        wo = sb("wo", [128, 8, D], BF16)
        f["load_weights_bf16"](wo, d["even_w_out"][i], D)
        kc = {}
        for nm in ("dmatT", "qdecB", "cdecB", "gam1B"):
            kc[nm] = sb(nm, [128, 512])
            K.dma("sp", kc[nm][:], d[nm][:], r=[], w=[kc[nm]])
        kdecC = sb("kdecC", [128, 4])
        K.dma("sp", kdecC[:], d["kdecC"][:], r=[], w=[kdecC])
        gnB = sb("gnB", [128, 512])
        K.dma("sp", gnB[:], d["ret_gn"][i:i + 1, :].partition_broadcast(128), r=[], w=[gnB])
        Sret = sb("Sret", [128, 512])
        K.memset("pool", Sret[:], 0.0, w=[Sret])
        pj = sb("pj", [128, EVEN_IN])
        xt = sb("xt", [128, D])
        rp = sb("rp", [128, 288])
        qk_r = sb("qk_r", [128, 1024])
        aq_rb = sb("aq_rb", [128, 512], BF16)
        kvrow = sb("kvrow", [128, 320])
        iq_r = sb("iq_r", [128, 256])
        t1 = sb("t1", [128, 1024])
        t2 = sb("t2", [128, 1024])
        qT, qTd, kT, AT, kd, cen, sq, sg = [sb(nm, [128, 512]) for nm in ("qT", "qTd", "kT", "AT", "kd", "cen", "sq", "sg")]
        akb = sb("akb", [128, 128], BF16)
        ikb = sb("ikb", [128, 64], BF16)
        iqsb = sb("iqsb", [128, 256], BF16)
        aqT = sb("aqT", [128, 512], BF16)
        iqsT = sb("iqsT", [64, 512], BF16)
        rls = [sb("rl%d" % j, [128, 512]) for j in range(2)]
        Es = [sb("E%d" % j, [128, 512], BF16) for j in range(2)]
        PTs = [sb("PT%d" % j, [128, 512], BF16) for j in range(2)]
        mTss = [sb("mTs%d" % j, [128, 128], BF16) for j in range(2)]
        cat = sb("cat", [128, D], BF16)
        catT = sb("catT", [128, 8, 128], BF16)
        z = sb("z", [128, D])
        xo = sb("xo", [128, D])
        st = sb("st", [128, 4])
        s4 = sb("s4", [128, 16])
        aw = sb("aw", [128, 8])
        bs = sb("bs", [128, 8])
        cnt = sb("cnt", [128, NIT])
        rs = sb("rs", [128, 4])

        def x_load(t, n, r0):
            if l == 0:
                src = d["xp"][r0:r0 + n, :] if t < NT else d["xs"][:, :]
                K.dma("sp", xt[0:n, :], src, r=[], w=[xt])
            else:
                K.dma("sp", xt[0:n, :], c["x_a"][r0:r0 + n, :], r=[c["tb_xa"][t]], w=[xt])

        def proj_rope(t, n, r0, rp_src, aq_dst=None):
            aq_dst = aq_rb if aq_dst is None else aq_dst
            K.dma("sp", pj[0:n, :], c["proj_d"][3 + r0:3 + r0 + n, 0:EVEN_IN], r=[c["tb_pj"][t]], w=[pj])
            K.dma("sp", rp[0:n, :], rp_src, r=[], w=[rp])
            v4 = lambda ap, H, half: ap.rearrange("p (h a c) -> p h a c", h=H, a=2)
            rope_ops(K, n, v4(pj[0:n, 0:1024], 8, 64), 8, 64, rp, 0, t1, t2,
                     [(0, 8, v4(qk_r[0:n, :], 8, 64), qk_r)], [pj])
            rope_ops(K, n, v4(pj[0:n, 2048:2688], 5, 64), 5, 64, rp, 0, t1, t2,
                     [(0, 4, v4(aq_dst[0:n, :], 4, 64), aq_dst), (4, 5, v4(kvrow[0:n, 0:128], 1, 64), kvrow)], [pj])
            rope_ops(K, n, v4(pj[0:n, 2816:3136], 5, 32), 5, 32, rp, 192, t1, t2,
                     [(0, 4, v4(iq_r[0:n, :], 4, 32), iq_r), (4, 5, v4(kvrow[0:n, 256:320], 1, 32), kvrow)], [pj])
            K.cp("act", kvrow[0:n, 128:256], pj[0:n, 2688:2816], r=[pj], w=[kvrow])
            K.act(aw[0:n, 0:4], pj[0:n, 3136:3140], AF.Abs, r=[pj], w=[aw], scale=1.0 / 16)
            K.ts("dve", aw[0:n, 4:8], pj[0:n, 3136:3140], 0.0, ALU.is_ge, r=[pj], w=[aw], s2=2.0, op1=ALU.mult)
            K.ts("dve", aw[0:n, 4:8], aw[0:n, 4:8], -1.0, ALU.add, r=[aw], w=[aw])

        def head_norm_gate(n, src, rsrc):
            pov = src.rearrange("p (h e) -> p h e", h=4)
            cv = cen[0:n, :].rearrange("p (h e) -> p h e", h=4)
            K.red("dve", s4[0:n, 0:4], pov, r=rsrc, w=[s4])
            K.ts("dve", s4[0:n, 4:8], s4[0:n, 0:4], -1.0 / 128, ALU.mult, r=[s4], w=[s4])
            K.tt("dve", cv, pov, s4[0:n, 4:8].unsqueeze(2).to_broadcast([n, 4, 128]), ALU.add, r=rsrc + [s4], w=[cen])
            K.tt("pool", sq[0:n, :], cen[0:n, :], cen[0:n, :], ALU.mult, r=[cen], w=[sq])
            K.red("dve", s4[0:n, 8:12], sq[0:n, :].rearrange("p (h e) -> p h e", h=4), r=[sq], w=[s4])
            K.rsqrt("dve", s4[0:n, 12:16], s4[0:n, 8:12], r=[s4], w=[s4], mult=1.0 / 128, add=LN_EPS)
            K.act(sg[0:n, :], pj[0:n, 1536:2048], AF.Silu, r=[pj], w=[sg])
            K.tt("pool", sg[0:n, :], sg[0:n, :], gnB[0:n, :], ALU.mult, r=[sg, gnB], w=[sg])
            K.tt("dve", cv, cv, s4[0:n, 12:16].unsqueeze(2).to_broadcast([n, 4, 128]), ALU.mult, r=[cen, s4], w=[cen])
            K.tt("dve", cat[0:n, 0:512], cen[0:n, :], sg[0:n, :], ALU.mult, r=[cen, sg], w=[cat])

        with K.scope() as pp:
            akT_all = K.sb(pp, "e_akT", [128, NT * 128], BF16)
            ikT_all = K.sb(pp, "e_ikT", [64, NT * 128], BF16)
            Vall = K.sb(pp, "e_Vall", [128, NT, 132], BF16)
            K.memset("pool", Vall[:], 1.0, w=[Vall])
            score = K.sb(pp, "e_score", [128, NT * 128], F32)
            junk = K.sb(pp, "e_junk", [128, NT * 128], F32)
            maskb = K.sb(pp, "e_maskb", [128, NT * 128], BF16)
            STOP = c["cfg"].get("stop", 99)
            for t in range(NT):
                r0, n = t * 128, 128
                x_load(t, n, r0)
                if STOP <= 0:
                    continue
                proj_rope(t, n, r0, d["ropeP"][t])
                if STOP <= 1:
                    continue
                K.dma("sp", o["kv_p"][i, r0:r0 + n, :], kvrow[:], r=[kvrow], w=[])
                pt = K.ps()
                for h in range(8):
                    K.tr(pt[:, h * 128:(h + 1) * 128], qk_r[:, h * 128:(h + 1) * 128], ident[:], r=[qk_r, ident], w=[(pt, h // 4)])
                K.cp("act", qT[:], pt[:, 0:512], r=[(pt, 0)], w=[qT])
                K.tt("dve", qTd[:], pt[:, 0:512], kc["qdecB"][:], ALU.mult, r=[(pt, 0), kc["qdecB"]], w=[qTd])
                K.cp("act", kT[:], pt[:, 512:1024], r=[(pt, 1)], w=[kT])
                pa = K.ps()
                for h in range(4):
                    hs = slice(h * 128, (h + 1) * 128)
                    K.mm(pa[:, hs], kT[:, hs], qT[:, hs], True, True, r=[kT, qT], w=[(pa, 0)])
                K.tt("dve", AT[:], pa[:, 0:512], kc["dmatT"][:], ALU.mult, r=[(pa, 0), kc["dmatT"]], w=[AT])
                if STOP <= 2:
                    continue
                K.tt("pool", kd[:].rearrange("p (h e) -> p h e", h=4), qk_r[:, 512:1024].rearrange("p (h e) -> p h e", h=4),
                     kdecC[:].unsqueeze(2).to_broadcast([128, 4, 128]), ALU.mult, r=[qk_r, kdecC], w=[kd])
                po = K.ps()
                for h in range(4):
                    hs = slice(h * 128, (h + 1) * 128)
                    K.mm(po[:, hs], AT[:, hs], pj[:, 1024 + h * 128:1024 + (h + 1) * 128], True, False, r=[AT, pj], w=[(po, 0)])
                    K.mm(po[:, hs], qTd[:, hs], Sret[:, hs], False, True, r=[qTd, Sret], w=[(po, 0)])
                pss = K.ps()
                for h in range(4):
                    hs = slice(h * 128, (h + 1) * 128)
                    K.mm(pss[:, hs], kd[:, hs], pj[:, 1024 + h * 128:1024 + (h + 1) * 128], True, True, r=[kd, pj], w=[(pss, 0)])
                K.tt("dve", Sret[:], Sret[:], kc["cdecB"][:], ALU.mult, r=[Sret, kc["cdecB"]], w=[Sret])
                K.tt("dve", Sret[:], Sret[:], pss[:, 0:512], ALU.add, r=[Sret, (pss, 0)], w=[Sret])
                if STOP <= 3:
                    continue
                head_norm_gate(n, po[0:n, 0:512], [(po, 0)])
                if STOP <= 4:
                    continue
                if t == NT - 1:
                    for h in range(4):
                        K.dma("sp", o["ret_p"][i, h], Sret[:, h * 128:(h + 1) * 128], r=[Sret], w=[])
                K.cp("pool", akb[:], kvrow[:, 0:128], r=[kvrow], w=[akb])
                K.cp("pool", ikb[:], kvrow[:, 256:320], r=[kvrow], w=[ikb])
                K.tt("dve", iqsb[:].rearrange("p (h e) -> p h e", h=4), iq_r[:].rearrange("p (h e) -> p h e", h=4),
                     aw[:, 0:4].unsqueeze(2).to_broadcast([128, 4, 64]), ALU.mult, r=[iq_r, aw], w=[iqsb])
                if STOP <= 4.2:
                    continue
                pt = K.ps()
                pbb = pt[:].bitcast(BF16)
                K.tr(pbb[:, 0:128], akb[:], identb[:], r=[akb, identb], w=[(pt, 0)])
                K.tr(pbb[0:64, 128:256], ikb[:], identb[:], r=[ikb, identb], w=[(pt, 0)])
                for h in range(4):
                    K.tr(pbb[:, 256 + h * 128:256 + (h + 1) * 128], aq_rb[:, h * 128:(h + 1) * 128], identb[:], r=[aq_rb, identb], w=[(pt, 0)])
                    K.tr(pbb[0:64, 1024 + h * 128:1024 + (h + 1) * 128], iqsb[:, h * 64:(h + 1) * 64], identb[:], r=[iqsb, identb], w=[(pt, 1)])
                if STOP <= 4.4:
                    continue
                K.cp("act", akT_all[:, r0:r0 + 128], pbb[:, 0:128], r=[(pt, 0)], w=[akT_all])
                K.cp("act", ikT_all[:, r0:r0 + 128], pbb[0:64, 128:256], r=[(pt, 0)], w=[ikT_all])
                K.cp("act", aqT[:], pbb[:, 256:768], r=[(pt, 0)], w=[aqT])
                K.cp("act", iqsT[:], pbb[0:64, 1024:1536], r=[(pt, 1)], w=[iqsT])
                K.cp("pool", Vall[:, t, 0:128], pj[:, 2688:2816], r=[pj], w=[Vall])
                if STOP <= 5:
                    continue
                n_k = (t + 1) * 128
                DBG = c["cfg"].get("dbg", "")
                for kb in range((n_k + 511) // 512 if "noidx" not in DBG else 0):
                    k0 = kb * 512
                    wb = min(512, n_k - k0)
                    pts = [K.ps(), K.ps()]
                    for h in range(4):
                        pp, hb = pts[h // 2], h % 2
                        K.mm(pp[:, hb * 512:hb * 512 + wb], iqsT[:, h * 128:(h + 1) * 128], ikT_all[:, k0:k0 + wb], True, True, r=[iqsT, ikT_all], w=[(pp, hb)])
                        rl = rls[h % 2]
                        K.act(rl[:, 0:wb], pp[:, hb * 512:hb * 512 + wb], AF.Relu, r=[(pp, hb)], w=[rl])
                        if h == 0:
                            K.ts("dve", score[:, k0:k0 + wb], rl[:, 0:wb], aw[:, 4:5], ALU.mult, r=[rl, aw], w=[score])
                        else:
                            K.stt("dve", score[:, k0:k0 + wb], rl[:, 0:wb], aw[:, 4 + h:5 + h], score[:, k0:k0 + wb], ALU.mult, ALU.add, r=[rl, aw, score], w=[score])
                if "noidx" in DBG:
                    K.memset("dve", score[:, 0:n_k], 0.0, w=[score])
                if n_k > TOPK:
                    K.red("dve", bs[:, 0:1], score[:, 0:n_k], r=[score], w=[bs], op=ALU.max)
                    K.red("dve", bs[:, 1:2], score[:, 0:n_k], r=[score], w=[bs], op=ALU.min)
                    K.tt("dve", bs[:, 2:3], bs[:, 0:1], bs[:, 1:2], ALU.subtract, r=[bs], w=[bs])
                else:
                    K.memset("dve", bs[:, 1:2], -1e29, w=[bs])
                K.tt("dve", score[:, r0:r0 + 128], score[:, r0:r0 + 128], cst["cmask"][:], ALU.add, r=[score, cst["cmask"]], w=[score])
                if n_k > TOPK:
                    K.memset("pool", cnt[:], 0.0, w=[cnt])
                    for it in range(NIT):
                        K.ts("dve", bs[:, 3:4], bs[:, 2:3], 0.5 ** (it + 1), ALU.mult, r=[bs], w=[bs])
                        K.tt("dve", bs[:, 4:5], bs[:, 1:2], bs[:, 3:4], ALU.add, r=[bs], w=[bs])
                        K.ts("dve", junk[:, 0:n_k], score[:, 0:n_k], bs[:, 4:5], ALU.is_ge, r=[score, bs, cnt], w=[junk, cnt], s2=0.0, op1=ALU.add, acc=cnt[:, it:it + 1])
                        K.ts("dve", bs[:, 5:6], cnt[:, it:it + 1], float(TOPK), ALU.is_ge, r=[cnt], w=[bs])
                        K.stt("dve", bs[:, 1:2], bs[:, 5:6], bs[:, 3:4], bs[:, 1:2], ALU.mult, ALU.add, r=[bs], w=[bs])
                K.ts("dve", maskb[:, 0:n_k], score[:, 0:n_k], bs[:, 1:2], ALU.is_ge, r=[score, bs], w=[maskb])
                if STOP <= 6:
                    continue
                if "noatt" in DBG:
                    for h in range(4):
                        K.mm(PSO[h // 2][:, (h % 2) * 512:(h % 2) * 512 + 130], ident[:], junk[:, 0:130], True, True, r=[ident, junk], w=[(PSO[h // 2], h % 2)])
                for kt in range(t + 1 if "noatt" not in DBG else 0):
                    ks = slice(kt * 128, (kt + 1) * 128)
                    pm = K.ps()
                    pmb = pm[:].bitcast(BF16)
                    K.tr(pmb[:, 0:128], maskb[:, ks], identb[:], r=[maskb, identb], w=[(pm, 0)])
                    K.mm(pm[:, 512:1024], akT_all[:, ks], aqT[:], True, True, r=[akT_all, aqT], w=[(pm, 1)])
                    E, PT = Es[kt % 2], PTs[kt % 2]
                    K.act(E[:], pm[:, 512:1024], AF.Exp, r=[(pm, 1)], w=[E], scale=SC_ATT)
                    mTs = mTss[kt % 2]
                    K.cp("act", mTs[:], pmb[:, 0:128], r=[(pm, 0)], w=[mTs])
                    K.tt("dve" if kt % 2 else "pool", PT[:].rearrange("p (h q) -> p h q", h=4), E[:].rearrange("p (h q) -> p h q", h=4),
                         mTs[:].unsqueeze(1).to_broadcast([128, 4, 128]), ALU.mult, r=[E, mTs], w=[PT])
                    for h in range(4):
                        c0 = (h % 2) * 512
                        K.mm(PSO[h // 2][:, c0:c0 + 130], PT[:, h * 128:(h + 1) * 128], Vall[:, kt, 0:130], kt == 0, kt == t, r=[PT, Vall], w=[(PSO[h // 2], h % 2)])
                for h in range(4):
                    c0 = (h % 2) * 512
                    pso = PSO[h // 2]
                    K.S.op("dve", lambda e, pso=pso, c0=c0, h=h: e.reciprocal(out=rs[:, h:h + 1], in_=pso[:, c0 + 128:c0 + 129]), r=[(pso, h % 2)], w=[rs])
                    K.ts("dve", cat[:, 512 + h * 128:512 + (h + 1) * 128], pso[:, c0:c0 + 128], rs[:, h:h + 1], ALU.mult, r=[(pso, h % 2), rs], w=[cat])
                out_proj_ln(K, t, n, cat, catT, wo, xt, z, xo, st)
        if not c["cfg"].get("skip_sample"):
            sample_even(K, l, pb, locals())


def bisect_thr(K, n, score, n_k, topk, bs, cnt, junk, nit):
    K.red("dve", bs[0:n, 0:1], score[0:n, 0:n_k], r=[score], w=[bs], op=ALU.max)
    K.red("dve", bs[0:n, 1:2], score[0:n, 0:n_k], r=[score], w=[bs], op=ALU.min)
    K.tt("dve", bs[0:n, 2:3], bs[0:n, 0:1], bs[0:n, 1:2], ALU.subtract, r=[bs], w=[bs])
    K.memset("pool", cnt[0:n, :], 0.0, w=[cnt])
    for it in range(nit):
        K.ts("dve", bs[0:n, 3:4], bs[0:n, 2:3], 0.5 ** (it + 1), ALU.mult, r=[bs], w=[bs])
        K.tt("dve", bs[0:n, 4:5], bs[0:n, 1:2], bs[0:n, 3:4], ALU.add, r=[bs], w=[bs])
        K.ts("dve", junk[0:n, 0:n_k], score[0:n, 0:n_k], bs[0:n, 4:5], ALU.is_ge, r=[score, bs, cnt], w=[junk, cnt], s2=0.0, op1=ALU.add, acc=cnt[0:n, it:it + 1])
        K.ts("dve", bs[0:n, 5:6], cnt[0:n, it:it + 1], float(topk), ALU.is_ge, r=[cnt], w=[bs])
        K.stt("dve", bs[0:n, 1:2], bs[0:n, 5:6], bs[0:n, 3:4], bs[0:n, 1:2], ALU.mult, ALU.add, r=[bs], w=[bs])


def sample_even(K, l, pb, L):
    c = K.ctx
    f = K.fn
    d, o, NT, NPG = c["d"], c["o"], c["NT"], c["NPG"]
    cst, ident, PSO = c["cst"], c["ident"], c["PSO"]
    i = l // 2
    TOPK = c["TOPK_S"]
    NIT = 18
    SC_ATT = 128 ** -0.5
    n, t, r0 = NS, NT, c["NTOK"]
    NK = NPG * 128
    pj, xt, qk_r, kvrow, iq_r, t1, t2 = [L[k] for k in ("pj", "xt", "qk_r", "kvrow", "iq_r", "t1", "t2")]
    AT, sq, aw, s4, cat, catT, z, xo, st, wo, kc, bs, cnt = [L[k] for k in ("AT", "sq", "aw", "s4", "cat", "catT", "z", "xo", "st", "wo", "kc", "bs", "cnt")]
    g4 = lambda ap: ap.rearrange("p (h e) -> p h e", h=4)
    with K.scope() as px:
        sb = lambda nm, sh, dt=F32: K.sb(px, "es_" + nm, sh, dt)
        aq_f = sb("aq_f", [NS, 512])
        f["load_sample_mod"](l, 0)
        L["x_load"](t, n, r0)
        L["proj_rope"](t, n, r0, d["ropeS"][:, :], aq_f)
        K.dma("sp", o["kv_s"][i, :, :], kvrow[0:n, :], r=[kvrow], w=[])
        v_ap = pj[0:n, 1024:1536]
        K.tt("dve", sq[0:n, :], qk_r[0:n, 0:512], qk_r[0:n, 512:1024], ALU.mult, r=[qk_r], w=[sq])
        K.red("dve", s4[0:n, 0:4], g4(sq[0:n, :]), r=[sq], w=[s4])
        K.ts("dve", s4[0:n, 0:4], s4[0:n, 0:4], 128 ** -0.5, ALU.mult, r=[s4], w=[s4])
        pt = K.ps()
        for h in range(4):
            K.tr(pt[:, h * 16:(h + 1) * 16], qk_r[0:n, h * 128:(h + 1) * 128], ident[0:n, 0:n], r=[qk_r, ident], w=[(pt, 0)])
        qTs = sb("qTs", [128, 64])
        K.cp("act", qTs[:], pt[:, 0:64], r=[(pt, 0)], w=[qTs])
        K.tt("dve", t1[:, 0:1024].rearrange("p (h s m) -> p h s m", h=4, s=16),
             qTs[:].rearrange("p (h s) -> p h s", h=4).unsqueeze(3).to_broadcast([128, 4, 16, 16]),
             cst["delta16"][:].rearrange("p (s m) -> p s m", s=16).unsqueeze(1).to_broadcast([128, 4, 16, 16]),
             ALU.mult, r=[qTs, cst["delta16"]], w=[t1])
        S0 = [sb("S0%d" % j, [128, 512]) for j in range(2)]
        Sn = [sb("Sn%d" % j, [128, 512]) for j in range(2)]
        km = sb("km", [NS, 512])
        for s in range(NS):
            S0s, Sns = S0[s % 2], Sn[s % 2]
            for h in range(4):
                K.dma("sp", S0s[:, h * 128:(h + 1) * 128], d["st_ret"][i, s, h], r=[], w=[S0s])
            for h in range(4):
                hs = slice(h * 128, (h + 1) * 128)
                K.mm(PSO[h // 2][0:n, (h % 2) * 512:(h % 2) * 512 + 128], t1[:, (h * 16 + s) * 16:(h * 16 + s + 1) * 16], S0s[:, hs], s == 0, s == NS - 1, r=[t1, S0s], w=[(PSO[h // 2], h % 2)])
            K.ts("dve", km[:], qk_r[0:n, 512:1024], ident[0:n, s:s + 1], ALU.mult, r=[qk_r, ident], w=[km], s2=128 ** -0.5, op1=ALU.mult)
            pss = K.ps()
            for h in range(4):
                hs = slice(h * 128, (h + 1) * 128)
                K.mm(pss[:, hs], km[:, hs], pj[0:n, 1024 + h * 128:1024 + (h + 1) * 128], True, True, r=[km, pj], w=[(pss, 0)])
            K.tt("pool", Sns[:], S0s[:], kc["gam1B"][:], ALU.mult, r=[S0s, kc["gam1B"]], w=[Sns])
            K.tt("dve", Sns[:], Sns[:], pss[:, 0:512], ALU.add, r=[Sns, (pss, 0)], w=[Sns])
            for h in range(4):
                K.dma("sp", o["ret_s"][i, s, h], Sns[:, h * 128:(h + 1) * 128], r=[Sns], w=[])
        for h in range(4):
            hs = slice(h * 128, (h + 1) * 128)
            K.tt("dve", AT[0:n, hs], PSO[h // 2][0:n, (h % 2) * 512:(h % 2) * 512 + 128], kc["gam1B"][0:n, hs], ALU.mult, r=[(PSO[h // 2], h % 2), kc["gam1B"]], w=[AT])
        K.tt("pool", g4(sq[0:n, :]), g4(v_ap), s4[0:n, 0:4].unsqueeze(2).to_broadcast([n, 4, 128]), ALU.mult, r=[pj, s4], w=[sq])
        K.tt("dve", AT[0:n, :], AT[0:n, :], sq[0:n, :], ALU.add, r=[AT, sq], w=[AT])
        L["head_norm_gate"](n, AT[0:n, :], [AT])
        iqs = sb("iqs", [NS, 256])
        K.tt("dve", g4(iqs[:]), g4(iq_r[0:n, :]), aw[0:n, 0:4].unsqueeze(2).to_broadcast([n, 4, 64]), ALU.mult, r=[iq_r, aw], w=[iqs])
        pt = K.ps()
        for h in range(4):
            K.tr(pt[:, h * 16:(h + 1) * 16], aq_f[:, h * 128:(h + 1) * 128], ident[0:n, 0:n], r=[aq_f, ident], w=[(pt, 0)])
            K.tr(pt[0:64, 512 + h * 16:512 + (h + 1) * 16], iqs[:, h * 64:(h + 1) * 64], ident[0:n, 0:n], r=[iqs, ident], w=[(pt, 1)])
        aqTa = sb("aqTa", [128, 64])
        iqTa = sb("iqTa", [64, 64])
        K.cp("act", aqTa[:], pt[:, 0:64], r=[(pt, 0)], w=[aqTa])
        K.cp("act", iqTa[:], pt[0:64, 512:576], r=[(pt, 1)], w=[iqTa])
        sgnB = sb("sgnB", [128, NS * 4])
        sg_d = K.dram("sg_d%d" % l, [1, NS * 4], F32, "Internal")
        K.dma("sp", sg_d[0:1, :].rearrange("o (s h) -> (o s) h", h=4), aw[0:n, 4:8], r=[aw], w=[sg_d])
        K.dma("sp", sgnB[:], sg_d[0:1, :].partition_broadcast(128), r=[sg_d], w=[sgnB])
        idxi = sb("idxi", [128, NS * NPG], I32)
        idxf = sb("idxf", [128, NS * NPG])
        K.dma("sp", idxi[:], d["ptab"][:, :].rearrange("(o s) j -> o (s j)", o=1).partition_broadcast(128), r=[], w=[idxi])
        K.cp("dve", idxf[:], idxi[:], r=[idxi], w=[idxf])
        K.ts("dve", idxf[:], idxf[:], 128.0, ALU.mult, r=[idxf, cst["iotap"]], w=[idxf], s2=cst["iotap"][:, 0:1], op1=ALU.add)
        if i > 0:
            K.ts("dve", idxf[:], idxf[:], float(i * c["cfg"]["NPOOL"] * 128), ALU.add, r=[idxf], w=[idxf])
        K.cp("dve", idxi[:], idxf[:], r=[idxf], w=[idxi])
        cache = d["cache_kv"]
        sc_d = K.dram("sc_d%d" % l, [NS, NK], F32, "Internal")
        os_d = K.dram("os_d%d" % l, [4, NS, 132], F32, "Internal")
        kig = [sb("kig%d" % j, [128, NPG, 320]) for j in range(1)] * 2
        kTs = sb("kTs", [128, NK + 4])
        kiT = kTs
        scT = sb("scT", [128, NS * NPG])
        rlS = sb("rlS", [128, NPG * 4])
        scr_t = sb("scr_t", [NPG, 128])
        for s in range(NS):
            kg = kig[s % 2]
            for j in range(NPG):
                col = s * NPG + j
                K.S.dma("pool", lambda e, kg=kg, j=j, col=col: e.indirect_dma_start(
                    out=kg[:, j, :], out_offset=None, in_=cache[:, :],
                    in_offset=bass.IndirectOffsetOnAxis(ap=idxi[:, col:col + 1], axis=0)), r=[idxi], w=[kg])
            for g in range((NPG + 3) // 4):
                pt = K.ps()
                nj = min(4, NPG - g * 4)
                for jj in range(nj):
                    K.tr(pt[0:64, jj * 128:(jj + 1) * 128], kg[:, g * 4 + jj, 256:320], ident[:], r=[kg, ident], w=[(pt, 0)])
                K.cp(K.evq(), kiT[0:64, g * 512:g * 512 + nj * 128], pt[0:64, 0:nj * 128], r=[(pt, 0)], w=[kiT])
            pq = K.ps()
            for j in range(NPG):
                K.mm(pq[:, j * 4:(j + 1) * 4], kiT[0:64, j * 128:(j + 1) * 128], iqTa[:].rearrange("p (h s) -> p h s", h=4)[:, :, s], True, True, r=[kiT, iqTa], w=[(pq, 0)])
            K.act(rlS[:], pq[:, 0:NPG * 4], AF.Relu, r=[(pq, 0)], w=[rlS])
            rv = rlS[:].rearrange("p (j h) -> p j h", h=4)
            K.tt("dve", rv, rv, sgnB[:, s * 4:(s + 1) * 4].unsqueeze(1).to_broadcast([128, NPG, 4]), ALU.mult, r=[rlS, sgnB], w=[rlS])
            K.red("dve", scT[:, s * NPG:(s + 1) * NPG], rv, r=[rlS], w=[scT])
            pt2 = K.ps()
            K.tr(pt2[0:NPG, 0:128], scT[:, s * NPG:(s + 1) * NPG], ident[:], r=[scT, ident], w=[(pt2, 0)])
            K.cp("act", scr_t[:], pt2[0:NPG, 0:128], r=[(pt2, 0)], w=[scr_t])
            K.dma("sp", sc_d[s, :].rearrange("(j p) -> j p", p=128), scr_t[:], r=[scr_t], w=[sc_d])
        K.tt("dve", t2[0:n, 0:256].rearrange("p (h e) -> p h e", h=4), g4(iqs[:]), kvrow[0:n, 256:320].unsqueeze(1).to_broadcast([n, 4, 64]), ALU.mult, r=[iqs, kvrow], w=[t2])
        K.red("dve", s4[0:n, 4:8], t2[0:n, 0:256].rearrange("p (h e) -> p h e", h=4), r=[t2], w=[s4])
        K.ts("dve", s4[0:n, 4:8], s4[0:n, 4:8], 0.0, ALU.max, r=[s4], w=[s4])
        K.tt("dve", s4[0:n, 4:8], s4[0:n, 4:8], aw[0:n, 4:8], ALU.mult, r=[s4, aw], w=[s4])
        srow = sb("srow", [NS, NK + 1])
        jrow = kTs
        K.dma("sp", srow[:, 0:NK], sc_d[:, :], r=[sc_d], w=[srow])
        K.red("dve", srow[:, NK:NK + 1], s4[0:n, 4:8], r=[s4, srow], w=[srow])
        if NK + 1 > TOPK:
            bisect_thr(K, n, srow, NK + 1, TOPK, bs, cnt, jrow, NIT)
        else:
            K.memset("dve", bs[0:n, 1:2], -1e29, w=[bs])
        dth = sb("dth", [NS, NS])
        K.ts("dve", dth[:], ident[0:NS, 0:NS], bs[0:n, 1:2], ALU.mult, r=[ident, bs], w=[dth])
        pq = K.ps()
        K.mm(pq[:, 0:NS], cst["ones"][0:NS, :], dth[:], True, True, r=[cst["ones"], dth], w=[(pq, 0)])
        thrB = sb("thrB", [128, NS])
        K.cp("act", thrB[:], pq[:, 0:NS], r=[(pq, 0)], w=[thrB])
        kvg = kig
        mT = sb("mT", [128, NPG])
        Es = sb("Es", [128, NPG * 4])
        PT2 = sb("PT2", [128, NPG * 4])
        osm = sb("osm", [4, NS, 132])
        K.memset("pool", osm[:], 0.0, w=[osm])
        for s in range(NS):
            kv = kvg[s % 2]
            for j in range(NPG):
                col = s * NPG + j
                K.S.dma("pool", lambda e, kv=kv, j=j, col=col: e.indirect_dma_start(
                    out=kv[:, j, :], out_offset=None, in_=cache[:, :],
                    in_offset=bass.IndirectOffsetOnAxis(ap=idxi[:, col:col + 1], axis=0)), r=[idxi], w=[kv])
            for g in range((NPG + 3) // 4):
                pt = K.ps()
                nj = min(4, NPG - g * 4)
                for jj in range(nj):
                    K.tr(pt[:, jj * 128:(jj + 1) * 128], kv[:, g * 4 + jj, 0:128], ident[:], r=[kv, ident], w=[(pt, 0)])
                K.cp(K.evq(), kTs[:, g * 512:g * 512 + nj * 128], pt[:, 0:nj * 128], r=[(pt, 0)], w=[kTs])
            pq = K.ps()
            for j in range(NPG):
                K.mm(pq[:, j * 4:(j + 1) * 4], kTs[:, j * 128:(j + 1) * 128], aqTa[:].rearrange("p (h s) -> p h s", h=4)[:, :, s], True, True, r=[kTs, aqTa], w=[(pq, 0)])
            K.act(Es[:], pq[:, 0:NPG * 4], AF.Exp, r=[(pq, 0)], w=[Es], scale=SC_ATT)
            K.ts("dve", mT[:], scT[:, s * NPG:(s + 1) * NPG], thrB[:, s:s + 1], ALU.is_ge, r=[scT, thrB], w=[mT])
            K.tt("dve", PT2[:].rearrange("p (j h) -> p j h", h=4), Es[:].rearrange("p (j h) -> p j h", h=4),
                 mT[:].unsqueeze(2).to_broadcast([128, NPG, 4]), ALU.mult, r=[Es, mT], w=[PT2])
            po2 = K.ps()
            for j in range(NPG):
                K.mm(po2[0:4, 0:128], PT2[:, j * 4:(j + 1) * 4], kv[:, j, 128:256], j == 0, j == NPG - 1, r=[PT2, kv], w=[(po2, 0)])
            for j in range(NPG):
                K.mm(po2[0:4, 512:513], PT2[:, j * 4:(j + 1) * 4], cst["ones"][:, 0:1], j == 0, j == NPG - 1, r=[PT2, cst["ones"]], w=[(po2, 1)])
            K.cp("act", osm[:, s, 0:128], po2[0:4, 0:128], r=[(po2, 0)], w=[osm])
            K.cp("act", osm[:, s, 128:129], po2[0:4, 512:513], r=[(po2, 1)], w=[osm])
        ot = sb("ot", [NS, 4, 132])
        K.dma("sp", os_d[:, :, :], osm[:], r=[osm], w=[os_d])
        K.dma("sp", ot[:], os_d[:, :, :].rearrange("h s e -> s h e"), r=[os_d], w=[ot])
        K.tt("dve", g4(t2[0:n, 0:512]), g4(aq_f[:]), kvrow[0:n, 0:128].unsqueeze(1).to_broadcast([n, 4, 128]), ALU.mult, r=[aq_f, kvrow], w=[t2])
        K.red("dve", s4[0:n, 8:12], g4(t2[0:n, 0:512]), r=[t2], w=[s4])
        K.act(s4[0:n, 8:12], s4[0:n, 8:12], AF.Exp, r=[s4], w=[s4], scale=SC_ATT)
        K.tt("dve", bs[0:n, 6:7], srow[:, NK:NK + 1], bs[0:n, 1:2], ALU.is_ge, r=[srow, bs], w=[bs])
        K.ts("dve", s4[0:n, 8:12], s4[0:n, 8:12], bs[0:n, 6:7], ALU.mult, r=[s4, bs], w=[s4])
        K.tt("dve", g4(t1[0:n, 0:512]), kvrow[0:n, 128:256].unsqueeze(1).to_broadcast([n, 4, 128]),
             s4[0:n, 8:12].unsqueeze(2).to_broadcast([n, 4, 128]), ALU.mult, r=[kvrow, s4], w=[t1])
        K.tt("dve", g4(t1[0:n, 0:512]), g4(t1[0:n, 0:512]), ot[:, :, 0:128], ALU.add, r=[t1, ot], w=[t1])
        K.tt("dve", s4[0:n, 12:16], ot[:, :, 128], s4[0:n, 8:12], ALU.add, r=[ot, s4], w=[s4])
        K.S.op("dve", lambda e: e.reciprocal(out=s4[0:n, 12:16], in_=s4[0:n, 12:16]), r=_bufs([s4]), w=_bufs([s4]))
        K.tt("dve", g4(cat[0:n, 512:1024]), g4(t1[0:n, 0:512]), s4[0:n, 12:16].unsqueeze(2).to_broadcast([n, 4, 128]), ALU.mult, r=[t1, s4], w=[cat])
        out_proj_ln(K, t, n, cat, catT, wo, xt, z, xo, st)


def phase_b_odd(K, l):
    c = K.ctx
    f = K.fn
    d, o, NT = c["d"], c["o"], c["NT"]
    cst, ident, PS = c["cst"], c["ident"], c["PS"]
    jl = l // 2
    NTOK = c["NTOK"]
    K.PSG = PS[:4]
    f["load_phase_consts"](l, 0)
    g8 = lambda ap: ap.rearrange("p (h e) -> p h e", h=8)
    bc8 = lambda ap, n, w: ap.unsqueeze(2).to_broadcast([n, 8, w])
    with K.scope() as pb:
        sb = lambda nm, sh, dt=F32: K.sb(pb, "o_" + nm, sh, dt)
        wo = sb("wo", [128, 8, D], BF16)
        f["load_weights_bf16"](wo, d["odd_w_out"][jl], D)
        cw = sb("cw", [128, 4, 3072])
        for tap in range(4):
            K.dma("sp", cw[:, tap, :], d["odd_conv"][jl, tap:tap + 1, :].partition_broadcast(128), r=[], w=[cw])
        gnB = sb("gnB", [128, 128])
        K.dma("sp", gnB[:], d["odd_gn"][jl:jl + 1, :].partition_broadcast(128), r=[], w=[gnB])
        nea = sb("nea", [128, 8])
        dtb = sb("dtb", [128, 8])
        K.dma("sp", nea[:], d["odd_a_log"][jl:jl + 1, :].partition_broadcast(128), r=[], w=[nea])
        K.dma("sp", dtb[:], d["odd_dt_bias"][jl:jl + 1, :].partition_broadcast(128), r=[], w=[dtb])
        K.act(nea[:], nea[:], AF.Exp, r=[nea], w=[nea])
        K.ts("dve", nea[:], nea[:], -1.0, ALU.mult, r=[nea], w=[nea])
        pj = sb("pj", [128, ODD_IN])
        xt = sb("xt", [128, D])
        sh = sb("sh", [128, 3072])
        cat = sb("cat", [128, D], BF16)
        catT = sb("catT", [128, 8, 128], BF16)
        z = sb("z", [128, D])
        xo = sb("xo", [128, D])
        st = sb("st", [128, 4])
        sm = sb("sm", [128, 96])
        ob = sb("ob", [128, 1024])
        sq = sh

        def x_load(t, n, r0):
            K.dma("sp", xt[0:n, :], c["x_a"][r0:r0 + n, :], r=[c["tb_xa"][t]], w=[xt])

        def conv_gates(t, n, r0, taps):
            K.dma("sp", pj[0:n, :], c["proj_d"][3 + r0:3 + r0 + n, :], r=[c["tb_pj"][t]], w=[pj])
            pq = pj[0:n, 0:3072]
            K.tt("pool", pq, pq, cw[0:n, 3, :], ALU.mult, r=[pj, cw], w=[pj])
            for tap in range(3):
                src, deps = taps[tap]
                K.dma("sp", sh[0:n, :], src, r=deps, w=[sh])
                K.tt("pool", sh[0:n, :], sh[0:n, :], cw[0:n, tap, :], ALU.mult, r=[sh, cw], w=[sh])
                K.tt("dve", pq, pq, sh[0:n, :], ALU.add, r=[pj, sh], w=[pj])
            K.act(pj[0:n, 0:4096], pj[0:n, 0:4096], AF.Silu, r=[pj], w=[pj])
            K.tt("pool", sq[0:n, 0:2048], pj[0:n, 0:2048], pj[0:n, 0:2048], ALU.mult, r=[pj], w=[sq])
            K.red("dve", sm[0:n, 16:32], sq[0:n, 0:2048].rearrange("p (h e) -> p h e", h=16), r=[sq], w=[sm])
            K.rsqrt("dve", sm[0:n, 16:32], sm[0:n, 16:32], r=[sm], w=[sm], mult=1.0, add=1e-6)
            K.ts("dve", sm[0:n, 16:24], sm[0:n, 16:24], 128 ** -0.5, ALU.mult, r=[sm], w=[sm])
            qk = pj[0:n, 0:2048].rearrange("p (h e) -> p h e", h=16)
            K.tt("dve", qk, qk, sm[0:n, 16:32].unsqueeze(2).to_broadcast([n, 16, 128]), ALU.mult, r=[pj, sm], w=[pj])
            K.tt("dve", sm[0:n, 32:40], pj[0:n, 4096:4104], dtb[0:n, :], ALU.add, r=[pj, dtb], w=[sm])
            K.act(sm[0:n, 40:48], sm[0:n, 32:40], AF.Abs, r=[sm], w=[sm])
            K.act(sm[0:n, 40:48], sm[0:n, 40:48], AF.Exp, r=[sm], w=[sm], scale=-1.0)
            K.act(sm[0:n, 40:48], sm[0:n, 40:48], AF.Ln, r=[sm], w=[sm], bias=1.0)
            K.ts("dve", sm[0:n, 32:40], sm[0:n, 32:40], 0.0, ALU.max, r=[sm], w=[sm])
            K.tt("dve", sm[0:n, 32:40], sm[0:n, 32:40], sm[0:n, 40:48], ALU.add, r=[sm], w=[sm])
            K.tt("dve", sm[0:n, 0:8], sm[0:n, 32:40], nea[0:n, :], ALU.mult, r=[sm, nea], w=[sm])
            K.act(sm[0:n, 8:16], pj[0:n, 4104:4112], AF.Sigmoid, r=[pj], w=[sm])

        def rms_gate(n, src, rsrc):
            K.tt("pool", sq[0:n, 0:1024], src, src, ALU.mult, r=rsrc, w=[sq])
            K.red("dve", sm[0:n, 48:56], g8(sq[0:n, 0:1024]), r=[sq], w=[sm])
            K.rsqrt("dve", sm[0:n, 48:56], sm[0:n, 48:56], r=[sm], w=[sm], mult=1.0 / 128, add=LN_EPS)
            K.tt("dve", g8(sq[0:n, 0:1024]), g8(src), bc8(sm[0:n, 48:56], n, 128), ALU.mult, r=rsrc + [sm], w=[sq])
            K.tt("pool", g8(sq[0:n, 0:1024]), g8(sq[0:n, 0:1024]), gnB[0:n, :].unsqueeze(1).to_broadcast([n, 8, 128]), ALU.mult, r=[sq, gnB], w=[sq])
            K.tt("dve", cat[0:n, :], sq[0:n, 0:1024], pj[0:n, 3072:4096], ALU.mult, r=[sq, pj], w=[cat])

        with K.scope() as pp:
            G = [K.sb(pp, "o_G%d" % j, [128, 1024], F32) for j in range(15)]
            Sst = K.sb(pp, "o_S", [128, 1024], F32)
            K.memset("pool", Sst[:], 0.0, w=[Sst])
            Dg, E, ES, _, kb, kbg, vb, kg, qT, kT, kbT, Nn, Mm, aqkT, R = G
            Pa, Pb, Qa, Qb = G[0], G[1], G[2], G[3]
            u_, wT, vnew = G[4], G[9], G[10]
            identB = ident[:].unsqueeze(1).to_broadcast([128, 8, 128])
            for t in range(NT):
                r0, n = t * 128, 128
                x_load(t, n, r0)
                prevb = c["tb_pj"][t - 1] if t > 0 else c["tb_pj"][NT + 1]
                taps = [(c["proj_d"][r0 + tap:r0 + tap + 128, 0:3072], [c["tb_pj"][t], prevb]) for tap in range(3)]
                conv_gates(t, n, r0, taps)
                q_ap, k_ap, v_ap = pj[:, 0:1024], pj[:, 1024:2048], pj[:, 2048:3072]
                g_, be = sm[:, 0:8], sm[:, 8:16]
                pg = K.ps()
                K.mm(pg[:, 0:8], cst["triu"][:], g_, True, True, r=[cst["triu"], sm], w=[(pg, 0)])
                gc = sm[:, 56:64]
                K.cp("act", gc, pg[:, 0:8], r=[(pg, 0)], w=[sm])
                K.tt("pool", g8(Dg[:]), identB, bc8(gc, 128, 128), ALU.mult, r=[ident, sm], w=[Dg])
                pr = K.ps()
                K.mm(pr[:, 0:512], cst["ones"][:], Dg[:, 0:512], True, True, r=[cst["ones"], Dg], w=[(pr, 0)])
                K.mm(pr[:, 512:1024], cst["ones"][:], Dg[:, 512:1024], True, True, r=[cst["ones"], Dg], w=[(pr, 1)])
                K.tt("dve", g8(E[:]), g8(pr[:]), bc8(gc, 128, 128), ALU.subtract, r=[pr, sm], w=[E])
                K.tt("dve", g8(E[:]), g8(E[:]), cst["mvinc"][:].unsqueeze(1).to_broadcast([128, 8, 128]), ALU.min, r=[E, cst["mvinc"]], w=[E])
                K.act(E[:], E[:], AF.Exp, r=[E], w=[E])
                K.tt("pool", g8(ES[:]), g8(E[:]), cst["strict"][:].unsqueeze(1).to_broadcast([128, 8, 128]), ALU.mult, r=[E, cst["strict"]], w=[ES])
                gl = sm[:, 64:72]
                K.cp("act", gl, g8(pr[:])[:, :, 127], r=[pr], w=[sm])
                K.tt("dve", sm[:, 72:80], gl, gc, ALU.subtract, r=[sm], w=[sm])
                K.act(sm[:, 72:80], sm[:, 72:80], AF.Exp, r=[sm], w=[sm])
                K.act(sm[:, 80:88], gl, AF.Exp, r=[sm], w=[sm])
                K.act(sm[:, 88:96], gc, AF.Exp, r=[sm], w=[sm])
                K.tt("pool", g8(kb[:]), g8(k_ap), bc8(be, 128, 128), ALU.mult, r=[pj, sm], w=[kb])
                K.tt("pool", g8(kbg[:]), g8(kb[:]), bc8(sm[:, 88:96], 128, 128), ALU.mult, r=[kb, sm], w=[kbg])
                K.tt("pool", g8(vb[:]), g8(v_ap), bc8(be, 128, 128), ALU.mult, r=[pj, sm], w=[vb])
                K.tt("pool", g8(kg[:]), g8(k_ap), bc8(sm[:, 72:80], 128, 128), ALU.mult, r=[pj, sm], w=[kg])
                for src_ap, rs, dst in ((q_ap, [pj], qT), (k_ap, [pj], kT), (kb[:], [kb], kbT)):
                    pt = K.ps()
                    for h in range(8):
                        K.tr(pt[:, h * 128:(h + 1) * 128], src_ap[:, h * 128:(h + 1) * 128], ident[:], r=rs + [ident], w=[(pt, h // 4)])
                    K.cp("act", dst[:], pt[:], r=[pt], w=[dst])
                pn = K.ps()
                pa = K.ps()
                for h in range(8):
                    hs = slice(h * 128, (h + 1) * 128)
                    K.mm(pn[:, hs], kT[:, hs], kbT[:, hs], True, True, r=[kT, kbT], w=[(pn, h // 4)])
                for h in range(8):
                    hs = slice(h * 128, (h + 1) * 128)
                    K.mm(pa[:, hs], kT[:, hs], qT[:, hs], True, True, r=[kT, qT], w=[(pa, h // 4)])
                K.tt("dve", Nn[:], pn[:], ES[:], ALU.mult, r=[pn, ES], w=[Nn])
                K.tt("dve", aqkT[:], pa[:], E[:], ALU.mult, r=[pa, E], w=[aqkT])
                pm = K.ps()
                for h in range(8):
                    hs = slice(h * 128, (h + 1) * 128)
                    K.tr(pm[:, hs], Nn[:, hs], ident[:], r=[Nn, ident], w=[(pm, h // 4)])
                K.cp("act", Mm[:], pm[:], r=[pm], w=[Mm])
                K.tt("pool", g8(R[:]), identB, g8(Nn[:]), ALU.subtract, r=[ident, Nn], w=[R])
                P, Q = Mm, Nn
                Pn, Qn = [Pa, Pb], [Qa, Qb]
                for lev in range(1, 7):
                    P2, Q2 = Pn[lev % 2], Qn[lev % 2]
                    pP = K.ps()
                    for h in range(8):
                        hs = slice(h * 128, (h + 1) * 128)
                        K.mm(pP[:, hs], Q[:, hs], P[:, hs], True, True, r=[Q, P], w=[(pP, h // 4)])
                    if lev < 6:
                        pQ = K.ps()
                        for h in range(8):
                            hs = slice(h * 128, (h + 1) * 128)
                            K.mm(pQ[:, hs], P[:, hs], Q[:, hs], True, True, r=[P, Q], w=[(pQ, h // 4)])
                    K.cp("act", P2[:], pP[:], r=[pP], w=[P2])
                    if lev < 6:
                        K.cp("dve", Q2[:], pQ[:], r=[pQ], w=[Q2])
                    pR = K.ps()
                    for h in range(8):
                        hs = slice(h * 128, (h + 1) * 128)
                        K.mm(pR[:, hs], P2[:, hs], R[:, hs], True, True, r=[P2, R], w=[(pR, h // 4)])
                    K.tt("dve", R[:], R[:], pR[:], ALU.add, r=[R, pR], w=[R])
                    P, Q = P2, Q2
                pu = K.ps()
                pw = K.ps()
                for h in range(8):
                    hs = slice(h * 128, (h + 1) * 128)
                    K.mm(pu[:, hs], R[:, hs], vb[:, hs], True, True, r=[R, vb], w=[(pu, h // 4)])
                for h in range(8):
                    hs = slice(h * 128, (h + 1) * 128)
                    K.mm(pw[:, hs], kbg[:, hs], R[:, hs], True, True, r=[kbg, R], w=[(pw, h // 4)])
                K.cp("act", u_[:], pu[:], r=[pu], w=[u_])
                K.cp("act", wT[:], pw[:], r=[pw], w=[wT])
                pv = K.ps()
                for h in range(8):
                    hs = slice(h * 128, (h + 1) * 128)
                    K.mm(pv[:, hs], wT[:, hs], Sst[:, hs], True, True, r=[wT, Sst], w=[(pv, h // 4)])
                K.tt("dve", vnew[:], u_[:], pv[:], ALU.subtract, r=[u_, pv], w=[vnew])
                po1 = K.ps()
                for h in range(8):
                    hs = slice(h * 128, (h + 1) * 128)
                    K.mm(po1[:, hs], qT[:, hs], Sst[:, hs], True, True, r=[qT, Sst], w=[(po1, h // 4)])
                K.tt("dve", g8(ob[:]), g8(po1[:]), bc8(sm[:, 88:96], 128, 128), ALU.mult, r=[po1, sm], w=[ob])
                po2 = K.ps()
                for h in range(8):
                    hs = slice(h * 128, (h + 1) * 128)
                    K.mm(po2[:, hs], aqkT[:, hs], vnew[:, hs], True, True, r=[aqkT, vnew], w=[(po2, h // 4)])
                K.tt("dve", ob[:], ob[:], po2[:], ALU.add, r=[ob, po2], w=[ob])
                pS = K.ps()
                for h in range(8):
                    hs = slice(h * 128, (h + 1) * 128)
                    K.mm(pS[:, hs], kg[:, hs], vnew[:, hs], True, True, r=[kg, vnew], w=[(pS, h // 4)])
                K.tt("pool", g8(Sst[:]), g8(Sst[:]), bc8(sm[:, 80:88], 128, 128), ALU.mult, r=[Sst, sm], w=[Sst])
                K.tt("dve", Sst[:], Sst[:], pS[:], ALU.add, r=[Sst, pS], w=[Sst])
                rms_gate(n, ob[:], [ob])
                out_proj_ln(K, t, n, cat, catT, wo, xt, z, xo, st)
            for h in range(8):
                K.dma("sp", o["ssm_p"][jl, h], Sst[:, h * 128:(h + 1) * 128], r=[Sst], w=[])
            K.dma("sp", o["conv_p"][jl], c["proj_d"][NTOK:NTOK + 3, 0:3072], r=[c["tb_pj"][NT - 1]], w=[])
        if not c["cfg"].get("skip_sample"):
            sample_odd(K, l, locals())
    K.PSG = PS[:2]


def sample_odd(K, l, L):
    c = K.ctx
    f = K.fn
    d, o, NT = c["d"], c["o"], c["NT"]
    cst, ident = c["cst"], c["ident"]
    jl = l // 2
    n, t, r0 = NS, NT, c["NTOK"]
    pj, xt, sm, ob, cat, catT, z, xo, st, wo = [L[k] for k in ("pj", "xt", "sm", "ob", "cat", "catT", "z", "xo", "st", "wo")]
    g8 = lambda ap: ap.rearrange("p (h e) -> p h e", h=8)
    with K.scope() as px:
        sb = lambda nm, sh, dt=F32: K.sb(px, "os_" + nm, sh, dt)
        f["load_sample_mod"](l, 0)
        L["x_load"](t, n, r0)
        taps = [(d["st_conv"][jl, :, tap, :], []) for tap in range(3)]
        L["conv_gates"](t, n, r0, taps)
        K.dma("sp", o["conv_s"][jl, :, 0:2, :], d["st_conv"][jl, :, 1:3, :], r=[], w=[])
        K.dma("sp", o["conv_s"][jl, :, 2, :], c["proj_d"][3 + r0:3 + r0 + n, 0:3072], r=[c["tb_pj"][t]], w=[])
        K.act(sm[0:n, 32:40], sm[0:n, 0:8], AF.Exp, r=[sm], w=[sm])
        sq = L["sq"]
        K.tt("pool", sq[0:n, 0:1024], pj[0:n, 0:1024], pj[0:n, 1024:2048], ALU.mult, r=[pj], w=[sq])
        K.red("dve", sm[0:n, 40:48], g8(sq[0:n, 0:1024]), r=[sq], w=[sm])
        row_d = K.dram("gr_d%d" % l, [NS, 3072 + 24], F32, "Internal")
        orow_d = K.dram("go_d%d" % l, [NS, 1024], F32, "Internal")
        K.dma("sp", row_d[:, 0:3072], pj[0:n, 0:3072], r=[pj], w=[row_d])
        K.dma("sp", row_d[:, 3072:3080], sm[0:n, 32:40], r=[sm], w=[row_d])
        K.dma("sp", row_d[:, 3080:3088], sm[0:n, 8:16], r=[sm], w=[row_d])
        K.dma("sp", row_d[:, 3088:3096], sm[0:n, 40:48], r=[sm], w=[row_d])
        egB = sb("egB", [128, NS * 8])
        egd = K.dram("ge_d%d" % l, [1, NS * 8], F32, "Internal")
        K.dma("sp", egd[0:1, :].rearrange("o (s h) -> (o s) h", h=8), sm[0:n, 32:40], r=[sm], w=[egd])
        K.dma("sp", egB[:], egd[0:1, :].partition_broadcast(128), r=[egd], w=[egB])
        kqT = sb("kqT", [128, 8 * NS * 2])
        kq4 = kqT[:].rearrange("p (h s a) -> p h s a", h=8, a=2)
        for which, c0 in ((0, 1024), (1, 0)):
            pt = K.ps()
            for h in range(8):
                K.tr(pt[:, h * 16:(h + 1) * 16], pj[0:n, c0 + h * 128:c0 + (h + 1) * 128], ident[0:n, 0:n], r=[pj, ident], w=[(pt, 0)])
            K.cp("act", kq4[:, :, :, which], pt[:, 0:128].rearrange("p (h s) -> p h s", h=8), r=[(pt, 0)], w=[kqT])
        rows = [sb("row%d" % j, [1, 3096]) for j in range(2)]
        S0 = [sb("S0%d" % j, [128, 1024]) for j in range(2)]
        vn = sb("vn", [1, 1024])
        orow = sb("orow", [1, 1024])
        tmp = sb("tmp", [1, 1024])
        r8 = lambda ap: ap.rearrange("p (h e) -> p h e", h=8)
        b8 = lambda ap: ap.unsqueeze(2).to_broadcast([1, 8, 128])
        for s in range(NS):
            row, S0s = rows[s % 2], S0[s % 2]
            K.dma("sp", row[:], row_d[s:s + 1, :], r=[row_d], w=[row])
            for h in range(8):
                K.dma("sp", S0s[:, h * 128:(h + 1) * 128], d["st_ssm"][jl, s, h], r=[], w=[S0s])
            pk = K.ps()
            pq = K.ps()
            for h in range(8):
                hs = slice(h * 128, (h + 1) * 128)
                K.mm(pk[0:1, hs], kq4[:, h, s, 0:1], S0s[:, hs], True, True, r=[kqT, S0s], w=[(pk, h // 4)])
            for h in range(8):
                hs = slice(h * 128, (h + 1) * 128)
                K.mm(pq[0:1, hs], kq4[:, h, s, 1:2], S0s[:, hs], True, True, r=[kqT, S0s], w=[(pq, h // 4)])
            eg, be, qk = row[:, 3072:3080], row[:, 3080:3088], row[:, 3088:3096]
            K.tt("dve", r8(tmp[:]), r8(pk[0:1, :]), b8(eg), ALU.mult, r=[pk, row], w=[tmp])
            K.tt("dve", vn[:], row[:, 2048:3072], tmp[:], ALU.subtract, r=[row, tmp], w=[vn])
            K.tt("dve", r8(vn[:]), r8(vn[:]), b8(be), ALU.mult, r=[vn, row], w=[vn])
            K.tt("dve", r8(orow[:]), r8(pq[0:1, :]), b8(eg), ALU.mult, r=[pq, row], w=[orow])
            K.tt("dve", r8(tmp[:]), r8(vn[:]), b8(qk), ALU.mult, r=[vn, row], w=[tmp])
            K.tt("dve", orow[:], orow[:], tmp[:], ALU.add, r=[orow, tmp], w=[orow])
            K.dma("sp", orow_d[s:s + 1, :], orow[:], r=[orow], w=[orow_d])
            pS = K.ps()
            for h in range(8):
                hs = slice(h * 128, (h + 1) * 128)
                K.mm(pS[:, hs], row[:, 1024 + h * 128:1024 + (h + 1) * 128], vn[:, hs], True, True, r=[row, vn], w=[(pS, h // 4)])
            K.tt("pool", g8(S0s[:]), g8(S0s[:]), egB[:, s * 8:(s + 1) * 8].unsqueeze(2).to_broadcast([128, 8, 128]), ALU.mult, r=[S0s, egB], w=[S0s])
            K.tt("dve", S0s[:], S0s[:], pS[:], ALU.add, r=[S0s, pS], w=[S0s])
            for h in range(8):
                K.dma("sp", o["ssm_s"][jl, s, h], S0s[:, h * 128:(h + 1) * 128], r=[S0s], w=[])
        K.dma("sp", ob[0:n, :], orow_d[:, :], r=[orow_d], w=[ob])
        L["rms_gate"](n, ob[0:n, :], [ob])
        out_proj_ln(K, t, n, cat, catT, wo, xt, z, xo, st)


_NC_CACHE = {}


def kernel(x_prompt, x_sample, c_prompt, c_sample, cache_kv, page_table, state_ret, state_ssm, state_conv,
           ada_w, ada_b, ln_g, ln_b, mlp_up, mlp_down, even_w_in, even_w_out, ret_gn,
           odd_w_in, odd_w_out, odd_conv, odd_a_log, odd_dt_bias, odd_gn):
    f = lambda a: np.ascontiguousarray(np.asarray(a))
    B, T, _ = x_prompt.shape
    NT = T // 128
    NPG = page_table.shape[1]
    NPOOL = cache_kv.shape[1]
    cfg = dict(NT=NT, NPG=NPG, NPOOL=NPOOL, DEPTH=4, mix="real")
    key = (NT, NPG, NPOOL)
    if key not in _NC_CACHE:
        _NC_CACHE[key] = build(cfg)
    nc = _NC_CACHE[key]
    cs = make_consts(NT, NPG * cache_kv.shape[2])
    shared = dict(cache_kv=f(cache_kv).reshape(-1, 320), ada_w=f(ada_w), ada_b=f(ada_b), ln_g=f(ln_g), ln_b=f(ln_b),
                  mlp_up=f(mlp_up), mlp_down=f(mlp_down), even_w_in=f(even_w_in), even_w_out=f(even_w_out),
                  ret_gn=f(ret_gn).reshape(2, 512), odd_w_in=f(odd_w_in), odd_w_out=f(odd_w_out), odd_conv=f(odd_conv),
                  odd_a_log=f(odd_a_log), odd_dt_bias=f(odd_dt_bias), odd_gn=f(odd_gn))
    shared.update({k: f(v) for k, v in cs.items()})
    in_maps = []
    for c in range(8):
        b = c // 2
        sl = slice(c * NS, (c + 1) * NS)
        m = dict(shared)
        m.update(xp=f(x_prompt[b]), xs=f(x_sample[sl, 0]), c17=f(np.concatenate([c_sample[sl], c_prompt[b:b + 1]], 0)),
                 ptab=f(page_table[sl]), st_ret=f(state_ret[:, sl]), st_ssm=f(state_ssm[:, sl]), st_conv=f(state_conv[:, sl]))
        in_maps.append(m)
    res = run_bass_kernel_spmd(nc, in_maps, core_ids=list(range(8))).results
    ev = [res[2 * b] for b in range(B)]
    y_p = np.stack([r["y_p"] for r in ev], 0)
    y_s = np.concatenate([r["y_s"] for r in res], 0)[:, None, :]
    kv_p = np.stack([r["kv_p"] for r in ev], 1)
    kv_s = np.concatenate([r["kv_s"] for r in res], 1)[:, :, None, :]
    ret_p = np.stack([r["ret_p"] for r in ev], 1)
    ret_s = np.concatenate([r["ret_s"] for r in res], 1)
    ssm_p = np.stack([r["ssm_p"] for r in ev], 1)
    ssm_s = np.concatenate([r["ssm_s"] for r in res], 1)
    conv_p = np.stack([r["conv_p"] for r in ev], 1)
    conv_s = np.concatenate([r["conv_s"] for r in res], 1)
    return tuple(np.asarray(a, np.float32) for a in (y_p, y_s, kv_p, kv_s, ret_p, ret_s, ssm_p, ssm_s, conv_p, conv_s))
```

```python
import math
import numpy as np
from contextlib import ExitStack
import concourse.bass as bass
import concourse.mybir as mybir
from concourse.bass_utils import run_bass_kernel_spmd

F32 = mybir.dt.float32
BF16 = mybir.dt.bfloat16
I32 = mybir.dt.int32
ALU = mybir.AluOpType
AF = mybir.ActivationFunctionType
AX = mybir.AxisListType

ENGS = ("pe", "act", "dve", "pool", "sp")
NDMA = 6

D = 1024
DFF = 4096
EVEN_IN = 3140
ODD_IN = 4112
NS = 16
ALPHA = 8.0 ** 0.25
LN_EPS = 1e-5
NEG = -1e30


class Buf:
    __slots__ = ("w", "r")

    def __init__(self):
        self.w = None
        self.r = {}


class TL:
    def __init__(self, t, nb=1):
        self.t = t
        self.b = [Buf() for _ in range(nb)]

    def __getitem__(self, k):
        return self.t[k]


def _bufs(lst):
    out = []
    for x in lst:
        if isinstance(x, Buf):
            out.append(x)
        elif isinstance(x, TL):
            out.extend(x.b)
        elif isinstance(x, tuple):
            out.append(x[0].b[x[1]])
        else:
            raise TypeError(x)
    return out


class Sched:
    def __init__(self, nc, es):
        self.nc = nc
        self.ops = {e: [] for e in ENGS}
        self.cnt = {e: 0 for e in ENGS}
        self.seen = {e: {} for e in ENGS}
        self.sem = {e: es.enter_context(nc.semaphore("s_" + e)) for e in ENGS}
        self.dsem = {}
        self.dcnt = {}
        self.pend = {e: [] for e in ENGS}
        for q in ("sp", "act", "pool"):
            for j in range(NDMA):
                k = "d_%s%d" % (q, j)
                self.dsem[k] = es.enter_context(nc.semaphore(k))
            self.dcnt[q] = 0

    def barrier(self):
        tgt = {e: self.cnt[e] for e in ENGS if self.cnt[e] > 0}
        for q in ("sp", "act", "pool"):
            for j in range(NDMA):
                if self.dcnt[q] > j:
                    tgt["d_%s%d" % (q, j)] = 16 * ((self.dcnt[q] - j + NDMA - 1) // NDMA)
        for e in ENGS:
            for k, v in tgt.items():
                if self.seen[e].get(k, 0) < v:
                    self.seen[e][k] = v
                    self.pend[e].append((k, v))

    def _semobj(self, key):
        return self.sem[key] if key in self.sem else self.dsem[key]

    def _collect(self, eng, r, w):
        deps = []
        for b in r:
            if b.w is not None:
                deps.append(b.w)
        for b in w:
            if b.w is not None:
                deps.append(b.w)
            for k, v in b.r.items():
                deps.append((k, v))
        waits = []
        sn = self.seen[eng]
        for k, v in deps:
            if eng == "pe" and k == "pe":
                continue
            if sn.get(k, 0) >= v:
                continue
            sn[k] = v
            waits.append((k, v))
        return waits

    def op(self, eng, fn, r=(), w=()):
        r = _bufs(r)
        w = _bufs(w)
        waits = self.pend[eng] + self._collect(eng, r, w)
        self.pend[eng] = []
        self.cnt[eng] += 1
        tok = (eng, self.cnt[eng])
        self.ops[eng].append((waits, fn, None))
        for b in r:
            if b.r.get(eng, 0) < tok[1]:
                b.r[eng] = tok[1]
        for b in w:
            b.w = tok
            b.r = {}

    def dma(self, q, fn, r=(), w=()):
        r = _bufs(r)
        w = _bufs(w)
        i = self.dcnt[q]
        self.dcnt[q] += 1
        key = "d_%s%d" % (q, i % NDMA)
        prev = 16 * (i // NDMA)
        waits = self.pend[q] + self._collect(q, r, w)
        self.pend[q] = []
        sn = self.seen[q]
        if prev > 0 and sn.get(key, 0) < prev:
            sn[key] = prev
            waits.append((key, prev))
        tok = (key, prev + 16)
        self.ops[q].append((waits, fn, key))
        for b in r:
            if b.r.get(key, 0) < tok[1]:
                b.r[key] = tok[1]
        for b in w:
            b.w = tok
            b.r = {}

    def finish(self):
        final = {}
        for q in ("sp", "act", "pool"):
            for j in range(NDMA):
                if self.dcnt[q] > j:
                    final["d_%s%d" % (q, j)] = 16 * ((self.dcnt[q] - j + NDMA - 1) // NDMA)
        nc = self.nc
        with nc.Block() as block:
            def replay(ename, eng):
                for waits, fn, key in self.ops[ename]:
                    for k, v in waits:
                        eng.wait_ge(self._semobj(k), v)
                    inst = fn(eng)
                    if key is None:
                        inst.then_inc(self.sem[ename], 1)
                    else:
                        inst.then_inc(self.dsem[key], 16)
                if ename == "sp":
                    for k, v in final.items():
                        eng.wait_ge(self._semobj(k), v)
                    for e in ("pe", "act", "dve", "pool"):
                        if self.cnt[e] > 0:
                            eng.wait_ge(self.sem[e], self.cnt[e])

            @block.tensor
            def _(e):
                replay("pe", e)

            @block.scalar
            def _(e):
                replay("act", e)

            @block.vector
            def _(e):
                replay("dve", e)

            @block.gpsimd
            def _(e):
                replay("pool", e)

            @block.sync
            def _(e):
                replay("sp", e)


def _rope_rows(pos):
    def tab(d):
        inv = (10000.0 ** (-np.arange(0, d, 2, dtype=np.float32) / d)).astype(np.float32)
        ang = (pos.astype(np.float32)[:, None] * inv[None, :]).astype(np.float32)
        return np.cos(ang.astype(np.float64)), np.sin(ang.astype(np.float64))
    c128, s128 = tab(128)
    c64, s64 = tab(64)
    return np.concatenate([c128, s128, -s128, c64, s64, -s64], 1).astype(np.float32)


def make_consts(NT, past_len):
    c = {}
    p = np.arange(128)
    c["ident"] = np.eye(128, dtype=np.float32)
    c["ones"] = np.ones((128, 128), np.float32)
    c["triu"] = (p[:, None] <= p[None, :]).astype(np.float32)
    c["mvinc"] = np.where(p[None, :] >= p[:, None], 0.0, -1e4).astype(np.float32)
    c["strict"] = (p[None, :] > p[:, None]).astype(np.float32)
    c["cmask"] = np.where(p[None, :] <= p[:, None], 0.0, NEG).astype(np.float32)
    lg = np.log(1.0 - 2.0 ** (-5.0 - np.arange(4, dtype=np.float64)))
    i = p.astype(np.float64)
    rel = i[None, :] - i[:, None]
    dm = np.where(rel >= 0, np.exp(lg[:, None, None] * np.maximum(rel, 0)[None]), 0.0)
    c["dmatT"] = (np.transpose(dm, (1, 0, 2)) * 128 ** -0.5).astype(np.float32).reshape(128, 512)
    qd = np.exp(lg[:, None] * (i[None, :] + 1.0))
    c["qdecB"] = np.broadcast_to(qd.reshape(1, 512), (128, 512)).astype(np.float32).copy()
    kd = np.exp(lg[None, :] * (127.0 - i[:, None])) * 128 ** -0.5
    c["kdecC"] = kd.astype(np.float32)
    cd = np.exp(lg * 128.0)
    c["cdecB"] = np.broadcast_to(np.repeat(cd, 128).reshape(1, 512), (128, 512)).astype(np.float32).copy()
    g1 = np.exp(lg)
    c["gam1B"] = np.broadcast_to(np.repeat(g1, 128).reshape(1, 512), (128, 512)).astype(np.float32).copy()
    c["ropeP"] = _rope_rows(np.arange(NT * 128)).reshape(NT, 128, 288)
    c["iotap"] = np.arange(128, dtype=np.float32).reshape(128, 1)
    c["ropeS"] = np.broadcast_to(_rope_rows(np.array([past_len])), (NS, 288)).copy()
    d16 = np.broadcast_to(np.eye(16, dtype=np.float32).reshape(1, 256), (128, 256)).copy()
    c["delta16"] = d16
    return c


class KB:
    def __init__(self, cfg):
        self.cfg = cfg
        self.nc = bass.Bass("TRN2", target_bir_lowering=False)
        self.es = ExitStack()
        self.rr = 0

    def scope(self):
        kb = self

        class _Sc(ExitStack):
            def __exit__(self_, *a):
                kb.S.barrier()
                return super().__exit__(*a)
        return _Sc()

    def dram(self, name, shape, dt, kind):
        return TL(self.nc.dram_tensor(name, list(shape), dt, kind=kind).ap())

    def sb(self, es, name, shape, dt, nb=1):
        self.uid = getattr(self, "uid", 0) + 1
        return TL(es.enter_context(self.nc.sbuf_tensor("%s_%d" % (name, self.uid), list(shape), dt)), nb)

    def tt(self, eng, out, a, b, op, r, w):
        self.S.op(eng, lambda e: e.tensor_tensor(out=out, in0=a, in1=b, op=op), r=r, w=w)

    def ts(self, eng, out, a, s1, op0, r, w, s2=None, op1=None, acc=None):
        if op1 is None:
            self.S.op(eng, lambda e: e.tensor_scalar(out=out, in0=a, scalar1=s1, scalar2=None, op0=op0), r=r, w=w)
        elif acc is None:
            self.S.op(eng, lambda e: e.tensor_scalar(out=out, in0=a, scalar1=s1, scalar2=s2, op0=op0, op1=op1), r=r, w=w)
        else:
            self.S.op(eng, lambda e: e.tensor_scalar(out=out, in0=a, scalar1=s1, scalar2=s2, op0=op0, op1=op1, accum_out=acc), r=r, w=w)

    def stt(self, eng, out, a, sc, b, op0, op1, r, w):
        self.S.op(eng, lambda e: e.scalar_tensor_tensor(out=out, in0=a, scalar=sc, in1=b, op0=op0, op1=op1), r=r, w=w)

    def act(self, out, in_, func, r, w, bias=None, scale=None, acc=None):
        kw = {}
        if bias is not None:
            kw["bias"] = bias
        if scale is not None:
            kw["scale"] = scale
        if acc is not None:
            kw["accum_out"] = acc
        self.S.op("act", lambda e: e.activation(out=out, in_=in_, func=func, **kw), r=r, w=w)

    def cp(self, eng, out, in_, r, w):
        if eng == "act":
            self.S.op("act", lambda e: e.copy(out=out, in_=in_), r=r, w=w)
        else:
            self.S.op(eng, lambda e: e.tensor_copy(out=out, in_=in_), r=r, w=w)

    def red(self, eng, out, in_, r, w, op=None):
        if op is None:
            self.S.op(eng, lambda e: e.reduce_sum(out=out, in_=in_, axis=AX.X), r=r, w=w)
        else:
            self.S.op(eng, lambda e: e.tensor_reduce(out=out, in_=in_, axis=AX.X, op=op), r=r, w=w)

    def mm(self, out, lhsT, rhs, start, stop, r, w):
        self.S.op("pe", lambda e: e.matmul(out, lhsT=lhsT, rhs=rhs, start=start, stop=stop), r=r, w=w)

    def tr(self, out, in_, ident, r, w):
        self.S.op("pe", lambda e: e.transpose(out=out, in_=in_, identity=ident), r=r, w=w)

    def dma(self, q, out, in_, r, w):
        self.S.dma(q, lambda e: e.dma_start(out=out, in_=in_), r=r, w=w)

    def memset(self, eng, ap, val, w):
        self.S.op(eng, lambda e: e.memset(ap, val), w=w)

    def evq(self):
        self.rr += 1
        return "act" if self.rr % 2 else "dve"

    def ps(self):
        self.prr = (getattr(self, "prr", -1) + 1) % len(self.PSG)
        return self.PSG[self.prr]

    def rsqrt(self, eng, out, in_, r, w, mult, add):
        self.act(out, in_, AF.Sqrt, r=r, w=w, scale=mult, bias=add)
        self.S.op("dve", lambda e: e.reciprocal(out=out, in_=out), r=w, w=w)


def build(cfg):
    NT = cfg["NT"]
    NPG = cfg["NPG"]
    NPOOL = cfg["NPOOL"]
    DEPTH = cfg["DEPTH"]
    TOPK_P = min(256, (NT * 128) // 4)
    TOPK_S = min(256, (NPG * 128 + 1) // 4)
    NTOK = NT * 128
    NROW = NTOK + NS
    K = KB(cfg)
    nc = K.nc
    es = K.es
    with es:
        S = K.S = Sched(nc, es)
        EI, EO, IN = "ExternalInput", "ExternalOutput", "Internal"
        d = {}
        for nm, sh, dt in [
            ("xp", [NTOK, D], F32), ("xs", [NS, D], F32), ("c17", [17, D], F32),
            ("cache_kv", [2 * NPOOL * 128, 320], F32), ("ptab", [NS, NPG], I32),
            ("st_ret", [2, NS, 4, 128, 128], F32), ("st_ssm", [2, NS, 8, 128, 128], F32),
            ("st_conv", [2, NS, 3, 3072], F32),
            ("ada_w", [4, D, 6 * D], F32), ("ada_b", [4, 6 * D], F32), ("ln_g", [4, 2, D], F32),
            ("ln_b", [4, 2, D], F32), ("mlp_up", [4, D, DFF], F32), ("mlp_down", [4, DFF, D], F32),
            ("even_w_in", [2, D, EVEN_IN], F32), ("even_w_out", [2, D, D], F32), ("ret_gn", [2, 512], F32),
            ("odd_w_in", [2, D, ODD_IN], F32), ("odd_w_out", [2, D, D], F32), ("odd_conv", [2, 4, 3072], F32),
            ("odd_a_log", [2, 8], F32), ("odd_dt_bias", [2, 8], F32), ("odd_gn", [2, 128], F32),
            ("ident", [128, 128], F32), ("ones", [128, 128], F32), ("triu", [128, 128], F32),
            ("mvinc", [128, 128], F32), ("strict", [128, 128], F32), ("cmask", [128, 128], F32),
            ("dmatT", [128, 512], F32), ("qdecB", [128, 512], F32), ("kdecC", [128, 4], F32),
            ("cdecB", [128, 512], F32), ("gam1B", [128, 512], F32), ("ropeP", [NT, 128, 288], F32),
            ("ropeS", [NS, 288], F32), ("delta16", [128, 256], F32), ("iotap", [128, 1], F32),
        ]:
            d[nm] = K.dram(nm, sh, dt, EI)
        o = {}
        for nm, sh in [
            ("y_p", [NTOK, D]), ("y_s", [NS, D]), ("kv_p", [2, NTOK, 320]), ("kv_s", [2, NS, 320]),
            ("ret_p", [2, 4, 128, 128]), ("ret_s", [2, NS, 4, 128, 128]), ("ssm_p", [2, 8, 128, 128]),
            ("ssm_s", [2, NS, 8, 128, 128]), ("conv_p", [2, 3, 3072]), ("conv_s", [2, NS, 3, 3072]),
        ]:
            o[nm] = K.dram(nm, sh, F32, EO)
        x_a = K.dram("x_a", [NROW, D], F32, IN)
        x_b = K.dram("x_b", [NROW, D], F32, IN)
        mod_d = K.dram("mod_d", [4, 17, 6 * D], F32, IN)
        proj_d = K.dram("proj_d", [NROW + 3, ODD_IN], F32, IN)
        tb_xa = [Buf() for _ in range(NT + 1)]
        tb_xb = [Buf() for _ in range(NT + 1)]
        tb_pj = [Buf() for _ in range(NT + 2)]

        def rows(t):
            return (t * 128, 128) if t < NT else (NTOK, NS)

        cst = {}
        for nm, sh in [("ident", [128, 128]), ("ones", [128, 128]), ("triu", [128, 128]), ("mvinc", [128, 128]),
                       ("strict", [128, 128]), ("cmask", [128, 128]), ("delta16", [128, 256]), ("iotap", [128, 1])]:
            cst[nm] = K.sb(es, "c_" + nm, sh, F32)
            K.dma("sp", cst[nm][:], d[nm][:], r=[], w=[cst[nm]])
        identb = K.sb(es, "identb", [128, 128], BF16)
        K.cp("dve", identb[:], cst["ident"][:], r=[cst["ident"]], w=[identb])
        ident = cst["ident"]
        PS = [TL(es.enter_context(nc.psum_tensor("ps%d" % i, [128, 1024], F32)), 2) for i in range(4)]
        K.PSG = PS[:4]
        PSO = [PS[2], PS[3]]
        lnG = K.sb(es, "lnG", [128, D], F32)
        lnB = K.sb(es, "lnB", [128, D], F32)
        modP = K.sb(es, "modP", [128, 3 * D], F32)

        def load_phase_consts(l, which):
            c0 = which * 3 * D
            K.dma("sp", modP[:], mod_d[l, 16:17, c0:c0 + 3 * D].partition_broadcast(128), r=[mod_d], w=[modP])
            K.dma("sp", lnG[:], d["ln_g"][l, which:which + 1, :].partition_broadcast(128), r=[], w=[lnG])
            K.dma("sp", lnB[:], d["ln_b"][l, which:which + 1, :].partition_broadcast(128), r=[], w=[lnB])
            K.ts("pool", modP[:, D:3 * D], modP[:, D:3 * D], 1.0, ALU.add, r=[modP], w=[modP])

        def load_sample_mod(l, which, dst=None):
            dst = modP if dst is None else dst
            c0 = which * 3 * D
            K.dma("sp", dst[0:NS, :], mod_d[l, 0:NS, c0:c0 + 3 * D], r=[mod_d], w=[dst])
            K.ts("pool", dst[0:NS, D:3 * D], dst[0:NS, D:3 * D], 1.0, ALU.add, r=[dst], w=[dst])

        K.modS = None

        def mod_of(t):
            return K.modS if (t == NT and K.modS is not None) else modP

        with K.scope() as p0:
            c17t = K.sb(p0, "c17t", [17, D], F32)
            scT = K.sb(p0, "scT", [128, 8, 17], BF16)
            K.dma("sp", c17t[:], d["c17"][:], r=[], w=[c17t])
            K.act(c17t[:], c17t[:], AF.Silu, r=[c17t], w=[c17t])
            pt = K.ps()
            for k in range(8):
                K.tr(pt[0:128, k * 32:k * 32 + 17], c17t[:, k * 128:(k + 1) * 128], ident[0:17, 0:17], r=[c17t, ident], w=[(pt, 0)])
            K.cp("dve", scT[:], pt[:, 0:256].rearrange("p (k c) -> p k c", c=32)[:, :, 0:17], r=[(pt, 0)], w=[scT])
            aw = [K.sb(p0, "aw%d" % i, [128, 8, 512], BF16) for i in range(2)]
            ab = [K.sb(p0, "ab%d" % i, [17, 512], F32) for i in range(2)]
            mo = [K.sb(p0, "mo%d" % i, [17, 512], F32) for i in range(2)]
            it = 0
            for l in range(DEPTH):
                for cb in range(12):
                    a, b_, m_ = aw[it % 2], ab[it % 2], mo[it % 2]
                    cs = slice(cb * 512, (cb + 1) * 512)
                    K.dma("pool", a[:], d["ada_w"][l, :, cs].rearrange("(k p) n -> p k n", p=128), r=[], w=[a])
                    K.dma("sp", b_[:], d["ada_b"][l:l + 1, cs].partition_broadcast(17), r=[], w=[b_])
                    pt = K.ps()
                    for k in range(8):
                        K.mm(pt[0:17, 0:512], scT[:, k, :], a[:, k, :], k == 0, k == 7, r=[scT, a], w=[(pt, 0)])
                    K.tt("dve", m_[:], pt[0:17, 0:512], b_[:], ALU.add, r=[(pt, 0), b_], w=[m_])
                    K.dma("sp", mod_d[l, :, cs], m_[:], r=[m_], w=[mod_d])
                    it += 1

        def load_weights_bf16(wt, src, ncols):
            nk = src.shape[0] // 128
            for k in range(nk):
                K.dma("pool", wt[:, k, 0:ncols], src[k * 128:(k + 1) * 128, :], r=[], w=[wt])

        def modulate_T(pp, xt, t, n, tmp, hm, hmT):
            md = mod_of(t)
            K.tt("dve", tmp[0:n, :], xt[0:n, :], md[0:n, D:2 * D], ALU.mult, r=[xt, md], w=[tmp])
            K.tt("pool", hm[0:n, :], tmp[0:n, :], md[0:n, 0:D], ALU.add, r=[tmp, md], w=[hm])
            to_T(hm, n, hmT)

        def to_T(hm, n, hmT):
            pt = K.ps()
            pb = pt[:].bitcast(BF16)
            for k in range(8):
                K.tr(pb[:, k * 128:k * 128 + n], hm[0:n, k * 128:(k + 1) * 128], identb[0:n, 0:n], r=[hm, identb], w=[(pt, 0)])
            K.cp("act", hmT[:, :, 0:n], pb[:, 0:1024].rearrange("p (k c) -> p k c", c=128)[:, :, 0:n], r=[(pt, 0)], w=[hmT])

        def layer_norm_out(z, n, xo, scr, st):
            K.memset("pool", st[0:n, :], 0.0, w=[st])
            K.red("dve", st[0:n, 0:1], z[0:n, :], r=[z], w=[st])
            K.ts("dve", st[0:n, 1:2], st[0:n, 0:1], -1.0 / D, ALU.mult, r=[st], w=[st])
            K.ts("dve", scr[0:n, :], z[0:n, :], st[0:n, 1:2], ALU.add, r=[z, st], w=[scr])
            K.act(xo[0:n, :], scr[0:n, :], AF.Square, r=[scr, st], w=[xo, st], acc=st[0:n, 2:3])
            K.rsqrt("dve", st[0:n, 3:4], st[0:n, 2:3], r=[st], w=[st], mult=1.0 / D, add=LN_EPS)
            K.stt("dve", xo[0:n, :], scr[0:n, :], st[0:n, 3:4], lnG[0:n, :], ALU.mult, ALU.mult, r=[scr, st, lnG], w=[xo])
            K.tt("pool", xo[0:n, :], xo[0:n, :], lnB[0:n, :], ALU.add, r=[xo, lnB], w=[xo])

        def residual_ln(py, t, n, xt, z, xo, scr, st):
            md = mod_of(t)
            K.tt("dve", z[0:n, :], py[0:n, :], md[0:n, 2 * D:3 * D], ALU.mult, r=[py, md], w=[z])
            K.stt("dve", z[0:n, :], xt[0:n, :], ALPHA, z[0:n, :], ALU.mult, ALU.add, r=[xt, z], w=[z])
            layer_norm_out(z, n, xo, scr, st)

        K.fn = dict(rows=rows, load_phase_consts=load_phase_consts, load_sample_mod=load_sample_mod, mod_of=mod_of, load_weights_bf16=load_weights_bf16,
                    modulate_T=modulate_T, to_T=to_T, residual_ln=residual_ln)
        K.ctx = dict(d=d, o=o, x_a=x_a, x_b=x_b, proj_d=proj_d, tb_xa=tb_xa, tb_xb=tb_xb, tb_pj=tb_pj, cst=cst,
                     ident=ident, identb=identb, PS=PS, PSO=PSO, NT=NT, NPG=NPG, NTOK=NTOK, NROW=NROW,
                     TOPK_P=TOPK_P, TOPK_S=TOPK_S, modP=modP, mod_d=mod_d, cfg=cfg)

        with K.scope() as pz:
            zt = K.sb(pz, "zt", [3, ODD_IN], F32)
            K.memset("pool", zt[:], 0.0, w=[zt])
            K.dma("sp", proj_d[0:3, :], zt[:], r=[zt], w=[tb_pj[NT + 1]])

        x_in = None
        for l in range(DEPTH):
            even = (l % 2 == 0)
            NIN = EVEN_IN if even else ODD_IN
            w_in_d = d["even_w_in"][l // 2] if even else d["odd_w_in"][l // 2]
            load_phase_consts(l, 0)
            with K.scope() as pa:
                wt = K.sb(pa, "w_in", [128, 8, ODD_IN], BF16)
                load_weights_bf16(wt, w_in_d, NIN)
                xts = [K.sb(pa, "a_x%d" % i, [128, D], F32) for i in range(2)]
                tmps = [K.sb(pa, "a_t%d" % i, [128, D], F32) for i in range(2)]
                hms = [K.sb(pa, "a_h%d" % i, [128, D], BF16) for i in range(2)]
                hmTs = [K.sb(pa, "a_hT%d" % i, [128, 8, 128], BF16) for i in range(2)]
                pjs = [K.sb(pa, "a_pj%d" % i, [128, ODD_IN], F32) for i in range(1)] * 2
                K.modS = K.sb(pa, "a_modS", [NS, 3 * D], F32)
                load_sample_mod(l, 0, K.modS)

                def a_s1(t):
                    r0, n = rows(t)
                    xt, tmp, hm, hmT = xts[t % 2], tmps[t % 2], hms[t % 2], hmTs[t % 2]
                    if l == 0:
                        src = d["xp"][r0:r0 + n, :] if t < NT else d["xs"][:, :]
                        K.dma("sp", xt[0:n, :], src, r=[], w=[xt])
                    else:
                        K.dma("sp", xt[0:n, :], x_a[r0:r0 + n, :], r=[tb_xa[t]], w=[xt])
                    modulate_T(pa, xt, t, n, tmp, hm, hmT)

                def a_s2(t):
                    r0, n = rows(t)
                    hmT, pj = hmTs[t % 2], pjs[0]
                    nb = (NIN + 511) // 512
                    for cb in range(nb):
                        c0 = cb * 512
                        wd = min(512, NIN - c0)
                        pt = K.ps()
                        hb = cb % 2
                        for k in range(8):
                            K.mm(pt[0:n, hb * 512:hb * 512 + wd], hmT[:, k, 0:n], wt[:, k, c0:c0 + wd], k == 0, k == 7, r=[hmT, wt], w=[(pt, hb)])
                        K.cp(K.evq(), pj[0:n, c0:c0 + wd], pt[0:n, hb * 512:hb * 512 + wd], r=[(pt, hb)], w=[pj])
                    K.dma("sp", proj_d[3 + r0:3 + r0 + n, 0:NIN], pj[0:n, 0:NIN], r=[pj], w=[tb_pj[t]])

                a_s1(0)
                for t in range(NT + 1):
                    if t + 1 <= NT:
                        a_s1(t + 1)
                    a_s2(t)
                K.modS = None
            if cfg.get("mix", "real") == "stub":
                phase_b_stub(K, l)
            elif even:
                phase_b_even(K, l)
            else:
                phase_b_odd(K, l)
            load_phase_consts(l, 1)
            with K.scope() as pf:
                wu = K.sb(pf, "w_up", [128, 8, DFF], BF16)
                wd_ = K.sb(pf, "w_dn", [128, 32, D], BF16)
                load_weights_bf16(wu, d["mlp_up"][l], DFF)
                load_weights_bf16(wd_, d["mlp_down"][l], D)
                xts = [K.sb(pf, "f_x%d" % i, [128, D], F32) for i in range(2)]
                tmps = [K.sb(pf, "f_t%d" % i, [128, D], F32) for i in range(1)] * 2
                hms = [K.sb(pf, "f_h%d" % i, [128, D], BF16) for i in range(1)] * 2
                hmTs = [K.sb(pf, "f_hT%d" % i, [128, 8, 128], BF16) for i in range(2)]
                uTs = [K.sb(pf, "f_uT%d" % i, [128, 32, 128], BF16) for i in range(1)] * 2
                rl = [K.sb(pf, "f_rl%d" % i, [128, 1024], F32) for i in range(1)] * 2
                xos = [K.sb(pf, "f_xo%d" % i, [128, D], F32) for i in range(1)] * 2
                sts = [K.sb(pf, "f_st%d" % i, [128, 4], F32) for i in range(2)]
                last = (l == DEPTH - 1)
                K.modS = K.sb(pf, "f_modS", [NS, 3 * D], F32)
                load_sample_mod(l, 1, K.modS)
                zb = K.sb(pf, "f_z", [128, D], F32)

                def f_s1(t):
                    r0, n = rows(t)
                    xt, tmp, hm, hmT = xts[t % 2], tmps[0], hms[0], hmTs[t % 2]
                    K.dma("sp", xt[0:n, :], x_b[r0:r0 + n, :], r=[tb_xb[t]], w=[xt])
                    modulate_T(pf, xt, t, n, tmp, hm, hmT)

                def f_s2(t):
                    r0, n = rows(t)
                    xt, hmT, uT, xo, st = xts[t % 2], hmTs[t % 2], uTs[0], xos[0], sts[t % 2]
                    for g in range(4):
                        pt = K.ps()
                        for j in range(8):
                            fc = g * 8 + j
                            for k in range(8):
                                K.mm(pt[:, j * 128:j * 128 + n], wu[:, k, fc * 128:(fc + 1) * 128], hmT[:, k, 0:n], k == 0, k == 7, r=[wu, hmT], w=[(pt, j // 4)])
                        r_ = rl[0]
                        pv = pt[:].rearrange("p (j c) -> p j c", c=128)[:, :, 0:n]
                        rv = r_[:].rearrange("p (j c) -> p j c", c=128)[:, :, 0:n]
                        K.act(rv, pv, AF.Relu, r=[pt], w=[r_])
                        K.tt("pool" if g % 2 else "dve", uT[:, g * 8:(g + 1) * 8, 0:n], rv, rv, ALU.mult, r=[r_], w=[uT])
                    py = K.ps()
                    for hb in range(2):
                        for k in range(32):
                            K.mm(py[0:n, hb * 512:(hb + 1) * 512], uT[:, k, 0:n], wd_[:, k, hb * 512:(hb + 1) * 512], k == 0, k == 31, r=[uT, wd_], w=[(py, hb)])
                    residual_ln(py, t, n, xt, zb, xo, zb, st)
                    if last:
                        dst = o["y_p"][r0:r0 + n, :] if t < NT else o["y_s"][:, :]
                        K.dma("sp", dst, xo[0:n, :], r=[xo], w=[])
                    else:
                        K.dma("sp", x_a[r0:r0 + n, :], xo[0:n, :], r=[xo], w=[tb_xa[t]])

                f_s1(0)
                for t in range(NT + 1):
                    if t + 1 <= NT:
                        f_s1(t + 1)
                    f_s2(t)
                K.modS = None
        S.finish()
    return nc


def phase_b_stub(K, l):
    c = K.ctx
    f = K.fn
    f["load_phase_consts"](l, 0)
    d, NT = c["d"], c["NT"]
    even = (l % 2 == 0)
    with K.scope() as pb:
        wo = K.sb(pb, "w_out", [128, 8, D], BF16)
        f["load_weights_bf16"](wo, (d["even_w_out"] if even else d["odd_w_out"])[l // 2], D)
        xt = K.sb(pb, "b_x", [128, D], F32)
        pj = K.sb(pb, "b_pj", [128, D], F32)
        cat = K.sb(pb, "b_cat", [128, D], BF16)
        catT = K.sb(pb, "b_catT", [128, 8, 128], BF16)
        z = K.sb(pb, "b_z", [128, D], F32)
        xo = K.sb(pb, "b_xo", [128, D], F32)
        st = K.sb(pb, "b_st", [128, 4], F32)
        for t in range(NT + 1):
            r0, n = f["rows"](t)
            if l == 0:
                src = d["xp"][r0:r0 + n, :] if t < NT else d["xs"][:, :]
                K.dma("sp", xt[0:n, :], src, r=[], w=[xt])
            else:
                K.dma("sp", xt[0:n, :], c["x_a"][r0:r0 + n, :], r=[c["tb_xa"][t]], w=[xt])
            if t == NT:
                f["load_sample_mod"](l, 0)
            K.dma("sp", pj[0:n, :], c["proj_d"][3 + r0:3 + r0 + n, 0:D], r=[c["tb_pj"][t]], w=[pj])
            K.cp("dve", cat[0:n, :], pj[0:n, :], r=[pj], w=[cat])
            out_proj_ln(K, t, n, cat, catT, wo, xt, z, xo, st)


def out_proj_ln(K, t, n, cat, catT, wo, xt, z, xo, st):
    c = K.ctx
    f = K.fn
    f["to_T"](cat, n, catT)
    py = K.ps()
    for hb in range(2):
        for k in range(8):
            K.mm(py[0:n, hb * 512:(hb + 1) * 512], catT[:, k, 0:n], wo[:, k, hb * 512:(hb + 1) * 512], k == 0, k == 7, r=[catT, wo], w=[(py, hb)])
    f["residual_ln"](py, t, n, xt, z, xo, z, st)
    r0, _ = f["rows"](t)
    K.dma("sp", c["x_b"][r0:r0 + n, :], xo[0:n, :], r=[xo], w=[c["tb_xb"][t]])


def rope_ops(K, n, src, H, half, rp, c0, t1, t2, dsts, r_src):
    cosb = rp[0:n, c0:c0 + half].unsqueeze(1).unsqueeze(1).to_broadcast([n, H, 2, half])
    sinb = rp[0:n, c0 + half:c0 + 2 * half].unsqueeze(1).to_broadcast([n, H, half])
    nsinb = rp[0:n, c0 + 2 * half:c0 + 3 * half].unsqueeze(1).to_broadcast([n, H, half])
    t1v = t1[0:n, 0:H * 2 * half].rearrange("p (h a c) -> p h a c", h=H, a=2)
    t2v = t2[0:n, 0:H * 2 * half].rearrange("p (h a c) -> p h a c", h=H, a=2)
    K.tt("dve", t1v, src, cosb, ALU.mult, r=r_src + [rp], w=[t1])
    K.tt("pool", t2v[:, :, 0, :], src[:, :, 1, :], nsinb, ALU.mult, r=r_src + [rp], w=[t2])
    K.tt("pool", t2v[:, :, 1, :], src[:, :, 0, :], sinb, ALU.mult, r=r_src + [rp], w=[t2])
    for h0, h1, dap, dt_ in dsts:
        K.tt("dve", dap, t1v[:, h0:h1], t2v[:, h0:h1], ALU.add, r=[t1, t2], w=[dt_])


def phase_b_even(K, l):
    c = K.ctx
    f = K.fn
    d, o, NT, NPG = c["d"], c["o"], c["NT"], c["NPG"]
    PSO, cst, ident, identb = c["PSO"], c["cst"], c["ident"], c["identb"]
    i = l // 2
    TOPK = c["TOPK_P"]
    NIT = 15
    SC_ATT = 128 ** -0.5
    f["load_phase_consts"](l, 0)
    K.PSG = c["PS"][:2]
    with K.scope() as pb:
        sb = lambda nm, sh, dt=F32: K.sb(pb, "e_" + nm, sh, dt)
        wo = sb("wo", [128, 8, D], BF16)
        f["load_weights_bf16"](wo, d["even_w_out"][i], D)
        kc = {}
        for nm in ("dmatT", "qdecB", "cdecB", "gam1B"):
            kc[nm] = sb(nm, [128, 512])
            K.dma("sp", kc[nm][:], d[nm][:], r=[], w=[kc[nm]])
        kdecC = sb("kdecC", [128, 4])
        K.dma("sp", kdecC[:], d["kdecC"][:], r=[], w=[kdecC])
        gnB = sb("gnB", [128, 512])
        K.dma("sp", gnB[:], d["ret_gn"][i:i + 1, :].partition_broadcast(128), r=[], w=[gnB])
        Sret = sb("Sret", [128, 512])
        K.memset("pool", Sret[:], 0.0, w=[Sret])
        pj = sb("pj", [128, EVEN_IN])
        xt = sb("xt", [128, D])
        rp = sb("rp", [128, 288])
        qk_r = sb("qk_r", [128, 1024])
        aq_rb = sb("aq_rb", [128, 512], BF16)
        kvrow = sb("kvrow", [128, 320])
        iq_r = sb("iq_r", [128, 256])
        t1 = sb("t1", [128, 1024])
        t2 = sb("t2", [128, 1024])
        qT, qTd, kT, AT, kd, cen, sq, sg = [sb(nm, [128, 512]) for nm in ("qT", "qTd", "kT", "AT", "kd", "cen", "sq", "sg")]
        akb = sb("akb", [128, 128], BF16)
        ikb = sb("ikb", [128, 64], BF16)
        iqsb = sb("iqsb", [128, 256], BF16)
        aqT = sb("aqT", [128, 512], BF16)
        iqsT = sb("iqsT", [64, 512], BF16)
        rls = [sb("rl%d" % j, [128, 512]) for j in range(2)]
        Es = [sb("E%d" % j, [128, 512], BF16) for j in range(2)]
        PTs = [sb("PT%d" % j, [128, 512], BF16) for j in range(2)]
        mTss = [sb("mTs%d" % j, [128, 128], BF16) for j in range(2)]
        cat = sb("cat", [128, D], BF16)
        catT = sb("catT", [128, 8, 128], BF16)
        z = sb("z", [128, D])
        xo = sb("xo", [128, D])
        st = sb("st", [128, 4])
        s4 = sb("s4", [128, 16])
        aw = sb("aw", [128, 8])
        bs = sb("bs", [128, 8])
        cnt = sb("cnt", [128, NIT])
        rs = sb("rs", [128, 4])

        def x_load(t, n, r0):
            if l == 0:
                src = d["xp"][r0:r0 + n, :] if t < NT else d["xs"][:, :]
                K.dma("sp", xt[0:n, :], src, r=[], w=[xt])
            else:
                K.dma("sp", xt[0:n, :], c["x_a"][r0:r0 + n, :], r=[c["tb_xa"][t]], w=[xt])

        def proj_rope(t, n, r0, rp_src, aq_dst=None):
            aq_dst = aq_rb if aq_dst is None else aq_dst
            K.dma("sp", pj[0:n, :], c["proj_d"][3 + r0:3 + r0 + n, 0:EVEN_IN], r=[c["tb_pj"][t]], w=[pj])
            K.dma("sp", rp[0:n, :], rp_src, r=[], w=[rp])
            v4 = lambda ap, H, half: ap.rearrange("p (h a c) -> p h a c", h=H, a=2)
            rope_ops(K, n, v4(pj[0:n, 0:1024], 8, 64), 8, 64, rp, 0, t1, t2,
                     [(0, 8, v4(qk_r[0:n, :], 8, 64), qk_r)], [pj])
            rope_ops(K, n, v4(pj[0:n, 2048:2688], 5, 64), 5, 64, rp, 0, t1, t2,
                     [(0, 4, v4(aq_dst[0:n, :], 4, 64), aq_dst), (4, 5, v4(kvrow[0:n, 0:128], 1, 64), kvrow)], [pj])
            rope_ops(K, n, v4(pj[0:n, 2816:3136], 5, 32), 5, 32, rp, 192, t1, t2,
                     [(0, 4, v4(iq_r[0:n, :], 4, 32), iq_r), (4, 5, v4(kvrow[0:n, 256:320], 1, 32), kvrow)], [pj])
            K.cp("act", kvrow[0:n, 128:256], pj[0:n, 2688:2816], r=[pj], w=[kvrow])
            K.act(aw[0:n, 0:4], pj[0:n, 3136:3140], AF.Abs, r=[pj], w=[aw], scale=1.0 / 16)
            K.ts("dve", aw[0:n, 4:8], pj[0:n, 3136:3140], 0.0, ALU.is_ge, r=[pj], w=[aw], s2=2.0, op1=ALU.mult)
            K.ts("dve", aw[0:n, 4:8], aw[0:n, 4:8], -1.0, ALU.add, r=[aw], w=[aw])

        def head_norm_gate(n, src, rsrc):
            pov = src.rearrange("p (h e) -> p h e", h=4)
            cv = cen[0:n, :].rearrange("p (h e) -> p h e", h=4)
            K.red("dve", s4[0:n, 0:4], pov, r=rsrc, w=[s4])
            K.ts("dve", s4[0:n, 4:8], s4[0:n, 0:4], -1.0 / 128, ALU.mult, r=[s4], w=[s4])
            K.tt("dve", cv, pov, s4[0:n, 4:8].unsqueeze(2).to_broadcast([n, 4, 128]), ALU.add, r=rsrc + [s4], w=[cen])
            K.tt("pool", sq[0:n, :], cen[0:n, :], cen[0:n, :], ALU.mult, r=[cen], w=[sq])
            K.red("dve", s4[0:n, 8:12], sq[0:n, :].rearrange("p (h e) -> p h e", h=4), r=[sq], w=[s4])
            K.rsqrt("dve", s4[0:n, 12:16], s4[0:n, 8:12], r=[s4], w=[s4], mult=1.0 / 128, add=LN_EPS)
            K.act(sg[0:n, :], pj[0:n, 1536:2048], AF.Silu, r=[pj], w=[sg])
            K.tt("pool", sg[0:n, :], sg[0:n, :], gnB[0:n, :], ALU.mult, r=[sg, gnB], w=[sg])
            K.tt("dve", cv, cv, s4[0:n, 12:16].unsqueeze(2).to_broadcast([n, 4, 128]), ALU.mult, r=[cen, s4], w=[cen])
            K.tt("dve", cat[0:n, 0:512], cen[0:n, :], sg[0:n, :], ALU.mult, r=[cen, sg], w=[cat])

        with K.scope() as pp:
            akT_all = K.sb(pp, "e_akT", [128, NT * 128], BF16)
            ikT_all = K.sb(pp, "e_ikT", [64, NT * 128], BF16)
            Vall = K.sb(pp, "e_Vall", [128, NT, 132], BF16)
            K.memset("pool", Vall[:], 1.0, w=[Vall])
            score = K.sb(pp, "e_score", [128, NT * 128], F32)
            junk = K.sb(pp, "e_junk", [128, NT * 128], F32)
            maskb = K.sb(pp, "e_maskb", [128, NT * 128], BF16)
            STOP = c["cfg"].get("stop", 99)
            for t in range(NT):
                r0, n = t * 128, 128
                x_load(t, n, r0)
                if STOP <= 0:
                    continue
                proj_rope(t, n, r0, d["ropeP"][t])
                if STOP <= 1:
                    continue
                K.dma("sp", o["kv_p"][i, r0:r0 + n, :], kvrow[:], r=[kvrow], w=[])
                pt = K.ps()
                for h in range(8):
                    K.tr(pt[:, h * 128:(h + 1) * 128], qk_r[:, h * 128:(h + 1) * 128], ident[:], r=[qk_r, ident], w=[(pt, h // 4)])
                K.cp("act", qT[:], pt[:, 0:512], r=[(pt, 0)], w=[qT])
                K.tt("dve", qTd[:], pt[:, 0:512], kc["qdecB"][:], ALU.mult, r=[(pt, 0), kc["qdecB"]], w=[qTd])
                K.cp("act", kT[:], pt[:, 512:1024], r=[(pt, 1)], w=[kT])
                pa = K.ps()
                for h in range(4):
                    hs = slice(h * 128, (h + 1) * 128)
                    K.mm(pa[:, hs], kT[:, hs], qT[:, hs], True, True, r=[kT, qT], w=[(pa, 0)])
                K.tt("dve", AT[:], pa[:, 0:512], kc["dmatT"][:], ALU.mult, r=[(pa, 0), kc["dmatT"]], w=[AT])
                if STOP <= 2:
                    continue
                K.tt("pool", kd[:].rearrange("p (h e) -> p h e", h=4), qk_r[:, 512:1024].rearrange("p (h e) -> p h e", h=4),
                     kdecC[:].unsqueeze(2).to_broadcast([128, 4, 128]), ALU.mult, r=[qk_r, kdecC], w=[kd])
                po = K.ps()
                for h in range(4):
                    hs = slice(h * 128, (h + 1) * 128)
                    K.mm(po[:, hs], AT[:, hs], pj[:, 1024 + h * 128:1024 + (h + 1) * 128], True, False, r=[AT, pj], w=[(po, 0)])
                    K.mm(po[:, hs], qTd[:, hs], Sret[:, hs], False, True, r=[qTd, Sret], w=[(po, 0)])
                pss = K.ps()
                for h in range(4):
                    hs = slice(h * 128, (h + 1) * 128)
                    K.mm(pss[:, hs], kd[:, hs], pj[:, 1024 + h * 128:1024 + (h + 1) * 128], True, True, r=[kd, pj], w=[(pss, 0)])
                K.tt("dve", Sret[:], Sret[:], kc["cdecB"][:], ALU.mult, r=[Sret, kc["cdecB"]], w=[Sret])
                K.tt("dve", Sret[:], Sret[:], pss[:, 0:512], ALU.add, r=[Sret, (pss, 0)], w=[Sret])
                if STOP <= 3:
                    continue
                head_norm_gate(n, po[0:n, 0:512], [(po, 0)])
                if STOP <= 4:
                    continue
                if t == NT - 1:
                    for h in range(4):
                        K.dma("sp", o["ret_p"][i, h], Sret[:, h * 128:(h + 1) * 128], r=[Sret], w=[])
                K.cp("pool", akb[:], kvrow[:, 0:128], r=[kvrow], w=[akb])
                K.cp("pool", ikb[:], kvrow[:, 256:320], r=[kvrow], w=[ikb])
                K.tt("dve", iqsb[:].rearrange("p (h e) -> p h e", h=4), iq_r[:].rearrange("p (h e) -> p h e", h=4),
                     aw[:, 0:4].unsqueeze(2).to_broadcast([128, 4, 64]), ALU.mult, r=[iq_r, aw], w=[iqsb])
                if STOP <= 4.2:
                    continue
                pt = K.ps()
                pbb = pt[:].bitcast(BF16)
                K.tr(pbb[:, 0:128], akb[:], identb[:], r=[akb, identb], w=[(pt, 0)])
                K.tr(pbb[0:64, 128:256], ikb[:], identb[:], r=[ikb, identb], w=[(pt, 0)])
                for h in range(4):
                    K.tr(pbb[:, 256 + h * 128:256 + (h + 1) * 128], aq_rb[:, h * 128:(h + 1) * 128], identb[:], r=[aq_rb, identb], w=[(pt, 0)])
                    K.tr(pbb[0:64, 1024 + h * 128:1024 + (h + 1) * 128], iqsb[:, h * 64:(h + 1) * 64], identb[:], r=[iqsb, identb], w=[(pt, 1)])
                if STOP <= 4.4:
                    continue
                K.cp("act", akT_all[:, r0:r0 + 128], pbb[:, 0:128], r=[(pt, 0)], w=[akT_all])
                K.cp("act", ikT_all[:, r0:r0 + 128], pbb[0:64, 128:256], r=[(pt, 0)], w=[ikT_all])
                K.cp("act", aqT[:], pbb[:, 256:768], r=[(pt, 0)], w=[aqT])
                K.cp("act", iqsT[:], pbb[0:64, 1024:1536], r=[(pt, 1)], w=[iqsT])
                K.cp("pool", Vall[:, t, 0:128], pj[:, 2688:2816], r=[pj], w=[Vall])
                if STOP <= 5:
                    continue
                n_k = (t + 1) * 128
                DBG = c["cfg"].get("dbg", "")
                for kb in range((n_k + 511) // 512 if "noidx" not in DBG else 0):
                    k0 = kb * 512
                    wb = min(512, n_k - k0)
                    pts = [K.ps(), K.ps()]
                    for h in range(4):
                        pp, hb = pts[h // 2], h % 2
                        K.mm(pp[:, hb * 512:hb * 512 + wb], iqsT[:, h * 128:(h + 1) * 128], ikT_all[:, k0:k0 + wb], True, True, r=[iqsT, ikT_all], w=[(pp, hb)])
                        rl = rls[h % 2]
                        K.act(rl[:, 0:wb], pp[:, hb * 512:hb * 512 + wb], AF.Relu, r=[(pp, hb)], w=[rl])
                        if h == 0:
                            K.ts("dve", score[:, k0:k0 + wb], rl[:, 0:wb], aw[:, 4:5], ALU.mult, r=[rl, aw], w=[score])
                        else:
                            K.stt("dve", score[:, k0:k0 + wb], rl[:, 0:wb], aw[:, 4 + h:5 + h], score[:, k0:k0 + wb], ALU.mult, ALU.add, r=[rl, aw, score], w=[score])
                if "noidx" in DBG:
                    K.memset("dve", score[:, 0:n_k], 0.0, w=[score])
                if n_k > TOPK:
                    K.red("dve", bs[:, 0:1], score[:, 0:n_k], r=[score], w=[bs], op=ALU.max)
                    K.red("dve", bs[:, 1:2], score[:, 0:n_k], r=[score], w=[bs], op=ALU.min)
                    K.tt("dve", bs[:, 2:3], bs[:, 0:1], bs[:, 1:2], ALU.subtract, r=[bs], w=[bs])
                else:
                    K.memset("dve", bs[:, 1:2], -1e29, w=[bs])
                K.tt("dve", score[:, r0:r0 + 128], score[:, r0:r0 + 128], cst["cmask"][:], ALU.add, r=[score, cst["cmask"]], w=[score])
                if n_k > TOPK:
                    K.memset("pool", cnt[:], 0.0, w=[cnt])
                    for it in range(NIT):
                        K.ts("dve", bs[:, 3:4], bs[:, 2:3], 0.5 ** (it + 1), ALU.mult, r=[bs], w=[bs])
                        K.tt("dve", bs[:, 4:5], bs[:, 1:2], bs[:, 3:4], ALU.add, r=[bs], w=[bs])
                        K.ts("dve", junk[:, 0:n_k], score[:, 0:n_k], bs[:, 4:5], ALU.is_ge, r=[score, bs, cnt], w=[junk, cnt], s2=0.0, op1=ALU.add, acc=cnt[:, it:it + 1])
                        K.ts("dve", bs[:, 5:6], cnt[:, it:it + 1], float(TOPK), ALU.is_ge, r=[cnt], w=[bs])
                        K.stt("dve", bs[:, 1:2], bs[:, 5:6], bs[:, 3:4], bs[:, 1:2], ALU.mult, ALU.add, r=[bs], w=[bs])
                K.ts("dve", maskb[:, 0:n_k], score[:, 0:n_k], bs[:, 1:2], ALU.is_ge, r=[score, bs], w=[maskb])
                if STOP <= 6:
                    continue
                if "noatt" in DBG:
                    for h in range(4):
                        K.mm(PSO[h // 2][:, (h % 2) * 512:(h % 2) * 512 + 130], ident[:], junk[:, 0:130], True, True, r=[ident, junk], w=[(PSO[h // 2], h % 2)])
                def att_front(kt):
                    ks = slice(kt * 128, (kt + 1) * 128)
                    pm = K.ps()
                    pmb = pm[:].bitcast(BF16)
                    K.tr(pmb[:, 0:128], maskb[:, ks], identb[:], r=[maskb, identb], w=[(pm, 0)])
                    K.mm(pm[:, 512:1024], akT_all[:, ks], aqT[:], True, True, r=[akT_all, aqT], w=[(pm, 1)])
                    E, PT, mTs = Es[kt % 2], PTs[kt % 2], mTss[kt % 2]
                    K.act(E[:], pm[:, 512:1024], AF.Exp, r=[(pm, 1)], w=[E], scale=SC_ATT)
                    K.cp("act", mTs[:], pmb[:, 0:128], r=[(pm, 0)], w=[mTs])
                    K.tt("pool", PT[:].rearrange("p (h q) -> p h q", h=4), E[:].rearrange("p (h q) -> p h q", h=4),
                         mTs[:].unsqueeze(1).to_broadcast([128, 4, 128]), ALU.mult, r=[E, mTs], w=[PT])

                def att_back(kt, t=t):
                    PT = PTs[kt % 2]
                    for h in range(4):
                        c0 = (h % 2) * 512
                        K.mm(PSO[h // 2][:, c0:c0 + 130], PT[:, h * 128:(h + 1) * 128], Vall[:, kt, 0:130], kt == 0, kt == t, r=[PT, Vall], w=[(PSO[h // 2], h % 2)])

                if "noatt" not in DBG:
                    att_front(0)
                    for kt in range(t + 1):
                        if kt + 1 <= t:
                            att_front(kt + 1)
                        att_back(kt)
                for h in range(4):
                    c0 = (h % 2) * 512
                    pso = PSO[h // 2]
                    K.S.op("dve", lambda e, pso=pso, c0=c0, h=h: e.reciprocal(out=rs[:, h:h + 1], in_=pso[:, c0 + 128:c0 + 129]), r=[(pso, h % 2)], w=[rs])
                    K.ts("dve", cat[:, 512 + h * 128:512 + (h + 1) * 128], pso[:, c0:c0 + 128], rs[:, h:h + 1], ALU.mult, r=[(pso, h % 2), rs], w=[cat])
                out_proj_ln(K, t, n, cat, catT, wo, xt, z, xo, st)
        if not c["cfg"].get("skip_sample"):
            sample_even(K, l, pb, locals())
    K.PSG = c["PS"][:4]


def bisect_thr(K, n, score, n_k, topk, bs, cnt, junk, nit):
    K.red("dve", bs[0:n, 0:1], score[0:n, 0:n_k], r=[score], w=[bs], op=ALU.max)
    K.red("dve", bs[0:n, 1:2], score[0:n, 0:n_k], r=[score], w=[bs], op=ALU.min)
    K.tt("dve", bs[0:n, 2:3], bs[0:n, 0:1], bs[0:n, 1:2], ALU.subtract, r=[bs], w=[bs])
    K.memset("pool", cnt[0:n, :], 0.0, w=[cnt])
    for it in range(nit):
        K.ts("dve", bs[0:n, 3:4], bs[0:n, 2:3], 0.5 ** (it + 1), ALU.mult, r=[bs], w=[bs])
        K.tt("dve", bs[0:n, 4:5], bs[0:n, 1:2], bs[0:n, 3:4], ALU.add, r=[bs], w=[bs])
        K.ts("dve", junk[0:n, 0:n_k], score[0:n, 0:n_k], bs[0:n, 4:5], ALU.is_ge, r=[score, bs, cnt], w=[junk, cnt], s2=0.0, op1=ALU.add, acc=cnt[0:n, it:it + 1])
        K.ts("dve", bs[0:n, 5:6], cnt[0:n, it:it + 1], float(topk), ALU.is_ge, r=[cnt], w=[bs])
        K.stt("dve", bs[0:n, 1:2], bs[0:n, 5:6], bs[0:n, 3:4], bs[0:n, 1:2], ALU.mult, ALU.add, r=[bs], w=[bs])


def sample_even(K, l, pb, L):
    c = K.ctx
    f = K.fn
    d, o, NT, NPG = c["d"], c["o"], c["NT"], c["NPG"]
    cst, ident, PSO = c["cst"], c["ident"], c["PSO"]
    i = l // 2
    TOPK = c["TOPK_S"]
    NIT = 15
    SC_ATT = 128 ** -0.5
    n, t, r0 = NS, NT, c["NTOK"]
    NK = NPG * 128
    pj, xt, qk_r, kvrow, iq_r, t1, t2 = [L[k] for k in ("pj", "xt", "qk_r", "kvrow", "iq_r", "t1", "t2")]
    AT, sq, aw, s4, cat, catT, z, xo, st, wo, kc, bs, cnt = [L[k] for k in ("AT", "sq", "aw", "s4", "cat", "catT", "z", "xo", "st", "wo", "kc", "bs", "cnt")]
    g4 = lambda ap: ap.rearrange("p (h e) -> p h e", h=4)
    with K.scope() as px:
        sb = lambda nm, sh, dt=F32: K.sb(px, "es_" + nm, sh, dt)
        aq_f = sb("aq_f", [NS, 512])
        f["load_sample_mod"](l, 0)
        L["x_load"](t, n, r0)
        L["proj_rope"](t, n, r0, d["ropeS"][:, :], aq_f)
        K.dma("sp", o["kv_s"][i, :, :], kvrow[0:n, :], r=[kvrow], w=[])
        v_ap = pj[0:n, 1024:1536]
        K.tt("dve", sq[0:n, :], qk_r[0:n, 0:512], qk_r[0:n, 512:1024], ALU.mult, r=[qk_r], w=[sq])
        K.red("dve", s4[0:n, 0:4], g4(sq[0:n, :]), r=[sq], w=[s4])
        K.ts("dve", s4[0:n, 0:4], s4[0:n, 0:4], 128 ** -0.5, ALU.mult, r=[s4], w=[s4])
        pt = K.ps()
        for h in range(4):
            K.tr(pt[:, h * 16:(h + 1) * 16], qk_r[0:n, h * 128:(h + 1) * 128], ident[0:n, 0:n], r=[qk_r, ident], w=[(pt, 0)])
        qTs = sb("qTs", [128, 64])
        K.cp("act", qTs[:], pt[:, 0:64], r=[(pt, 0)], w=[qTs])
        K.tt("dve", t1[:, 0:1024].rearrange("p (h s m) -> p h s m", h=4, s=16),
             qTs[:].rearrange("p (h s) -> p h s", h=4).unsqueeze(3).to_broadcast([128, 4, 16, 16]),
             cst["delta16"][:].rearrange("p (s m) -> p s m", s=16).unsqueeze(1).to_broadcast([128, 4, 16, 16]),
             ALU.mult, r=[qTs, cst["delta16"]], w=[t1])
        S0 = [sb("S0%d" % j, [128, 512]) for j in range(2)]
        Sn = [sb("Sn%d" % j, [128, 512]) for j in range(2)]
        km = sb("km", [NS, 512])
        for s in range(NS):
            S0s, Sns = S0[s % 2], Sn[s % 2]
            for h in range(4):
                K.dma("sp", S0s[:, h * 128:(h + 1) * 128], d["st_ret"][i, s, h], r=[], w=[S0s])
            for h in range(4):
                hs = slice(h * 128, (h + 1) * 128)
                K.mm(PSO[h // 2][0:n, (h % 2) * 512:(h % 2) * 512 + 128], t1[:, (h * 16 + s) * 16:(h * 16 + s + 1) * 16], S0s[:, hs], s == 0, s == NS - 1, r=[t1, S0s], w=[(PSO[h // 2], h % 2)])
            K.ts("dve", km[:], qk_r[0:n, 512:1024], ident[0:n, s:s + 1], ALU.mult, r=[qk_r, ident], w=[km], s2=128 ** -0.5, op1=ALU.mult)
            pss = K.ps()
            for h in range(4):
                hs = slice(h * 128, (h + 1) * 128)
                K.mm(pss[:, hs], km[:, hs], pj[0:n, 1024 + h * 128:1024 + (h + 1) * 128], True, True, r=[km, pj], w=[(pss, 0)])
            K.tt("pool", Sns[:], S0s[:], kc["gam1B"][:], ALU.mult, r=[S0s, kc["gam1B"]], w=[Sns])
            K.tt("dve", Sns[:], Sns[:], pss[:, 0:512], ALU.add, r=[Sns, (pss, 0)], w=[Sns])
            for h in range(4):
                K.dma("sp", o["ret_s"][i, s, h], Sns[:, h * 128:(h + 1) * 128], r=[Sns], w=[])
        for h in range(4):
            hs = slice(h * 128, (h + 1) * 128)
            K.tt("dve", AT[0:n, hs], PSO[h // 2][0:n, (h % 2) * 512:(h % 2) * 512 + 128], kc["gam1B"][0:n, hs], ALU.mult, r=[(PSO[h // 2], h % 2), kc["gam1B"]], w=[AT])
        K.tt("pool", g4(sq[0:n, :]), g4(v_ap), s4[0:n, 0:4].unsqueeze(2).to_broadcast([n, 4, 128]), ALU.mult, r=[pj, s4], w=[sq])
        K.tt("dve", AT[0:n, :], AT[0:n, :], sq[0:n, :], ALU.add, r=[AT, sq], w=[AT])
        L["head_norm_gate"](n, AT[0:n, :], [AT])
        iqs = sb("iqs", [NS, 256])
        K.tt("dve", g4(iqs[:]), g4(iq_r[0:n, :]), aw[0:n, 0:4].unsqueeze(2).to_broadcast([n, 4, 64]), ALU.mult, r=[iq_r, aw], w=[iqs])
        pt = K.ps()
        for h in range(4):
            K.tr(pt[:, h * 16:(h + 1) * 16], aq_f[:, h * 128:(h + 1) * 128], ident[0:n, 0:n], r=[aq_f, ident], w=[(pt, 0)])
            K.tr(pt[0:64, 512 + h * 16:512 + (h + 1) * 16], iqs[:, h * 64:(h + 1) * 64], ident[0:n, 0:n], r=[iqs, ident], w=[(pt, 1)])
        aqTa = sb("aqTa", [128, 64])
        iqTa = sb("iqTa", [64, 64])
        K.cp("act", aqTa[:], pt[:, 0:64], r=[(pt, 0)], w=[aqTa])
        K.cp("act", iqTa[:], pt[0:64, 512:576], r=[(pt, 1)], w=[iqTa])
        sgnB = sb("sgnB", [128, NS * 4])
        sg_d = K.dram("sg_d%d" % l, [1, NS * 4], F32, "Internal")
        K.dma("sp", sg_d[0:1, :].rearrange("o (s h) -> (o s) h", h=4), aw[0:n, 4:8], r=[aw], w=[sg_d])
        K.dma("sp", sgnB[:], sg_d[0:1, :].partition_broadcast(128), r=[sg_d], w=[sgnB])
        idxi = sb("idxi", [128, NS * NPG], I32)
        idxf = sb("idxf", [128, NS * NPG])
        K.dma("sp", idxi[:], d["ptab"][:, :].rearrange("(o s) j -> o (s j)", o=1).partition_broadcast(128), r=[], w=[idxi])
        K.cp("dve", idxf[:], idxi[:], r=[idxi], w=[idxf])
        K.ts("dve", idxf[:], idxf[:], 128.0, ALU.mult, r=[idxf, cst["iotap"]], w=[idxf], s2=cst["iotap"][:, 0:1], op1=ALU.add)
        if i > 0:
            K.ts("dve", idxf[:], idxf[:], float(i * c["cfg"]["NPOOL"] * 128), ALU.add, r=[idxf], w=[idxf])
        K.cp("dve", idxi[:], idxf[:], r=[idxf], w=[idxi])
        cache = d["cache_kv"]
        sc_d = K.dram("sc_d%d" % l, [NS, NK], F32, "Internal")
        os_d = K.dram("os_d%d" % l, [4, NS, 132], F32, "Internal")
        kig = [sb("kig%d" % j, [128, NPG, 320]) for j in range(1)] * 2
        kTs = sb("kTs", [128, NK + 4])
        kiT = kTs
        scT = sb("scT", [128, NS * NPG])
        rlS = sb("rlS", [128, NPG * 4])
        scr_t = sb("scr_t", [NPG, 128])
        for s in range(NS):
            kg = kig[s % 2]
            for j in range(NPG):
                col = s * NPG + j
                K.S.dma("pool", lambda e, kg=kg, j=j, col=col: e.indirect_dma_start(
                    out=kg[:, j, :], out_offset=None, in_=cache[:, :],
                    in_offset=bass.IndirectOffsetOnAxis(ap=idxi[:, col:col + 1], axis=0)), r=[idxi], w=[kg])
            for g in range((NPG + 3) // 4):
                pt = K.ps()
                nj = min(4, NPG - g * 4)
                for jj in range(nj):
                    K.tr(pt[0:64, jj * 128:(jj + 1) * 128], kg[:, g * 4 + jj, 256:320], ident[:], r=[kg, ident], w=[(pt, 0)])
                K.cp(K.evq(), kiT[0:64, g * 512:g * 512 + nj * 128], pt[0:64, 0:nj * 128], r=[(pt, 0)], w=[kiT])
            pq = K.ps()
            for j in range(NPG):
                K.mm(pq[:, j * 4:(j + 1) * 4], kiT[0:64, j * 128:(j + 1) * 128], iqTa[:].rearrange("p (h s) -> p h s", h=4)[:, :, s], True, True, r=[kiT, iqTa], w=[(pq, 0)])
            K.act(rlS[:], pq[:, 0:NPG * 4], AF.Relu, r=[(pq, 0)], w=[rlS])
            rv = rlS[:].rearrange("p (j h) -> p j h", h=4)
            K.tt("dve", rv, rv, sgnB[:, s * 4:(s + 1) * 4].unsqueeze(1).to_broadcast([128, NPG, 4]), ALU.mult, r=[rlS, sgnB], w=[rlS])
            K.red("dve", scT[:, s * NPG:(s + 1) * NPG], rv, r=[rlS], w=[scT])
            pt2 = K.ps()
            K.tr(pt2[0:NPG, 0:128], scT[:, s * NPG:(s + 1) * NPG], ident[:], r=[scT, ident], w=[(pt2, 0)])
            K.cp("act", scr_t[:], pt2[0:NPG, 0:128], r=[(pt2, 0)], w=[scr_t])
            K.dma("sp", sc_d[s, :].rearrange("(j p) -> j p", p=128), scr_t[:], r=[scr_t], w=[sc_d])
        K.tt("dve", t2[0:n, 0:256].rearrange("p (h e) -> p h e", h=4), g4(iqs[:]), kvrow[0:n, 256:320].unsqueeze(1).to_broadcast([n, 4, 64]), ALU.mult, r=[iqs, kvrow], w=[t2])
        K.red("dve", s4[0:n, 4:8], t2[0:n, 0:256].rearrange("p (h e) -> p h e", h=4), r=[t2], w=[s4])
        K.ts("dve", s4[0:n, 4:8], s4[0:n, 4:8], 0.0, ALU.max, r=[s4], w=[s4])
        K.tt("dve", s4[0:n, 4:8], s4[0:n, 4:8], aw[0:n, 4:8], ALU.mult, r=[s4, aw], w=[s4])
        srow = sb("srow", [NS, NK + 1])
        jrow = kTs
        K.dma("sp", srow[:, 0:NK], sc_d[:, :], r=[sc_d], w=[srow])
        K.red("dve", srow[:, NK:NK + 1], s4[0:n, 4:8], r=[s4, srow], w=[srow])
        if NK + 1 > TOPK:
            bisect_thr(K, n, srow, NK + 1, TOPK, bs, cnt, jrow, NIT)
        else:
            K.memset("dve", bs[0:n, 1:2], -1e29, w=[bs])
        dth = sb("dth", [NS, NS])
        K.ts("dve", dth[:], ident[0:NS, 0:NS], bs[0:n, 1:2], ALU.mult, r=[ident, bs], w=[dth])
        pq = K.ps()
        K.mm(pq[:, 0:NS], cst["ones"][0:NS, :], dth[:], True, True, r=[cst["ones"], dth], w=[(pq, 0)])
        thrB = sb("thrB", [128, NS])
        K.cp("act", thrB[:], pq[:, 0:NS], r=[(pq, 0)], w=[thrB])
        kvg = kig
        mT = sb("mT", [128, NPG])
        Es = sb("Es", [128, NPG * 4])
        PT2 = sb("PT2", [128, NPG * 4])
        osm = sb("osm", [4, NS, 132])
        K.memset("pool", osm[:], 0.0, w=[osm])
        for s in range(NS):
            kv = kvg[s % 2]
            for j in range(NPG):
                col = s * NPG + j
                K.S.dma("pool", lambda e, kv=kv, j=j, col=col: e.indirect_dma_start(
                    out=kv[:, j, :], out_offset=None, in_=cache[:, :],
                    in_offset=bass.IndirectOffsetOnAxis(ap=idxi[:, col:col + 1], axis=0)), r=[idxi], w=[kv])
            for g in range((NPG + 3) // 4):
                pt = K.ps()
                nj = min(4, NPG - g * 4)
                for jj in range(nj):
                    K.tr(pt[:, jj * 128:(jj + 1) * 128], kv[:, g * 4 + jj, 0:128], ident[:], r=[kv, ident], w=[(pt, 0)])
                K.cp(K.evq(), kTs[:, g * 512:g * 512 + nj * 128], pt[:, 0:nj * 128], r=[(pt, 0)], w=[kTs])
            pq = K.ps()
            for j in range(NPG):
                K.mm(pq[:, j * 4:(j + 1) * 4], kTs[:, j * 128:(j + 1) * 128], aqTa[:].rearrange("p (h s) -> p h s", h=4)[:, :, s], True, True, r=[kTs, aqTa], w=[(pq, 0)])
            K.act(Es[:], pq[:, 0:NPG * 4], AF.Exp, r=[(pq, 0)], w=[Es], scale=SC_ATT)
            K.ts("dve", mT[:], scT[:, s * NPG:(s + 1) * NPG], thrB[:, s:s + 1], ALU.is_ge, r=[scT, thrB], w=[mT])
            K.tt("dve", PT2[:].rearrange("p (j h) -> p j h", h=4), Es[:].rearrange("p (j h) -> p j h", h=4),
                 mT[:].unsqueeze(2).to_broadcast([128, NPG, 4]), ALU.mult, r=[Es, mT], w=[PT2])
            po2 = K.ps()
            for j in range(NPG):
                K.mm(po2[0:4, 0:128], PT2[:, j * 4:(j + 1) * 4], kv[:, j, 128:256], j == 0, j == NPG - 1, r=[PT2, kv], w=[(po2, 0)])
            for j in range(NPG):
                K.mm(po2[0:4, 512:513], PT2[:, j * 4:(j + 1) * 4], cst["ones"][:, 0:1], j == 0, j == NPG - 1, r=[PT2, cst["ones"]], w=[(po2, 1)])
            K.cp("act", osm[:, s, 0:128], po2[0:4, 0:128], r=[(po2, 0)], w=[osm])
            K.cp("act", osm[:, s, 128:129], po2[0:4, 512:513], r=[(po2, 1)], w=[osm])
        ot = sb("ot", [NS, 4, 132])
        K.dma("sp", os_d[:, :, :], osm[:], r=[osm], w=[os_d])
        K.dma("sp", ot[:], os_d[:, :, :].rearrange("h s e -> s h e"), r=[os_d], w=[ot])
        K.tt("dve", g4(t2[0:n, 0:512]), g4(aq_f[:]), kvrow[0:n, 0:128].unsqueeze(1).to_broadcast([n, 4, 128]), ALU.mult, r=[aq_f, kvrow], w=[t2])
        K.red("dve", s4[0:n, 8:12], g4(t2[0:n, 0:512]), r=[t2], w=[s4])
        K.act(s4[0:n, 8:12], s4[0:n, 8:12], AF.Exp, r=[s4], w=[s4], scale=SC_ATT)
        K.tt("dve", bs[0:n, 6:7], srow[:, NK:NK + 1], bs[0:n, 1:2], ALU.is_ge, r=[srow, bs], w=[bs])
        K.ts("dve", s4[0:n, 8:12], s4[0:n, 8:12], bs[0:n, 6:7], ALU.mult, r=[s4, bs], w=[s4])
        K.tt("dve", g4(t1[0:n, 0:512]), kvrow[0:n, 128:256].unsqueeze(1).to_broadcast([n, 4, 128]),
             s4[0:n, 8:12].unsqueeze(2).to_broadcast([n, 4, 128]), ALU.mult, r=[kvrow, s4], w=[t1])
        K.tt("dve", g4(t1[0:n, 0:512]), g4(t1[0:n, 0:512]), ot[:, :, 0:128], ALU.add, r=[t1, ot], w=[t1])
        K.tt("dve", s4[0:n, 12:16], ot[:, :, 128], s4[0:n, 8:12], ALU.add, r=[ot, s4], w=[s4])
        K.S.op("dve", lambda e: e.reciprocal(out=s4[0:n, 12:16], in_=s4[0:n, 12:16]), r=_bufs([s4]), w=_bufs([s4]))
        K.tt("dve", g4(cat[0:n, 512:1024]), g4(t1[0:n, 0:512]), s4[0:n, 12:16].unsqueeze(2).to_broadcast([n, 4, 128]), ALU.mult, r=[t1, s4], w=[cat])
        out_proj_ln(K, t, n, cat, catT, wo, xt, z, xo, st)


def phase_b_odd(K, l):
    c = K.ctx
    f = K.fn
    d, o, NT = c["d"], c["o"], c["NT"]
    cst, ident, PS = c["cst"], c["ident"], c["PS"]
    jl = l // 2
    NTOK = c["NTOK"]
    K.PSG = PS[:4]
    f["load_phase_consts"](l, 0)
    g8 = lambda ap: ap.rearrange("p (h e) -> p h e", h=8)
    bc8 = lambda ap, n, w: ap.unsqueeze(2).to_broadcast([n, 8, w])
    with K.scope() as pb:
        sb = lambda nm, sh, dt=F32: K.sb(pb, "o_" + nm, sh, dt)
        wo = sb("wo", [128, 8, D], BF16)
        f["load_weights_bf16"](wo, d["odd_w_out"][jl], D)
        cw = sb("cw", [128, 4, 3072])
        for tap in range(4):
            K.dma("sp", cw[:, tap, :], d["odd_conv"][jl, tap:tap + 1, :].partition_broadcast(128), r=[], w=[cw])
        gnB = sb("gnB", [128, 128])
        K.dma("sp", gnB[:], d["odd_gn"][jl:jl + 1, :].partition_broadcast(128), r=[], w=[gnB])
        nea = sb("nea", [128, 8])
        dtb = sb("dtb", [128, 8])
        K.dma("sp", nea[:], d["odd_a_log"][jl:jl + 1, :].partition_broadcast(128), r=[], w=[nea])
        K.dma("sp", dtb[:], d["odd_dt_bias"][jl:jl + 1, :].partition_broadcast(128), r=[], w=[dtb])
        K.act(nea[:], nea[:], AF.Exp, r=[nea], w=[nea])
        K.ts("dve", nea[:], nea[:], -1.0, ALU.mult, r=[nea], w=[nea])
        pj = sb("pj", [128, ODD_IN])
        xt = sb("xt", [128, D])
        sh = sb("sh", [128, 3072])
        cat = sb("cat", [128, D], BF16)
        catT = sb("catT", [128, 8, 128], BF16)
        z = sb("z", [128, D])
        xo = sb("xo", [128, D])
        st = sb("st", [128, 4])
        sm = sb("sm", [128, 96])
        ob = sb("ob", [128, 1024])
        sq = sh

        def x_load(t, n, r0):
            K.dma("sp", xt[0:n, :], c["x_a"][r0:r0 + n, :], r=[c["tb_xa"][t]], w=[xt])

        def conv_gates(t, n, r0, taps):
            K.dma("sp", pj[0:n, :], c["proj_d"][3 + r0:3 + r0 + n, :], r=[c["tb_pj"][t]], w=[pj])
            pq = pj[0:n, 0:3072]
            K.tt("pool", pq, pq, cw[0:n, 3, :], ALU.mult, r=[pj, cw], w=[pj])
            for tap in range(3):
                src, deps = taps[tap]
                K.dma("sp", sh[0:n, :], src, r=deps, w=[sh])
                K.tt("pool", sh[0:n, :], sh[0:n, :], cw[0:n, tap, :], ALU.mult, r=[sh, cw], w=[sh])
                K.tt("dve", pq, pq, sh[0:n, :], ALU.add, r=[pj, sh], w=[pj])
            K.act(pj[0:n, 0:4096], pj[0:n, 0:4096], AF.Silu, r=[pj], w=[pj])
            K.tt("pool", sq[0:n, 0:2048], pj[0:n, 0:2048], pj[0:n, 0:2048], ALU.mult, r=[pj], w=[sq])
            K.red("dve", sm[0:n, 16:32], sq[0:n, 0:2048].rearrange("p (h e) -> p h e", h=16), r=[sq], w=[sm])
            K.rsqrt("dve", sm[0:n, 16:32], sm[0:n, 16:32], r=[sm], w=[sm], mult=1.0, add=1e-6)
            K.ts("dve", sm[0:n, 16:24], sm[0:n, 16:24], 128 ** -0.5, ALU.mult, r=[sm], w=[sm])
            qk = pj[0:n, 0:2048].rearrange("p (h e) -> p h e", h=16)
            K.tt("dve", qk, qk, sm[0:n, 16:32].unsqueeze(2).to_broadcast([n, 16, 128]), ALU.mult, r=[pj, sm], w=[pj])
            K.tt("dve", sm[0:n, 32:40], pj[0:n, 4096:4104], dtb[0:n, :], ALU.add, r=[pj, dtb], w=[sm])
            K.act(sm[0:n, 40:48], sm[0:n, 32:40], AF.Abs, r=[sm], w=[sm])
            K.act(sm[0:n, 40:48], sm[0:n, 40:48], AF.Exp, r=[sm], w=[sm], scale=-1.0)
            K.act(sm[0:n, 40:48], sm[0:n, 40:48], AF.Ln, r=[sm], w=[sm], bias=1.0)
            K.ts("dve", sm[0:n, 32:40], sm[0:n, 32:40], 0.0, ALU.max, r=[sm], w=[sm])
            K.tt("dve", sm[0:n, 32:40], sm[0:n, 32:40], sm[0:n, 40:48], ALU.add, r=[sm], w=[sm])
            K.tt("dve", sm[0:n, 0:8], sm[0:n, 32:40], nea[0:n, :], ALU.mult, r=[sm, nea], w=[sm])
            K.act(sm[0:n, 8:16], pj[0:n, 4104:4112], AF.Sigmoid, r=[pj], w=[sm])

        def rms_gate(n, src, rsrc):
            K.tt("pool", sq[0:n, 0:1024], src, src, ALU.mult, r=rsrc, w=[sq])
            K.red("dve", sm[0:n, 48:56], g8(sq[0:n, 0:1024]), r=[sq], w=[sm])
            K.rsqrt("dve", sm[0:n, 48:56], sm[0:n, 48:56], r=[sm], w=[sm], mult=1.0 / 128, add=LN_EPS)
            K.tt("dve", g8(sq[0:n, 0:1024]), g8(src), bc8(sm[0:n, 48:56], n, 128), ALU.mult, r=rsrc + [sm], w=[sq])
            K.tt("pool", g8(sq[0:n, 0:1024]), g8(sq[0:n, 0:1024]), gnB[0:n, :].unsqueeze(1).to_broadcast([n, 8, 128]), ALU.mult, r=[sq, gnB], w=[sq])
            K.tt("dve", cat[0:n, :], sq[0:n, 0:1024], pj[0:n, 3072:4096], ALU.mult, r=[sq, pj], w=[cat])

        with K.scope() as pp:
            G = [K.sb(pp, "o_G%d" % j, [128, 1024], F32) for j in range(15)]
            Sst = K.sb(pp, "o_S", [128, 1024], F32)
            K.memset("pool", Sst[:], 0.0, w=[Sst])
            Dg, E, ES, _, kb, kbg, vb, kg, qT, kT, kbT, Nn, Mm, aqkT, R = G
            Pa, Pb, Qa, Qb = G[0], G[1], G[2], G[3]
            u_, wT, vnew = G[4], G[9], G[10]
            identB = ident[:].unsqueeze(1).to_broadcast([128, 8, 128])
            for t in range(NT):
                r0, n = t * 128, 128
                x_load(t, n, r0)
                prevb = c["tb_pj"][t - 1] if t > 0 else c["tb_pj"][NT + 1]
                taps = [(c["proj_d"][r0 + tap:r0 + tap + 128, 0:3072], [c["tb_pj"][t], prevb]) for tap in range(3)]
                conv_gates(t, n, r0, taps)
                q_ap, k_ap, v_ap = pj[:, 0:1024], pj[:, 1024:2048], pj[:, 2048:3072]
                g_, be = sm[:, 0:8], sm[:, 8:16]
                pg = K.ps()
                K.mm(pg[:, 0:8], cst["triu"][:], g_, True, True, r=[cst["triu"], sm], w=[(pg, 0)])
                gc = sm[:, 56:64]
                K.cp("act", gc, pg[:, 0:8], r=[(pg, 0)], w=[sm])
                K.tt("pool", g8(Dg[:]), identB, bc8(gc, 128, 128), ALU.mult, r=[ident, sm], w=[Dg])
                pr = K.ps()
                K.mm(pr[:, 0:512], cst["ones"][:], Dg[:, 0:512], True, True, r=[cst["ones"], Dg], w=[(pr, 0)])
                K.mm(pr[:, 512:1024], cst["ones"][:], Dg[:, 512:1024], True, True, r=[cst["ones"], Dg], w=[(pr, 1)])
                K.tt("dve", g8(E[:]), g8(pr[:]), bc8(gc, 128, 128), ALU.subtract, r=[pr, sm], w=[E])
                K.tt("dve", g8(E[:]), g8(E[:]), cst["mvinc"][:].unsqueeze(1).to_broadcast([128, 8, 128]), ALU.min, r=[E, cst["mvinc"]], w=[E])
                K.act(E[:], E[:], AF.Exp, r=[E], w=[E])
                K.tt("pool", g8(ES[:]), g8(E[:]), cst["strict"][:].unsqueeze(1).to_broadcast([128, 8, 128]), ALU.mult, r=[E, cst["strict"]], w=[ES])
                gl = sm[:, 64:72]
                K.cp("act", gl, g8(pr[:])[:, :, 127], r=[pr], w=[sm])
                K.tt("dve", sm[:, 72:80], gl, gc, ALU.subtract, r=[sm], w=[sm])
                K.act(sm[:, 72:80], sm[:, 72:80], AF.Exp, r=[sm], w=[sm])
                K.act(sm[:, 80:88], gl, AF.Exp, r=[sm], w=[sm])
                K.act(sm[:, 88:96], gc, AF.Exp, r=[sm], w=[sm])
                K.tt("pool", g8(kb[:]), g8(k_ap), bc8(be, 128, 128), ALU.mult, r=[pj, sm], w=[kb])
                K.tt("pool", g8(kbg[:]), g8(kb[:]), bc8(sm[:, 88:96], 128, 128), ALU.mult, r=[kb, sm], w=[kbg])
                K.tt("pool", g8(vb[:]), g8(v_ap), bc8(be, 128, 128), ALU.mult, r=[pj, sm], w=[vb])
                K.tt("pool", g8(kg[:]), g8(k_ap), bc8(sm[:, 72:80], 128, 128), ALU.mult, r=[pj, sm], w=[kg])
                for src_ap, rs, dst in ((q_ap, [pj], qT), (k_ap, [pj], kT), (kb[:], [kb], kbT)):
                    pt = K.ps()
                    for h in range(8):
                        K.tr(pt[:, h * 128:(h + 1) * 128], src_ap[:, h * 128:(h + 1) * 128], ident[:], r=rs + [ident], w=[(pt, h // 4)])
                    K.cp("act", dst[:], pt[:], r=[pt], w=[dst])
                pn = K.ps()
                pa = K.ps()
                for h in range(8):
                    hs = slice(h * 128, (h + 1) * 128)
                    K.mm(pn[:, hs], kT[:, hs], kbT[:, hs], True, True, r=[kT, kbT], w=[(pn, h // 4)])
                for h in range(8):
                    hs = slice(h * 128, (h + 1) * 128)
                    K.mm(pa[:, hs], kT[:, hs], qT[:, hs], True, True, r=[kT, qT], w=[(pa, h // 4)])
                K.tt("dve", Nn[:], pn[:], ES[:], ALU.mult, r=[pn, ES], w=[Nn])
                K.tt("dve", aqkT[:], pa[:], E[:], ALU.mult, r=[pa, E], w=[aqkT])
                pm = K.ps()
                for h in range(8):
                    hs = slice(h * 128, (h + 1) * 128)
                    K.tr(pm[:, hs], Nn[:, hs], ident[:], r=[Nn, ident], w=[(pm, h // 4)])
                K.cp("act", Mm[:], pm[:], r=[pm], w=[Mm])
                K.tt("pool", g8(R[:]), identB, g8(Nn[:]), ALU.subtract, r=[ident, Nn], w=[R])
                P, Q = Mm, Nn
                Pn, Qn = [Pa, Pb], [Qa, Qb]
                for lev in range(1, 7):
                    P2, Q2 = Pn[lev % 2], Qn[lev % 2]
                    pP = K.ps()
                    for h in range(8):
                        hs = slice(h * 128, (h + 1) * 128)
                        K.mm(pP[:, hs], Q[:, hs], P[:, hs], True, True, r=[Q, P], w=[(pP, h // 4)])
                    if lev < 6:
                        pQ = K.ps()
                        for h in range(8):
                            hs = slice(h * 128, (h + 1) * 128)
                            K.mm(pQ[:, hs], P[:, hs], Q[:, hs], True, True, r=[P, Q], w=[(pQ, h // 4)])
                    K.cp("act", P2[:], pP[:], r=[pP], w=[P2])
                    if lev < 6:
                        K.cp("dve", Q2[:], pQ[:], r=[pQ], w=[Q2])
                    pR = K.ps()
                    for h in range(8):
                        hs = slice(h * 128, (h + 1) * 128)
                        K.mm(pR[:, hs], P2[:, hs], R[:, hs], True, True, r=[P2, R], w=[(pR, h // 4)])
                    K.tt("dve", R[:], R[:], pR[:], ALU.add, r=[R, pR], w=[R])
                    P, Q = P2, Q2
                pu = K.ps()
                pw = K.ps()
                for h in range(8):
                    hs = slice(h * 128, (h + 1) * 128)
                    K.mm(pu[:, hs], R[:, hs], vb[:, hs], True, True, r=[R, vb], w=[(pu, h // 4)])
                for h in range(8):
                    hs = slice(h * 128, (h + 1) * 128)
                    K.mm(pw[:, hs], kbg[:, hs], R[:, hs], True, True, r=[kbg, R], w=[(pw, h // 4)])
                K.cp("act", u_[:], pu[:], r=[pu], w=[u_])
                K.cp("act", wT[:], pw[:], r=[pw], w=[wT])
                pv = K.ps()
                for h in range(8):
                    hs = slice(h * 128, (h + 1) * 128)
                    K.mm(pv[:, hs], wT[:, hs], Sst[:, hs], True, True, r=[wT, Sst], w=[(pv, h // 4)])
                K.tt("dve", vnew[:], u_[:], pv[:], ALU.subtract, r=[u_, pv], w=[vnew])
                po1 = K.ps()
                for h in range(8):
                    hs = slice(h * 128, (h + 1) * 128)
                    K.mm(po1[:, hs], qT[:, hs], Sst[:, hs], True, True, r=[qT, Sst], w=[(po1, h // 4)])
                K.tt("dve", g8(ob[:]), g8(po1[:]), bc8(sm[:, 88:96], 128, 128), ALU.mult, r=[po1, sm], w=[ob])
                po2 = K.ps()
                for h in range(8):
                    hs = slice(h * 128, (h + 1) * 128)
                    K.mm(po2[:, hs], aqkT[:, hs], vnew[:, hs], True, True, r=[aqkT, vnew], w=[(po2, h // 4)])
                K.tt("dve", ob[:], ob[:], po2[:], ALU.add, r=[ob, po2], w=[ob])
                pS = K.ps()
                for h in range(8):
                    hs = slice(h * 128, (h + 1) * 128)
                    K.mm(pS[:, hs], kg[:, hs], vnew[:, hs], True, True, r=[kg, vnew], w=[(pS, h // 4)])
                K.tt("pool", g8(Sst[:]), g8(Sst[:]), bc8(sm[:, 80:88], 128, 128), ALU.mult, r=[Sst, sm], w=[Sst])
                K.tt("dve", Sst[:], Sst[:], pS[:], ALU.add, r=[Sst, pS], w=[Sst])
                rms_gate(n, ob[:], [ob])
                out_proj_ln(K, t, n, cat, catT, wo, xt, z, xo, st)
            for h in range(8):
                K.dma("sp", o["ssm_p"][jl, h], Sst[:, h * 128:(h + 1) * 128], r=[Sst], w=[])
            K.dma("sp", o["conv_p"][jl], c["proj_d"][NTOK:NTOK + 3, 0:3072], r=[c["tb_pj"][NT - 1]], w=[])
        if not c["cfg"].get("skip_sample"):
            sample_odd(K, l, locals())
    K.PSG = PS[:4]


def sample_odd(K, l, L):
    c = K.ctx
    f = K.fn
    d, o, NT = c["d"], c["o"], c["NT"]
    cst, ident = c["cst"], c["ident"]
    jl = l // 2
    n, t, r0 = NS, NT, c["NTOK"]
    pj, xt, sm, ob, cat, catT, z, xo, st, wo = [L[k] for k in ("pj", "xt", "sm", "ob", "cat", "catT", "z", "xo", "st", "wo")]
    g8 = lambda ap: ap.rearrange("p (h e) -> p h e", h=8)
    with K.scope() as px:
        sb = lambda nm, sh, dt=F32: K.sb(px, "os_" + nm, sh, dt)
        f["load_sample_mod"](l, 0)
        L["x_load"](t, n, r0)
        taps = [(d["st_conv"][jl, :, tap, :], []) for tap in range(3)]
        L["conv_gates"](t, n, r0, taps)
        K.dma("sp", o["conv_s"][jl, :, 0:2, :], d["st_conv"][jl, :, 1:3, :], r=[], w=[])
        K.dma("sp", o["conv_s"][jl, :, 2, :], c["proj_d"][3 + r0:3 + r0 + n, 0:3072], r=[c["tb_pj"][t]], w=[])
        K.act(sm[0:n, 32:40], sm[0:n, 0:8], AF.Exp, r=[sm], w=[sm])
        sq = L["sq"]
        K.tt("pool", sq[0:n, 0:1024], pj[0:n, 0:1024], pj[0:n, 1024:2048], ALU.mult, r=[pj], w=[sq])
        K.red("dve", sm[0:n, 40:48], g8(sq[0:n, 0:1024]), r=[sq], w=[sm])
        row_d = K.dram("gr_d%d" % l, [NS, 3072 + 24], F32, "Internal")
        orow_d = K.dram("go_d%d" % l, [NS, 1024], F32, "Internal")
        K.dma("sp", row_d[:, 0:3072], pj[0:n, 0:3072], r=[pj], w=[row_d])
        K.dma("sp", row_d[:, 3072:3080], sm[0:n, 32:40], r=[sm], w=[row_d])
        K.dma("sp", row_d[:, 3080:3088], sm[0:n, 8:16], r=[sm], w=[row_d])
        K.dma("sp", row_d[:, 3088:3096], sm[0:n, 40:48], r=[sm], w=[row_d])
        egB = sb("egB", [128, NS * 8])
        egd = K.dram("ge_d%d" % l, [1, NS * 8], F32, "Internal")
        K.dma("sp", egd[0:1, :].rearrange("o (s h) -> (o s) h", h=8), sm[0:n, 32:40], r=[sm], w=[egd])
        K.dma("sp", egB[:], egd[0:1, :].partition_broadcast(128), r=[egd], w=[egB])
        kqT = sb("kqT", [128, 8 * NS * 2])
        kq4 = kqT[:].rearrange("p (h s a) -> p h s a", h=8, a=2)
        for which, c0 in ((0, 1024), (1, 0)):
            pt = K.ps()
            for h in range(8):
                K.tr(pt[:, h * 16:(h + 1) * 16], pj[0:n, c0 + h * 128:c0 + (h + 1) * 128], ident[0:n, 0:n], r=[pj, ident], w=[(pt, 0)])
            K.cp("act", kq4[:, :, :, which], pt[:, 0:128].rearrange("p (h s) -> p h s", h=8), r=[(pt, 0)], w=[kqT])
        rows = [sb("row%d" % j, [1, 3096]) for j in range(2)]
        S0 = [sb("S0%d" % j, [128, 1024]) for j in range(2)]
        vn = sb("vn", [1, 1024])
        orow = sb("orow", [1, 1024])
        tmp = sb("tmp", [1, 1024])
        r8 = lambda ap: ap.rearrange("p (h e) -> p h e", h=8)
        b8 = lambda ap: ap.unsqueeze(2).to_broadcast([1, 8, 128])
        for s in range(NS):
            row, S0s = rows[s % 2], S0[s % 2]
            K.dma("sp", row[:], row_d[s:s + 1, :], r=[row_d], w=[row])
            for h in range(8):
                K.dma("sp", S0s[:, h * 128:(h + 1) * 128], d["st_ssm"][jl, s, h], r=[], w=[S0s])
            pk = K.ps()
            pq = K.ps()
            for h in range(8):
                hs = slice(h * 128, (h + 1) * 128)
                K.mm(pk[0:1, hs], kq4[:, h, s, 0:1], S0s[:, hs], True, True, r=[kqT, S0s], w=[(pk, h // 4)])
            for h in range(8):
                hs = slice(h * 128, (h + 1) * 128)
                K.mm(pq[0:1, hs], kq4[:, h, s, 1:2], S0s[:, hs], True, True, r=[kqT, S0s], w=[(pq, h // 4)])
            eg, be, qk = row[:, 3072:3080], row[:, 3080:3088], row[:, 3088:3096]
            K.tt("dve", r8(tmp[:]), r8(pk[0:1, :]), b8(eg), ALU.mult, r=[pk, row], w=[tmp])
            K.tt("dve", vn[:], row[:, 2048:3072], tmp[:], ALU.subtract, r=[row, tmp], w=[vn])
            K.tt("dve", r8(vn[:]), r8(vn[:]), b8(be), ALU.mult, r=[vn, row], w=[vn])
            K.tt("dve", r8(orow[:]), r8(pq[0:1, :]), b8(eg), ALU.mult, r=[pq, row], w=[orow])
            K.tt("dve", r8(tmp[:]), r8(vn[:]), b8(qk), ALU.mult, r=[vn, row], w=[tmp])
            K.tt("dve", orow[:], orow[:], tmp[:], ALU.add, r=[orow, tmp], w=[orow])
            K.dma("sp", orow_d[s:s + 1, :], orow[:], r=[orow], w=[orow_d])
            pS = K.ps()
            for h in range(8):
                hs = slice(h * 128, (h + 1) * 128)
                K.mm(pS[:, hs], row[:, 1024 + h * 128:1024 + (h + 1) * 128], vn[:, hs], True, True, r=[row, vn], w=[(pS, h // 4)])
            K.tt("pool", g8(S0s[:]), g8(S0s[:]), egB[:, s * 8:(s + 1) * 8].unsqueeze(2).to_broadcast([128, 8, 128]), ALU.mult, r=[S0s, egB], w=[S0s])
            K.tt("dve", S0s[:], S0s[:], pS[:], ALU.add, r=[S0s, pS], w=[S0s])
            for h in range(8):
                K.dma("sp", o["ssm_s"][jl, s, h], S0s[:, h * 128:(h + 1) * 128], r=[S0s], w=[])
        K.dma("sp", ob[0:n, :], orow_d[:, :], r=[orow_d], w=[ob])
        L["rms_gate"](n, ob[0:n, :], [ob])
        out_proj_ln(K, t, n, cat, catT, wo, xt, z, xo, st)


_NC_CACHE = {}


def kernel(x_prompt, x_sample, c_prompt, c_sample, cache_kv, page_table, state_ret, state_ssm, state_conv,
           ada_w, ada_b, ln_g, ln_b, mlp_up, mlp_down, even_w_in, even_w_out, ret_gn,
           odd_w_in, odd_w_out, odd_conv, odd_a_log, odd_dt_bias, odd_gn):
    f = lambda a: np.ascontiguousarray(np.asarray(a))
    B, T, _ = x_prompt.shape
    NT = T // 128
    NPG = page_table.shape[1]
    NPOOL = cache_kv.shape[1]
    cfg = dict(NT=NT, NPG=NPG, NPOOL=NPOOL, DEPTH=4, mix="real")
    key = (NT, NPG, NPOOL)
    if key not in _NC_CACHE:
        _NC_CACHE[key] = build(cfg)
    nc = _NC_CACHE[key]
    cs = make_consts(NT, NPG * cache_kv.shape[2])
    shared = dict(cache_kv=f(cache_kv).reshape(-1, 320), ada_w=f(ada_w), ada_b=f(ada_b), ln_g=f(ln_g), ln_b=f(ln_b),
                  mlp_up=f(mlp_up), mlp_down=f(mlp_down), even_w_in=f(even_w_in), even_w_out=f(even_w_out),
                  ret_gn=f(ret_gn).reshape(2, 512), odd_w_in=f(odd_w_in), odd_w_out=f(odd_w_out), odd_conv=f(odd_conv),
                  odd_a_log=f(odd_a_log), odd_dt_bias=f(odd_dt_bias), odd_gn=f(odd_gn))
    shared.update({k: f(v) for k, v in cs.items()})
    in_maps = []
    for c in range(8):
        b = c // 2
        sl = slice(c * NS, (c + 1) * NS)
        m = dict(shared)
        m.update(xp=f(x_prompt[b]), xs=f(x_sample[sl, 0]), c17=f(np.concatenate([c_sample[sl], c_prompt[b:b + 1]], 0)),
                 ptab=f(page_table[sl]), st_ret=f(state_ret[:, sl]), st_ssm=f(state_ssm[:, sl]), st_conv=f(state_conv[:, sl]))
        in_maps.append(m)
    res = run_bass_kernel_spmd(nc, in_maps, core_ids=list(range(8))).results
    ev = [res[2 * b] for b in range(B)]
    y_p = np.stack([r["y_p"] for r in ev], 0)
    y_s = np.concatenate([r["y_s"] for r in res], 0)[:, None, :]
    kv_p = np.stack([r["kv_p"] for r in ev], 1)
    kv_s = np.concatenate([r["kv_s"] for r in res], 1)[:, :, None, :]
    ret_p = np.stack([r["ret_p"] for r in ev], 1)
    ret_s = np.concatenate([r["ret_s"] for r in res], 1)
    ssm_p = np.stack([r["ssm_p"] for r in ev], 1)
    ssm_s = np.concatenate([r["ssm_s"] for r in res], 1)
    conv_p = np.stack([r["conv_p"] for r in ev], 1)
    conv_s = np.concatenate([r["conv_s"] for r in res], 1)
    return tuple(np.asarray(a, np.float32) for a in (y_p, y_s, kv_p, kv_s, ret_p, ret_s, ssm_p, ssm_s, conv_p, conv_s))
```

```python
import math
import numpy as np
from contextlib import ExitStack
import concourse.bass as bass
import concourse.mybir as mybir
from concourse.bass_utils import run_bass_kernel_spmd

F32 = mybir.dt.float32
BF16 = mybir.dt.bfloat16
I32 = mybir.dt.int32
ALU = mybir.AluOpType
AF = mybir.ActivationFunctionType
AX = mybir.AxisListType

ENGS = ("pe", "act", "dve", "pool", "sp")
NDMA = 6

D = 1024
DFF = 4096
EVEN_IN = 3140
ODD_IN = 4112
NS = 16
ALPHA = 8.0 ** 0.25
LN_EPS = 1e-5
NEG = -1e30


class Buf:
    __slots__ = ("w", "r")

    def __init__(self):
        self.w = None
        self.r = {}


class TL:
    def __init__(self, t, nb=1):
        self.t = t
        self.b = [Buf() for _ in range(nb)]

    def __getitem__(self, k):
        return self.t[k]


def _bufs(lst):
    out = []
    for x in lst:
        if isinstance(x, Buf):
            out.append(x)
        elif isinstance(x, TL):
            out.extend(x.b)
        elif isinstance(x, tuple):
            out.append(x[0].b[x[1]])
        else:
            raise TypeError(x)
    return out


class Sched:
    def __init__(self, nc, es):
        self.nc = nc
        self.ops = {e: [] for e in ENGS}
        self.cnt = {e: 0 for e in ENGS}
        self.seen = {e: {} for e in ENGS}
        self.sem = {e: es.enter_context(nc.semaphore("s_" + e)) for e in ENGS}
        self.dsem = {}
        self.dcnt = {}
        self.pend = {e: [] for e in ENGS}
        for q in ("sp", "act", "pool"):
            for j in range(NDMA):
                k = "d_%s%d" % (q, j)
                self.dsem[k] = es.enter_context(nc.semaphore(k))
            self.dcnt[q] = 0

    def barrier(self):
        tgt = {e: self.cnt[e] for e in ENGS if self.cnt[e] > 0}
        for q in ("sp", "act", "pool"):
            for j in range(NDMA):
                if self.dcnt[q] > j:
                    tgt["d_%s%d" % (q, j)] = 16 * ((self.dcnt[q] - j + NDMA - 1) // NDMA)
        for e in ENGS:
            for k, v in tgt.items():
                if self.seen[e].get(k, 0) < v:
                    self.seen[e][k] = v
                    self.pend[e].append((k, v))

    def _semobj(self, key):
        return self.sem[key] if key in self.sem else self.dsem[key]

    def _collect(self, eng, r, w):
        deps = []
        for b in r:
            if b.w is not None:
                deps.append(b.w)
        for b in w:
            if b.w is not None:
                deps.append(b.w)
            for k, v in b.r.items():
                deps.append((k, v))
        waits = []
        sn = self.seen[eng]
        for k, v in deps:
            if eng == "pe" and k == "pe":
                continue
            if sn.get(k, 0) >= v:
                continue
            sn[k] = v
            waits.append((k, v))
        return waits

    def begin_capture(self):
        self.cap = []

    def end_capture(self):
        c_, self.cap = self.cap, None
        units, cur, open_ = [], [], False
        for it in c_:
            cur.append(it)
            g = it[-1]
            if g == "open":
                open_ = True
            elif g == "close":
                open_ = False
            if not open_:
                units.append(cur)
                cur = []
        if cur:
            units.append(cur)
        return units

    def emit_units(self, A, B):
        if getattr(self, "no_interleave", False):
            A, B = A + B, []
        i = j = 0
        while i < len(A) or j < len(B):
            if j >= len(B) or (i < len(A) and i * len(B) <= j * len(A)):
                u = A[i]
                i += 1
            else:
                u = B[j]
                j += 1
            for kind, a0, a1, a2, a3, _ in u:
                if kind == "op":
                    self.op(a0, a1, a2, a3)
                else:
                    self.dma(a0, a1, a2, a3)

    def op(self, eng, fn, r=(), w=(), grp=None):
        if getattr(self, "cap", None) is not None:
            self.cap.append(("op", eng, fn, r, w, grp))
            return
        r = _bufs(r)
        w = _bufs(w)
        waits = self.pend[eng] + self._collect(eng, r, w)
        self.pend[eng] = []
        self.cnt[eng] += 1
        tok = (eng, self.cnt[eng])
        self.ops[eng].append((waits, fn, None))
        for b in r:
            if b.r.get(eng, 0) < tok[1]:
                b.r[eng] = tok[1]
        for b in w:
            b.w = tok
            b.r = {}

    def dma(self, q, fn, r=(), w=()):
        if getattr(self, "cap", None) is not None:
            self.cap.append(("dma", q, fn, r, w, None))
            return
        r = _bufs(r)
        w = _bufs(w)
        i = self.dcnt[q]
        self.dcnt[q] += 1
        key = "d_%s%d" % (q, i % NDMA)
        prev = 16 * (i // NDMA)
        waits = self.pend[q] + self._collect(q, r, w)
        self.pend[q] = []
        sn = self.seen[q]
        if prev > 0 and sn.get(key, 0) < prev:
            sn[key] = prev
            waits.append((key, prev))
        tok = (key, prev + 16)
        self.ops[q].append((waits, fn, key))
        for b in r:
            if b.r.get(key, 0) < tok[1]:
                b.r[key] = tok[1]
        for b in w:
            b.w = tok
            b.r = {}

    def finish(self):
        final = {}
        for q in ("sp", "act", "pool"):
            for j in range(NDMA):
                if self.dcnt[q] > j:
                    final["d_%s%d" % (q, j)] = 16 * ((self.dcnt[q] - j + NDMA - 1) // NDMA)
        nc = self.nc
        with nc.Block() as block:
            def replay(ename, eng):
                for waits, fn, key in self.ops[ename]:
                    for k, v in waits:
                        eng.wait_ge(self._semobj(k), v)
                    inst = fn(eng)
                    if key is None:
                        inst.then_inc(self.sem[ename], 1)
                    else:
                        inst.then_inc(self.dsem[key], 16)
                if ename == "sp":
                    for k, v in final.items():
                        eng.wait_ge(self._semobj(k), v)
                    for e in ("pe", "act", "dve", "pool"):
                        if self.cnt[e] > 0:
                            eng.wait_ge(self.sem[e], self.cnt[e])

            @block.tensor
            def _(e):
                replay("pe", e)

            @block.scalar
            def _(e):
                replay("act", e)

            @block.vector
            def _(e):
                replay("dve", e)

            @block.gpsimd
            def _(e):
                replay("pool", e)

            @block.sync
            def _(e):
                replay("sp", e)


def _rope_rows(pos):
    def tab(d):
        inv = (10000.0 ** (-np.arange(0, d, 2, dtype=np.float32) / d)).astype(np.float32)
        ang = (pos.astype(np.float32)[:, None] * inv[None, :]).astype(np.float32)
        return np.cos(ang.astype(np.float64)), np.sin(ang.astype(np.float64))
    c128, s128 = tab(128)
    c64, s64 = tab(64)
    return np.concatenate([c128, s128, -s128, c64, s64, -s64], 1).astype(np.float32)


def make_consts(NT, past_len):
    c = {}
    p = np.arange(128)
    c["ident"] = np.eye(128, dtype=np.float32)
    c["ones"] = np.ones((128, 128), np.float32)
    c["triu"] = (p[:, None] <= p[None, :]).astype(np.float32)
    c["mvinc"] = np.where(p[None, :] >= p[:, None], 0.0, -1e4).astype(np.float32)
    c["strict"] = (p[None, :] > p[:, None]).astype(np.float32)
    c["cmask"] = np.where(p[None, :] <= p[:, None], 0.0, NEG).astype(np.float32)
    lg = np.log(1.0 - 2.0 ** (-5.0 - np.arange(4, dtype=np.float64)))
    i = p.astype(np.float64)
    rel = i[None, :] - i[:, None]
    dm = np.where(rel >= 0, np.exp(lg[:, None, None] * np.maximum(rel, 0)[None]), 0.0)
    c["dmatT"] = (np.transpose(dm, (1, 0, 2)) * 128 ** -0.5).astype(np.float32).reshape(128, 512)
    qd = np.exp(lg[:, None] * (i[None, :] + 1.0))
    c["qdecB"] = np.broadcast_to(qd.reshape(1, 512), (128, 512)).astype(np.float32).copy()
    kd = np.exp(lg[None, :] * (127.0 - i[:, None])) * 128 ** -0.5
    c["kdecC"] = kd.astype(np.float32)
    cd = np.exp(lg * 128.0)
    c["cdecB"] = np.broadcast_to(np.repeat(cd, 128).reshape(1, 512), (128, 512)).astype(np.float32).copy()
    g1 = np.exp(lg)
    c["gam1B"] = np.broadcast_to(np.repeat(g1, 128).reshape(1, 512), (128, 512)).astype(np.float32).copy()
    c["ropeP"] = _rope_rows(np.arange(NT * 128)).reshape(NT, 128, 288)
    c["iotap"] = np.arange(128, dtype=np.float32).reshape(128, 1)
    c["pw2"] = np.broadcast_to((0.5 ** (np.arange(32, dtype=np.float64) + 1)).astype(np.float32).reshape(1, 32), (128, 32)).copy()
    c["ropeS"] = np.broadcast_to(_rope_rows(np.array([past_len])), (NS, 288)).copy()
    d16 = np.broadcast_to(np.eye(16, dtype=np.float32).reshape(1, 256), (128, 256)).copy()
    c["delta16"] = d16
    return c


class KB:
    def __init__(self, cfg):
        self.cfg = cfg
        self.nc = bass.Bass("TRN2", target_bir_lowering=False)
        self.es = ExitStack()
        self.rr = 0

    def scope(self):
        kb = self

        class _Sc(ExitStack):
            def __exit__(self_, *a):
                kb.S.barrier()
                return super().__exit__(*a)
        return _Sc()

    def dram(self, name, shape, dt, kind):
        return TL(self.nc.dram_tensor(name, list(shape), dt, kind=kind).ap())

    def sb(self, es, name, shape, dt, nb=1):
        self.uid = getattr(self, "uid", 0) + 1
        return TL(es.enter_context(self.nc.sbuf_tensor("%s_%d" % (name, self.uid), list(shape), dt)), nb)

    def tt(self, eng, out, a, b, op, r, w):
        self.S.op(eng, lambda e: e.tensor_tensor(out=out, in0=a, in1=b, op=op), r=r, w=w)

    def ts(self, eng, out, a, s1, op0, r, w, s2=None, op1=None, acc=None):
        if op1 is None:
            self.S.op(eng, lambda e: e.tensor_scalar(out=out, in0=a, scalar1=s1, scalar2=None, op0=op0), r=r, w=w)
        elif acc is None:
            self.S.op(eng, lambda e: e.tensor_scalar(out=out, in0=a, scalar1=s1, scalar2=s2, op0=op0, op1=op1), r=r, w=w)
        else:
            self.S.op(eng, lambda e: e.tensor_scalar(out=out, in0=a, scalar1=s1, scalar2=s2, op0=op0, op1=op1, accum_out=acc), r=r, w=w)

    def stt(self, eng, out, a, sc, b, op0, op1, r, w):
        self.S.op(eng, lambda e: e.scalar_tensor_tensor(out=out, in0=a, scalar=sc, in1=b, op0=op0, op1=op1), r=r, w=w)

    def act(self, out, in_, func, r, w, bias=None, scale=None, acc=None):
        kw = {}
        if bias is not None:
            kw["bias"] = bias
        if scale is not None:
            kw["scale"] = scale
        if acc is not None:
            kw["accum_out"] = acc
        self.S.op("act", lambda e: e.activation(out=out, in_=in_, func=func, **kw), r=r, w=w)

    def cp(self, eng, out, in_, r, w):
        if eng == "act":
            self.S.op("act", lambda e: e.copy(out=out, in_=in_), r=r, w=w)
        else:
            self.S.op(eng, lambda e: e.tensor_copy(out=out, in_=in_), r=r, w=w)

    def red(self, eng, out, in_, r, w, op=None):
        if op is None:
            self.S.op(eng, lambda e: e.reduce_sum(out=out, in_=in_, axis=AX.X), r=r, w=w)
        else:
            self.S.op(eng, lambda e: e.tensor_reduce(out=out, in_=in_, axis=AX.X, op=op), r=r, w=w)

    def mm(self, out, lhsT, rhs, start, stop, r, w, dedicated=False):
        grp = None
        if not dedicated and not (start and stop):
            grp = "open" if start else ("close" if stop else "mid")
        self.S.op("pe", lambda e: e.matmul(out, lhsT=lhsT, rhs=rhs, start=start, stop=stop), r=r, w=w, grp=grp)

    def tr(self, out, in_, ident, r, w):
        self.S.op("pe", lambda e: e.transpose(out=out, in_=in_, identity=ident), r=r, w=w)

    def dma(self, q, out, in_, r, w):
        self.S.dma(q, lambda e: e.dma_start(out=out, in_=in_), r=r, w=w)

    def memset(self, eng, ap, val, w):
        self.S.op(eng, lambda e: e.memset(ap, val), w=w)

    def evq(self):
        self.rr += 1
        return "act" if self.rr % 2 else "dve"

    def ps(self):
        self.prr = (getattr(self, "prr", -1) + 1) % len(self.PSG)
        return self.PSG[self.prr]

    def rsqrt(self, eng, out, in_, r, w, mult, add):
        self.act(out, in_, AF.Sqrt, r=r, w=w, scale=mult, bias=add)
        self.S.op("dve", lambda e: e.reciprocal(out=out, in_=out), r=w, w=w)


def build(cfg):
    NT = cfg["NT"]
    NPG = cfg["NPG"]
    NPOOL = cfg["NPOOL"]
    DEPTH = cfg["DEPTH"]
    TOPK_P = min(256, (NT * 128) // 4)
    TOPK_S = min(256, (NPG * 128 + 1) // 4)
    NTOK = NT * 128
    NROW = NTOK + NS
    K = KB(cfg)
    nc = K.nc
    es = K.es
    with es:
        S = K.S = Sched(nc, es)
        S.no_interleave = bool(cfg.get("noilv"))
        EI, EO, IN = "ExternalInput", "ExternalOutput", "Internal"
        d = {}
        for nm, sh, dt in [
            ("xp", [NTOK, D], F32), ("xs", [NS, D], F32), ("c17", [17, D], F32),
            ("cache_kv", [2 * NPOOL * 128, 320], F32), ("ptab", [NS, NPG], I32),
            ("st_ret", [2, NS, 4, 128, 128], F32), ("st_ssm", [2, NS, 8, 128, 128], F32),
            ("st_conv", [2, NS, 3, 3072], F32),
            ("ada_w", [4, D, 6 * D], F32), ("ada_b", [4, 6 * D], F32), ("ln_g", [4, 2, D], F32),
            ("ln_b", [4, 2, D], F32), ("mlp_up", [4, D, DFF], F32), ("mlp_down", [4, DFF, D], F32),
            ("even_w_in", [2, D, EVEN_IN], F32), ("even_w_out", [2, D, D], F32), ("ret_gn", [2, 512], F32),
            ("odd_w_in", [2, D, ODD_IN], F32), ("odd_w_out", [2, D, D], F32), ("odd_conv", [2, 4, 3072], F32),
            ("odd_a_log", [2, 8], F32), ("odd_dt_bias", [2, 8], F32), ("odd_gn", [2, 128], F32),
            ("ident", [128, 128], F32), ("ones", [128, 128], F32), ("triu", [128, 128], F32),
            ("mvinc", [128, 128], F32), ("strict", [128, 128], F32), ("cmask", [128, 128], F32),
            ("dmatT", [128, 512], F32), ("qdecB", [128, 512], F32), ("kdecC", [128, 4], F32),
            ("cdecB", [128, 512], F32), ("gam1B", [128, 512], F32), ("ropeP", [NT, 128, 288], F32),
            ("ropeS", [NS, 288], F32), ("delta16", [128, 256], F32), ("iotap", [128, 1], F32), ("pw2", [128, 32], F32),
        ]:
            d[nm] = K.dram(nm, sh, dt, EI)
        o = {}
        for nm, sh in [
            ("y_p", [NTOK, D]), ("y_s", [NS, D]), ("kv_p", [2, NTOK, 320]), ("kv_s", [2, NS, 320]),
            ("ret_p", [2, 4, 128, 128]), ("ret_s", [2, NS, 4, 128, 128]), ("ssm_p", [2, 8, 128, 128]),
            ("ssm_s", [2, NS, 8, 128, 128]), ("conv_p", [2, 3, 3072]), ("conv_s", [2, NS, 3, 3072]),
        ]:
            o[nm] = K.dram(nm, sh, F32, EO)
        x_a = K.dram("x_a", [NROW, D], F32, IN)
        x_b = K.dram("x_b", [NROW, D], F32, IN)
        mod_d = K.dram("mod_d", [4, 17, 6 * D], F32, IN)
        proj_d = K.dram("proj_d", [NROW + 3, ODD_IN], F32, IN)
        tb_xa = [Buf() for _ in range(NT + 1)]
        tb_xb = [Buf() for _ in range(NT + 1)]
        tb_pj = [Buf() for _ in range(NT + 2)]

        def rows(t):
            return (t * 128, 128) if t < NT else (NTOK, NS)

        cst = {}
        for nm, sh in [("ident", [128, 128]), ("ones", [128, 128]), ("triu", [128, 128]), ("mvinc", [128, 128]),
                       ("strict", [128, 128]), ("cmask", [128, 128]), ("delta16", [128, 256]), ("iotap", [128, 1]), ("pw2", [128, 32])]:
            cst[nm] = K.sb(es, "c_" + nm, sh, F32)
            K.dma("sp", cst[nm][:], d[nm][:], r=[], w=[cst[nm]])
        identb = K.sb(es, "identb", [128, 128], BF16)
        K.cp("dve", identb[:], cst["ident"][:], r=[cst["ident"]], w=[identb])
        ident = cst["ident"]
        PS = [TL(es.enter_context(nc.psum_tensor("ps%d" % i, [128, 1024], F32)), 2) for i in range(4)]
        K.PSG = PS[:4]
        PSO = [PS[2], PS[3]]
        lnG = K.sb(es, "lnG", [128, D], F32)
        lnB = K.sb(es, "lnB", [128, D], F32)
        modP = K.sb(es, "modP", [128, 3 * D], F32)

        def load_phase_consts(l, which):
            c0 = which * 3 * D
            K.dma("sp", modP[:], mod_d[l, 16:17, c0:c0 + 3 * D].partition_broadcast(128), r=[mod_d], w=[modP])
            K.dma("sp", lnG[:], d["ln_g"][l, which:which + 1, :].partition_broadcast(128), r=[], w=[lnG])
            K.dma("sp", lnB[:], d["ln_b"][l, which:which + 1, :].partition_broadcast(128), r=[], w=[lnB])
            K.ts("pool", modP[:, D:3 * D], modP[:, D:3 * D], 1.0, ALU.add, r=[modP], w=[modP])

        def load_sample_mod(l, which, dst=None):
            dst = modP if dst is None else dst
            c0 = which * 3 * D
            K.dma("sp", dst[0:NS, :], mod_d[l, 0:NS, c0:c0 + 3 * D], r=[mod_d], w=[dst])
            K.ts("pool", dst[0:NS, D:3 * D], dst[0:NS, D:3 * D], 1.0, ALU.add, r=[dst], w=[dst])

        K.modS = None

        def mod_of(t):
            return K.modS if (t == NT and K.modS is not None) else modP

        with K.scope() as p0:
            c17t = K.sb(p0, "c17t", [17, D], F32)
            scT = K.sb(p0, "scT", [128, 8, 17], BF16)
            K.dma("sp", c17t[:], d["c17"][:], r=[], w=[c17t])
            K.act(c17t[:], c17t[:], AF.Silu, r=[c17t], w=[c17t])
            pt = K.ps()
            for k in range(8):
                K.tr(pt[0:128, k * 32:k * 32 + 17], c17t[:, k * 128:(k + 1) * 128], ident[0:17, 0:17], r=[c17t, ident], w=[(pt, 0)])
            K.cp("dve", scT[:], pt[:, 0:256].rearrange("p (k c) -> p k c", c=32)[:, :, 0:17], r=[(pt, 0)], w=[scT])
            aw = [K.sb(p0, "aw%d" % i, [128, 8, 512], BF16) for i in range(2)]
            ab = [K.sb(p0, "ab%d" % i, [17, 512], F32) for i in range(2)]
            mo = [K.sb(p0, "mo%d" % i, [17, 512], F32) for i in range(2)]
            it = 0
            for l in range(DEPTH):
                for cb in range(12):
                    a, b_, m_ = aw[it % 2], ab[it % 2], mo[it % 2]
                    cs = slice(cb * 512, (cb + 1) * 512)
                    K.dma("pool", a[:], d["ada_w"][l, :, cs].rearrange("(k p) n -> p k n", p=128), r=[], w=[a])
                    K.dma("sp", b_[:], d["ada_b"][l:l + 1, cs].partition_broadcast(17), r=[], w=[b_])
                    pt = K.ps()
                    for k in range(8):
                        K.mm(pt[0:17, 0:512], scT[:, k, :], a[:, k, :], k == 0, k == 7, r=[scT, a], w=[(pt, 0)])
                    K.tt("dve", m_[:], pt[0:17, 0:512], b_[:], ALU.add, r=[(pt, 0), b_], w=[m_])
                    K.dma("sp", mod_d[l, :, cs], m_[:], r=[m_], w=[mod_d])
                    it += 1

        def load_weights_bf16(wt, src, ncols):
            nk = src.shape[0] // 128
            for k in range(nk):
                K.dma("pool", wt[:, k, 0:ncols], src[k * 128:(k + 1) * 128, :], r=[], w=[wt])

        def modulate_T(pp, xt, t, n, tmp, hm, hmT):
            md = mod_of(t)
            K.tt("dve", tmp[0:n, :], xt[0:n, :], md[0:n, D:2 * D], ALU.mult, r=[xt, md], w=[tmp])
            K.tt("pool", hm[0:n, :], tmp[0:n, :], md[0:n, 0:D], ALU.add, r=[tmp, md], w=[hm])
            to_T(hm, n, hmT)

        def to_T(hm, n, hmT):
            pt = K.ps()
            pb = pt[:].bitcast(BF16)
            for k in range(8):
                K.tr(pb[:, k * 128:k * 128 + n], hm[0:n, k * 128:(k + 1) * 128], identb[0:n, 0:n], r=[hm, identb], w=[(pt, 0)])
            K.cp("act", hmT[:, :, 0:n], pb[:, 0:1024].rearrange("p (k c) -> p k c", c=128)[:, :, 0:n], r=[(pt, 0)], w=[hmT])

        def layer_norm_out(z, n, xo, scr, st):
            K.memset("pool", st[0:n, :], 0.0, w=[st])
            K.red("dve", st[0:n, 0:1], z[0:n, :], r=[z], w=[st])
            K.ts("dve", st[0:n, 1:2], st[0:n, 0:1], -1.0 / D, ALU.mult, r=[st], w=[st])
            K.ts("dve", scr[0:n, :], z[0:n, :], st[0:n, 1:2], ALU.add, r=[z, st], w=[scr])
            K.act(xo[0:n, :], scr[0:n, :], AF.Square, r=[scr, st], w=[xo, st], acc=st[0:n, 2:3])
            K.rsqrt("dve", st[0:n, 3:4], st[0:n, 2:3], r=[st], w=[st], mult=1.0 / D, add=LN_EPS)
            K.stt("dve", xo[0:n, :], scr[0:n, :], st[0:n, 3:4], lnG[0:n, :], ALU.mult, ALU.mult, r=[scr, st, lnG], w=[xo])
            K.tt("pool", xo[0:n, :], xo[0:n, :], lnB[0:n, :], ALU.add, r=[xo, lnB], w=[xo])

        def residual_ln(py, t, n, xt, z, xo, scr, st):
            md = mod_of(t)
            K.tt("dve", z[0:n, :], py[0:n, :], md[0:n, 2 * D:3 * D], ALU.mult, r=[py, md], w=[z])
            K.stt("dve", z[0:n, :], xt[0:n, :], ALPHA, z[0:n, :], ALU.mult, ALU.add, r=[xt, z], w=[z])
            layer_norm_out(z, n, xo, scr, st)

        K.fn = dict(rows=rows, load_phase_consts=load_phase_consts, load_sample_mod=load_sample_mod, mod_of=mod_of, load_weights_bf16=load_weights_bf16,
                    modulate_T=modulate_T, to_T=to_T, residual_ln=residual_ln)
        K.ctx = dict(d=d, o=o, x_a=x_a, x_b=x_b, proj_d=proj_d, tb_xa=tb_xa, tb_xb=tb_xb, tb_pj=tb_pj, cst=cst,
                     ident=ident, identb=identb, PS=PS, PSO=PSO, NT=NT, NPG=NPG, NTOK=NTOK, NROW=NROW,
                     TOPK_P=TOPK_P, TOPK_S=TOPK_S, modP=modP, mod_d=mod_d, cfg=cfg)

        with K.scope() as pz:
            zt = K.sb(pz, "zt", [3, ODD_IN], F32)
            K.memset("pool", zt[:], 0.0, w=[zt])
            K.dma("sp", proj_d[0:3, :], zt[:], r=[zt], w=[tb_pj[NT + 1]])

        x_in = None
        for l in range(DEPTH):
            even = (l % 2 == 0)
            NIN = EVEN_IN if even else ODD_IN
            w_in_d = d["even_w_in"][l // 2] if even else d["odd_w_in"][l // 2]
            load_phase_consts(l, 0)
            with K.scope() as pa:
                wt = K.sb(pa, "w_in", [128, 8, ODD_IN], BF16)
                load_weights_bf16(wt, w_in_d, NIN)
                xts = [K.sb(pa, "a_x%d" % i, [128, D], F32) for i in range(2)]
                tmps = [K.sb(pa, "a_t%d" % i, [128, D], F32) for i in range(2)]
                hms = [K.sb(pa, "a_h%d" % i, [128, D], BF16) for i in range(2)]
                hmTs = [K.sb(pa, "a_hT%d" % i, [128, 8, 128], BF16) for i in range(2)]
                pjs = [K.sb(pa, "a_pj%d" % i, [128, ODD_IN], F32) for i in range(1)] * 2
                K.modS = K.sb(pa, "a_modS", [NS, 3 * D], F32)
                load_sample_mod(l, 0, K.modS)

                def a_s1(t):
                    r0, n = rows(t)
                    xt, tmp, hm, hmT = xts[t % 2], tmps[t % 2], hms[t % 2], hmTs[t % 2]
                    if l == 0:
                        src = d["xp"][r0:r0 + n, :] if t < NT else d["xs"][:, :]
                        K.dma("sp", xt[0:n, :], src, r=[], w=[xt])
                    else:
                        K.dma("sp", xt[0:n, :], x_a[r0:r0 + n, :], r=[tb_xa[t]], w=[xt])
                    modulate_T(pa, xt, t, n, tmp, hm, hmT)

                def a_s2(t):
                    r0, n = rows(t)
                    hmT, pj = hmTs[t % 2], pjs[0]
                    nb = (NIN + 511) // 512
                    for cb in range(nb):
                        c0 = cb * 512
                        wd = min(512, NIN - c0)
                        pt = K.ps()
                        hb = cb % 2
                        for k in range(8):
                            K.mm(pt[0:n, hb * 512:hb * 512 + wd], hmT[:, k, 0:n], wt[:, k, c0:c0 + wd], k == 0, k == 7, r=[hmT, wt], w=[(pt, hb)])
                        K.cp(K.evq(), pj[0:n, c0:c0 + wd], pt[0:n, hb * 512:hb * 512 + wd], r=[(pt, hb)], w=[pj])
                    K.dma("sp", proj_d[3 + r0:3 + r0 + n, 0:NIN], pj[0:n, 0:NIN], r=[pj], w=[tb_pj[t]])

                a_s1(0)
                for t in range(NT + 1):
                    if t + 1 <= NT:
                        a_s1(t + 1)
                    a_s2(t)
                K.modS = None
            if cfg.get("mix", "real") == "stub":
                phase_b_stub(K, l)
            elif even:
                phase_b_even(K, l)
            else:
                phase_b_odd(K, l)
            load_phase_consts(l, 1)
            with K.scope() as pf:
                wu = K.sb(pf, "w_up", [128, 8, DFF], BF16)
                wd_ = K.sb(pf, "w_dn", [128, 32, D], BF16)
                load_weights_bf16(wu, d["mlp_up"][l], DFF)
                load_weights_bf16(wd_, d["mlp_down"][l], D)
                xts = [K.sb(pf, "f_x%d" % i, [128, D], F32) for i in range(2)]
                tmps = [K.sb(pf, "f_t%d" % i, [128, D], F32) for i in range(1)] * 2
                hms = [K.sb(pf, "f_h%d" % i, [128, D], BF16) for i in range(1)] * 2
                hmTs = [K.sb(pf, "f_hT%d" % i, [128, 8, 128], BF16) for i in range(2)]
                uTs = [K.sb(pf, "f_uT%d" % i, [128, 32, 128], BF16) for i in range(1)] * 2
                rl = [K.sb(pf, "f_rl%d" % i, [128, 1024], F32) for i in range(1)] * 2
                xos = [K.sb(pf, "f_xo%d" % i, [128, D], F32) for i in range(1)] * 2
                sts = [K.sb(pf, "f_st%d" % i, [128, 4], F32) for i in range(2)]
                last = (l == DEPTH - 1)
                K.modS = K.sb(pf, "f_modS", [NS, 3 * D], F32)
                load_sample_mod(l, 1, K.modS)
                zb = K.sb(pf, "f_z", [128, D], F32)

                def f_s1(t):
                    r0, n = rows(t)
                    xt, tmp, hm, hmT = xts[t % 2], tmps[0], hms[0], hmTs[t % 2]
                    K.dma("sp", xt[0:n, :], x_b[r0:r0 + n, :], r=[tb_xb[t]], w=[xt])
                    modulate_T(pf, xt, t, n, tmp, hm, hmT)

                def f_s2(t):
                    r0, n = rows(t)
                    xt, hmT, uT, xo, st = xts[t % 2], hmTs[t % 2], uTs[0], xos[0], sts[t % 2]
                    for g in range(4):
                        pt = K.ps()
                        for j in range(8):
                            fc = g * 8 + j
                            for k in range(8):
                                K.mm(pt[:, j * 128:j * 128 + n], wu[:, k, fc * 128:(fc + 1) * 128], hmT[:, k, 0:n], k == 0, k == 7, r=[wu, hmT], w=[(pt, j // 4)])
                        r_ = rl[0]
                        pv = pt[:].rearrange("p (j c) -> p j c", c=128)[:, :, 0:n]
                        rv = r_[:].rearrange("p (j c) -> p j c", c=128)[:, :, 0:n]
                        K.act(rv, pv, AF.Relu, r=[pt], w=[r_])
                        K.tt("pool" if g % 2 else "dve", uT[:, g * 8:(g + 1) * 8, 0:n], rv, rv, ALU.mult, r=[r_], w=[uT])
                    py = K.ps()
                    for hb in range(2):
                        for k in range(32):
                            K.mm(py[0:n, hb * 512:(hb + 1) * 512], uT[:, k, 0:n], wd_[:, k, hb * 512:(hb + 1) * 512], k == 0, k == 31, r=[uT, wd_], w=[(py, hb)])
                    residual_ln(py, t, n, xt, zb, xo, zb, st)
                    if last:
                        dst = o["y_p"][r0:r0 + n, :] if t < NT else o["y_s"][:, :]
                        K.dma("sp", dst, xo[0:n, :], r=[xo], w=[])
                    else:
                        K.dma("sp", x_a[r0:r0 + n, :], xo[0:n, :], r=[xo], w=[tb_xa[t]])

                f_s1(0)
                for t in range(NT + 1):
                    if t + 1 <= NT:
                        f_s1(t + 1)
                    f_s2(t)
                K.modS = None
        S.finish()
    return nc


def phase_b_stub(K, l):
    c = K.ctx
    f = K.fn
    f["load_phase_consts"](l, 0)
    d, NT = c["d"], c["NT"]
    even = (l % 2 == 0)
    with K.scope() as pb:
        wo = K.sb(pb, "w_out", [128, 8, D], BF16)
        f["load_weights_bf16"](wo, (d["even_w_out"] if even else d["odd_w_out"])[l // 2], D)
        xt = K.sb(pb, "b_x", [128, D], F32)
        pj = K.sb(pb, "b_pj", [128, D], F32)
        cat = K.sb(pb, "b_cat", [128, D], BF16)
        catT = K.sb(pb, "b_catT", [128, 8, 128], BF16)
        z = K.sb(pb, "b_z", [128, D], F32)
        xo = K.sb(pb, "b_xo", [128, D], F32)
        st = K.sb(pb, "b_st", [128, 4], F32)
        for t in range(NT + 1):
            r0, n = f["rows"](t)
            if l == 0:
                src = d["xp"][r0:r0 + n, :] if t < NT else d["xs"][:, :]
                K.dma("sp", xt[0:n, :], src, r=[], w=[xt])
            else:
                K.dma("sp", xt[0:n, :], c["x_a"][r0:r0 + n, :], r=[c["tb_xa"][t]], w=[xt])
            if t == NT:
                f["load_sample_mod"](l, 0)
            K.dma("sp", pj[0:n, :], c["proj_d"][3 + r0:3 + r0 + n, 0:D], r=[c["tb_pj"][t]], w=[pj])
            K.cp("dve", cat[0:n, :], pj[0:n, :], r=[pj], w=[cat])
            out_proj_ln(K, t, n, cat, catT, wo, xt, z, xo, st)


def out_proj_ln(K, t, n, cat, catT, wo, xt, z, xo, st):
    c = K.ctx
    f = K.fn
    f["to_T"](cat, n, catT)
    py = K.ps()
    for hb in range(2):
        for k in range(8):
            K.mm(py[0:n, hb * 512:(hb + 1) * 512], catT[:, k, 0:n], wo[:, k, hb * 512:(hb + 1) * 512], k == 0, k == 7, r=[catT, wo], w=[(py, hb)])
    f["residual_ln"](py, t, n, xt, z, xo, z, st)
    r0, _ = f["rows"](t)
    K.dma("sp", c["x_b"][r0:r0 + n, :], xo[0:n, :], r=[xo], w=[c["tb_xb"][t]])


def rope_ops(K, n, src, H, half, rp, c0, t1, t2, dsts, r_src):
    cosb = rp[0:n, c0:c0 + half].unsqueeze(1).unsqueeze(1).to_broadcast([n, H, 2, half])
    sinb = rp[0:n, c0 + half:c0 + 2 * half].unsqueeze(1).to_broadcast([n, H, half])
    nsinb = rp[0:n, c0 + 2 * half:c0 + 3 * half].unsqueeze(1).to_broadcast([n, H, half])
    t1v = t1[0:n, 0:H * 2 * half].rearrange("p (h a c) -> p h a c", h=H, a=2)
    t2v = t2[0:n, 0:H * 2 * half].rearrange("p (h a c) -> p h a c", h=H, a=2)
    K.tt("dve", t1v, src, cosb, ALU.mult, r=r_src + [rp], w=[t1])
    K.tt("pool", t2v[:, :, 0, :], src[:, :, 1, :], nsinb, ALU.mult, r=r_src + [rp], w=[t2])
    K.tt("pool", t2v[:, :, 1, :], src[:, :, 0, :], sinb, ALU.mult, r=r_src + [rp], w=[t2])
    for h0, h1, dap, dt_ in dsts:
        K.tt("dve", dap, t1v[:, h0:h1], t2v[:, h0:h1], ALU.add, r=[t1, t2], w=[dt_])


def phase_b_even(K, l):
    c = K.ctx
    f = K.fn
    d, o, NT, NPG = c["d"], c["o"], c["NT"], c["NPG"]
    PSO, cst, ident, identb = c["PSO"], c["cst"], c["ident"], c["identb"]
    i = l // 2
    TOPK = c["TOPK_P"]
    NIT = 15
    SC_ATT = 128 ** -0.5
    f["load_phase_consts"](l, 0)
    K.PSG = c["PS"][:2]
    with K.scope() as pb:
        sb = lambda nm, sh, dt=F32: K.sb(pb, "e_" + nm, sh, dt)
        wo = sb("wo", [128, 8, D], BF16)
        f["load_weights_bf16"](wo, d["even_w_out"][i], D)
        kc = {}
        for nm in ("dmatT", "qdecB", "cdecB", "gam1B"):
            kc[nm] = sb(nm, [128, 512])
            K.dma("sp", kc[nm][:], d[nm][:], r=[], w=[kc[nm]])
        kdecC = sb("kdecC", [128, 4])
        K.dma("sp", kdecC[:], d["kdecC"][:], r=[], w=[kdecC])
        gnB = sb("gnB", [128, 512])
        K.dma("sp", gnB[:], d["ret_gn"][i:i + 1, :].partition_broadcast(128), r=[], w=[gnB])
        Sret = sb("Sret", [128, 512])
        K.memset("pool", Sret[:], 0.0, w=[Sret])
        pj = sb("pj", [128, EVEN_IN])
        xt = sb("xt", [128, D])
        rp = sb("rp", [128, 288])
        qk_r = sb("qk_r", [128, 1024])
        aq_rb = sb("aq_rb", [128, 512], BF16)
        kvrow = sb("kvrow", [128, 320])
        iq_r = sb("iq_r", [128, 256])
        t1 = sb("t1", [128, 1024])
        t2 = sb("t2", [128, 1024])
        qT, qTd, kT, AT, kd, cen, sq, sg = [sb(nm, [128, 512]) for nm in ("qT", "qTd", "kT", "AT", "kd", "cen", "sq", "sg")]
        akb = sb("akb", [128, 128], BF16)
        ikb = sb("ikb", [128, 64], BF16)
        iqsb = sb("iqsb", [128, 256], BF16)
        aqT = sb("aqT", [128, 512], BF16)
        iqsT = sb("iqsT", [64, 512], BF16)
        rls = [sb("rl%d" % j, [128, 512]) for j in range(2)]
        Es = [sb("E%d" % j, [128, 512], BF16) for j in range(2)]
        PTs = [sb("PT%d" % j, [128, 512], BF16) for j in range(2)]
        mTss = [sb("mTs%d" % j, [128, 128], BF16) for j in range(2)]
        cat = sb("cat", [128, D], BF16)
        catT = sb("catT", [128, 8, 128], BF16)
        z = sb("z", [128, D])
        xo = sb("xo", [128, D])
        st = sb("st", [128, 4])
        s4 = sb("s4", [128, 16])
        aw = sb("aw", [128, 8])
        bs = sb("bs", [128, 8])
        cnt = sb("cnt", [128, NIT])
        Hh = sb("Hh", [128, 32])
        rs = sb("rs", [128, 4])

        def x_load(t, n, r0):
            if l == 0:
                src = d["xp"][r0:r0 + n, :] if t < NT else d["xs"][:, :]
                K.dma("sp", xt[0:n, :], src, r=[], w=[xt])
            else:
                K.dma("sp", xt[0:n, :], c["x_a"][r0:r0 + n, :], r=[c["tb_xa"][t]], w=[xt])

        def proj_rope(t, n, r0, rp_src, aq_dst=None):
            aq_dst = aq_rb if aq_dst is None else aq_dst
            K.dma("sp", pj[0:n, :], c["proj_d"][3 + r0:3 + r0 + n, 0:EVEN_IN], r=[c["tb_pj"][t]], w=[pj])
            K.dma("sp", rp[0:n, :], rp_src, r=[], w=[rp])
            v4 = lambda ap, H, half: ap.rearrange("p (h a c) -> p h a c", h=H, a=2)
            rope_ops(K, n, v4(pj[0:n, 0:1024], 8, 64), 8, 64, rp, 0, t1, t2,
                     [(0, 8, v4(qk_r[0:n, :], 8, 64), qk_r)], [pj])
            rope_ops(K, n, v4(pj[0:n, 2048:2688], 5, 64), 5, 64, rp, 0, t1, t2,
                     [(0, 4, v4(aq_dst[0:n, :], 4, 64), aq_dst), (4, 5, v4(kvrow[0:n, 0:128], 1, 64), kvrow)], [pj])
            rope_ops(K, n, v4(pj[0:n, 2816:3136], 5, 32), 5, 32, rp, 192, t1, t2,
                     [(0, 4, v4(iq_r[0:n, :], 4, 32), iq_r), (4, 5, v4(kvrow[0:n, 256:320], 1, 32), kvrow)], [pj])
            K.cp("act", kvrow[0:n, 128:256], pj[0:n, 2688:2816], r=[pj], w=[kvrow])
            K.act(aw[0:n, 0:4], pj[0:n, 3136:3140], AF.Abs, r=[pj], w=[aw], scale=1.0 / 16)
            K.ts("dve", aw[0:n, 4:8], pj[0:n, 3136:3140], 0.0, ALU.is_ge, r=[pj], w=[aw], s2=2.0, op1=ALU.mult)
            K.ts("dve", aw[0:n, 4:8], aw[0:n, 4:8], -1.0, ALU.add, r=[aw], w=[aw])

        def head_norm_gate(n, src, rsrc):
            pov = src.rearrange("p (h e) -> p h e", h=4)
            cv = cen[0:n, :].rearrange("p (h e) -> p h e", h=4)
            K.red("dve", s4[0:n, 0:4], pov, r=rsrc, w=[s4])
            K.ts("dve", s4[0:n, 4:8], s4[0:n, 0:4], -1.0 / 128, ALU.mult, r=[s4], w=[s4])
            K.tt("dve", cv, pov, s4[0:n, 4:8].unsqueeze(2).to_broadcast([n, 4, 128]), ALU.add, r=rsrc + [s4], w=[cen])
            K.tt("pool", sq[0:n, :], cen[0:n, :], cen[0:n, :], ALU.mult, r=[cen], w=[sq])
            K.red("dve", s4[0:n, 8:12], sq[0:n, :].rearrange("p (h e) -> p h e", h=4), r=[sq], w=[s4])
            K.rsqrt("dve", s4[0:n, 12:16], s4[0:n, 8:12], r=[s4], w=[s4], mult=1.0 / 128, add=LN_EPS)
            K.act(sg[0:n, :], pj[0:n, 1536:2048], AF.Silu, r=[pj], w=[sg])
            K.tt("pool", sg[0:n, :], sg[0:n, :], gnB[0:n, :], ALU.mult, r=[sg, gnB], w=[sg])
            K.tt("dve", cv, cv, s4[0:n, 12:16].unsqueeze(2).to_broadcast([n, 4, 128]), ALU.mult, r=[cen, s4], w=[cen])
            K.tt("dve", cat[0:n, 0:512], cen[0:n, :], sg[0:n, :], ALU.mult, r=[cen, sg], w=[cat])

        with K.scope() as pp:
            akT_all = K.sb(pp, "e_akT", [128, NT * 128], BF16, nb=NT)
            ikT_all = K.sb(pp, "e_ikT", [64, NT * 128], BF16)
            Vall = K.sb(pp, "e_Vall", [128, NT, 132], BF16, nb=NT)
            K.memset("pool", Vall[:], 1.0, w=[Vall])
            score = K.sb(pp, "e_score", [128, NT * 128], F32)
            junk = K.sb(pp, "e_junk", [128, NT * 128], F32)
            maskb = K.sb(pp, "e_maskb", [128, NT * 128], BF16)
            def front(t):
                r0, n = t * 128, 128
                x_load(t, n, r0)
                proj_rope(t, n, r0, d["ropeP"][t])
                K.dma("sp", o["kv_p"][i, r0:r0 + n, :], kvrow[:], r=[kvrow], w=[])
                pt = K.ps()
                for h in range(8):
                    K.tr(pt[:, h * 128:(h + 1) * 128], qk_r[:, h * 128:(h + 1) * 128], ident[:], r=[qk_r, ident], w=[(pt, h // 4)])
                K.cp("act", qT[:], pt[:, 0:512], r=[(pt, 0)], w=[qT])
                K.tt("dve", qTd[:], pt[:, 0:512], kc["qdecB"][:], ALU.mult, r=[(pt, 0), kc["qdecB"]], w=[qTd])
                K.cp("act", kT[:], pt[:, 512:1024], r=[(pt, 1)], w=[kT])
                pa = K.ps()
                for h in range(4):
                    hs = slice(h * 128, (h + 1) * 128)
                    K.mm(pa[:, hs], kT[:, hs], qT[:, hs], True, True, r=[kT, qT], w=[(pa, 0)])
                K.tt("dve", AT[:], pa[:, 0:512], kc["dmatT"][:], ALU.mult, r=[(pa, 0), kc["dmatT"]], w=[AT])
                K.tt("pool", kd[:].rearrange("p (h e) -> p h e", h=4), qk_r[:, 512:1024].rearrange("p (h e) -> p h e", h=4),
                     kdecC[:].unsqueeze(2).to_broadcast([128, 4, 128]), ALU.mult, r=[qk_r, kdecC], w=[kd])
                po = K.ps()
                for h in range(4):
                    hs = slice(h * 128, (h + 1) * 128)
                    K.mm(po[:, hs], AT[:, hs], pj[:, 1024 + h * 128:1024 + (h + 1) * 128], True, False, r=[AT, pj], w=[(po, 0)])
                    K.mm(po[:, hs], qTd[:, hs], Sret[:, hs], False, True, r=[qTd, Sret], w=[(po, 0)])
                pss = po
                for h in range(4):
                    hs = slice(h * 128, (h + 1) * 128)
                    K.mm(pss[:, 512 + h * 128:512 + (h + 1) * 128], kd[:, hs], pj[:, 1024 + h * 128:1024 + (h + 1) * 128], True, True, r=[kd, pj], w=[(pss, 1)])
                K.tt("dve", Sret[:], Sret[:], kc["cdecB"][:], ALU.mult, r=[Sret, kc["cdecB"]], w=[Sret])
                K.tt("dve", Sret[:], Sret[:], pss[:, 512:1024], ALU.add, r=[Sret, (pss, 1)], w=[Sret])
                head_norm_gate(n, po[0:n, 0:512], [(po, 0)])
                if t == NT - 1:
                    for h in range(4):
                        K.dma("sp", o["ret_p"][i, h], Sret[:, h * 128:(h + 1) * 128], r=[Sret], w=[])
                K.cp("pool", akb[:], kvrow[:, 0:128], r=[kvrow], w=[akb])
                K.cp("pool", ikb[:], kvrow[:, 256:320], r=[kvrow], w=[ikb])
                K.tt("dve", iqsb[:].rearrange("p (h e) -> p h e", h=4), iq_r[:].rearrange("p (h e) -> p h e", h=4),
                     aw[:, 0:4].unsqueeze(2).to_broadcast([128, 4, 64]), ALU.mult, r=[iq_r, aw], w=[iqsb])
                pt = K.ps()
                pbb = pt[:].bitcast(BF16)
                K.tr(pbb[:, 0:128], akb[:], identb[:], r=[akb, identb], w=[(pt, 0)])
                K.tr(pbb[0:64, 128:256], ikb[:], identb[:], r=[ikb, identb], w=[(pt, 0)])
                for h in range(4):
                    K.tr(pbb[:, 256 + h * 128:256 + (h + 1) * 128], aq_rb[:, h * 128:(h + 1) * 128], identb[:], r=[aq_rb, identb], w=[(pt, 0)])
                    K.tr(pbb[0:64, 1024 + h * 128:1024 + (h + 1) * 128], iqsb[:, h * 64:(h + 1) * 64], identb[:], r=[iqsb, identb], w=[(pt, 1)])
                K.cp("act", akT_all[:, r0:r0 + 128], pbb[:, 0:128], r=[(pt, 0)], w=[(akT_all, t)])
                K.cp("act", ikT_all[:, r0:r0 + 128], pbb[0:64, 128:256], r=[(pt, 0)], w=[ikT_all])
                K.cp("act", aqT[:], pbb[:, 256:768], r=[(pt, 0)], w=[aqT])
                K.cp("act", iqsT[:], pbb[0:64, 1024:1536], r=[(pt, 1)], w=[iqsT])
                K.cp("pool", Vall[:, t, 0:128], pj[:, 2688:2816], r=[pj], w=[(Vall, t)])
                n_k = (t + 1) * 128
                for kb in range((n_k + 511) // 512):
                    k0 = kb * 512
                    wb = min(512, n_k - k0)
                    pts = [K.ps(), K.ps()]
                    for h in range(4):
                        pp, hb = pts[h // 2], h % 2
                        K.mm(pp[:, hb * 512:hb * 512 + wb], iqsT[:, h * 128:(h + 1) * 128], ikT_all[:, k0:k0 + wb], True, True, r=[iqsT, ikT_all], w=[(pp, hb)])
                        rl = rls[h % 2]
                        K.act(rl[:, 0:wb], pp[:, hb * 512:hb * 512 + wb], AF.Relu, r=[(pp, hb)], w=[rl])
                        if h == 0:
                            K.ts("dve", score[:, k0:k0 + wb], rl[:, 0:wb], aw[:, 4:5], ALU.mult, r=[rl, aw], w=[score])
                        else:
                            K.stt("dve", score[:, k0:k0 + wb], rl[:, 0:wb], aw[:, 4 + h:5 + h], score[:, k0:k0 + wb], ALU.mult, ALU.add, r=[rl, aw, score], w=[score])
                if n_k > TOPK:
                    K.red("dve", bs[:, 0:1], score[:, 0:n_k], r=[score], w=[bs], op=ALU.max)
                    K.red("dve", bs[:, 1:2], score[:, 0:n_k], r=[score], w=[bs], op=ALU.min)
                    K.tt("dve", bs[:, 2:3], bs[:, 0:1], bs[:, 1:2], ALU.subtract, r=[bs], w=[bs])
                else:
                    K.memset("dve", bs[:, 1:2], -1e29, w=[bs])
                K.tt("dve", score[:, r0:r0 + 128], score[:, r0:r0 + 128], cst["cmask"][:], ALU.add, r=[score, cst["cmask"]], w=[score])

            def bis(t):
                n_k = (t + 1) * 128
                if n_k > TOPK:
                    K.ts("dve", Hh[:, 0:NIT], cst["pw2"][:, 0:NIT], bs[:, 2:3], ALU.mult, r=[bs, cst["pw2"]], w=[Hh])
                    for it in range(NIT):
                        K.tt("dve", bs[:, 4:5], bs[:, 1:2], Hh[:, it:it + 1], ALU.add, r=[bs, Hh], w=[bs])
                        K.ts("dve", junk[:, 0:n_k], score[:, 0:n_k], bs[:, 4:5], ALU.is_ge, r=[score, bs], w=[junk, cnt], s2=0.0, op1=ALU.add, acc=cnt[:, it:it + 1])
                        K.ts("dve", bs[:, 5:6], cnt[:, it:it + 1], float(TOPK), ALU.is_ge, r=[cnt], w=[bs])
                        K.stt("dve", bs[:, 1:2], bs[:, 5:6], Hh[:, it:it + 1], bs[:, 1:2], ALU.mult, ALU.add, r=[bs, Hh], w=[bs])

            def tail(t):
                n_k = (t + 1) * 128
                K.ts("dve", maskb[:, 0:n_k], score[:, 0:n_k], bs[:, 1:2], ALU.is_ge, r=[score, bs], w=[maskb])

            def back(t):
                r0, n = t * 128, 128
                def att_front(kt):
                    ks = slice(kt * 128, (kt + 1) * 128)
                    pm = K.ps()
                    pmb = pm[:].bitcast(BF16)
                    K.tr(pmb[:, 0:128], maskb[:, ks], identb[:], r=[maskb, identb], w=[(pm, 0)])
                    K.mm(pm[:, 512:1024], akT_all[:, ks], aqT[:], True, True, r=[(akT_all, kt), aqT], w=[(pm, 1)])
                    E, PT, mTs = Es[kt % 2], PTs[kt % 2], mTss[kt % 2]
                    K.act(E[:], pm[:, 512:1024], AF.Exp, r=[(pm, 1)], w=[E], scale=SC_ATT)
                    K.cp("act", mTs[:], pmb[:, 0:128], r=[(pm, 0)], w=[mTs])
                    K.tt("pool", PT[:].rearrange("p (h q) -> p h q", h=4), E[:].rearrange("p (h q) -> p h q", h=4),
                         mTs[:].unsqueeze(1).to_broadcast([128, 4, 128]), ALU.mult, r=[E, mTs], w=[PT])

                def att_back(kt, t=t):
                    PT = PTs[kt % 2]
                    for h in range(4):
                        c0 = (h % 2) * 512
                        K.mm(PSO[h // 2][:, c0:c0 + 130], PT[:, h * 128:(h + 1) * 128], Vall[:, kt, 0:130], kt == 0, kt == t, r=[PT, (Vall, kt)], w=[(PSO[h // 2], h % 2)], dedicated=True)

                att_front(0)
                for kt in range(t + 1):
                    if kt + 1 <= t:
                        att_front(kt + 1)
                    att_back(kt)
                for h in range(4):
                    c0 = (h % 2) * 512
                    pso = PSO[h // 2]
                    K.S.op("dve", lambda e, pso=pso, c0=c0, h=h: e.reciprocal(out=rs[:, h:h + 1], in_=pso[:, c0 + 128:c0 + 129]), r=[(pso, h % 2)], w=[rs])
                    K.ts("dve", cat[:, 512 + h * 128:512 + (h + 1) * 128], pso[:, c0:c0 + 128], rs[:, h:h + 1], ALU.mult, r=[(pso, h % 2), rs], w=[cat])
                out_proj_ln(K, t, n, cat, catT, wo, xt, z, xo, st)

            xts2 = [xt, K.sb(pp, "e_xt2", [128, D], F32)]
            cats2 = [cat, K.sb(pp, "e_cat2", [128, D], BF16)]
            aqTs2 = [aqT, K.sb(pp, "e_aqT2", [128, 512], BF16)]
            PSg = c["PS"]
            xt, cat, aqT = xts2[0], cats2[0], aqTs2[0]
            K.PSG = [PSg[0]]
            front(0)
            bis(0)
            tail(0)
            for t in range(NT):
                A = []
                if t + 1 < NT:
                    xt, cat, aqT = xts2[(t + 1) % 2], cats2[(t + 1) % 2], aqTs2[(t + 1) % 2]
                    K.PSG = [PSg[0]]
                    front(t + 1)
                    K.S.begin_capture()
                    bis(t + 1)
                    A = K.S.end_capture()
                xt, cat, aqT = xts2[t % 2], cats2[t % 2], aqTs2[t % 2]
                K.PSG = [PSg[1]]
                K.S.begin_capture()
                back(t)
                Bc = K.S.end_capture()
                K.S.emit_units(A, Bc)
                if t + 1 < NT:
                    tail(t + 1)
            K.PSG = PSg[:2]
            xt, cat, aqT = xts2[0], cats2[0], aqTs2[0]
        if not c["cfg"].get("skip_sample"):
            sample_even(K, l, pb, locals())
    K.PSG = c["PS"][:4]


def bisect_thr(K, n, score, n_k, topk, bs, cnt, junk, nit):
    K.red("dve", bs[0:n, 0:1], score[0:n, 0:n_k], r=[score], w=[bs], op=ALU.max)
    K.red("dve", bs[0:n, 1:2], score[0:n, 0:n_k], r=[score], w=[bs], op=ALU.min)
    K.tt("dve", bs[0:n, 2:3], bs[0:n, 0:1], bs[0:n, 1:2], ALU.subtract, r=[bs], w=[bs])
    K.memset("pool", cnt[0:n, :], 0.0, w=[cnt])
    for it in range(nit):
        K.ts("dve", bs[0:n, 3:4], bs[0:n, 2:3], 0.5 ** (it + 1), ALU.mult, r=[bs], w=[bs])
        K.tt("dve", bs[0:n, 4:5], bs[0:n, 1:2], bs[0:n, 3:4], ALU.add, r=[bs], w=[bs])
        K.ts("dve", junk[0:n, 0:n_k], score[0:n, 0:n_k], bs[0:n, 4:5], ALU.is_ge, r=[score, bs, cnt], w=[junk, cnt], s2=0.0, op1=ALU.add, acc=cnt[0:n, it:it + 1])
        K.ts("dve", bs[0:n, 5:6], cnt[0:n, it:it + 1], float(topk), ALU.is_ge, r=[cnt], w=[bs])
        K.stt("dve", bs[0:n, 1:2], bs[0:n, 5:6], bs[0:n, 3:4], bs[0:n, 1:2], ALU.mult, ALU.add, r=[bs], w=[bs])


def sample_even(K, l, pb, L):
    c = K.ctx
    f = K.fn
    d, o, NT, NPG = c["d"], c["o"], c["NT"], c["NPG"]
    cst, ident, PSO = c["cst"], c["ident"], c["PSO"]
    i = l // 2
    TOPK = c["TOPK_S"]
    NIT = 15
    SC_ATT = 128 ** -0.5
    n, t, r0 = NS, NT, c["NTOK"]
    NK = NPG * 128
    pj, xt, qk_r, kvrow, iq_r, t1, t2 = [L[k] for k in ("pj", "xt", "qk_r", "kvrow", "iq_r", "t1", "t2")]
    AT, sq, aw, s4, cat, catT, z, xo, st, wo, kc, bs, cnt = [L[k] for k in ("AT", "sq", "aw", "s4", "cat", "catT", "z", "xo", "st", "wo", "kc", "bs", "cnt")]
    g4 = lambda ap: ap.rearrange("p (h e) -> p h e", h=4)
    with K.scope() as px:
        sb = lambda nm, sh, dt=F32: K.sb(px, "es_" + nm, sh, dt)
        aq_f = sb("aq_f", [NS, 512])
        f["load_sample_mod"](l, 0)
        L["x_load"](t, n, r0)
        L["proj_rope"](t, n, r0, d["ropeS"][:, :], aq_f)
        K.dma("sp", o["kv_s"][i, :, :], kvrow[0:n, :], r=[kvrow], w=[])
        v_ap = pj[0:n, 1024:1536]
        K.tt("dve", sq[0:n, :], qk_r[0:n, 0:512], qk_r[0:n, 512:1024], ALU.mult, r=[qk_r], w=[sq])
        K.red("dve", s4[0:n, 0:4], g4(sq[0:n, :]), r=[sq], w=[s4])
        K.ts("dve", s4[0:n, 0:4], s4[0:n, 0:4], 128 ** -0.5, ALU.mult, r=[s4], w=[s4])
        pt = K.ps()
        for h in range(4):
            K.tr(pt[:, h * 16:(h + 1) * 16], qk_r[0:n, h * 128:(h + 1) * 128], ident[0:n, 0:n], r=[qk_r, ident], w=[(pt, 0)])
        qTs = sb("qTs", [128, 64])
        K.cp("act", qTs[:], pt[:, 0:64], r=[(pt, 0)], w=[qTs])
        K.tt("dve", t1[:, 0:1024].rearrange("p (h s m) -> p h s m", h=4, s=16),
             qTs[:].rearrange("p (h s) -> p h s", h=4).unsqueeze(3).to_broadcast([128, 4, 16, 16]),
             cst["delta16"][:].rearrange("p (s m) -> p s m", s=16).unsqueeze(1).to_broadcast([128, 4, 16, 16]),
             ALU.mult, r=[qTs, cst["delta16"]], w=[t1])
        S0 = [sb("S0%d" % j, [128, 512]) for j in range(2)]
        Sn = [sb("Sn%d" % j, [128, 512]) for j in range(2)]
        km = sb("km", [NS, 512])
        for s in range(NS):
            S0s, Sns = S0[s % 2], Sn[s % 2]
            for h in range(4):
                K.dma("sp", S0s[:, h * 128:(h + 1) * 128], d["st_ret"][i, s, h], r=[], w=[S0s])
            for h in range(4):
                hs = slice(h * 128, (h + 1) * 128)
                K.mm(PSO[h // 2][0:n, (h % 2) * 512:(h % 2) * 512 + 128], t1[:, (h * 16 + s) * 16:(h * 16 + s + 1) * 16], S0s[:, hs], s == 0, s == NS - 1, r=[t1, S0s], w=[(PSO[h // 2], h % 2)])
            K.ts("dve", km[:], qk_r[0:n, 512:1024], ident[0:n, s:s + 1], ALU.mult, r=[qk_r, ident], w=[km], s2=128 ** -0.5, op1=ALU.mult)
            pss = K.ps()
            for h in range(4):
                hs = slice(h * 128, (h + 1) * 128)
                K.mm(pss[:, hs], km[:, hs], pj[0:n, 1024 + h * 128:1024 + (h + 1) * 128], True, True, r=[km, pj], w=[(pss, 0)])
            K.tt("pool", Sns[:], S0s[:], kc["gam1B"][:], ALU.mult, r=[S0s, kc["gam1B"]], w=[Sns])
            K.tt("dve", Sns[:], Sns[:], pss[:, 0:512], ALU.add, r=[Sns, (pss, 0)], w=[Sns])
            for h in range(4):
                K.dma("sp", o["ret_s"][i, s, h], Sns[:, h * 128:(h + 1) * 128], r=[Sns], w=[])
        for h in range(4):
            hs = slice(h * 128, (h + 1) * 128)
            K.tt("dve", AT[0:n, hs], PSO[h // 2][0:n, (h % 2) * 512:(h % 2) * 512 + 128], kc["gam1B"][0:n, hs], ALU.mult, r=[(PSO[h // 2], h % 2), kc["gam1B"]], w=[AT])
        K.tt("pool", g4(sq[0:n, :]), g4(v_ap), s4[0:n, 0:4].unsqueeze(2).to_broadcast([n, 4, 128]), ALU.mult, r=[pj, s4], w=[sq])
        K.tt("dve", AT[0:n, :], AT[0:n, :], sq[0:n, :], ALU.add, r=[AT, sq], w=[AT])
        L["head_norm_gate"](n, AT[0:n, :], [AT])
        iqs = sb("iqs", [NS, 256])
        K.tt("dve", g4(iqs[:]), g4(iq_r[0:n, :]), aw[0:n, 0:4].unsqueeze(2).to_broadcast([n, 4, 64]), ALU.mult, r=[iq_r, aw], w=[iqs])
        pt = K.ps()
        for h in range(4):
            K.tr(pt[:, h * 16:(h + 1) * 16], aq_f[:, h * 128:(h + 1) * 128], ident[0:n, 0:n], r=[aq_f, ident], w=[(pt, 0)])
            K.tr(pt[0:64, 512 + h * 16:512 + (h + 1) * 16], iqs[:, h * 64:(h + 1) * 64], ident[0:n, 0:n], r=[iqs, ident], w=[(pt, 1)])
        aqTa = sb("aqTa", [128, 64])
        iqTa = sb("iqTa", [64, 64])
        K.cp("act", aqTa[:], pt[:, 0:64], r=[(pt, 0)], w=[aqTa])
        K.cp("act", iqTa[:], pt[0:64, 512:576], r=[(pt, 1)], w=[iqTa])
        sgnB = sb("sgnB", [128, NS * 4])
        sg_d = K.dram("sg_d%d" % l, [1, NS * 4], F32, "Internal")
        K.dma("sp", sg_d[0:1, :].rearrange("o (s h) -> (o s) h", h=4), aw[0:n, 4:8], r=[aw], w=[sg_d])
        K.dma("sp", sgnB[:], sg_d[0:1, :].partition_broadcast(128), r=[sg_d], w=[sgnB])
        idxi = sb("idxi", [128, NS * NPG], I32)
        idxf = sb("idxf", [128, NS * NPG])
        K.dma("sp", idxi[:], d["ptab"][:, :].rearrange("(o s) j -> o (s j)", o=1).partition_broadcast(128), r=[], w=[idxi])
        K.cp("dve", idxf[:], idxi[:], r=[idxi], w=[idxf])
        K.ts("dve", idxf[:], idxf[:], 128.0, ALU.mult, r=[idxf, cst["iotap"]], w=[idxf], s2=cst["iotap"][:, 0:1], op1=ALU.add)
        if i > 0:
            K.ts("dve", idxf[:], idxf[:], float(i * c["cfg"]["NPOOL"] * 128), ALU.add, r=[idxf], w=[idxf])
        K.cp("dve", idxi[:], idxf[:], r=[idxf], w=[idxi])
        cache = d["cache_kv"]
        sc_d = K.dram("sc_d%d" % l, [NS, NK], F32, "Internal")
        os_d = K.dram("os_d%d" % l, [4, NS, 132], F32, "Internal")
        kig = [sb("kig%d" % j, [128, NPG, 320]) for j in range(1)] * 2
        kTs = sb("kTs", [128, NK + 4])
        kiT = kTs
        scT = sb("scT", [128, NS * NPG])
        rlS = sb("rlS", [128, NPG * 4])
        scr_t = sb("scr_t", [NPG, 128])
        for s in range(NS):
            kg = kig[s % 2]
            for j in range(NPG):
                col = s * NPG + j
                K.S.dma("pool", lambda e, kg=kg, j=j, col=col: e.indirect_dma_start(
                    out=kg[:, j, :], out_offset=None, in_=cache[:, :],
                    in_offset=bass.IndirectOffsetOnAxis(ap=idxi[:, col:col + 1], axis=0)), r=[idxi], w=[kg])
            for g in range((NPG + 3) // 4):
                pt = K.ps()
                nj = min(4, NPG - g * 4)
                for jj in range(nj):
                    K.tr(pt[0:64, jj * 128:(jj + 1) * 128], kg[:, g * 4 + jj, 256:320], ident[:], r=[kg, ident], w=[(pt, 0)])
                K.cp(K.evq(), kiT[0:64, g * 512:g * 512 + nj * 128], pt[0:64, 0:nj * 128], r=[(pt, 0)], w=[kiT])
            pq = K.ps()
            for j in range(NPG):
                K.mm(pq[:, j * 4:(j + 1) * 4], kiT[0:64, j * 128:(j + 1) * 128], iqTa[:].rearrange("p (h s) -> p h s", h=4)[:, :, s], True, True, r=[kiT, iqTa], w=[(pq, 0)])
            K.act(rlS[:], pq[:, 0:NPG * 4], AF.Relu, r=[(pq, 0)], w=[rlS])
            rv = rlS[:].rearrange("p (j h) -> p j h", h=4)
            K.tt("dve", rv, rv, sgnB[:, s * 4:(s + 1) * 4].unsqueeze(1).to_broadcast([128, NPG, 4]), ALU.mult, r=[rlS, sgnB], w=[rlS])
            K.red("dve", scT[:, s * NPG:(s + 1) * NPG], rv, r=[rlS], w=[scT])
            pt2 = K.ps()
            K.tr(pt2[0:NPG, 0:128], scT[:, s * NPG:(s + 1) * NPG], ident[:], r=[scT, ident], w=[(pt2, 0)])
            K.cp("act", scr_t[:], pt2[0:NPG, 0:128], r=[(pt2, 0)], w=[scr_t])
            K.dma("sp", sc_d[s, :].rearrange("(j p) -> j p", p=128), scr_t[:], r=[scr_t], w=[sc_d])
        K.tt("dve", t2[0:n, 0:256].rearrange("p (h e) -> p h e", h=4), g4(iqs[:]), kvrow[0:n, 256:320].unsqueeze(1).to_broadcast([n, 4, 64]), ALU.mult, r=[iqs, kvrow], w=[t2])
        K.red("dve", s4[0:n, 4:8], t2[0:n, 0:256].rearrange("p (h e) -> p h e", h=4), r=[t2], w=[s4])
        K.ts("dve", s4[0:n, 4:8], s4[0:n, 4:8], 0.0, ALU.max, r=[s4], w=[s4])
        K.tt("dve", s4[0:n, 4:8], s4[0:n, 4:8], aw[0:n, 4:8], ALU.mult, r=[s4, aw], w=[s4])
        srow = sb("srow", [NS, NK + 1])
        jrow = kTs
        K.dma("sp", srow[:, 0:NK], sc_d[:, :], r=[sc_d], w=[srow])
        K.red("dve", srow[:, NK:NK + 1], s4[0:n, 4:8], r=[s4, srow], w=[srow])
        if NK + 1 > TOPK:
            bisect_thr(K, n, srow, NK + 1, TOPK, bs, cnt, jrow, NIT)
        else:
            K.memset("dve", bs[0:n, 1:2], -1e29, w=[bs])
        dth = sb("dth", [NS, NS])
        K.ts("dve", dth[:], ident[0:NS, 0:NS], bs[0:n, 1:2], ALU.mult, r=[ident, bs], w=[dth])
        pq = K.ps()
        K.mm(pq[:, 0:NS], cst["ones"][0:NS, :], dth[:], True, True, r=[cst["ones"], dth], w=[(pq, 0)])
        thrB = sb("thrB", [128, NS])
        K.cp("act", thrB[:], pq[:, 0:NS], r=[(pq, 0)], w=[thrB])
        kvg = kig
        mT = sb("mT", [128, NPG])
        Es = sb("Es", [128, NPG * 4])
        PT2 = sb("PT2", [128, NPG * 4])
        osm = sb("osm", [4, NS, 132])
        K.memset("pool", osm[:], 0.0, w=[osm])
        for s in range(NS):
            kv = kvg[s % 2]
            for j in range(NPG):
                col = s * NPG + j
                K.S.dma("pool", lambda e, kv=kv, j=j, col=col: e.indirect_dma_start(
                    out=kv[:, j, :], out_offset=None, in_=cache[:, :],
                    in_offset=bass.IndirectOffsetOnAxis(ap=idxi[:, col:col + 1], axis=0)), r=[idxi], w=[kv])
            for g in range((NPG + 3) // 4):
                pt = K.ps()
                nj = min(4, NPG - g * 4)
                for jj in range(nj):
                    K.tr(pt[:, jj * 128:(jj + 1) * 128], kv[:, g * 4 + jj, 0:128], ident[:], r=[kv, ident], w=[(pt, 0)])
                K.cp(K.evq(), kTs[:, g * 512:g * 512 + nj * 128], pt[:, 0:nj * 128], r=[(pt, 0)], w=[kTs])
            pq = K.ps()
            for j in range(NPG):
                K.mm(pq[:, j * 4:(j + 1) * 4], kTs[:, j * 128:(j + 1) * 128], aqTa[:].rearrange("p (h s) -> p h s", h=4)[:, :, s], True, True, r=[kTs, aqTa], w=[(pq, 0)])
            K.act(Es[:], pq[:, 0:NPG * 4], AF.Exp, r=[(pq, 0)], w=[Es], scale=SC_ATT)
            K.ts("dve", mT[:], scT[:, s * NPG:(s + 1) * NPG], thrB[:, s:s + 1], ALU.is_ge, r=[scT, thrB], w=[mT])
            K.tt("dve", PT2[:].rearrange("p (j h) -> p j h", h=4), Es[:].rearrange("p (j h) -> p j h", h=4),
                 mT[:].unsqueeze(2).to_broadcast([128, NPG, 4]), ALU.mult, r=[Es, mT], w=[PT2])
            po2 = K.ps()
            for j in range(NPG):
                K.mm(po2[0:4, 0:128], PT2[:, j * 4:(j + 1) * 4], kv[:, j, 128:256], j == 0, j == NPG - 1, r=[PT2, kv], w=[(po2, 0)])
            for j in range(NPG):
                K.mm(po2[0:4, 512:513], PT2[:, j * 4:(j + 1) * 4], cst["ones"][:, 0:1], j == 0, j == NPG - 1, r=[PT2, cst["ones"]], w=[(po2, 1)])
            K.cp("act", osm[:, s, 0:128], po2[0:4, 0:128], r=[(po2, 0)], w=[osm])
            K.cp("act", osm[:, s, 128:129], po2[0:4, 512:513], r=[(po2, 1)], w=[osm])
        ot = sb("ot", [NS, 4, 132])
        K.dma("sp", os_d[:, :, :], osm[:], r=[osm], w=[os_d])
        K.dma("sp", ot[:], os_d[:, :, :].rearrange("h s e -> s h e"), r=[os_d], w=[ot])
        K.tt("dve", g4(t2[0:n, 0:512]), g4(aq_f[:]), kvrow[0:n, 0:128].unsqueeze(1).to_broadcast([n, 4, 128]), ALU.mult, r=[aq_f, kvrow], w=[t2])
        K.red("dve", s4[0:n, 8:12], g4(t2[0:n, 0:512]), r=[t2], w=[s4])
        K.act(s4[0:n, 8:12], s4[0:n, 8:12], AF.Exp, r=[s4], w=[s4], scale=SC_ATT)
        K.tt("dve", bs[0:n, 6:7], srow[:, NK:NK + 1], bs[0:n, 1:2], ALU.is_ge, r=[srow, bs], w=[bs])
        K.ts("dve", s4[0:n, 8:12], s4[0:n, 8:12], bs[0:n, 6:7], ALU.mult, r=[s4, bs], w=[s4])
        K.tt("dve", g4(t1[0:n, 0:512]), kvrow[0:n, 128:256].unsqueeze(1).to_broadcast([n, 4, 128]),
             s4[0:n, 8:12].unsqueeze(2).to_broadcast([n, 4, 128]), ALU.mult, r=[kvrow, s4], w=[t1])
        K.tt("dve", g4(t1[0:n, 0:512]), g4(t1[0:n, 0:512]), ot[:, :, 0:128], ALU.add, r=[t1, ot], w=[t1])
        K.tt("dve", s4[0:n, 12:16], ot[:, :, 128], s4[0:n, 8:12], ALU.add, r=[ot, s4], w=[s4])
        K.S.op("dve", lambda e: e.reciprocal(out=s4[0:n, 12:16], in_=s4[0:n, 12:16]), r=_bufs([s4]), w=_bufs([s4]))
        K.tt("dve", g4(cat[0:n, 512:1024]), g4(t1[0:n, 0:512]), s4[0:n, 12:16].unsqueeze(2).to_broadcast([n, 4, 128]), ALU.mult, r=[t1, s4], w=[cat])
        out_proj_ln(K, t, n, cat, catT, wo, xt, z, xo, st)


def phase_b_odd(K, l):
    c = K.ctx
    f = K.fn
    d, o, NT = c["d"], c["o"], c["NT"]
    cst, ident, PS = c["cst"], c["ident"], c["PS"]
    jl = l // 2
    NTOK = c["NTOK"]
    K.PSG = PS[:4]
    f["load_phase_consts"](l, 0)
    g8 = lambda ap: ap.rearrange("p (h e) -> p h e", h=8)
    bc8 = lambda ap, n, w: ap.unsqueeze(2).to_broadcast([n, 8, w])
    with K.scope() as pb:
        sb = lambda nm, sh, dt=F32: K.sb(pb, "o_" + nm, sh, dt)
        wo = sb("wo", [128, 8, D], BF16)
        f["load_weights_bf16"](wo, d["odd_w_out"][jl], D)
        cw = sb("cw", [128, 4, 3072])
        for tap in range(4):
            K.dma("sp", cw[:, tap, :], d["odd_conv"][jl, tap:tap + 1, :].partition_broadcast(128), r=[], w=[cw])
        gnB = sb("gnB", [128, 128])
        K.dma("sp", gnB[:], d["odd_gn"][jl:jl + 1, :].partition_broadcast(128), r=[], w=[gnB])
        nea = sb("nea", [128, 8])
        dtb = sb("dtb", [128, 8])
        K.dma("sp", nea[:], d["odd_a_log"][jl:jl + 1, :].partition_broadcast(128), r=[], w=[nea])
        K.dma("sp", dtb[:], d["odd_dt_bias"][jl:jl + 1, :].partition_broadcast(128), r=[], w=[dtb])
        K.act(nea[:], nea[:], AF.Exp, r=[nea], w=[nea])
        K.ts("dve", nea[:], nea[:], -1.0, ALU.mult, r=[nea], w=[nea])
        pj = sb("pj", [128, ODD_IN])
        xt = sb("xt", [128, D])
        sh = sb("sh", [128, 3072])
        cat = sb("cat", [128, D], BF16)
        catT = sb("catT", [128, 8, 128], BF16)
        z = sb("z", [128, D])
        xo = sb("xo", [128, D])
        st = sb("st", [128, 4])
        sm = sb("sm", [128, 96])
        ob = sb("ob", [128, 1024])
        sq = sh

        def x_load(t, n, r0):
            K.dma("sp", xt[0:n, :], c["x_a"][r0:r0 + n, :], r=[c["tb_xa"][t]], w=[xt])

        def conv_gates(t, n, r0, taps):
            K.dma("sp", pj[0:n, :], c["proj_d"][3 + r0:3 + r0 + n, :], r=[c["tb_pj"][t]], w=[pj])
            pq = pj[0:n, 0:3072]
            K.tt("pool", pq, pq, cw[0:n, 3, :], ALU.mult, r=[pj, cw], w=[pj])
            for tap in range(3):
                src, deps = taps[tap]
                K.dma("sp", sh[0:n, :], src, r=deps, w=[sh])
                K.tt("pool", sh[0:n, :], sh[0:n, :], cw[0:n, tap, :], ALU.mult, r=[sh, cw], w=[sh])
                K.tt("dve", pq, pq, sh[0:n, :], ALU.add, r=[pj, sh], w=[pj])
            K.act(pj[0:n, 0:4096], pj[0:n, 0:4096], AF.Silu, r=[pj], w=[pj])
            K.tt("pool", sq[0:n, 0:2048], pj[0:n, 0:2048], pj[0:n, 0:2048], ALU.mult, r=[pj], w=[sq])
            K.red("dve", sm[0:n, 16:32], sq[0:n, 0:2048].rearrange("p (h e) -> p h e", h=16), r=[sq], w=[sm])
            K.rsqrt("dve", sm[0:n, 16:32], sm[0:n, 16:32], r=[sm], w=[sm], mult=1.0, add=1e-6)
            K.ts("dve", sm[0:n, 16:24], sm[0:n, 16:24], 128 ** -0.5, ALU.mult, r=[sm], w=[sm])
            qk = pj[0:n, 0:2048].rearrange("p (h e) -> p h e", h=16)
            K.tt("dve", qk, qk, sm[0:n, 16:32].unsqueeze(2).to_broadcast([n, 16, 128]), ALU.mult, r=[pj, sm], w=[pj])
            K.tt("dve", sm[0:n, 32:40], pj[0:n, 4096:4104], dtb[0:n, :], ALU.add, r=[pj, dtb], w=[sm])
            K.act(sm[0:n, 40:48], sm[0:n, 32:40], AF.Abs, r=[sm], w=[sm])
            K.act(sm[0:n, 40:48], sm[0:n, 40:48], AF.Exp, r=[sm], w=[sm], scale=-1.0)
            K.act(sm[0:n, 40:48], sm[0:n, 40:48], AF.Ln, r=[sm], w=[sm], bias=1.0)
            K.ts("dve", sm[0:n, 32:40], sm[0:n, 32:40], 0.0, ALU.max, r=[sm], w=[sm])
            K.tt("dve", sm[0:n, 32:40], sm[0:n, 32:40], sm[0:n, 40:48], ALU.add, r=[sm], w=[sm])
            K.tt("dve", sm[0:n, 0:8], sm[0:n, 32:40], nea[0:n, :], ALU.mult, r=[sm, nea], w=[sm])
            K.act(sm[0:n, 8:16], pj[0:n, 4104:4112], AF.Sigmoid, r=[pj], w=[sm])

        def rms_gate(n, src, rsrc):
            K.tt("pool", sq[0:n, 0:1024], src, src, ALU.mult, r=rsrc, w=[sq])
            K.red("dve", sm[0:n, 48:56], g8(sq[0:n, 0:1024]), r=[sq], w=[sm])
            K.rsqrt("dve", sm[0:n, 48:56], sm[0:n, 48:56], r=[sm], w=[sm], mult=1.0 / 128, add=LN_EPS)
            K.tt("dve", g8(sq[0:n, 0:1024]), g8(src), bc8(sm[0:n, 48:56], n, 128), ALU.mult, r=rsrc + [sm], w=[sq])
            K.tt("pool", g8(sq[0:n, 0:1024]), g8(sq[0:n, 0:1024]), gnB[0:n, :].unsqueeze(1).to_broadcast([n, 8, 128]), ALU.mult, r=[sq, gnB], w=[sq])
            K.tt("dve", cat[0:n, :], sq[0:n, 0:1024], pj[0:n, 3072:4096], ALU.mult, r=[sq, pj], w=[cat])

        with K.scope() as pp:
            G = [K.sb(pp, "o_G%d" % j, [128, 1024], F32) for j in range(15)]
            Sst = K.sb(pp, "o_S", [128, 1024], F32)
            K.memset("pool", Sst[:], 0.0, w=[Sst])
            Dg, E, ES, _, kb, kbg, vb, kg, qT, kT, kbT, Nn, Mm, aqkT, R = G
            Pa, Pb, Qa, Qb = G[0], G[1], G[2], G[3]
            u_, wT, vnew = G[4], G[9], G[10]
            identB = ident[:].unsqueeze(1).to_broadcast([128, 8, 128])
            for t in range(NT):
                r0, n = t * 128, 128
                x_load(t, n, r0)
                prevb = c["tb_pj"][t - 1] if t > 0 else c["tb_pj"][NT + 1]
                taps = [(c["proj_d"][r0 + tap:r0 + tap + 128, 0:3072], [c["tb_pj"][t], prevb]) for tap in range(3)]
                conv_gates(t, n, r0, taps)
                q_ap, k_ap, v_ap = pj[:, 0:1024], pj[:, 1024:2048], pj[:, 2048:3072]
                g_, be = sm[:, 0:8], sm[:, 8:16]
                pg = K.ps()
                K.mm(pg[:, 0:8], cst["triu"][:], g_, True, True, r=[cst["triu"], sm], w=[(pg, 0)])
                gc = sm[:, 56:64]
                K.cp("act", gc, pg[:, 0:8], r=[(pg, 0)], w=[sm])
                K.tt("pool", g8(Dg[:]), identB, bc8(gc, 128, 128), ALU.mult, r=[ident, sm], w=[Dg])
                pr = K.ps()
                K.mm(pr[:, 0:512], cst["ones"][:], Dg[:, 0:512], True, True, r=[cst["ones"], Dg], w=[(pr, 0)])
                K.mm(pr[:, 512:1024], cst["ones"][:], Dg[:, 512:1024], True, True, r=[cst["ones"], Dg], w=[(pr, 1)])
                K.tt("dve", g8(E[:]), g8(pr[:]), bc8(gc, 128, 128), ALU.subtract, r=[pr, sm], w=[E])
                K.tt("dve", g8(E[:]), g8(E[:]), cst["mvinc"][:].unsqueeze(1).to_broadcast([128, 8, 128]), ALU.min, r=[E, cst["mvinc"]], w=[E])
                K.act(E[:], E[:], AF.Exp, r=[E], w=[E])
                K.tt("pool", g8(ES[:]), g8(E[:]), cst["strict"][:].unsqueeze(1).to_broadcast([128, 8, 128]), ALU.mult, r=[E, cst["strict"]], w=[ES])
                gl = sm[:, 64:72]
                K.cp("act", gl, g8(pr[:])[:, :, 127], r=[pr], w=[sm])
                K.tt("dve", sm[:, 72:80], gl, gc, ALU.subtract, r=[sm], w=[sm])
                K.act(sm[:, 72:80], sm[:, 72:80], AF.Exp, r=[sm], w=[sm])
                K.act(sm[:, 80:88], gl, AF.Exp, r=[sm], w=[sm])
                K.act(sm[:, 88:96], gc, AF.Exp, r=[sm], w=[sm])
                K.tt("pool", g8(kb[:]), g8(k_ap), bc8(be, 128, 128), ALU.mult, r=[pj, sm], w=[kb])
                K.tt("pool", g8(kbg[:]), g8(kb[:]), bc8(sm[:, 88:96], 128, 128), ALU.mult, r=[kb, sm], w=[kbg])
                K.tt("pool", g8(vb[:]), g8(v_ap), bc8(be, 128, 128), ALU.mult, r=[pj, sm], w=[vb])
                K.tt("pool", g8(kg[:]), g8(k_ap), bc8(sm[:, 72:80], 128, 128), ALU.mult, r=[pj, sm], w=[kg])
                for src_ap, rs, dst in ((q_ap, [pj], qT), (k_ap, [pj], kT), (kb[:], [kb], kbT)):
                    pt = K.ps()
                    for h in range(8):
                        K.tr(pt[:, h * 128:(h + 1) * 128], src_ap[:, h * 128:(h + 1) * 128], ident[:], r=rs + [ident], w=[(pt, h // 4)])
                    K.cp("act", dst[:], pt[:], r=[pt], w=[dst])
                pn = K.ps()
                pa = K.ps()
                for h in range(8):
                    hs = slice(h * 128, (h + 1) * 128)
                    K.mm(pn[:, hs], kT[:, hs], kbT[:, hs], True, True, r=[kT, kbT], w=[(pn, h // 4)])
                for h in range(8):
                    hs = slice(h * 128, (h + 1) * 128)
                    K.mm(pa[:, hs], kT[:, hs], qT[:, hs], True, True, r=[kT, qT], w=[(pa, h // 4)])
                K.tt("dve", Nn[:], pn[:], ES[:], ALU.mult, r=[pn, ES], w=[Nn])
                K.tt("dve", aqkT[:], pa[:], E[:], ALU.mult, r=[pa, E], w=[aqkT])
                pm = K.ps()
                for h in range(8):
                    hs = slice(h * 128, (h + 1) * 128)
                    K.tr(pm[:, hs], Nn[:, hs], ident[:], r=[Nn, ident], w=[(pm, h // 4)])
                K.cp("act", Mm[:], pm[:], r=[pm], w=[Mm])
                K.tt("pool", g8(R[:]), identB, g8(Nn[:]), ALU.subtract, r=[ident, Nn], w=[R])
                P, Q = Mm, Nn
                Pn, Qn = [Pa, Pb], [Qa, Qb]
                for lev in range(1, 7):
                    P2, Q2 = Pn[lev % 2], Qn[lev % 2]
                    pP = K.ps()
                    for h in range(8):
                        hs = slice(h * 128, (h + 1) * 128)
                        K.mm(pP[:, hs], Q[:, hs], P[:, hs], True, True, r=[Q, P], w=[(pP, h // 4)])
                    if lev < 6:
                        pQ = K.ps()
                        for h in range(8):
                            hs = slice(h * 128, (h + 1) * 128)
                            K.mm(pQ[:, hs], P[:, hs], Q[:, hs], True, True, r=[P, Q], w=[(pQ, h // 4)])
                    K.cp("act", P2[:], pP[:], r=[pP], w=[P2])
                    if lev < 6:
                        K.cp("dve", Q2[:], pQ[:], r=[pQ], w=[Q2])
                    pR = K.ps()
                    for h in range(8):
                        hs = slice(h * 128, (h + 1) * 128)
                        K.mm(pR[:, hs], P2[:, hs], R[:, hs], True, True, r=[P2, R], w=[(pR, h // 4)])
                    K.tt("dve", R[:], R[:], pR[:], ALU.add, r=[R, pR], w=[R])
                    P, Q = P2, Q2
                pu = K.ps()
                pw = K.ps()
                for h in range(8):
                    hs = slice(h * 128, (h + 1) * 128)
                    K.mm(pu[:, hs], R[:, hs], vb[:, hs], True, True, r=[R, vb], w=[(pu, h // 4)])
                for h in range(8):
                    hs = slice(h * 128, (h + 1) * 128)
                    K.mm(pw[:, hs], kbg[:, hs], R[:, hs], True, True, r=[kbg, R], w=[(pw, h // 4)])
                K.cp("act", u_[:], pu[:], r=[pu], w=[u_])
                K.cp("act", wT[:], pw[:], r=[pw], w=[wT])
                pv = K.ps()
                for h in range(8):
                    hs = slice(h * 128, (h + 1) * 128)
                    K.mm(pv[:, hs], wT[:, hs], Sst[:, hs], True, True, r=[wT, Sst], w=[(pv, h // 4)])
                K.tt("dve", vnew[:], u_[:], pv[:], ALU.subtract, r=[u_, pv], w=[vnew])
                po1 = K.ps()
                for h in range(8):
                    hs = slice(h * 128, (h + 1) * 128)
                    K.mm(po1[:, hs], qT[:, hs], Sst[:, hs], True, True, r=[qT, Sst], w=[(po1, h // 4)])
                K.tt("dve", g8(ob[:]), g8(po1[:]), bc8(sm[:, 88:96], 128, 128), ALU.mult, r=[po1, sm], w=[ob])
                po2 = K.ps()
                for h in range(8):
                    hs = slice(h * 128, (h + 1) * 128)
                    K.mm(po2[:, hs], aqkT[:, hs], vnew[:, hs], True, True, r=[aqkT, vnew], w=[(po2, h // 4)])
                K.tt("dve", ob[:], ob[:], po2[:], ALU.add, r=[ob, po2], w=[ob])
                pS = K.ps()
                for h in range(8):
                    hs = slice(h * 128, (h + 1) * 128)
                    K.mm(pS[:, hs], kg[:, hs], vnew[:, hs], True, True, r=[kg, vnew], w=[(pS, h // 4)])
                K.tt("pool", g8(Sst[:]), g8(Sst[:]), bc8(sm[:, 80:88], 128, 128), ALU.mult, r=[Sst, sm], w=[Sst])
                K.tt("dve", Sst[:], Sst[:], pS[:], ALU.add, r=[Sst, pS], w=[Sst])
                rms_gate(n, ob[:], [ob])
                out_proj_ln(K, t, n, cat, catT, wo, xt, z, xo, st)
            for h in range(8):
                K.dma("sp", o["ssm_p"][jl, h], Sst[:, h * 128:(h + 1) * 128], r=[Sst], w=[])
            K.dma("sp", o["conv_p"][jl], c["proj_d"][NTOK:NTOK + 3, 0:3072], r=[c["tb_pj"][NT - 1]], w=[])
        if not c["cfg"].get("skip_sample"):
            sample_odd(K, l, locals())
    K.PSG = PS[:4]


def sample_odd(K, l, L):
    c = K.ctx
    f = K.fn
    d, o, NT = c["d"], c["o"], c["NT"]
    cst, ident = c["cst"], c["ident"]
    jl = l // 2
    n, t, r0 = NS, NT, c["NTOK"]
    pj, xt, sm, ob, cat, catT, z, xo, st, wo = [L[k] for k in ("pj", "xt", "sm", "ob", "cat", "catT", "z", "xo", "st", "wo")]
    g8 = lambda ap: ap.rearrange("p (h e) -> p h e", h=8)
    with K.scope() as px:
        sb = lambda nm, sh, dt=F32: K.sb(px, "os_" + nm, sh, dt)
        f["load_sample_mod"](l, 0)
        L["x_load"](t, n, r0)
        taps = [(d["st_conv"][jl, :, tap, :], []) for tap in range(3)]
        L["conv_gates"](t, n, r0, taps)
        K.dma("sp", o["conv_s"][jl, :, 0:2, :], d["st_conv"][jl, :, 1:3, :], r=[], w=[])
        K.dma("sp", o["conv_s"][jl, :, 2, :], c["proj_d"][3 + r0:3 + r0 + n, 0:3072], r=[c["tb_pj"][t]], w=[])
        K.act(sm[0:n, 32:40], sm[0:n, 0:8], AF.Exp, r=[sm], w=[sm])
        sq = L["sq"]
        K.tt("pool", sq[0:n, 0:1024], pj[0:n, 0:1024], pj[0:n, 1024:2048], ALU.mult, r=[pj], w=[sq])
        K.red("dve", sm[0:n, 40:48], g8(sq[0:n, 0:1024]), r=[sq], w=[sm])
        row_d = K.dram("gr_d%d" % l, [NS, 3072 + 24], F32, "Internal")
        orow_d = K.dram("go_d%d" % l, [NS, 1024], F32, "Internal")
        K.dma("sp", row_d[:, 0:3072], pj[0:n, 0:3072], r=[pj], w=[row_d])
        K.dma("sp", row_d[:, 3072:3080], sm[0:n, 32:40], r=[sm], w=[row_d])
        K.dma("sp", row_d[:, 3080:3088], sm[0:n, 8:16], r=[sm], w=[row_d])
        K.dma("sp", row_d[:, 3088:3096], sm[0:n, 40:48], r=[sm], w=[row_d])
        egB = sb("egB", [128, NS * 8])
        egd = K.dram("ge_d%d" % l, [1, NS * 8], F32, "Internal")
        K.dma("sp", egd[0:1, :].rearrange("o (s h) -> (o s) h", h=8), sm[0:n, 32:40], r=[sm], w=[egd])
        K.dma("sp", egB[:], egd[0:1, :].partition_broadcast(128), r=[egd], w=[egB])
        kqT = sb("kqT", [128, 8 * NS * 2])
        kq4 = kqT[:].rearrange("p (h s a) -> p h s a", h=8, a=2)
        for which, c0 in ((0, 1024), (1, 0)):
            pt = K.ps()
            for h in range(8):
                K.tr(pt[:, h * 16:(h + 1) * 16], pj[0:n, c0 + h * 128:c0 + (h + 1) * 128], ident[0:n, 0:n], r=[pj, ident], w=[(pt, 0)])
            K.cp("act", kq4[:, :, :, which], pt[:, 0:128].rearrange("p (h s) -> p h s", h=8), r=[(pt, 0)], w=[kqT])
        rows = [sb("row%d" % j, [1, 3096]) for j in range(2)]
        S0 = [sb("S0%d" % j, [128, 1024]) for j in range(2)]
        vn = sb("vn", [1, 1024])
        orow = sb("orow", [1, 1024])
        tmp = sb("tmp", [1, 1024])
        r8 = lambda ap: ap.rearrange("p (h e) -> p h e", h=8)
        b8 = lambda ap: ap.unsqueeze(2).to_broadcast([1, 8, 128])
        for s in range(NS):
            row, S0s = rows[s % 2], S0[s % 2]
            K.dma("sp", row[:], row_d[s:s + 1, :], r=[row_d], w=[row])
            for h in range(8):
                K.dma("sp", S0s[:, h * 128:(h + 1) * 128], d["st_ssm"][jl, s, h], r=[], w=[S0s])
            pk = K.ps()
            pq = K.ps()
            for h in range(8):
                hs = slice(h * 128, (h + 1) * 128)
                K.mm(pk[0:1, hs], kq4[:, h, s, 0:1], S0s[:, hs], True, True, r=[kqT, S0s], w=[(pk, h // 4)])
            for h in range(8):
                hs = slice(h * 128, (h + 1) * 128)
                K.mm(pq[0:1, hs], kq4[:, h, s, 1:2], S0s[:, hs], True, True, r=[kqT, S0s], w=[(pq, h // 4)])
            eg, be, qk = row[:, 3072:3080], row[:, 3080:3088], row[:, 3088:3096]
            K.tt("dve", r8(tmp[:]), r8(pk[0:1, :]), b8(eg), ALU.mult, r=[pk, row], w=[tmp])
            K.tt("dve", vn[:], row[:, 2048:3072], tmp[:], ALU.subtract, r=[row, tmp], w=[vn])
            K.tt("dve", r8(vn[:]), r8(vn[:]), b8(be), ALU.mult, r=[vn, row], w=[vn])
            K.tt("dve", r8(orow[:]), r8(pq[0:1, :]), b8(eg), ALU.mult, r=[pq, row], w=[orow])
            K.tt("dve", r8(tmp[:]), r8(vn[:]), b8(qk), ALU.mult, r=[vn, row], w=[tmp])
            K.tt("dve", orow[:], orow[:], tmp[:], ALU.add, r=[orow, tmp], w=[orow])
            K.dma("sp", orow_d[s:s + 1, :], orow[:], r=[orow], w=[orow_d])
            pS = K.ps()
            for h in range(8):
                hs = slice(h * 128, (h + 1) * 128)
                K.mm(pS[:, hs], row[:, 1024 + h * 128:1024 + (h + 1) * 128], vn[:, hs], True, True, r=[row, vn], w=[(pS, h // 4)])
            K.tt("pool", g8(S0s[:]), g8(S0s[:]), egB[:, s * 8:(s + 1) * 8].unsqueeze(2).to_broadcast([128, 8, 128]), ALU.mult, r=[S0s, egB], w=[S0s])
            K.tt("dve", S0s[:], S0s[:], pS[:], ALU.add, r=[S0s, pS], w=[S0s])
            for h in range(8):
                K.dma("sp", o["ssm_s"][jl, s, h], S0s[:, h * 128:(h + 1) * 128], r=[S0s], w=[])
        K.dma("sp", ob[0:n, :], orow_d[:, :], r=[orow_d], w=[ob])
        L["rms_gate"](n, ob[0:n, :], [ob])
        out_proj_ln(K, t, n, cat, catT, wo, xt, z, xo, st)


_NC_CACHE = {}


def kernel(x_prompt, x_sample, c_prompt, c_sample, cache_kv, page_table, state_ret, state_ssm, state_conv,
           ada_w, ada_b, ln_g, ln_b, mlp_up, mlp_down, even_w_in, even_w_out, ret_gn,
           odd_w_in, odd_w_out, odd_conv, odd_a_log, odd_dt_bias, odd_gn):
    f = lambda a: np.ascontiguousarray(np.asarray(a))
    B, T, _ = x_prompt.shape
    NT = T // 128
    NPG = page_table.shape[1]
    NPOOL = cache_kv.shape[1]
    cfg = dict(NT=NT, NPG=NPG, NPOOL=NPOOL, DEPTH=4, mix="real")
    key = (NT, NPG, NPOOL)
    if key not in _NC_CACHE:
        _NC_CACHE[key] = build(cfg)
    nc = _NC_CACHE[key]
    cs = make_consts(NT, NPG * cache_kv.shape[2])
    shared = dict(cache_kv=f(cache_kv).reshape(-1, 320), ada_w=f(ada_w), ada_b=f(ada_b), ln_g=f(ln_g), ln_b=f(ln_b),
                  mlp_up=f(mlp_up), mlp_down=f(mlp_down), even_w_in=f(even_w_in), even_w_out=f(even_w_out),
                  ret_gn=f(ret_gn).reshape(2, 512), odd_w_in=f(odd_w_in), odd_w_out=f(odd_w_out), odd_conv=f(odd_conv),
                  odd_a_log=f(odd_a_log), odd_dt_bias=f(odd_dt_bias), odd_gn=f(odd_gn))
    shared.update({k: f(v) for k, v in cs.items()})
    in_maps = []
    for c in range(8):
        b = c // 2
        sl = slice(c * NS, (c + 1) * NS)
        m = dict(shared)
        m.update(xp=f(x_prompt[b]), xs=f(x_sample[sl, 0]), c17=f(np.concatenate([c_sample[sl], c_prompt[b:b + 1]], 0)),
                 ptab=f(page_table[sl]), st_ret=f(state_ret[:, sl]), st_ssm=f(state_ssm[:, sl]), st_conv=f(state_conv[:, sl]))
        in_maps.append(m)
    res = run_bass_kernel_spmd(nc, in_maps, core_ids=list(range(8))).results
    ev = [res[2 * b] for b in range(B)]
    y_p = np.stack([r["y_p"] for r in ev], 0)
    y_s = np.concatenate([r["y_s"] for r in res], 0)[:, None, :]
    kv_p = np.stack([r["kv_p"] for r in ev], 1)
    kv_s = np.concatenate([r["kv_s"] for r in res], 1)[:, :, None, :]
    ret_p = np.stack([r["ret_p"] for r in ev], 1)
    ret_s = np.concatenate([r["ret_s"] for r in res], 1)
    ssm_p = np.stack([r["ssm_p"] for r in ev], 1)
    ssm_s = np.concatenate([r["ssm_s"] for r in res], 1)
    conv_p = np.stack([r["conv_p"] for r in ev], 1)
    conv_s = np.concatenate([r["conv_s"] for r in res], 1)
    return tuple(np.asarray(a, np.float32) for a in (y_p, y_s, kv_p, kv_s, ret_p, ret_s, ssm_p, ssm_s, conv_p, conv_s))
```

```python
import math
import numpy as np
from contextlib import ExitStack
import concourse.bass as bass
import concourse.mybir as mybir
from concourse.bass_utils import run_bass_kernel_spmd

F32 = mybir.dt.float32
BF16 = mybir.dt.bfloat16
I32 = mybir.dt.int32
ALU = mybir.AluOpType
AF = mybir.ActivationFunctionType
AX = mybir.AxisListType

ENGS = ("pe", "act", "dve", "pool", "sp")
NDMA = 6

D = 1024
DFF = 4096
EVEN_IN = 3140
ODD_IN = 4112
NS = 16
ALPHA = 8.0 ** 0.25
LN_EPS = 1e-5
NEG = -1e30


class Buf:
    __slots__ = ("w", "r")

    def __init__(self):
        self.w = None
        self.r = {}


class TL:
    def __init__(self, t, nb=1):
        self.t = t
        self.b = [Buf() for _ in range(nb)]

    def __getitem__(self, k):
        return self.t[k]


def _bufs(lst):
    out = []
    for x in lst:
        if isinstance(x, Buf):
            out.append(x)
        elif isinstance(x, TL):
            out.extend(x.b)
        elif isinstance(x, tuple):
            out.append(x[0].b[x[1]])
        else:
            raise TypeError(x)
    return out


class Sched:
    def __init__(self, nc, es):
        self.nc = nc
        self.ops = {e: [] for e in ENGS}
        self.cnt = {e: 0 for e in ENGS}
        self.seen = {e: {} for e in ENGS}
        self.sem = {e: es.enter_context(nc.semaphore("s_" + e)) for e in ENGS}
        self.dsem = {}
        self.dcnt = {}
        self.pend = {e: [] for e in ENGS}
        for q in ("sp", "act", "pool"):
            for j in range(NDMA):
                k = "d_%s%d" % (q, j)
                self.dsem[k] = es.enter_context(nc.semaphore(k))
            self.dcnt[q] = 0

    def barrier(self):
        tgt = {e: self.cnt[e] for e in ENGS if self.cnt[e] > 0}
        for q in ("sp", "act", "pool"):
            for j in range(NDMA):
                if self.dcnt[q] > j:
                    tgt["d_%s%d" % (q, j)] = 16 * ((self.dcnt[q] - j + NDMA - 1) // NDMA)
        for e in ENGS:
            for k, v in tgt.items():
                if self.seen[e].get(k, 0) < v:
                    self.seen[e][k] = v
                    self.pend[e].append((k, v))

    def _semobj(self, key):
        return self.sem[key] if key in self.sem else self.dsem[key]

    def _collect(self, eng, r, w):
        deps = []
        for b in r:
            if b.w is not None:
                deps.append(b.w)
        for b in w:
            if b.w is not None:
                deps.append(b.w)
            for k, v in b.r.items():
                deps.append((k, v))
        waits = []
        sn = self.seen[eng]
        for k, v in deps:
            if eng == "pe" and k == "pe":
                continue
            if sn.get(k, 0) >= v:
                continue
            sn[k] = v
            waits.append((k, v))
        return waits

    def begin_capture(self):
        self.cap = []

    def end_capture(self):
        c_, self.cap = self.cap, None
        units, cur, open_ = [], [], False
        for it in c_:
            cur.append(it)
            g = it[-1]
            if g == "open":
                open_ = True
            elif g == "close":
                open_ = False
            if not open_:
                units.append(cur)
                cur = []
        if cur:
            units.append(cur)
        return units

    def emit_units(self, A, B):
        if getattr(self, "no_interleave", False):
            A, B = A + B, []
        i = j = 0
        while i < len(A) or j < len(B):
            if j >= len(B) or (i < len(A) and i * len(B) <= j * len(A)):
                u = A[i]
                i += 1
            else:
                u = B[j]
                j += 1
            for kind, a0, a1, a2, a3, _ in u:
                if kind == "op":
                    self.op(a0, a1, a2, a3)
                else:
                    self.dma(a0, a1, a2, a3)

    def op(self, eng, fn, r=(), w=(), grp=None):
        if getattr(self, "cap", None) is not None:
            self.cap.append(("op", eng, fn, r, w, grp))
            return
        r = _bufs(r)
        w = _bufs(w)
        waits = self.pend[eng] + self._collect(eng, r, w)
        self.pend[eng] = []
        self.cnt[eng] += 1
        tok = (eng, self.cnt[eng])
        self.ops[eng].append((waits, fn, None))
        for b in r:
            if b.r.get(eng, 0) < tok[1]:
                b.r[eng] = tok[1]
        for b in w:
            b.w = tok
            b.r = {}

    def dma(self, q, fn, r=(), w=()):
        if getattr(self, "cap", None) is not None:
            self.cap.append(("dma", q, fn, r, w, None))
            return
        r = _bufs(r)
        w = _bufs(w)
        i = self.dcnt[q]
        self.dcnt[q] += 1
        key = "d_%s%d" % (q, i % NDMA)
        prev = 16 * (i // NDMA)
        waits = self.pend[q] + self._collect(q, r, w)
        self.pend[q] = []
        sn = self.seen[q]
        if prev > 0 and sn.get(key, 0) < prev:
            sn[key] = prev
            waits.append((key, prev))
        tok = (key, prev + 16)
        self.ops[q].append((waits, fn, key))
        for b in r:
            if b.r.get(key, 0) < tok[1]:
                b.r[key] = tok[1]
        for b in w:
            b.w = tok
            b.r = {}

    def finish(self):
        final = {}
        for q in ("sp", "act", "pool"):
            for j in range(NDMA):
                if self.dcnt[q] > j:
                    final["d_%s%d" % (q, j)] = 16 * ((self.dcnt[q] - j + NDMA - 1) // NDMA)
        nc = self.nc
        with nc.Block() as block:
            def replay(ename, eng):
                for waits, fn, key in self.ops[ename]:
                    for k, v in waits:
                        eng.wait_ge(self._semobj(k), v)
                    inst = fn(eng)
                    if key is None:
                        inst.then_inc(self.sem[ename], 1)
                    else:
                        inst.then_inc(self.dsem[key], 16)
                if ename == "sp":
                    for k, v in final.items():
                        eng.wait_ge(self._semobj(k), v)
                    for e in ("pe", "act", "dve", "pool"):
                        if self.cnt[e] > 0:
                            eng.wait_ge(self.sem[e], self.cnt[e])

            @block.tensor
            def _(e):
                replay("pe", e)

            @block.scalar
            def _(e):
                replay("act", e)

            @block.vector
            def _(e):
                replay("dve", e)

            @block.gpsimd
            def _(e):
                replay("pool", e)

            @block.sync
            def _(e):
                replay("sp", e)


def _rope_rows(pos):
    def tab(d):
        inv = (10000.0 ** (-np.arange(0, d, 2, dtype=np.float32) / d)).astype(np.float32)
        ang = (pos.astype(np.float32)[:, None] * inv[None, :]).astype(np.float32)
        return np.cos(ang.astype(np.float64)), np.sin(ang.astype(np.float64))
    c128, s128 = tab(128)
    c64, s64 = tab(64)
    return np.concatenate([c128, s128, -s128, c64, s64, -s64], 1).astype(np.float32)


def make_consts(NT, past_len):
    c = {}
    p = np.arange(128)
    c["ident"] = np.eye(128, dtype=np.float32)
    c["ones"] = np.ones((128, 128), np.float32)
    c["triu"] = (p[:, None] <= p[None, :]).astype(np.float32)
    c["mvinc"] = np.where(p[None, :] >= p[:, None], 0.0, -1e4).astype(np.float32)
    c["strict"] = (p[None, :] > p[:, None]).astype(np.float32)
    c["cmask"] = np.where(p[None, :] <= p[:, None], 0.0, NEG).astype(np.float32)
    lg = np.log(1.0 - 2.0 ** (-5.0 - np.arange(4, dtype=np.float64)))
    i = p.astype(np.float64)
    rel = i[None, :] - i[:, None]
    dm = np.where(rel >= 0, np.exp(lg[:, None, None] * np.maximum(rel, 0)[None]), 0.0)
    c["dmatT"] = (np.transpose(dm, (1, 0, 2)) * 128 ** -0.5).astype(np.float32).reshape(128, 512)
    qd = np.exp(lg[:, None] * (i[None, :] + 1.0))
    c["qdecB"] = np.broadcast_to(qd.reshape(1, 512), (128, 512)).astype(np.float32).copy()
    kd = np.exp(lg[None, :] * (127.0 - i[:, None])) * 128 ** -0.5
    c["kdecC"] = kd.astype(np.float32)
    cd = np.exp(lg * 128.0)
    c["cdecB"] = np.broadcast_to(np.repeat(cd, 128).reshape(1, 512), (128, 512)).astype(np.float32).copy()
    g1 = np.exp(lg)
    c["gam1B"] = np.broadcast_to(np.repeat(g1, 128).reshape(1, 512), (128, 512)).astype(np.float32).copy()
    c["ropeP"] = _rope_rows(np.arange(NT * 128)).reshape(NT, 128, 288)
    c["iotap"] = np.arange(128, dtype=np.float32).reshape(128, 1)
    c["pw2"] = np.broadcast_to((0.5 ** (np.arange(32, dtype=np.float64) + 1)).astype(np.float32).reshape(1, 32), (128, 32)).copy()
    c["ropeS"] = np.broadcast_to(_rope_rows(np.array([past_len])), (NS, 288)).copy()
    d16 = np.broadcast_to(np.eye(16, dtype=np.float32).reshape(1, 256), (128, 256)).copy()
    c["delta16"] = d16
    return c


class KB:
    def __init__(self, cfg):
        self.cfg = cfg
        self.nc = bass.Bass("TRN2", target_bir_lowering=False)
        self.es = ExitStack()
        self.rr = 0

    def scope(self):
        kb = self

        class _Sc(ExitStack):
            def __exit__(self_, *a):
                kb.S.barrier()
                return super().__exit__(*a)
        return _Sc()

    def dram(self, name, shape, dt, kind):
        return TL(self.nc.dram_tensor(name, list(shape), dt, kind=kind).ap())

    def sb(self, es, name, shape, dt, nb=1):
        self.uid = getattr(self, "uid", 0) + 1
        return TL(es.enter_context(self.nc.sbuf_tensor("%s_%d" % (name, self.uid), list(shape), dt)), nb)

    def tt(self, eng, out, a, b, op, r, w):
        self.S.op(eng, lambda e: e.tensor_tensor(out=out, in0=a, in1=b, op=op), r=r, w=w)

    def ts(self, eng, out, a, s1, op0, r, w, s2=None, op1=None, acc=None):
        if op1 is None:
            self.S.op(eng, lambda e: e.tensor_scalar(out=out, in0=a, scalar1=s1, scalar2=None, op0=op0), r=r, w=w)
        elif acc is None:
            self.S.op(eng, lambda e: e.tensor_scalar(out=out, in0=a, scalar1=s1, scalar2=s2, op0=op0, op1=op1), r=r, w=w)
        else:
            self.S.op(eng, lambda e: e.tensor_scalar(out=out, in0=a, scalar1=s1, scalar2=s2, op0=op0, op1=op1, accum_out=acc), r=r, w=w)

    def stt(self, eng, out, a, sc, b, op0, op1, r, w):
        self.S.op(eng, lambda e: e.scalar_tensor_tensor(out=out, in0=a, scalar=sc, in1=b, op0=op0, op1=op1), r=r, w=w)

    def act(self, out, in_, func, r, w, bias=None, scale=None, acc=None):
        kw = {}
        if bias is not None:
            kw["bias"] = bias
        if scale is not None:
            kw["scale"] = scale
        if acc is not None:
            kw["accum_out"] = acc
        self.S.op("act", lambda e: e.activation(out=out, in_=in_, func=func, **kw), r=r, w=w)

    def cp(self, eng, out, in_, r, w):
        if eng == "act":
            self.S.op("act", lambda e: e.copy(out=out, in_=in_), r=r, w=w)
        else:
            self.S.op(eng, lambda e: e.tensor_copy(out=out, in_=in_), r=r, w=w)

    def red(self, eng, out, in_, r, w, op=None):
        if op is None:
            self.S.op(eng, lambda e: e.reduce_sum(out=out, in_=in_, axis=AX.X), r=r, w=w)
        else:
            self.S.op(eng, lambda e: e.tensor_reduce(out=out, in_=in_, axis=AX.X, op=op), r=r, w=w)

    def mm(self, out, lhsT, rhs, start, stop, r, w, dedicated=False):
        grp = None
        if not dedicated and not (start and stop):
            grp = "open" if start else ("close" if stop else "mid")
        self.S.op("pe", lambda e: e.matmul(out, lhsT=lhsT, rhs=rhs, start=start, stop=stop), r=r, w=w, grp=grp)

    def tr(self, out, in_, ident, r, w):
        self.S.op("pe", lambda e: e.transpose(out=out, in_=in_, identity=ident), r=r, w=w)

    def dma(self, q, out, in_, r, w):
        self.S.dma(q, lambda e: e.dma_start(out=out, in_=in_), r=r, w=w)

    def memset(self, eng, ap, val, w):
        self.S.op(eng, lambda e: e.memset(ap, val), w=w)

    def evq(self):
        self.rr += 1
        return "act" if self.rr % 2 else "dve"

    def ps(self):
        self.prr = (getattr(self, "prr", -1) + 1) % len(self.PSG)
        return self.PSG[self.prr]

    def rsqrt(self, eng, out, in_, r, w, mult, add):
        self.act(out, in_, AF.Sqrt, r=r, w=w, scale=mult, bias=add)
        self.S.op("dve", lambda e: e.reciprocal(out=out, in_=out), r=w, w=w)


def build(cfg):
    NT = cfg["NT"]
    NPG = cfg["NPG"]
    NPOOL = cfg["NPOOL"]
    DEPTH = cfg["DEPTH"]
    TOPK_P = min(256, (NT * 128) // 4)
    TOPK_S = min(256, (NPG * 128 + 1) // 4)
    NTOK = NT * 128
    NROW = NTOK + NS
    K = KB(cfg)
    nc = K.nc
    es = K.es
    with es:
        S = K.S = Sched(nc, es)
        S.no_interleave = bool(cfg.get("noilv"))
        EI, EO, IN = "ExternalInput", "ExternalOutput", "Internal"
        d = {}
        for nm, sh, dt in [
            ("xp", [NTOK, D], F32), ("xs", [NS, D], F32), ("c17", [17, D], F32),
            ("cache_kv", [2 * NPOOL * 128, 320], F32), ("ptab", [NS, NPG], I32),
            ("st_ret", [2, NS, 4, 128, 128], F32), ("st_ssm", [2, NS, 8, 128, 128], F32),
            ("st_conv", [2, NS, 3, 3072], F32),
            ("ada_w", [4, D, 6 * D], F32), ("ada_b", [4, 6 * D], F32), ("ln_g", [4, 2, D], F32),
            ("ln_b", [4, 2, D], F32), ("mlp_up", [4, D, DFF], F32), ("mlp_down", [4, DFF, D], F32),
            ("even_w_in", [2, D, EVEN_IN], F32), ("even_w_out", [2, D, D], F32), ("ret_gn", [2, 512], F32),
            ("odd_w_in", [2, D, ODD_IN], F32), ("odd_w_out", [2, D, D], F32), ("odd_conv", [2, 4, 3072], F32),
            ("odd_a_log", [2, 8], F32), ("odd_dt_bias", [2, 8], F32), ("odd_gn", [2, 128], F32),
            ("ident", [128, 128], F32), ("ones", [128, 128], F32), ("triu", [128, 128], F32),
            ("mvinc", [128, 128], F32), ("strict", [128, 128], F32), ("cmask", [128, 128], F32),
            ("dmatT", [128, 512], F32), ("qdecB", [128, 512], F32), ("kdecC", [128, 4], F32),
            ("cdecB", [128, 512], F32), ("gam1B", [128, 512], F32), ("ropeP", [NT, 128, 288], F32),
            ("ropeS", [NS, 288], F32), ("delta16", [128, 256], F32), ("iotap", [128, 1], F32), ("pw2", [128, 32], F32),
        ]:
            d[nm] = K.dram(nm, sh, dt, EI)
        o = {}
        for nm, sh in [
            ("y_p", [NTOK, D]), ("y_s", [NS, D]), ("kv_p", [2, NTOK, 320]), ("kv_s", [2, NS, 320]),
            ("ret_p", [2, 4, 128, 128]), ("ret_s", [2, NS, 4, 128, 128]), ("ssm_p", [2, 8, 128, 128]),
            ("ssm_s", [2, NS, 8, 128, 128]), ("conv_p", [2, 3, 3072]), ("conv_s", [2, NS, 3, 3072]),
        ]:
            o[nm] = K.dram(nm, sh, F32, EO)
        x_a = K.dram("x_a", [NROW, D], F32, IN)
        x_b = K.dram("x_b", [NROW, D], F32, IN)
        mod_d = K.dram("mod_d", [4, 17, 6 * D], F32, IN)
        proj_d = K.dram("proj_d", [NROW + 3, ODD_IN], F32, IN)
        tb_xa = [Buf() for _ in range(NT + 1)]
        tb_xb = [Buf() for _ in range(NT + 1)]
        tb_pj = [Buf() for _ in range(NT + 2)]

        def rows(t):
            return (t * 128, 128) if t < NT else (NTOK, NS)

        cst = {}
        for nm, sh in [("ident", [128, 128]), ("ones", [128, 128]), ("triu", [128, 128]), ("mvinc", [128, 128]),
                       ("strict", [128, 128]), ("cmask", [128, 128]), ("delta16", [128, 256]), ("iotap", [128, 1]), ("pw2", [128, 32])]:
            cst[nm] = K.sb(es, "c_" + nm, sh, F32)
            K.dma("sp", cst[nm][:], d[nm][:], r=[], w=[cst[nm]])
        identb = K.sb(es, "identb", [128, 128], BF16)
        K.cp("dve", identb[:], cst["ident"][:], r=[cst["ident"]], w=[identb])
        ident = cst["ident"]
        PS = [TL(es.enter_context(nc.psum_tensor("ps%d" % i, [128, 1024], F32)), 2) for i in range(4)]
        K.PSG = PS[:4]
        PSO = [PS[2], PS[3]]
        lnG = K.sb(es, "lnG", [128, D], F32)
        lnB = K.sb(es, "lnB", [128, D], F32)
        modP = K.sb(es, "modP", [128, 3 * D], F32)

        def load_phase_consts(l, which):
            c0 = which * 3 * D
            K.dma("sp", modP[:], mod_d[l, 16:17, c0:c0 + 3 * D].partition_broadcast(128), r=[mod_d], w=[modP])
            K.dma("sp", lnG[:], d["ln_g"][l, which:which + 1, :].partition_broadcast(128), r=[], w=[lnG])
            K.dma("sp", lnB[:], d["ln_b"][l, which:which + 1, :].partition_broadcast(128), r=[], w=[lnB])
            K.ts("pool", modP[:, D:3 * D], modP[:, D:3 * D], 1.0, ALU.add, r=[modP], w=[modP])

        def load_sample_mod(l, which, dst=None):
            dst = modP if dst is None else dst
            c0 = which * 3 * D
            K.dma("sp", dst[0:NS, :], mod_d[l, 0:NS, c0:c0 + 3 * D], r=[mod_d], w=[dst])
            K.ts("pool", dst[0:NS, D:3 * D], dst[0:NS, D:3 * D], 1.0, ALU.add, r=[dst], w=[dst])

        K.modS = None

        def mod_of(t):
            return K.modS if (t == NT and K.modS is not None) else modP

        with K.scope() as p0:
            c17t = K.sb(p0, "c17t", [17, D], F32)
            scT = K.sb(p0, "scT", [128, 8, 17], BF16)
            K.dma("sp", c17t[:], d["c17"][:], r=[], w=[c17t])
            K.act(c17t[:], c17t[:], AF.Silu, r=[c17t], w=[c17t])
            pt = K.ps()
            for k in range(8):
                K.tr(pt[0:128, k * 32:k * 32 + 17], c17t[:, k * 128:(k + 1) * 128], ident[0:17, 0:17], r=[c17t, ident], w=[(pt, 0)])
            K.cp("dve", scT[:], pt[:, 0:256].rearrange("p (k c) -> p k c", c=32)[:, :, 0:17], r=[(pt, 0)], w=[scT])
            aw = [K.sb(p0, "aw%d" % i, [128, 8, 512], BF16) for i in range(2)]
            ab = [K.sb(p0, "ab%d" % i, [17, 512], F32) for i in range(2)]
            mo = [K.sb(p0, "mo%d" % i, [17, 512], F32) for i in range(2)]
            it = 0
            for l in range(DEPTH):
                for cb in range(12):
                    a, b_, m_ = aw[it % 2], ab[it % 2], mo[it % 2]
                    cs = slice(cb * 512, (cb + 1) * 512)
                    K.dma("pool", a[:], d["ada_w"][l, :, cs].rearrange("(k p) n -> p k n", p=128), r=[], w=[a])
                    K.dma("sp", b_[:], d["ada_b"][l:l + 1, cs].partition_broadcast(17), r=[], w=[b_])
                    pt = K.ps()
                    for k in range(8):
                        K.mm(pt[0:17, 0:512], scT[:, k, :], a[:, k, :], k == 0, k == 7, r=[scT, a], w=[(pt, 0)])
                    K.tt("dve", m_[:], pt[0:17, 0:512], b_[:], ALU.add, r=[(pt, 0), b_], w=[m_])
                    K.dma("sp", mod_d[l, :, cs], m_[:], r=[m_], w=[mod_d])
                    it += 1

        def load_weights_bf16(wt, src, ncols):
            nk = src.shape[0] // 128
            for k in range(nk):
                K.dma("pool", wt[:, k, 0:ncols], src[k * 128:(k + 1) * 128, :], r=[], w=[wt])

        def modulate_T(pp, xt, t, n, tmp, hm, hmT):
            md = mod_of(t)
            K.tt("dve", tmp[0:n, :], xt[0:n, :], md[0:n, D:2 * D], ALU.mult, r=[xt, md], w=[tmp])
            K.tt("pool", hm[0:n, :], tmp[0:n, :], md[0:n, 0:D], ALU.add, r=[tmp, md], w=[hm])
            to_T(hm, n, hmT)

        def to_T(hm, n, hmT):
            pt = K.ps()
            pb = pt[:].bitcast(BF16)
            for k in range(8):
                K.tr(pb[:, k * 128:k * 128 + n], hm[0:n, k * 128:(k + 1) * 128], identb[0:n, 0:n], r=[hm, identb], w=[(pt, 0)])
            K.cp("act", hmT[:, :, 0:n], pb[:, 0:1024].rearrange("p (k c) -> p k c", c=128)[:, :, 0:n], r=[(pt, 0)], w=[hmT])

        def layer_norm_out(z, n, xo, scr, st):
            K.memset("pool", st[0:n, :], 0.0, w=[st])
            K.red("dve", st[0:n, 0:1], z[0:n, :], r=[z], w=[st])
            K.ts("dve", st[0:n, 1:2], st[0:n, 0:1], -1.0 / D, ALU.mult, r=[st], w=[st])
            K.ts("dve", scr[0:n, :], z[0:n, :], st[0:n, 1:2], ALU.add, r=[z, st], w=[scr])
            K.act(xo[0:n, :], scr[0:n, :], AF.Square, r=[scr, st], w=[xo, st], acc=st[0:n, 2:3])
            K.rsqrt("dve", st[0:n, 3:4], st[0:n, 2:3], r=[st], w=[st], mult=1.0 / D, add=LN_EPS)
            K.stt("dve", xo[0:n, :], scr[0:n, :], st[0:n, 3:4], lnG[0:n, :], ALU.mult, ALU.mult, r=[scr, st, lnG], w=[xo])
            K.tt("pool", xo[0:n, :], xo[0:n, :], lnB[0:n, :], ALU.add, r=[xo, lnB], w=[xo])

        def residual_ln(py, t, n, xt, z, xo, scr, st):
            md = mod_of(t)
            K.tt("dve", z[0:n, :], py[0:n, :], md[0:n, 2 * D:3 * D], ALU.mult, r=[py, md], w=[z])
            K.stt("dve", z[0:n, :], xt[0:n, :], ALPHA, z[0:n, :], ALU.mult, ALU.add, r=[xt, z], w=[z])
            layer_norm_out(z, n, xo, scr, st)

        K.fn = dict(rows=rows, load_phase_consts=load_phase_consts, load_sample_mod=load_sample_mod, mod_of=mod_of, load_weights_bf16=load_weights_bf16,
                    modulate_T=modulate_T, to_T=to_T, residual_ln=residual_ln)
        K.ctx = dict(d=d, o=o, x_a=x_a, x_b=x_b, proj_d=proj_d, tb_xa=tb_xa, tb_xb=tb_xb, tb_pj=tb_pj, cst=cst,
                     ident=ident, identb=identb, PS=PS, PSO=PSO, NT=NT, NPG=NPG, NTOK=NTOK, NROW=NROW,
                     TOPK_P=TOPK_P, TOPK_S=TOPK_S, modP=modP, mod_d=mod_d, cfg=cfg)

        with K.scope() as pz:
            zt = K.sb(pz, "zt", [3, ODD_IN], F32)
            K.memset("pool", zt[:], 0.0, w=[zt])
            K.dma("sp", proj_d[0:3, :], zt[:], r=[zt], w=[tb_pj[NT + 1]])

        x_in = None
        for l in range(DEPTH):
            even = (l % 2 == 0)
            NIN = EVEN_IN if even else ODD_IN
            w_in_d = d["even_w_in"][l // 2] if even else d["odd_w_in"][l // 2]
            load_phase_consts(l, 0)
            with K.scope() as pa:
                wt = K.sb(pa, "w_in", [128, 8, ODD_IN], BF16)
                load_weights_bf16(wt, w_in_d, NIN)
                xts = [K.sb(pa, "a_x%d" % i, [128, D], F32) for i in range(2)]
                tmps = [K.sb(pa, "a_t%d" % i, [128, D], F32) for i in range(2)]
                hms = [K.sb(pa, "a_h%d" % i, [128, D], BF16) for i in range(2)]
                hmTs = [K.sb(pa, "a_hT%d" % i, [128, 8, 128], BF16) for i in range(2)]
                pjs = [K.sb(pa, "a_pj%d" % i, [128, ODD_IN], F32) for i in range(1)] * 2
                K.modS = K.sb(pa, "a_modS", [NS, 3 * D], F32)
                load_sample_mod(l, 0, K.modS)

                def a_s1(t):
                    r0, n = rows(t)
                    xt, tmp, hm, hmT = xts[t % 2], tmps[t % 2], hms[t % 2], hmTs[t % 2]
                    if l == 0:
                        src = d["xp"][r0:r0 + n, :] if t < NT else d["xs"][:, :]
                        K.dma("sp", xt[0:n, :], src, r=[], w=[xt])
                    else:
                        K.dma("sp", xt[0:n, :], x_a[r0:r0 + n, :], r=[tb_xa[t]], w=[xt])
                    modulate_T(pa, xt, t, n, tmp, hm, hmT)

                def a_s2(t):
                    r0, n = rows(t)
                    hmT, pj = hmTs[t % 2], pjs[0]
                    nb = (NIN + 511) // 512
                    for cb in range(nb):
                        c0 = cb * 512
                        wd = min(512, NIN - c0)
                        pt = K.ps()
                        hb = cb % 2
                        for k in range(8):
                            K.mm(pt[0:n, hb * 512:hb * 512 + wd], hmT[:, k, 0:n], wt[:, k, c0:c0 + wd], k == 0, k == 7, r=[hmT, wt], w=[(pt, hb)])
                        K.cp(K.evq(), pj[0:n, c0:c0 + wd], pt[0:n, hb * 512:hb * 512 + wd], r=[(pt, hb)], w=[pj])
                    K.dma("sp", proj_d[3 + r0:3 + r0 + n, 0:NIN], pj[0:n, 0:NIN], r=[pj], w=[tb_pj[t]])

                a_s1(0)
                for t in range(NT + 1):
                    if t + 1 <= NT:
                        a_s1(t + 1)
                    a_s2(t)
                K.modS = None
            if cfg.get("mix", "real") == "stub":
                phase_b_stub(K, l)
            elif even:
                phase_b_even(K, l)
            else:
                phase_b_odd(K, l)
            load_phase_consts(l, 1)
            with K.scope() as pf:
                wu = K.sb(pf, "w_up", [128, 8, DFF], BF16)
                wd_ = K.sb(pf, "w_dn", [128, 32, D], BF16)
                load_weights_bf16(wu, d["mlp_up"][l], DFF)
                load_weights_bf16(wd_, d["mlp_down"][l], D)
                xts = [K.sb(pf, "f_x%d" % i, [128, D], F32) for i in range(2)]
                hms = [K.sb(pf, "f_h%d" % i, [128, D], BF16) for i in range(1)] * 2
                hmTs = [K.sb(pf, "f_hT%d" % i, [128, 8, 128], BF16) for i in range(2)]
                uTs = [K.sb(pf, "f_uT%d" % i, [128, 32, 128], BF16) for i in range(1)] * 2
                rl = [K.sb(pf, "f_rl%d" % i, [128, 1024], F32) for i in range(1)] * 2
                xos = [K.sb(pf, "f_xo%d" % i, [128, D], F32) for i in range(1)] * 2
                sts = [K.sb(pf, "f_st%d" % i, [128, 4], F32) for i in range(2)]
                last = (l == DEPTH - 1)
                K.modS = K.sb(pf, "f_modS", [NS, 3 * D], F32)
                load_sample_mod(l, 1, K.modS)
                zb = K.sb(pf, "f_z", [128, D], F32)

                xts.append(K.sb(pf, "f_x2", [128, D], F32))

                def f_s1(t):
                    r0, n = rows(t)
                    xt, tmp, hm, hmT = xts[t % 3], rl[0], hms[0], hmTs[t % 2]
                    K.dma("sp", xt[0:n, :], x_b[r0:r0 + n, :], r=[tb_xb[t]], w=[xt])
                    modulate_T(pf, xt, t, n, tmp, hm, hmT)

                def f_s2a(t):
                    r0, n = rows(t)
                    hmT, uT = hmTs[t % 2], uTs[0]
                    for g in range(4):
                        pt = K.ps()
                        for j in range(8):
                            fc = g * 8 + j
                            for k in range(8):
                                K.mm(pt[:, j * 128:j * 128 + n], wu[:, k, fc * 128:(fc + 1) * 128], hmT[:, k, 0:n], k == 0, k == 7, r=[wu, hmT], w=[(pt, j // 4)])
                        r_ = rl[0]
                        pv = pt[:].rearrange("p (j c) -> p j c", c=128)[:, :, 0:n]
                        rv = r_[:].rearrange("p (j c) -> p j c", c=128)[:, :, 0:n]
                        K.act(rv, pv, AF.Relu, r=[pt], w=[r_])
                        K.tt("pool" if g % 2 else "dve", uT[:, g * 8:(g + 1) * 8, 0:n], rv, rv, ALU.mult, r=[r_], w=[uT])

                def f_s2b(t):
                    r0, n = rows(t)
                    xt, uT, xo, st = xts[t % 3], uTs[0], xos[0], sts[t % 2]
                    py = K.ps()
                    for hb in range(2):
                        for k in range(32):
                            K.mm(py[0:n, hb * 512:(hb + 1) * 512], uT[:, k, 0:n], wd_[:, k, hb * 512:(hb + 1) * 512], k == 0, k == 31, r=[uT, wd_], w=[(py, hb)])
                    residual_ln(py, t, n, xt, zb, xo, zb, st)
                    if last:
                        dst = o["y_p"][r0:r0 + n, :] if t < NT else o["y_s"][:, :]
                        K.dma("sp", dst, xo[0:n, :], r=[xo], w=[])
                    else:
                        K.dma("sp", x_a[r0:r0 + n, :], xo[0:n, :], r=[xo], w=[tb_xa[t]])

                f_s1(0)
                if NT >= 1:
                    f_s1(1)
                for t in range(NT + 1):
                    f_s2a(t)
                    if t + 2 <= NT:
                        f_s1(t + 2)
                    f_s2b(t)
                K.modS = None
        S.finish()
    return nc


def phase_b_stub(K, l):
    c = K.ctx
    f = K.fn
    f["load_phase_consts"](l, 0)
    d, NT = c["d"], c["NT"]
    even = (l % 2 == 0)
    with K.scope() as pb:
        wo = K.sb(pb, "w_out", [128, 8, D], BF16)
        f["load_weights_bf16"](wo, (d["even_w_out"] if even else d["odd_w_out"])[l // 2], D)
        xt = K.sb(pb, "b_x", [128, D], F32)
        pj = K.sb(pb, "b_pj", [128, D], F32)
        cat = K.sb(pb, "b_cat", [128, D], BF16)
        catT = K.sb(pb, "b_catT", [128, 8, 128], BF16)
        z = K.sb(pb, "b_z", [128, D], F32)
        xo = K.sb(pb, "b_xo", [128, D], F32)
        st = K.sb(pb, "b_st", [128, 4], F32)
        for t in range(NT + 1):
            r0, n = f["rows"](t)
            if l == 0:
                src = d["xp"][r0:r0 + n, :] if t < NT else d["xs"][:, :]
                K.dma("sp", xt[0:n, :], src, r=[], w=[xt])
            else:
                K.dma("sp", xt[0:n, :], c["x_a"][r0:r0 + n, :], r=[c["tb_xa"][t]], w=[xt])
            if t == NT:
                f["load_sample_mod"](l, 0)
            K.dma("sp", pj[0:n, :], c["proj_d"][3 + r0:3 + r0 + n, 0:D], r=[c["tb_pj"][t]], w=[pj])
            K.cp("dve", cat[0:n, :], pj[0:n, :], r=[pj], w=[cat])
            out_proj_ln(K, t, n, cat, catT, wo, xt, z, xo, st)


def out_proj_ln(K, t, n, cat, catT, wo, xt, z, xo, st):
    c = K.ctx
    f = K.fn
    f["to_T"](cat, n, catT)
    py = K.ps()
    for hb in range(2):
        for k in range(8):
            K.mm(py[0:n, hb * 512:(hb + 1) * 512], catT[:, k, 0:n], wo[:, k, hb * 512:(hb + 1) * 512], k == 0, k == 7, r=[catT, wo], w=[(py, hb)])
    f["residual_ln"](py, t, n, xt, z, xo, z, st)
    r0, _ = f["rows"](t)
    K.dma("sp", c["x_b"][r0:r0 + n, :], xo[0:n, :], r=[xo], w=[c["tb_xb"][t]])


def rope_ops(K, n, src, H, half, rp, c0, t1, t2, dsts, r_src):
    cosb = rp[0:n, c0:c0 + half].unsqueeze(1).unsqueeze(1).to_broadcast([n, H, 2, half])
    sinb = rp[0:n, c0 + half:c0 + 2 * half].unsqueeze(1).to_broadcast([n, H, half])
    nsinb = rp[0:n, c0 + 2 * half:c0 + 3 * half].unsqueeze(1).to_broadcast([n, H, half])
    t1v = t1[0:n, 0:H * 2 * half].rearrange("p (h a c) -> p h a c", h=H, a=2)
    t2v = t2[0:n, 0:H * 2 * half].rearrange("p (h a c) -> p h a c", h=H, a=2)
    K.tt("dve", t1v, src, cosb, ALU.mult, r=r_src + [rp], w=[t1])
    K.tt("pool", t2v[:, :, 0, :], src[:, :, 1, :], nsinb, ALU.mult, r=r_src + [rp], w=[t2])
    K.tt("pool", t2v[:, :, 1, :], src[:, :, 0, :], sinb, ALU.mult, r=r_src + [rp], w=[t2])
    for h0, h1, dap, dt_ in dsts:
        K.tt("dve", dap, t1v[:, h0:h1], t2v[:, h0:h1], ALU.add, r=[t1, t2], w=[dt_])


def phase_b_even(K, l):
    c = K.ctx
    f = K.fn
    d, o, NT, NPG = c["d"], c["o"], c["NT"], c["NPG"]
    PSO, cst, ident, identb = c["PSO"], c["cst"], c["ident"], c["identb"]
    i = l // 2
    TOPK = c["TOPK_P"]
    NIT = 15
    SC_ATT = 128 ** -0.5
    f["load_phase_consts"](l, 0)
    K.PSG = c["PS"][:2]
    with K.scope() as pb:
        sb = lambda nm, sh, dt=F32: K.sb(pb, "e_" + nm, sh, dt)
        wo = sb("wo", [128, 8, D], BF16)
        f["load_weights_bf16"](wo, d["even_w_out"][i], D)
        kc = {}
        for nm in ("dmatT", "qdecB", "cdecB", "gam1B"):
            kc[nm] = sb(nm, [128, 512])
            K.dma("sp", kc[nm][:], d[nm][:], r=[], w=[kc[nm]])
        kdecC = sb("kdecC", [128, 4])
        K.dma("sp", kdecC[:], d["kdecC"][:], r=[], w=[kdecC])
        gnB = sb("gnB", [128, 512])
        K.dma("sp", gnB[:], d["ret_gn"][i:i + 1, :].partition_broadcast(128), r=[], w=[gnB])
        Sret = sb("Sret", [128, 512])
        K.memset("pool", Sret[:], 0.0, w=[Sret])
        pj = sb("pj", [128, EVEN_IN])
        xt = sb("xt", [128, D])
        rp = sb("rp", [128, 288])
        qk_r = sb("qk_r", [128, 1024])
        aq_rb = sb("aq_rb", [128, 512], BF16)
        kvrow = sb("kvrow", [128, 320])
        iq_r = sb("iq_r", [128, 256])
        t1 = sb("t1", [128, 1024])
        t2 = sb("t2", [128, 1024])
        qT, qTd, kT, AT, kd, cen, sq, sg = [sb(nm, [128, 512]) for nm in ("qT", "qTd", "kT", "AT", "kd", "cen", "sq", "sg")]
        akb = sb("akb", [128, 128], BF16)
        ikb = sb("ikb", [128, 64], BF16)
        iqsb = sb("iqsb", [128, 256], BF16)
        aqT = sb("aqT", [128, 512], BF16)
        iqsT = sb("iqsT", [64, 512], BF16)
        rls = [sb("rl%d" % j, [128, 512]) for j in range(2)]
        Es = [sb("E%d" % j, [128, 512], BF16) for j in range(2)]
        PTs = [sb("PT%d" % j, [128, 512], BF16) for j in range(2)]
        mTss = [sb("mTs%d" % j, [128, 128], BF16) for j in range(2)]
        cat = sb("cat", [128, D], BF16)
        catT = sb("catT", [128, 8, 128], BF16)
        z = sb("z", [128, D])
        xo = sb("xo", [128, D])
        st = sb("st", [128, 4])
        s4 = sb("s4", [128, 16])
        aw = sb("aw", [128, 8])
        bs = sb("bs", [128, 8])
        cnt = sb("cnt", [128, NIT])
        Hh = sb("Hh", [128, 32])
        rs = sb("rs", [128, 4])

        def x_load(t, n, r0):
            if l == 0:
                src = d["xp"][r0:r0 + n, :] if t < NT else d["xs"][:, :]
                K.dma("sp", xt[0:n, :], src, r=[], w=[xt])
            else:
                K.dma("sp", xt[0:n, :], c["x_a"][r0:r0 + n, :], r=[c["tb_xa"][t]], w=[xt])

        def proj_rope(t, n, r0, rp_src, aq_dst=None):
            aq_dst = aq_rb if aq_dst is None else aq_dst
            K.dma("sp", pj[0:n, :], c["proj_d"][3 + r0:3 + r0 + n, 0:EVEN_IN], r=[c["tb_pj"][t]], w=[pj])
            K.dma("sp", rp[0:n, :], rp_src, r=[], w=[rp])
            v4 = lambda ap, H, half: ap.rearrange("p (h a c) -> p h a c", h=H, a=2)
            rope_ops(K, n, v4(pj[0:n, 0:1024], 8, 64), 8, 64, rp, 0, t1, t2,
                     [(0, 8, v4(qk_r[0:n, :], 8, 64), qk_r)], [pj])
            rope_ops(K, n, v4(pj[0:n, 2048:2688], 5, 64), 5, 64, rp, 0, t1, t2,
                     [(0, 4, v4(aq_dst[0:n, :], 4, 64), aq_dst), (4, 5, v4(kvrow[0:n, 0:128], 1, 64), kvrow)], [pj])
            rope_ops(K, n, v4(pj[0:n, 2816:3136], 5, 32), 5, 32, rp, 192, t1, t2,
                     [(0, 4, v4(iq_r[0:n, :], 4, 32), iq_r), (4, 5, v4(kvrow[0:n, 256:320], 1, 32), kvrow)], [pj])
            K.cp("act", kvrow[0:n, 128:256], pj[0:n, 2688:2816], r=[pj], w=[kvrow])
            K.act(aw[0:n, 0:4], pj[0:n, 3136:3140], AF.Abs, r=[pj], w=[aw], scale=1.0 / 16)
            K.ts("dve", aw[0:n, 4:8], pj[0:n, 3136:3140], 0.0, ALU.is_ge, r=[pj], w=[aw], s2=2.0, op1=ALU.mult)
            K.ts("dve", aw[0:n, 4:8], aw[0:n, 4:8], -1.0, ALU.add, r=[aw], w=[aw])

        def head_norm_gate(n, src, rsrc):
            pov = src.rearrange("p (h e) -> p h e", h=4)
            cv = cen[0:n, :].rearrange("p (h e) -> p h e", h=4)
            K.red("dve", s4[0:n, 0:4], pov, r=rsrc, w=[s4])
            K.ts("dve", s4[0:n, 4:8], s4[0:n, 0:4], -1.0 / 128, ALU.mult, r=[s4], w=[s4])
            K.tt("dve", cv, pov, s4[0:n, 4:8].unsqueeze(2).to_broadcast([n, 4, 128]), ALU.add, r=rsrc + [s4], w=[cen])
            K.tt("pool", sq[0:n, :], cen[0:n, :], cen[0:n, :], ALU.mult, r=[cen], w=[sq])
            K.red("dve", s4[0:n, 8:12], sq[0:n, :].rearrange("p (h e) -> p h e", h=4), r=[sq], w=[s4])
            K.rsqrt("dve", s4[0:n, 12:16], s4[0:n, 8:12], r=[s4], w=[s4], mult=1.0 / 128, add=LN_EPS)
            K.act(sg[0:n, :], pj[0:n, 1536:2048], AF.Silu, r=[pj], w=[sg])
            K.tt("pool", sg[0:n, :], sg[0:n, :], gnB[0:n, :], ALU.mult, r=[sg, gnB], w=[sg])
            K.tt("dve", cv, cv, s4[0:n, 12:16].unsqueeze(2).to_broadcast([n, 4, 128]), ALU.mult, r=[cen, s4], w=[cen])
            K.tt("dve", cat[0:n, 0:512], cen[0:n, :], sg[0:n, :], ALU.mult, r=[cen, sg], w=[cat])

        with K.scope() as pp:
            akT_all = K.sb(pp, "e_akT", [128, NT * 128], BF16, nb=NT)
            ikT_all = K.sb(pp, "e_ikT", [64, NT * 128], BF16)
            Vall = K.sb(pp, "e_Vall", [128, NT, 132], BF16, nb=NT)
            K.memset("pool", Vall[:], 1.0, w=[Vall])
            score = K.sb(pp, "e_score", [128, NT * 128], F32)
            junk = K.sb(pp, "e_junk", [128, NT * 128], F32)
            maskb = K.sb(pp, "e_maskb", [128, NT * 128], BF16)
            def front(t):
                r0, n = t * 128, 128
                x_load(t, n, r0)
                proj_rope(t, n, r0, d["ropeP"][t])
                K.dma("sp", o["kv_p"][i, r0:r0 + n, :], kvrow[:], r=[kvrow], w=[])
                pt = K.ps()
                for h in range(8):
                    K.tr(pt[:, h * 128:(h + 1) * 128], qk_r[:, h * 128:(h + 1) * 128], ident[:], r=[qk_r, ident], w=[(pt, h // 4)])
                K.cp("act", qT[:], pt[:, 0:512], r=[(pt, 0)], w=[qT])
                K.tt("dve", qTd[:], pt[:, 0:512], kc["qdecB"][:], ALU.mult, r=[(pt, 0), kc["qdecB"]], w=[qTd])
                K.cp("act", kT[:], pt[:, 512:1024], r=[(pt, 1)], w=[kT])
                pa = K.ps()
                for h in range(4):
                    hs = slice(h * 128, (h + 1) * 128)
                    K.mm(pa[:, hs], kT[:, hs], qT[:, hs], True, True, r=[kT, qT], w=[(pa, 0)])
                K.tt("dve", AT[:], pa[:, 0:512], kc["dmatT"][:], ALU.mult, r=[(pa, 0), kc["dmatT"]], w=[AT])
                K.tt("pool", kd[:].rearrange("p (h e) -> p h e", h=4), qk_r[:, 512:1024].rearrange("p (h e) -> p h e", h=4),
                     kdecC[:].unsqueeze(2).to_broadcast([128, 4, 128]), ALU.mult, r=[qk_r, kdecC], w=[kd])
                po = K.ps()
                for h in range(4):
                    hs = slice(h * 128, (h + 1) * 128)
                    K.mm(po[:, hs], AT[:, hs], pj[:, 1024 + h * 128:1024 + (h + 1) * 128], True, False, r=[AT, pj], w=[(po, 0)])
                    K.mm(po[:, hs], qTd[:, hs], Sret[:, hs], False, True, r=[qTd, Sret], w=[(po, 0)])
                pss = po
                for h in range(4):
                    hs = slice(h * 128, (h + 1) * 128)
                    K.mm(pss[:, 512 + h * 128:512 + (h + 1) * 128], kd[:, hs], pj[:, 1024 + h * 128:1024 + (h + 1) * 128], True, True, r=[kd, pj], w=[(pss, 1)])
                K.tt("dve", Sret[:], Sret[:], kc["cdecB"][:], ALU.mult, r=[Sret, kc["cdecB"]], w=[Sret])
                K.tt("dve", Sret[:], Sret[:], pss[:, 512:1024], ALU.add, r=[Sret, (pss, 1)], w=[Sret])
                head_norm_gate(n, po[0:n, 0:512], [(po, 0)])
                if t == NT - 1:
                    for h in range(4):
                        K.dma("sp", o["ret_p"][i, h], Sret[:, h * 128:(h + 1) * 128], r=[Sret], w=[])
                K.cp("pool", akb[:], kvrow[:, 0:128], r=[kvrow], w=[akb])
                K.cp("pool", ikb[:], kvrow[:, 256:320], r=[kvrow], w=[ikb])
                K.tt("dve", iqsb[:].rearrange("p (h e) -> p h e", h=4), iq_r[:].rearrange("p (h e) -> p h e", h=4),
                     aw[:, 0:4].unsqueeze(2).to_broadcast([128, 4, 64]), ALU.mult, r=[iq_r, aw], w=[iqsb])
                pt = K.ps()
                pbb = pt[:].bitcast(BF16)
                K.tr(pbb[:, 0:128], akb[:], identb[:], r=[akb, identb], w=[(pt, 0)])
                K.tr(pbb[0:64, 128:256], ikb[:], identb[:], r=[ikb, identb], w=[(pt, 0)])
                for h in range(4):
                    K.tr(pbb[:, 256 + h * 128:256 + (h + 1) * 128], aq_rb[:, h * 128:(h + 1) * 128], identb[:], r=[aq_rb, identb], w=[(pt, 0)])
                    K.tr(pbb[0:64, 1024 + h * 128:1024 + (h + 1) * 128], iqsb[:, h * 64:(h + 1) * 64], identb[:], r=[iqsb, identb], w=[(pt, 1)])
                K.cp("act", akT_all[:, r0:r0 + 128], pbb[:, 0:128], r=[(pt, 0)], w=[(akT_all, t)])
                K.cp("act", ikT_all[:, r0:r0 + 128], pbb[0:64, 128:256], r=[(pt, 0)], w=[ikT_all])
                K.cp("act", aqT[:], pbb[:, 256:768], r=[(pt, 0)], w=[aqT])
                K.cp("act", iqsT[:], pbb[0:64, 1024:1536], r=[(pt, 1)], w=[iqsT])
                K.cp("pool", Vall[:, t, 0:128], pj[:, 2688:2816], r=[pj], w=[(Vall, t)])
                n_k = (t + 1) * 128
                for kb in range((n_k + 511) // 512):
                    k0 = kb * 512
                    wb = min(512, n_k - k0)
                    pts = [K.ps(), K.ps()]
                    for h in range(4):
                        pp, hb = pts[h // 2], h % 2
                        K.mm(pp[:, hb * 512:hb * 512 + wb], iqsT[:, h * 128:(h + 1) * 128], ikT_all[:, k0:k0 + wb], True, True, r=[iqsT, ikT_all], w=[(pp, hb)])
                        rl = rls[h % 2]
                        K.act(rl[:, 0:wb], pp[:, hb * 512:hb * 512 + wb], AF.Relu, r=[(pp, hb)], w=[rl])
                        if h == 0:
                            K.ts("dve", score[:, k0:k0 + wb], rl[:, 0:wb], aw[:, 4:5], ALU.mult, r=[rl, aw], w=[score])
                        else:
                            K.stt("dve", score[:, k0:k0 + wb], rl[:, 0:wb], aw[:, 4 + h:5 + h], score[:, k0:k0 + wb], ALU.mult, ALU.add, r=[rl, aw, score], w=[score])
                if n_k > TOPK:
                    K.red("dve", bs[:, 0:1], score[:, 0:n_k], r=[score], w=[bs], op=ALU.max)
                    K.red("dve", bs[:, 1:2], score[:, 0:n_k], r=[score], w=[bs], op=ALU.min)
                    K.tt("dve", bs[:, 2:3], bs[:, 0:1], bs[:, 1:2], ALU.subtract, r=[bs], w=[bs])
                else:
                    K.memset("dve", bs[:, 1:2], -1e29, w=[bs])
                K.tt("dve", score[:, r0:r0 + 128], score[:, r0:r0 + 128], cst["cmask"][:], ALU.add, r=[score, cst["cmask"]], w=[score])

            def bis(t):
                n_k = (t + 1) * 128
                if n_k > TOPK:
                    K.ts("dve", Hh[:, 0:NIT], cst["pw2"][:, 0:NIT], bs[:, 2:3], ALU.mult, r=[bs, cst["pw2"]], w=[Hh])
                    for it in range(NIT):
                        K.tt("dve", bs[:, 4:5], bs[:, 1:2], Hh[:, it:it + 1], ALU.add, r=[bs, Hh], w=[bs])
                        K.ts("dve", junk[:, 0:n_k], score[:, 0:n_k], bs[:, 4:5], ALU.is_ge, r=[score, bs], w=[junk, cnt], s2=0.0, op1=ALU.add, acc=cnt[:, it:it + 1])
                        K.ts("dve", bs[:, 5:6], cnt[:, it:it + 1], float(TOPK), ALU.is_ge, r=[cnt], w=[bs])
                        K.stt("dve", bs[:, 1:2], bs[:, 5:6], Hh[:, it:it + 1], bs[:, 1:2], ALU.mult, ALU.add, r=[bs, Hh], w=[bs])

            def tail(t):
                n_k = (t + 1) * 128
                K.ts("dve", maskb[:, 0:n_k], score[:, 0:n_k], bs[:, 1:2], ALU.is_ge, r=[score, bs], w=[maskb])

            def back(t):
                r0, n = t * 128, 128
                def att_front(kt):
                    ks = slice(kt * 128, (kt + 1) * 128)
                    pm = K.ps()
                    pmb = pm[:].bitcast(BF16)
                    K.tr(pmb[:, 0:128], maskb[:, ks], identb[:], r=[maskb, identb], w=[(pm, 0)])
                    K.mm(pm[:, 512:1024], akT_all[:, ks], aqT[:], True, True, r=[(akT_all, kt), aqT], w=[(pm, 1)])
                    E, PT, mTs = Es[kt % 2], PTs[kt % 2], mTss[kt % 2]
                    K.act(E[:], pm[:, 512:1024], AF.Exp, r=[(pm, 1)], w=[E], scale=SC_ATT)
                    K.cp("act", mTs[:], pmb[:, 0:128], r=[(pm, 0)], w=[mTs])
                    K.tt("pool", PT[:].rearrange("p (h q) -> p h q", h=4), E[:].rearrange("p (h q) -> p h q", h=4),
                         mTs[:].unsqueeze(1).to_broadcast([128, 4, 128]), ALU.mult, r=[E, mTs], w=[PT])

                def att_back(kt, t=t):
                    PT = PTs[kt % 2]
                    for h in range(4):
                        c0 = (h % 2) * 512
                        K.mm(PSO[h // 2][:, c0:c0 + 130], PT[:, h * 128:(h + 1) * 128], Vall[:, kt, 0:130], kt == 0, kt == t, r=[PT, (Vall, kt)], w=[(PSO[h // 2], h % 2)], dedicated=True)

                att_front(0)
                for kt in range(t + 1):
                    if kt + 1 <= t:
                        att_front(kt + 1)
                    att_back(kt)
                for h in range(4):
                    c0 = (h % 2) * 512
                    pso = PSO[h // 2]
                    K.S.op("dve", lambda e, pso=pso, c0=c0, h=h: e.reciprocal(out=rs[:, h:h + 1], in_=pso[:, c0 + 128:c0 + 129]), r=[(pso, h % 2)], w=[rs])
                    K.ts("dve", cat[:, 512 + h * 128:512 + (h + 1) * 128], pso[:, c0:c0 + 128], rs[:, h:h + 1], ALU.mult, r=[(pso, h % 2), rs], w=[cat])
                out_proj_ln(K, t, n, cat, catT, wo, xt, z, xo, st)

            xts2 = [xt, K.sb(pp, "e_xt2", [128, D], F32)]
            cats2 = [cat, K.sb(pp, "e_cat2", [128, D], BF16)]
            aqTs2 = [aqT, K.sb(pp, "e_aqT2", [128, 512], BF16)]
            PSg = c["PS"]
            xt, cat, aqT = xts2[0], cats2[0], aqTs2[0]
            K.PSG = [PSg[0]]
            front(0)
            bis(0)
            tail(0)
            for t in range(NT):
                A = []
                if t + 1 < NT:
                    xt, cat, aqT = xts2[(t + 1) % 2], cats2[(t + 1) % 2], aqTs2[(t + 1) % 2]
                    K.PSG = [PSg[0]]
                    front(t + 1)
                    K.S.begin_capture()
                    bis(t + 1)
                    A = K.S.end_capture()
                xt, cat, aqT = xts2[t % 2], cats2[t % 2], aqTs2[t % 2]
                K.PSG = [PSg[1]]
                K.S.begin_capture()
                back(t)
                Bc = K.S.end_capture()
                K.S.emit_units(A, Bc)
                if t + 1 < NT:
                    tail(t + 1)
            K.PSG = PSg[:2]
            xt, cat, aqT = xts2[0], cats2[0], aqTs2[0]
        if not c["cfg"].get("skip_sample"):
            sample_even(K, l, pb, locals())
    K.PSG = c["PS"][:4]


def bisect_thr(K, n, score, n_k, topk, bs, cnt, junk, nit):
    K.red("dve", bs[0:n, 0:1], score[0:n, 0:n_k], r=[score], w=[bs], op=ALU.max)
    K.red("dve", bs[0:n, 1:2], score[0:n, 0:n_k], r=[score], w=[bs], op=ALU.min)
    K.tt("dve", bs[0:n, 2:3], bs[0:n, 0:1], bs[0:n, 1:2], ALU.subtract, r=[bs], w=[bs])
    K.memset("pool", cnt[0:n, :], 0.0, w=[cnt])
    for it in range(nit):
        K.ts("dve", bs[0:n, 3:4], bs[0:n, 2:3], 0.5 ** (it + 1), ALU.mult, r=[bs], w=[bs])
        K.tt("dve", bs[0:n, 4:5], bs[0:n, 1:2], bs[0:n, 3:4], ALU.add, r=[bs], w=[bs])
        K.ts("dve", junk[0:n, 0:n_k], score[0:n, 0:n_k], bs[0:n, 4:5], ALU.is_ge, r=[score, bs, cnt], w=[junk, cnt], s2=0.0, op1=ALU.add, acc=cnt[0:n, it:it + 1])
        K.ts("dve", bs[0:n, 5:6], cnt[0:n, it:it + 1], float(topk), ALU.is_ge, r=[cnt], w=[bs])
        K.stt("dve", bs[0:n, 1:2], bs[0:n, 5:6], bs[0:n, 3:4], bs[0:n, 1:2], ALU.mult, ALU.add, r=[bs], w=[bs])


def sample_even(K, l, pb, L):
    c = K.ctx
    f = K.fn
    d, o, NT, NPG = c["d"], c["o"], c["NT"], c["NPG"]
    cst, ident, PSO = c["cst"], c["ident"], c["PSO"]
    i = l // 2
    TOPK = c["TOPK_S"]
    NIT = 15
    SC_ATT = 128 ** -0.5
    n, t, r0 = NS, NT, c["NTOK"]
    NK = NPG * 128
    pj, xt, qk_r, kvrow, iq_r, t1, t2 = [L[k] for k in ("pj", "xt", "qk_r", "kvrow", "iq_r", "t1", "t2")]
    AT, sq, aw, s4, cat, catT, z, xo, st, wo, kc, bs, cnt = [L[k] for k in ("AT", "sq", "aw", "s4", "cat", "catT", "z", "xo", "st", "wo", "kc", "bs", "cnt")]
    g4 = lambda ap: ap.rearrange("p (h e) -> p h e", h=4)
    with K.scope() as px:
        sb = lambda nm, sh, dt=F32: K.sb(px, "es_" + nm, sh, dt)
        aq_f = sb("aq_f", [NS, 512])
        f["load_sample_mod"](l, 0)
        L["x_load"](t, n, r0)
        L["proj_rope"](t, n, r0, d["ropeS"][:, :], aq_f)
        K.dma("sp", o["kv_s"][i, :, :], kvrow[0:n, :], r=[kvrow], w=[])
        v_ap = pj[0:n, 1024:1536]
        K.tt("dve", sq[0:n, :], qk_r[0:n, 0:512], qk_r[0:n, 512:1024], ALU.mult, r=[qk_r], w=[sq])
        K.red("dve", s4[0:n, 0:4], g4(sq[0:n, :]), r=[sq], w=[s4])
        K.ts("dve", s4[0:n, 0:4], s4[0:n, 0:4], 128 ** -0.5, ALU.mult, r=[s4], w=[s4])
        pt = K.ps()
        for h in range(4):
            K.tr(pt[:, h * 16:(h + 1) * 16], qk_r[0:n, h * 128:(h + 1) * 128], ident[0:n, 0:n], r=[qk_r, ident], w=[(pt, 0)])
        qTs = sb("qTs", [128, 64])
        K.cp("act", qTs[:], pt[:, 0:64], r=[(pt, 0)], w=[qTs])
        K.tt("dve", t1[:, 0:1024].rearrange("p (h s m) -> p h s m", h=4, s=16),
             qTs[:].rearrange("p (h s) -> p h s", h=4).unsqueeze(3).to_broadcast([128, 4, 16, 16]),
             cst["delta16"][:].rearrange("p (s m) -> p s m", s=16).unsqueeze(1).to_broadcast([128, 4, 16, 16]),
             ALU.mult, r=[qTs, cst["delta16"]], w=[t1])
        S0 = [sb("S0%d" % j, [128, 512]) for j in range(2)]
        Sn = [sb("Sn%d" % j, [128, 512]) for j in range(2)]
        km = sb("km", [NS, 512])
        for s in range(NS):
            S0s, Sns = S0[s % 2], Sn[s % 2]
            for h in range(4):
                K.dma("sp", S0s[:, h * 128:(h + 1) * 128], d["st_ret"][i, s, h], r=[], w=[S0s])
            for h in range(4):
                hs = slice(h * 128, (h + 1) * 128)
                K.mm(PSO[h // 2][0:n, (h % 2) * 512:(h % 2) * 512 + 128], t1[:, (h * 16 + s) * 16:(h * 16 + s + 1) * 16], S0s[:, hs], s == 0, s == NS - 1, r=[t1, S0s], w=[(PSO[h // 2], h % 2)])
            K.ts("dve", km[:], qk_r[0:n, 512:1024], ident[0:n, s:s + 1], ALU.mult, r=[qk_r, ident], w=[km], s2=128 ** -0.5, op1=ALU.mult)
            pss = K.ps()
            for h in range(4):
                hs = slice(h * 128, (h + 1) * 128)
                K.mm(pss[:, hs], km[:, hs], pj[0:n, 1024 + h * 128:1024 + (h + 1) * 128], True, True, r=[km, pj], w=[(pss, 0)])
            K.tt("pool", Sns[:], S0s[:], kc["gam1B"][:], ALU.mult, r=[S0s, kc["gam1B"]], w=[Sns])
            K.tt("dve", Sns[:], Sns[:], pss[:, 0:512], ALU.add, r=[Sns, (pss, 0)], w=[Sns])
            for h in range(4):
                K.dma("sp", o["ret_s"][i, s, h], Sns[:, h * 128:(h + 1) * 128], r=[Sns], w=[])
        for h in range(4):
            hs = slice(h * 128, (h + 1) * 128)
            K.tt("dve", AT[0:n, hs], PSO[h // 2][0:n, (h % 2) * 512:(h % 2) * 512 + 128], kc["gam1B"][0:n, hs], ALU.mult, r=[(PSO[h // 2], h % 2), kc["gam1B"]], w=[AT])
        K.tt("pool", g4(sq[0:n, :]), g4(v_ap), s4[0:n, 0:4].unsqueeze(2).to_broadcast([n, 4, 128]), ALU.mult, r=[pj, s4], w=[sq])
        K.tt("dve", AT[0:n, :], AT[0:n, :], sq[0:n, :], ALU.add, r=[AT, sq], w=[AT])
        L["head_norm_gate"](n, AT[0:n, :], [AT])
        iqs = sb("iqs", [NS, 256])
        K.tt("dve", g4(iqs[:]), g4(iq_r[0:n, :]), aw[0:n, 0:4].unsqueeze(2).to_broadcast([n, 4, 64]), ALU.mult, r=[iq_r, aw], w=[iqs])
        pt = K.ps()
        for h in range(4):
            K.tr(pt[:, h * 16:(h + 1) * 16], aq_f[:, h * 128:(h + 1) * 128], ident[0:n, 0:n], r=[aq_f, ident], w=[(pt, 0)])
            K.tr(pt[0:64, 512 + h * 16:512 + (h + 1) * 16], iqs[:, h * 64:(h + 1) * 64], ident[0:n, 0:n], r=[iqs, ident], w=[(pt, 1)])
        aqTa = sb("aqTa", [128, 64])
        iqTa = sb("iqTa", [64, 64])
        K.cp("act", aqTa[:], pt[:, 0:64], r=[(pt, 0)], w=[aqTa])
        K.cp("act", iqTa[:], pt[0:64, 512:576], r=[(pt, 1)], w=[iqTa])
        sgnB = sb("sgnB", [128, NS * 4])
        sg_d = K.dram("sg_d%d" % l, [1, NS * 4], F32, "Internal")
        K.dma("sp", sg_d[0:1, :].rearrange("o (s h) -> (o s) h", h=4), aw[0:n, 4:8], r=[aw], w=[sg_d])
        K.dma("sp", sgnB[:], sg_d[0:1, :].partition_broadcast(128), r=[sg_d], w=[sgnB])
        idxi = sb("idxi", [128, NS * NPG], I32)
        idxf = sb("idxf", [128, NS * NPG])
        K.dma("sp", idxi[:], d["ptab"][:, :].rearrange("(o s) j -> o (s j)", o=1).partition_broadcast(128), r=[], w=[idxi])
        K.cp("dve", idxf[:], idxi[:], r=[idxi], w=[idxf])
        K.ts("dve", idxf[:], idxf[:], 128.0, ALU.mult, r=[idxf, cst["iotap"]], w=[idxf], s2=cst["iotap"][:, 0:1], op1=ALU.add)
        if i > 0:
            K.ts("dve", idxf[:], idxf[:], float(i * c["cfg"]["NPOOL"] * 128), ALU.add, r=[idxf], w=[idxf])
        K.cp("dve", idxi[:], idxf[:], r=[idxf], w=[idxi])
        cache = d["cache_kv"]
        sc_d = K.dram("sc_d%d" % l, [NS, NK], F32, "Internal")
        os_d = K.dram("os_d%d" % l, [4, NS, 132], F32, "Internal")
        kig = [sb("kig%d" % j, [128, NPG, 320]) for j in range(1)] * 2
        kTs = sb("kTs", [128, NK + 4])
        kiT = kTs
        scT = sb("scT", [128, NS * NPG])
        rlS = sb("rlS", [128, NPG * 4])
        scr_t = sb("scr_t", [NPG, 128])
        for s in range(NS):
            kg = kig[s % 2]
            for j in range(NPG):
                col = s * NPG + j
                K.S.dma("pool", lambda e, kg=kg, j=j, col=col: e.indirect_dma_start(
                    out=kg[:, j, :], out_offset=None, in_=cache[:, :],
                    in_offset=bass.IndirectOffsetOnAxis(ap=idxi[:, col:col + 1], axis=0)), r=[idxi], w=[kg])
            for g in range((NPG + 3) // 4):
                pt = K.ps()
                nj = min(4, NPG - g * 4)
                for jj in range(nj):
                    K.tr(pt[0:64, jj * 128:(jj + 1) * 128], kg[:, g * 4 + jj, 256:320], ident[:], r=[kg, ident], w=[(pt, 0)])
                K.cp(K.evq(), kiT[0:64, g * 512:g * 512 + nj * 128], pt[0:64, 0:nj * 128], r=[(pt, 0)], w=[kiT])
            pq = K.ps()
            for j in range(NPG):
                K.mm(pq[:, j * 4:(j + 1) * 4], kiT[0:64, j * 128:(j + 1) * 128], iqTa[:].rearrange("p (h s) -> p h s", h=4)[:, :, s], True, True, r=[kiT, iqTa], w=[(pq, 0)])
            K.act(rlS[:], pq[:, 0:NPG * 4], AF.Relu, r=[(pq, 0)], w=[rlS])
            rv = rlS[:].rearrange("p (j h) -> p j h", h=4)
            K.tt("dve", rv, rv, sgnB[:, s * 4:(s + 1) * 4].unsqueeze(1).to_broadcast([128, NPG, 4]), ALU.mult, r=[rlS, sgnB], w=[rlS])
            K.red("dve", scT[:, s * NPG:(s + 1) * NPG], rv, r=[rlS], w=[scT])
            pt2 = K.ps()
            K.tr(pt2[0:NPG, 0:128], scT[:, s * NPG:(s + 1) * NPG], ident[:], r=[scT, ident], w=[(pt2, 0)])
            K.cp("act", scr_t[:], pt2[0:NPG, 0:128], r=[(pt2, 0)], w=[scr_t])
            K.dma("sp", sc_d[s, :].rearrange("(j p) -> j p", p=128), scr_t[:], r=[scr_t], w=[sc_d])
        K.tt("dve", t2[0:n, 0:256].rearrange("p (h e) -> p h e", h=4), g4(iqs[:]), kvrow[0:n, 256:320].unsqueeze(1).to_broadcast([n, 4, 64]), ALU.mult, r=[iqs, kvrow], w=[t2])
        K.red("dve", s4[0:n, 4:8], t2[0:n, 0:256].rearrange("p (h e) -> p h e", h=4), r=[t2], w=[s4])
        K.ts("dve", s4[0:n, 4:8], s4[0:n, 4:8], 0.0, ALU.max, r=[s4], w=[s4])
        K.tt("dve", s4[0:n, 4:8], s4[0:n, 4:8], aw[0:n, 4:8], ALU.mult, r=[s4, aw], w=[s4])
        srow = sb("srow", [NS, NK + 1])
        jrow = kTs
        K.dma("sp", srow[:, 0:NK], sc_d[:, :], r=[sc_d], w=[srow])
        K.red("dve", srow[:, NK:NK + 1], s4[0:n, 4:8], r=[s4, srow], w=[srow])
        if NK + 1 > TOPK:
            bisect_thr(K, n, srow, NK + 1, TOPK, bs, cnt, jrow, NIT)
        else:
            K.memset("dve", bs[0:n, 1:2], -1e29, w=[bs])
        dth = sb("dth", [NS, NS])
        K.ts("dve", dth[:], ident[0:NS, 0:NS], bs[0:n, 1:2], ALU.mult, r=[ident, bs], w=[dth])
        pq = K.ps()
        K.mm(pq[:, 0:NS], cst["ones"][0:NS, :], dth[:], True, True, r=[cst["ones"], dth], w=[(pq, 0)])
        thrB = sb("thrB", [128, NS])
        K.cp("act", thrB[:], pq[:, 0:NS], r=[(pq, 0)], w=[thrB])
        kvg = kig
        mT = sb("mT", [128, NPG])
        Es = sb("Es", [128, NPG * 4])
        PT2 = sb("PT2", [128, NPG * 4])
        osm = sb("osm", [4, NS, 132])
        K.memset("pool", osm[:], 0.0, w=[osm])
        for s in range(NS):
            kv = kvg[s % 2]
            for j in range(NPG):
                col = s * NPG + j
                K.S.dma("pool", lambda e, kv=kv, j=j, col=col: e.indirect_dma_start(
                    out=kv[:, j, :], out_offset=None, in_=cache[:, :],
                    in_offset=bass.IndirectOffsetOnAxis(ap=idxi[:, col:col + 1], axis=0)), r=[idxi], w=[kv])
            for g in range((NPG + 3) // 4):
                pt = K.ps()
                nj = min(4, NPG - g * 4)
                for jj in range(nj):
                    K.tr(pt[:, jj * 128:(jj + 1) * 128], kv[:, g * 4 + jj, 0:128], ident[:], r=[kv, ident], w=[(pt, 0)])
                K.cp(K.evq(), kTs[:, g * 512:g * 512 + nj * 128], pt[:, 0:nj * 128], r=[(pt, 0)], w=[kTs])
            pq = K.ps()
            for j in range(NPG):
                K.mm(pq[:, j * 4:(j + 1) * 4], kTs[:, j * 128:(j + 1) * 128], aqTa[:].rearrange("p (h s) -> p h s", h=4)[:, :, s], True, True, r=[kTs, aqTa], w=[(pq, 0)])
            K.act(Es[:], pq[:, 0:NPG * 4], AF.Exp, r=[(pq, 0)], w=[Es], scale=SC_ATT)
            K.ts("dve", mT[:], scT[:, s * NPG:(s + 1) * NPG], thrB[:, s:s + 1], ALU.is_ge, r=[scT, thrB], w=[mT])
            K.tt("dve", PT2[:].rearrange("p (j h) -> p j h", h=4), Es[:].rearrange("p (j h) -> p j h", h=4),
                 mT[:].unsqueeze(2).to_broadcast([128, NPG, 4]), ALU.mult, r=[Es, mT], w=[PT2])
            po2 = K.ps()
            for j in range(NPG):
                K.mm(po2[0:4, 0:128], PT2[:, j * 4:(j + 1) * 4], kv[:, j, 128:256], j == 0, j == NPG - 1, r=[PT2, kv], w=[(po2, 0)])
            for j in range(NPG):
                K.mm(po2[0:4, 512:513], PT2[:, j * 4:(j + 1) * 4], cst["ones"][:, 0:1], j == 0, j == NPG - 1, r=[PT2, cst["ones"]], w=[(po2, 1)])
            K.cp("act", osm[:, s, 0:128], po2[0:4, 0:128], r=[(po2, 0)], w=[osm])
            K.cp("act", osm[:, s, 128:129], po2[0:4, 512:513], r=[(po2, 1)], w=[osm])
        ot = sb("ot", [NS, 4, 132])
        K.dma("sp", os_d[:, :, :], osm[:], r=[osm], w=[os_d])
        K.dma("sp", ot[:], os_d[:, :, :].rearrange("h s e -> s h e"), r=[os_d], w=[ot])
        K.tt("dve", g4(t2[0:n, 0:512]), g4(aq_f[:]), kvrow[0:n, 0:128].unsqueeze(1).to_broadcast([n, 4, 128]), ALU.mult, r=[aq_f, kvrow], w=[t2])
        K.red("dve", s4[0:n, 8:12], g4(t2[0:n, 0:512]), r=[t2], w=[s4])
        K.act(s4[0:n, 8:12], s4[0:n, 8:12], AF.Exp, r=[s4], w=[s4], scale=SC_ATT)
        K.tt("dve", bs[0:n, 6:7], srow[:, NK:NK + 1], bs[0:n, 1:2], ALU.is_ge, r=[srow, bs], w=[bs])
        K.ts("dve", s4[0:n, 8:12], s4[0:n, 8:12], bs[0:n, 6:7], ALU.mult, r=[s4, bs], w=[s4])
        K.tt("dve", g4(t1[0:n, 0:512]), kvrow[0:n, 128:256].unsqueeze(1).to_broadcast([n, 4, 128]),
             s4[0:n, 8:12].unsqueeze(2).to_broadcast([n, 4, 128]), ALU.mult, r=[kvrow, s4], w=[t1])
        K.tt("dve", g4(t1[0:n, 0:512]), g4(t1[0:n, 0:512]), ot[:, :, 0:128], ALU.add, r=[t1, ot], w=[t1])
        K.tt("dve", s4[0:n, 12:16], ot[:, :, 128], s4[0:n, 8:12], ALU.add, r=[ot, s4], w=[s4])
        K.S.op("dve", lambda e: e.reciprocal(out=s4[0:n, 12:16], in_=s4[0:n, 12:16]), r=_bufs([s4]), w=_bufs([s4]))
        K.tt("dve", g4(cat[0:n, 512:1024]), g4(t1[0:n, 0:512]), s4[0:n, 12:16].unsqueeze(2).to_broadcast([n, 4, 128]), ALU.mult, r=[t1, s4], w=[cat])
        out_proj_ln(K, t, n, cat, catT, wo, xt, z, xo, st)


def phase_b_odd(K, l):
    c = K.ctx
    f = K.fn
    d, o, NT = c["d"], c["o"], c["NT"]
    cst, ident, PS = c["cst"], c["ident"], c["PS"]
    jl = l // 2
    NTOK = c["NTOK"]
    K.PSG = PS[:4]
    f["load_phase_consts"](l, 0)
    g8 = lambda ap: ap.rearrange("p (h e) -> p h e", h=8)
    bc8 = lambda ap, n, w: ap.unsqueeze(2).to_broadcast([n, 8, w])
    with K.scope() as pb:
        sb = lambda nm, sh, dt=F32: K.sb(pb, "o_" + nm, sh, dt)
        wo = sb("wo", [128, 8, D], BF16)
        f["load_weights_bf16"](wo, d["odd_w_out"][jl], D)
        cw = sb("cw", [128, 4, 3072])
        for tap in range(4):
            K.dma("sp", cw[:, tap, :], d["odd_conv"][jl, tap:tap + 1, :].partition_broadcast(128), r=[], w=[cw])
        gnB = sb("gnB", [128, 128])
        K.dma("sp", gnB[:], d["odd_gn"][jl:jl + 1, :].partition_broadcast(128), r=[], w=[gnB])
        nea = sb("nea", [128, 8])
        dtb = sb("dtb", [128, 8])
        K.dma("sp", nea[:], d["odd_a_log"][jl:jl + 1, :].partition_broadcast(128), r=[], w=[nea])
        K.dma("sp", dtb[:], d["odd_dt_bias"][jl:jl + 1, :].partition_broadcast(128), r=[], w=[dtb])
        K.act(nea[:], nea[:], AF.Exp, r=[nea], w=[nea])
        K.ts("dve", nea[:], nea[:], -1.0, ALU.mult, r=[nea], w=[nea])
        pj = sb("pj", [128, ODD_IN])
        xt = sb("xt", [128, D])
        sh = sb("sh", [128, 3072])
        cat = sb("cat", [128, D], BF16)
        catT = sb("catT", [128, 8, 128], BF16)
        z = sb("z", [128, D])
        xo = sb("xo", [128, D])
        st = sb("st", [128, 4])
        sm = sb("sm", [128, 96])
        ob = sb("ob", [128, 1024])
        sq = sh

        def x_load(t, n, r0):
            K.dma("sp", xt[0:n, :], c["x_a"][r0:r0 + n, :], r=[c["tb_xa"][t]], w=[xt])

        def conv_gates(t, n, r0, taps):
            K.dma("sp", pj[0:n, :], c["proj_d"][3 + r0:3 + r0 + n, :], r=[c["tb_pj"][t]], w=[pj])
            pq = pj[0:n, 0:3072]
            K.tt("pool", pq, pq, cw[0:n, 3, :], ALU.mult, r=[pj, cw], w=[pj])
            for tap in range(3):
                src, deps = taps[tap]
                K.dma("sp", sh[0:n, :], src, r=deps, w=[sh])
                K.tt("pool", sh[0:n, :], sh[0:n, :], cw[0:n, tap, :], ALU.mult, r=[sh, cw], w=[sh])
                K.tt("dve", pq, pq, sh[0:n, :], ALU.add, r=[pj, sh], w=[pj])
            K.act(pj[0:n, 0:4096], pj[0:n, 0:4096], AF.Silu, r=[pj], w=[pj])
            K.tt("pool", sq[0:n, 0:2048], pj[0:n, 0:2048], pj[0:n, 0:2048], ALU.mult, r=[pj], w=[sq])
            K.red("dve", sm[0:n, 16:32], sq[0:n, 0:2048].rearrange("p (h e) -> p h e", h=16), r=[sq], w=[sm])
            K.rsqrt("dve", sm[0:n, 16:32], sm[0:n, 16:32], r=[sm], w=[sm], mult=1.0, add=1e-6)
            K.ts("dve", sm[0:n, 16:24], sm[0:n, 16:24], 128 ** -0.5, ALU.mult, r=[sm], w=[sm])
            qk = pj[0:n, 0:2048].rearrange("p (h e) -> p h e", h=16)
            K.tt("dve", qk, qk, sm[0:n, 16:32].unsqueeze(2).to_broadcast([n, 16, 128]), ALU.mult, r=[pj, sm], w=[pj])
            K.tt("dve", sm[0:n, 32:40], pj[0:n, 4096:4104], dtb[0:n, :], ALU.add, r=[pj, dtb], w=[sm])
            K.act(sm[0:n, 40:48], sm[0:n, 32:40], AF.Abs, r=[sm], w=[sm])
            K.act(sm[0:n, 40:48], sm[0:n, 40:48], AF.Exp, r=[sm], w=[sm], scale=-1.0)
            K.act(sm[0:n, 40:48], sm[0:n, 40:48], AF.Ln, r=[sm], w=[sm], bias=1.0)
            K.ts("dve", sm[0:n, 32:40], sm[0:n, 32:40], 0.0, ALU.max, r=[sm], w=[sm])
            K.tt("dve", sm[0:n, 32:40], sm[0:n, 32:40], sm[0:n, 40:48], ALU.add, r=[sm], w=[sm])
            K.tt("dve", sm[0:n, 0:8], sm[0:n, 32:40], nea[0:n, :], ALU.mult, r=[sm, nea], w=[sm])
            K.act(sm[0:n, 8:16], pj[0:n, 4104:4112], AF.Sigmoid, r=[pj], w=[sm])

        def rms_gate(n, src, rsrc):
            K.tt("pool", sq[0:n, 0:1024], src, src, ALU.mult, r=rsrc, w=[sq])
            K.red("dve", sm[0:n, 48:56], g8(sq[0:n, 0:1024]), r=[sq], w=[sm])
            K.rsqrt("dve", sm[0:n, 48:56], sm[0:n, 48:56], r=[sm], w=[sm], mult=1.0 / 128, add=LN_EPS)
            K.tt("dve", g8(sq[0:n, 0:1024]), g8(src), bc8(sm[0:n, 48:56], n, 128), ALU.mult, r=rsrc + [sm], w=[sq])
            K.tt("pool", g8(sq[0:n, 0:1024]), g8(sq[0:n, 0:1024]), gnB[0:n, :].unsqueeze(1).to_broadcast([n, 8, 128]), ALU.mult, r=[sq, gnB], w=[sq])
            K.tt("dve", cat[0:n, :], sq[0:n, 0:1024], pj[0:n, 3072:4096], ALU.mult, r=[sq, pj], w=[cat])

        with K.scope() as pp:
            G = [K.sb(pp, "o_G%d" % j, [128, 1024], F32) for j in range(15)]
            Sst = K.sb(pp, "o_S", [128, 1024], F32)
            K.memset("pool", Sst[:], 0.0, w=[Sst])
            Dg, E, ES, _, kb, kbg, vb, kg, qT, kT, kbT, Nn, Mm, aqkT, R = G
            Pa, Pb, Qa, Qb = G[0], G[1], G[2], G[3]
            u_, wT, vnew = G[4], G[9], G[10]
            identB = ident[:].unsqueeze(1).to_broadcast([128, 8, 128])
            for t in range(NT):
                r0, n = t * 128, 128
                x_load(t, n, r0)
                prevb = c["tb_pj"][t - 1] if t > 0 else c["tb_pj"][NT + 1]
                taps = [(c["proj_d"][r0 + tap:r0 + tap + 128, 0:3072], [c["tb_pj"][t], prevb]) for tap in range(3)]
                conv_gates(t, n, r0, taps)
                q_ap, k_ap, v_ap = pj[:, 0:1024], pj[:, 1024:2048], pj[:, 2048:3072]
                g_, be = sm[:, 0:8], sm[:, 8:16]
                pg = K.ps()
                K.mm(pg[:, 0:8], cst["triu"][:], g_, True, True, r=[cst["triu"], sm], w=[(pg, 0)])
                gc = sm[:, 56:64]
                K.cp("act", gc, pg[:, 0:8], r=[(pg, 0)], w=[sm])
                K.tt("pool", g8(Dg[:]), identB, bc8(gc, 128, 128), ALU.mult, r=[ident, sm], w=[Dg])
                pr = K.ps()
                K.mm(pr[:, 0:512], cst["ones"][:], Dg[:, 0:512], True, True, r=[cst["ones"], Dg], w=[(pr, 0)])
                K.mm(pr[:, 512:1024], cst["ones"][:], Dg[:, 512:1024], True, True, r=[cst["ones"], Dg], w=[(pr, 1)])
                K.tt("dve", g8(E[:]), g8(pr[:]), bc8(gc, 128, 128), ALU.subtract, r=[pr, sm], w=[E])
                K.tt("dve", g8(E[:]), g8(E[:]), cst["mvinc"][:].unsqueeze(1).to_broadcast([128, 8, 128]), ALU.min, r=[E, cst["mvinc"]], w=[E])
                K.act(E[:], E[:], AF.Exp, r=[E], w=[E])
                K.tt("pool", g8(ES[:]), g8(E[:]), cst["strict"][:].unsqueeze(1).to_broadcast([128, 8, 128]), ALU.mult, r=[E, cst["strict"]], w=[ES])
                gl = sm[:, 64:72]
                K.cp("act", gl, g8(pr[:])[:, :, 127], r=[pr], w=[sm])
                K.tt("dve", sm[:, 72:80], gl, gc, ALU.subtract, r=[sm], w=[sm])
                K.act(sm[:, 72:80], sm[:, 72:80], AF.Exp, r=[sm], w=[sm])
                K.act(sm[:, 80:88], gl, AF.Exp, r=[sm], w=[sm])
                K.act(sm[:, 88:96], gc, AF.Exp, r=[sm], w=[sm])
                K.tt("pool", g8(kb[:]), g8(k_ap), bc8(be, 128, 128), ALU.mult, r=[pj, sm], w=[kb])
                K.tt("pool", g8(kbg[:]), g8(kb[:]), bc8(sm[:, 88:96], 128, 128), ALU.mult, r=[kb, sm], w=[kbg])
                K.tt("pool", g8(vb[:]), g8(v_ap), bc8(be, 128, 128), ALU.mult, r=[pj, sm], w=[vb])
                K.tt("pool", g8(kg[:]), g8(k_ap), bc8(sm[:, 72:80], 128, 128), ALU.mult, r=[pj, sm], w=[kg])
                for src_ap, rs, dst in ((q_ap, [pj], qT), (k_ap, [pj], kT), (kb[:], [kb], kbT)):
                    pt = K.ps()
                    for h in range(8):
                        K.tr(pt[:, h * 128:(h + 1) * 128], src_ap[:, h * 128:(h + 1) * 128], ident[:], r=rs + [ident], w=[(pt, h // 4)])
                    K.cp("act", dst[:], pt[:], r=[pt], w=[dst])
                pn = K.ps()
                pa = K.ps()
                for h in range(8):
                    hs = slice(h * 128, (h + 1) * 128)
                    K.mm(pn[:, hs], kT[:, hs], kbT[:, hs], True, True, r=[kT, kbT], w=[(pn, h // 4)])
                for h in range(8):
                    hs = slice(h * 128, (h + 1) * 128)
                    K.mm(pa[:, hs], kT[:, hs], qT[:, hs], True, True, r=[kT, qT], w=[(pa, h // 4)])
                K.tt("dve", Nn[:], pn[:], ES[:], ALU.mult, r=[pn, ES], w=[Nn])
                K.tt("dve", aqkT[:], pa[:], E[:], ALU.mult, r=[pa, E], w=[aqkT])
                pm = K.ps()
                for h in range(8):
                    hs = slice(h * 128, (h + 1) * 128)
                    K.tr(pm[:, hs], Nn[:, hs], ident[:], r=[Nn, ident], w=[(pm, h // 4)])
                K.cp("act", Mm[:], pm[:], r=[pm], w=[Mm])
                K.tt("pool", g8(R[:]), identB, g8(Nn[:]), ALU.subtract, r=[ident, Nn], w=[R])
                P, Q = Mm, Nn
                Pn, Qn = [Pa, Pb], [Qa, Qb]
                for lev in range(1, 7):
                    P2, Q2 = Pn[lev % 2], Qn[lev % 2]
                    pP = K.ps()
                    for h in range(8):
                        hs = slice(h * 128, (h + 1) * 128)
                        K.mm(pP[:, hs], Q[:, hs], P[:, hs], True, True, r=[Q, P], w=[(pP, h // 4)])
                    if lev < 6:
                        pQ = K.ps()
                        for h in range(8):
                            hs = slice(h * 128, (h + 1) * 128)
                            K.mm(pQ[:, hs], P[:, hs], Q[:, hs], True, True, r=[P, Q], w=[(pQ, h // 4)])
                    K.cp("act", P2[:], pP[:], r=[pP], w=[P2])
                    if lev < 6:
                        K.cp("dve", Q2[:], pQ[:], r=[pQ], w=[Q2])
                    pR = K.ps()
                    for h in range(8):
                        hs = slice(h * 128, (h + 1) * 128)
                        K.mm(pR[:, hs], P2[:, hs], R[:, hs], True, True, r=[P2, R], w=[(pR, h // 4)])
                    K.tt("dve", R[:], R[:], pR[:], ALU.add, r=[R, pR], w=[R])
                    P, Q = P2, Q2
                pu = K.ps()
                pw = K.ps()
                for h in range(8):
                    hs = slice(h * 128, (h + 1) * 128)
                    K.mm(pu[:, hs], R[:, hs], vb[:, hs], True, True, r=[R, vb], w=[(pu, h // 4)])
                for h in range(8):
                    hs = slice(h * 128, (h + 1) * 128)
                    K.mm(pw[:, hs], kbg[:, hs], R[:, hs], True, True, r=[kbg, R], w=[(pw, h // 4)])
                K.cp("act", u_[:], pu[:], r=[pu], w=[u_])
                K.cp("act", wT[:], pw[:], r=[pw], w=[wT])
                pv = K.ps()
                for h in range(8):
                    hs = slice(h * 128, (h + 1) * 128)
                    K.mm(pv[:, hs], wT[:, hs], Sst[:, hs], True, True, r=[wT, Sst], w=[(pv, h // 4)])
                K.tt("dve", vnew[:], u_[:], pv[:], ALU.subtract, r=[u_, pv], w=[vnew])
                po1 = K.ps()
                for h in range(8):
                    hs = slice(h * 128, (h + 1) * 128)
                    K.mm(po1[:, hs], qT[:, hs], Sst[:, hs], True, True, r=[qT, Sst], w=[(po1, h // 4)])
                K.tt("dve", g8(ob[:]), g8(po1[:]), bc8(sm[:, 88:96], 128, 128), ALU.mult, r=[po1, sm], w=[ob])
                po2 = K.ps()
                for h in range(8):
                    hs = slice(h * 128, (h + 1) * 128)
                    K.mm(po2[:, hs], aqkT[:, hs], vnew[:, hs], True, True, r=[aqkT, vnew], w=[(po2, h // 4)])
                K.tt("dve", ob[:], ob[:], po2[:], ALU.add, r=[ob, po2], w=[ob])
                pS = K.ps()
                for h in range(8):
                    hs = slice(h * 128, (h + 1) * 128)
                    K.mm(pS[:, hs], kg[:, hs], vnew[:, hs], True, True, r=[kg, vnew], w=[(pS, h // 4)])
                K.tt("pool", g8(Sst[:]), g8(Sst[:]), bc8(sm[:, 80:88], 128, 128), ALU.mult, r=[Sst, sm], w=[Sst])
                K.tt("dve", Sst[:], Sst[:], pS[:], ALU.add, r=[Sst, pS], w=[Sst])
                rms_gate(n, ob[:], [ob])
                out_proj_ln(K, t, n, cat, catT, wo, xt, z, xo, st)
            for h in range(8):
                K.dma("sp", o["ssm_p"][jl, h], Sst[:, h * 128:(h + 1) * 128], r=[Sst], w=[])
            K.dma("sp", o["conv_p"][jl], c["proj_d"][NTOK:NTOK + 3, 0:3072], r=[c["tb_pj"][NT - 1]], w=[])
        if not c["cfg"].get("skip_sample"):
            sample_odd(K, l, locals())
    K.PSG = PS[:4]


def sample_odd(K, l, L):
    c = K.ctx
    f = K.fn
    d, o, NT = c["d"], c["o"], c["NT"]
    cst, ident = c["cst"], c["ident"]
    jl = l // 2
    n, t, r0 = NS, NT, c["NTOK"]
    pj, xt, sm, ob, cat, catT, z, xo, st, wo = [L[k] for k in ("pj", "xt", "sm", "ob", "cat", "catT", "z", "xo", "st", "wo")]
    g8 = lambda ap: ap.rearrange("p (h e) -> p h e", h=8)
    with K.scope() as px:
        sb = lambda nm, sh, dt=F32: K.sb(px, "os_" + nm, sh, dt)
        f["load_sample_mod"](l, 0)
        L["x_load"](t, n, r0)
        taps = [(d["st_conv"][jl, :, tap, :], []) for tap in range(3)]
        L["conv_gates"](t, n, r0, taps)
        K.dma("sp", o["conv_s"][jl, :, 0:2, :], d["st_conv"][jl, :, 1:3, :], r=[], w=[])
        K.dma("sp", o["conv_s"][jl, :, 2, :], c["proj_d"][3 + r0:3 + r0 + n, 0:3072], r=[c["tb_pj"][t]], w=[])
        K.act(sm[0:n, 32:40], sm[0:n, 0:8], AF.Exp, r=[sm], w=[sm])
        sq = L["sq"]
        K.tt("pool", sq[0:n, 0:1024], pj[0:n, 0:1024], pj[0:n, 1024:2048], ALU.mult, r=[pj], w=[sq])
        K.red("dve", sm[0:n, 40:48], g8(sq[0:n, 0:1024]), r=[sq], w=[sm])
        row_d = K.dram("gr_d%d" % l, [NS, 3072 + 24], F32, "Internal")
        orow_d = K.dram("go_d%d" % l, [NS, 1024], F32, "Internal")
        K.dma("sp", row_d[:, 0:3072], pj[0:n, 0:3072], r=[pj], w=[row_d])
        K.dma("sp", row_d[:, 3072:3080], sm[0:n, 32:40], r=[sm], w=[row_d])
        K.dma("sp", row_d[:, 3080:3088], sm[0:n, 8:16], r=[sm], w=[row_d])
        K.dma("sp", row_d[:, 3088:3096], sm[0:n, 40:48], r=[sm], w=[row_d])
        egB = sb("egB", [128, NS * 8])
        egd = K.dram("ge_d%d" % l, [1, NS * 8], F32, "Internal")
        K.dma("sp", egd[0:1, :].rearrange("o (s h) -> (o s) h", h=8), sm[0:n, 32:40], r=[sm], w=[egd])
        K.dma("sp", egB[:], egd[0:1, :].partition_broadcast(128), r=[egd], w=[egB])
        kqT = sb("kqT", [128, 8 * NS * 2])
        kq4 = kqT[:].rearrange("p (h s a) -> p h s a", h=8, a=2)
        for which, c0 in ((0, 1024), (1, 0)):
            pt = K.ps()
            for h in range(8):
                K.tr(pt[:, h * 16:(h + 1) * 16], pj[0:n, c0 + h * 128:c0 + (h + 1) * 128], ident[0:n, 0:n], r=[pj, ident], w=[(pt, 0)])
            K.cp("act", kq4[:, :, :, which], pt[:, 0:128].rearrange("p (h s) -> p h s", h=8), r=[(pt, 0)], w=[kqT])
        rows = [sb("row%d" % j, [1, 3096]) for j in range(2)]
        S0 = [sb("S0%d" % j, [128, 1024]) for j in range(2)]
        vn = sb("vn", [1, 1024])
        orow = sb("orow", [1, 1024])
        tmp = sb("tmp", [1, 1024])
        r8 = lambda ap: ap.rearrange("p (h e) -> p h e", h=8)
        b8 = lambda ap: ap.unsqueeze(2).to_broadcast([1, 8, 128])
        for s in range(NS):
            row, S0s = rows[s % 2], S0[s % 2]
            K.dma("sp", row[:], row_d[s:s + 1, :], r=[row_d], w=[row])
            for h in range(8):
                K.dma("sp", S0s[:, h * 128:(h + 1) * 128], d["st_ssm"][jl, s, h], r=[], w=[S0s])
            pk = K.ps()
            pq = K.ps()
            for h in range(8):
                hs = slice(h * 128, (h + 1) * 128)
                K.mm(pk[0:1, hs], kq4[:, h, s, 0:1], S0s[:, hs], True, True, r=[kqT, S0s], w=[(pk, h // 4)])
            for h in range(8):
                hs = slice(h * 128, (h + 1) * 128)
                K.mm(pq[0:1, hs], kq4[:, h, s, 1:2], S0s[:, hs], True, True, r=[kqT, S0s], w=[(pq, h // 4)])
            eg, be, qk = row[:, 3072:3080], row[:, 3080:3088], row[:, 3088:3096]
            K.tt("dve", r8(tmp[:]), r8(pk[0:1, :]), b8(eg), ALU.mult, r=[pk, row], w=[tmp])
            K.tt("dve", vn[:], row[:, 2048:3072], tmp[:], ALU.subtract, r=[row, tmp], w=[vn])
            K.tt("dve", r8(vn[:]), r8(vn[:]), b8(be), ALU.mult, r=[vn, row], w=[vn])
            K.tt("dve", r8(orow[:]), r8(pq[0:1, :]), b8(eg), ALU.mult, r=[pq, row], w=[orow])
            K.tt("dve", r8(tmp[:]), r8(vn[:]), b8(qk), ALU.mult, r=[vn, row], w=[tmp])
            K.tt("dve", orow[:], orow[:], tmp[:], ALU.add, r=[orow, tmp], w=[orow])
            K.dma("sp", orow_d[s:s + 1, :], orow[:], r=[orow], w=[orow_d])
            pS = K.ps()
            for h in range(8):
                hs = slice(h * 128, (h + 1) * 128)
                K.mm(pS[:, hs], row[:, 1024 + h * 128:1024 + (h + 1) * 128], vn[:, hs], True, True, r=[row, vn], w=[(pS, h // 4)])
            K.tt("pool", g8(S0s[:]), g8(S0s[:]), egB[:, s * 8:(s + 1) * 8].unsqueeze(2).to_broadcast([128, 8, 128]), ALU.mult, r=[S0s, egB], w=[S0s])
            K.tt("dve", S0s[:], S0s[:], pS[:], ALU.add, r=[S0s, pS], w=[S0s])
            for h in range(8):
                K.dma("sp", o["ssm_s"][jl, s, h], S0s[:, h * 128:(h + 1) * 128], r=[S0s], w=[])
        K.dma("sp", ob[0:n, :], orow_d[:, :], r=[orow_d], w=[ob])
        L["rms_gate"](n, ob[0:n, :], [ob])
        out_proj_ln(K, t, n, cat, catT, wo, xt, z, xo, st)


_NC_CACHE = {}


def kernel(x_prompt, x_sample, c_prompt, c_sample, cache_kv, page_table, state_ret, state_ssm, state_conv,
           ada_w, ada_b, ln_g, ln_b, mlp_up, mlp_down, even_w_in, even_w_out, ret_gn,
           odd_w_in, odd_w_out, odd_conv, odd_a_log, odd_dt_bias, odd_gn):
    f = lambda a: np.ascontiguousarray(np.asarray(a))
    B, T, _ = x_prompt.shape
    NT = T // 128
    NPG = page_table.shape[1]
    NPOOL = cache_kv.shape[1]
    cfg = dict(NT=NT, NPG=NPG, NPOOL=NPOOL, DEPTH=4, mix="real")
    key = (NT, NPG, NPOOL)
    if key not in _NC_CACHE:
        _NC_CACHE[key] = build(cfg)
    nc = _NC_CACHE[key]
    cs = make_consts(NT, NPG * cache_kv.shape[2])
    shared = dict(cache_kv=f(cache_kv).reshape(-1, 320), ada_w=f(ada_w), ada_b=f(ada_b), ln_g=f(ln_g), ln_b=f(ln_b),
                  mlp_up=f(mlp_up), mlp_down=f(mlp_down), even_w_in=f(even_w_in), even_w_out=f(even_w_out),
                  ret_gn=f(ret_gn).reshape(2, 512), odd_w_in=f(odd_w_in), odd_w_out=f(odd_w_out), odd_conv=f(odd_conv),
                  odd_a_log=f(odd_a_log), odd_dt_bias=f(odd_dt_bias), odd_gn=f(odd_gn))
    shared.update({k: f(v) for k, v in cs.items()})
    in_maps = []
    for c in range(8):
        b = c // 2
        sl = slice(c * NS, (c + 1) * NS)
        m = dict(shared)
        m.update(xp=f(x_prompt[b]), xs=f(x_sample[sl, 0]), c17=f(np.concatenate([c_sample[sl], c_prompt[b:b + 1]], 0)),
                 ptab=f(page_table[sl]), st_ret=f(state_ret[:, sl]), st_ssm=f(state_ssm[:, sl]), st_conv=f(state_conv[:, sl]))
        in_maps.append(m)
    res = run_bass_kernel_spmd(nc, in_maps, core_ids=list(range(8))).results
    ev = [res[2 * b] for b in range(B)]
    y_p = np.stack([r["y_p"] for r in ev], 0)
    y_s = np.concatenate([r["y_s"] for r in res], 0)[:, None, :]
    kv_p = np.stack([r["kv_p"] for r in ev], 1)
    kv_s = np.concatenate([r["kv_s"] for r in res], 1)[:, :, None, :]
    ret_p = np.stack([r["ret_p"] for r in ev], 1)
    ret_s = np.concatenate([r["ret_s"] for r in res], 1)
    ssm_p = np.stack([r["ssm_p"] for r in ev], 1)
    ssm_s = np.concatenate([r["ssm_s"] for r in res], 1)
    conv_p = np.stack([r["conv_p"] for r in ev], 1)
    conv_s = np.concatenate([r["conv_s"] for r in res], 1)
    return tuple(np.asarray(a, np.float32) for a in (y_p, y_s, kv_p, kv_s, ret_p, ret_s, ssm_p, ssm_s, conv_p, conv_s))
```

```python
import math
import numpy as np
from contextlib import ExitStack
import concourse.bass as bass
import concourse.mybir as mybir
from concourse.bass_utils import run_bass_kernel_spmd

F32 = mybir.dt.float32
BF16 = mybir.dt.bfloat16
I32 = mybir.dt.int32
ALU = mybir.AluOpType
AF = mybir.ActivationFunctionType
AX = mybir.AxisListType

ENGS = ("pe", "act", "dve", "pool", "sp")
NDMA = 12

D = 1024
DFF = 4096
EVEN_IN = 3140
ODD_IN = 4112
NS = 16
ALPHA = 8.0 ** 0.25
LN_EPS = 1e-5
NEG = -1e30


class Buf:
    __slots__ = ("w", "r")

    def __init__(self):
        self.w = None
        self.r = {}


class TL:
    def __init__(self, t, nb=1):
        self.t = t
        self.b = [Buf() for _ in range(nb)]

    def __getitem__(self, k):
        return self.t[k]


def _bufs(lst):
    out = []
    for x in lst:
        if isinstance(x, Buf):
            out.append(x)
        elif isinstance(x, TL):
            out.extend(x.b)
        elif isinstance(x, tuple):
            out.append(x[0].b[x[1]])
        else:
            raise TypeError(x)
    return out


class Sched:
    def __init__(self, nc, es):
        self.nc = nc
        self.ops = {e: [] for e in ENGS}
        self.cnt = {e: 0 for e in ENGS}
        self.seen = {e: {} for e in ENGS}
        self.sem = {e: es.enter_context(nc.semaphore("s_" + e)) for e in ENGS}
        self.dsem = {}
        self.dcnt = {}
        self.pend = {e: [] for e in ENGS}
        for q in ("sp", "act", "pool"):
            for j in range(NDMA):
                k = "d_%s%d" % (q, j)
                self.dsem[k] = es.enter_context(nc.semaphore(k))
            self.dcnt[q] = 0

    def barrier(self):
        tgt = {e: self.cnt[e] for e in ENGS if self.cnt[e] > 0}
        for q in ("sp", "act", "pool"):
            for j in range(NDMA):
                if self.dcnt[q] > j:
                    tgt["d_%s%d" % (q, j)] = 16 * ((self.dcnt[q] - j + NDMA - 1) // NDMA)
        for e in ENGS:
            for k, v in tgt.items():
                if self.seen[e].get(k, 0) < v:
                    self.seen[e][k] = v
                    self.pend[e].append((k, v))

    def _semobj(self, key):
        return self.sem[key] if key in self.sem else self.dsem[key]

    def _collect(self, eng, r, w):
        deps = []
        for b in r:
            if b.w is not None:
                deps.append(b.w)
        for b in w:
            if b.w is not None:
                deps.append(b.w)
            for k, v in b.r.items():
                deps.append((k, v))
        waits = []
        sn = self.seen[eng]
        for k, v in deps:
            if eng == "pe" and k == "pe":
                continue
            if sn.get(k, 0) >= v:
                continue
            sn[k] = v
            waits.append((k, v))
        return waits

    def begin_capture(self):
        self.cap = []

    def end_capture(self):
        c_, self.cap = self.cap, None
        units, cur, open_ = [], [], False
        for it in c_:
            cur.append(it)
            g = it[-1]
            if g == "open":
                open_ = True
            elif g == "close":
                open_ = False
            if not open_:
                units.append(cur)
                cur = []
        if cur:
            units.append(cur)
        return units

    def emit_units(self, A, B):
        if getattr(self, "no_interleave", False):
            A, B = A + B, []
        i = j = 0
        while i < len(A) or j < len(B):
            if j >= len(B) or (i < len(A) and i * len(B) <= j * len(A)):
                u = A[i]
                i += 1
            else:
                u = B[j]
                j += 1
            for kind, a0, a1, a2, a3, _ in u:
                if kind == "op":
                    self.op(a0, a1, a2, a3)
                else:
                    self.dma(a0, a1, a2, a3)

    def op(self, eng, fn, r=(), w=(), grp=None):
        if getattr(self, "cap", None) is not None:
            self.cap.append(("op", eng, fn, r, w, grp))
            return
        r = _bufs(r)
        w = _bufs(w)
        waits = self.pend[eng] + self._collect(eng, r, w)
        self.pend[eng] = []
        self.cnt[eng] += 1
        tok = (eng, self.cnt[eng])
        self.ops[eng].append((waits, fn, None))
        for b in r:
            if b.r.get(eng, 0) < tok[1]:
                b.r[eng] = tok[1]
        for b in w:
            b.w = tok
            b.r = {}

    def dma(self, q, fn, r=(), w=()):
        if getattr(self, "cap", None) is not None:
            self.cap.append(("dma", q, fn, r, w, None))
            return
        r = _bufs(r)
        w = _bufs(w)
        i = self.dcnt[q]
        self.dcnt[q] += 1
        key = "d_%s%d" % (q, i % NDMA)
        prev = 16 * (i // NDMA)
        waits = self.pend[q] + self._collect(q, r, w)
        self.pend[q] = []
        sn = self.seen[q]
        if prev > 0 and sn.get(key, 0) < prev:
            sn[key] = prev
            waits.append((key, prev))
        tok = (key, prev + 16)
        self.ops[q].append((waits, fn, key))
        for b in r:
            if b.r.get(key, 0) < tok[1]:
                b.r[key] = tok[1]
        for b in w:
            b.w = tok
            b.r = {}

    def finish(self):
        final = {}
        for q in ("sp", "act", "pool"):
            for j in range(NDMA):
                if self.dcnt[q] > j:
                    final["d_%s%d" % (q, j)] = 16 * ((self.dcnt[q] - j + NDMA - 1) // NDMA)
        nc = self.nc
        with nc.Block() as block:
            def replay(ename, eng):
                for waits, fn, key in self.ops[ename]:
                    for k, v in waits:
                        eng.wait_ge(self._semobj(k), v)
                    inst = fn(eng)
                    if key is None:
                        inst.then_inc(self.sem[ename], 1)
                    else:
                        inst.then_inc(self.dsem[key], 16)
                if ename == "sp":
                    for k, v in final.items():
                        eng.wait_ge(self._semobj(k), v)
                    for e in ("pe", "act", "dve", "pool"):
                        if self.cnt[e] > 0:
                            eng.wait_ge(self.sem[e], self.cnt[e])

            @block.tensor
            def _(e):
                replay("pe", e)

            @block.scalar
            def _(e):
                replay("act", e)

            @block.vector
            def _(e):
                replay("dve", e)

            @block.gpsimd
            def _(e):
                replay("pool", e)

            @block.sync
            def _(e):
                replay("sp", e)


def _rope_rows(pos):
    def tab(d):
        inv = (10000.0 ** (-np.arange(0, d, 2, dtype=np.float32) / d)).astype(np.float32)
        ang = (pos.astype(np.float32)[:, None] * inv[None, :]).astype(np.float32)
        return np.cos(ang.astype(np.float64)), np.sin(ang.astype(np.float64))
    c128, s128 = tab(128)
    c64, s64 = tab(64)
    return np.concatenate([c128, s128, -s128, c64, s64, -s64], 1).astype(np.float32)


def make_consts(NT, past_len):
    c = {}
    p = np.arange(128)
    c["ident"] = np.eye(128, dtype=np.float32)
    c["ones"] = np.ones((128, 128), np.float32)
    c["triu"] = (p[:, None] <= p[None, :]).astype(np.float32)
    c["mvinc"] = np.where(p[None, :] >= p[:, None], 0.0, -1e4).astype(np.float32)
    c["strict"] = (p[None, :] > p[:, None]).astype(np.float32)
    c["cmask"] = np.where(p[None, :] <= p[:, None], 0.0, NEG).astype(np.float32)
    lg = np.log(1.0 - 2.0 ** (-5.0 - np.arange(4, dtype=np.float64)))
    i = p.astype(np.float64)
    rel = i[None, :] - i[:, None]
    dm = np.where(rel >= 0, np.exp(lg[:, None, None] * np.maximum(rel, 0)[None]), 0.0)
    c["dmatT"] = (np.transpose(dm, (1, 0, 2)) * 128 ** -0.5).astype(np.float32).reshape(128, 512)
    qd = np.exp(lg[:, None] * (i[None, :] + 1.0))
    c["qdecB"] = np.broadcast_to(qd.reshape(1, 512), (128, 512)).astype(np.float32).copy()
    kd = np.exp(lg[None, :] * (127.0 - i[:, None])) * 128 ** -0.5
    c["kdecC"] = kd.astype(np.float32)
    cd = np.exp(lg * 128.0)
    c["cdecB"] = np.broadcast_to(np.repeat(cd, 128).reshape(1, 512), (128, 512)).astype(np.float32).copy()
    g1 = np.exp(lg)
    c["gam1B"] = np.broadcast_to(np.repeat(g1, 128).reshape(1, 512), (128, 512)).astype(np.float32).copy()
    c["ropeP"] = _rope_rows(np.arange(NT * 128)).reshape(NT, 128, 288)
    c["iotap"] = np.arange(128, dtype=np.float32).reshape(128, 1)
    c["pw2"] = np.broadcast_to((0.5 ** (np.arange(32, dtype=np.float64) + 1)).astype(np.float32).reshape(1, 32), (128, 32)).copy()
    c["ropeS"] = np.broadcast_to(_rope_rows(np.array([past_len])), (NS, 288)).copy()
    d16 = np.broadcast_to(np.eye(16, dtype=np.float32).reshape(1, 256), (128, 256)).copy()
    c["delta16"] = d16
    return c


class KB:
    def __init__(self, cfg):
        self.cfg = cfg
        self.nc = bass.Bass("TRN2", target_bir_lowering=False)
        self.es = ExitStack()
        self.rr = 0

    def scope(self):
        kb = self

        class _Sc(ExitStack):
            def __exit__(self_, *a):
                kb.S.barrier()
                return super().__exit__(*a)
        return _Sc()

    def dram(self, name, shape, dt, kind):
        return TL(self.nc.dram_tensor(name, list(shape), dt, kind=kind).ap())

    def sb(self, es, name, shape, dt, nb=1):
        self.uid = getattr(self, "uid", 0) + 1
        return TL(es.enter_context(self.nc.sbuf_tensor("%s_%d" % (name, self.uid), list(shape), dt)), nb)

    def tt(self, eng, out, a, b, op, r, w):
        self.S.op(eng, lambda e: e.tensor_tensor(out=out, in0=a, in1=b, op=op), r=r, w=w)

    def ts(self, eng, out, a, s1, op0, r, w, s2=None, op1=None, acc=None):
        if op1 is None:
            self.S.op(eng, lambda e: e.tensor_scalar(out=out, in0=a, scalar1=s1, scalar2=None, op0=op0), r=r, w=w)
        elif acc is None:
            self.S.op(eng, lambda e: e.tensor_scalar(out=out, in0=a, scalar1=s1, scalar2=s2, op0=op0, op1=op1), r=r, w=w)
        else:
            self.S.op(eng, lambda e: e.tensor_scalar(out=out, in0=a, scalar1=s1, scalar2=s2, op0=op0, op1=op1, accum_out=acc), r=r, w=w)

    def stt(self, eng, out, a, sc, b, op0, op1, r, w):
        self.S.op(eng, lambda e: e.scalar_tensor_tensor(out=out, in0=a, scalar=sc, in1=b, op0=op0, op1=op1), r=r, w=w)

    def act(self, out, in_, func, r, w, bias=None, scale=None, acc=None):
        kw = {}
        if bias is not None:
            kw["bias"] = bias
        if scale is not None:
            kw["scale"] = scale
        if acc is not None:
            kw["accum_out"] = acc
        self.S.op("act", lambda e: e.activation(out=out, in_=in_, func=func, **kw), r=r, w=w)

    def cp(self, eng, out, in_, r, w):
        if eng == "act":
            self.S.op("act", lambda e: e.copy(out=out, in_=in_), r=r, w=w)
        else:
            self.S.op(eng, lambda e: e.tensor_copy(out=out, in_=in_), r=r, w=w)

    def red(self, eng, out, in_, r, w, op=None):
        if op is None:
            self.S.op(eng, lambda e: e.reduce_sum(out=out, in_=in_, axis=AX.X), r=r, w=w)
        else:
            self.S.op(eng, lambda e: e.tensor_reduce(out=out, in_=in_, axis=AX.X, op=op), r=r, w=w)

    def mm(self, out, lhsT, rhs, start, stop, r, w, dedicated=False):
        grp = None
        if not dedicated and not (start and stop):
            grp = "open" if start else ("close" if stop else "mid")
        self.S.op("pe", lambda e: e.matmul(out, lhsT=lhsT, rhs=rhs, start=start, stop=stop), r=r, w=w, grp=grp)

    def tr(self, out, in_, ident, r, w):
        self.S.op("pe", lambda e: e.transpose(out=out, in_=in_, identity=ident), r=r, w=w)

    def dma(self, q, out, in_, r, w):
        self.S.dma(q, lambda e: e.dma_start(out=out, in_=in_), r=r, w=w)

    def memset(self, eng, ap, val, w):
        self.S.op(eng, lambda e: e.memset(ap, val), w=w)

    def evq(self):
        self.rr += 1
        return "act" if self.rr % 2 else "dve"

    def ps(self):
        self.prr = (getattr(self, "prr", -1) + 1) % len(self.PSG)
        return self.PSG[self.prr]

    def rsqrt(self, eng, out, in_, r, w, mult, add):
        self.act(out, in_, AF.Sqrt, r=r, w=w, scale=mult, bias=add)
        self.S.op("dve", lambda e: e.reciprocal(out=out, in_=out), r=w, w=w)


def build(cfg):
    NT = cfg["NT"]
    NPG = cfg["NPG"]
    NPOOL = cfg["NPOOL"]
    DEPTH = cfg["DEPTH"]
    TOPK_P = min(256, (NT * 128) // 4)
    TOPK_S = min(256, (NPG * 128 + 1) // 4)
    NTOK = NT * 128
    NROW = NTOK + NS
    K = KB(cfg)
    nc = K.nc
    es = K.es
    with es:
        S = K.S = Sched(nc, es)
        S.no_interleave = bool(cfg.get("noilv"))
        EI, EO, IN = "ExternalInput", "ExternalOutput", "Internal"
        d = {}
        for nm, sh, dt in [
            ("xp", [NTOK, D], F32), ("xs", [NS, D], F32), ("c17", [17, D], F32),
            ("cache_kv", [2 * NPOOL * 128, 320], F32), ("ptab", [NS, NPG], I32),
            ("st_ret", [2, NS, 4, 128, 128], F32), ("st_ssm", [2, NS, 8, 128, 128], F32),
            ("st_conv", [2, NS, 3, 3072], F32),
            ("ada_w", [4, D, 6 * D], F32), ("ada_b", [4, 6 * D], F32), ("ln_g", [4, 2, D], F32),
            ("ln_b", [4, 2, D], F32), ("mlp_up", [4, D, DFF], F32), ("mlp_down", [4, DFF, D], F32),
            ("even_w_in", [2, D, EVEN_IN], F32), ("even_w_out", [2, D, D], F32), ("ret_gn", [2, 512], F32),
            ("odd_w_in", [2, D, ODD_IN], F32), ("odd_w_out", [2, D, D], F32), ("odd_conv", [2, 4, 3072], F32),
            ("odd_a_log", [2, 8], F32), ("odd_dt_bias", [2, 8], F32), ("odd_gn", [2, 128], F32),
            ("ident", [128, 128], F32), ("ones", [128, 128], F32), ("triu", [128, 128], F32),
            ("mvinc", [128, 128], F32), ("strict", [128, 128], F32), ("cmask", [128, 128], F32),
            ("dmatT", [128, 512], F32), ("qdecB", [128, 512], F32), ("kdecC", [128, 4], F32),
            ("cdecB", [128, 512], F32), ("gam1B", [128, 512], F32), ("ropeP", [NT, 128, 288], F32),
            ("ropeS", [NS, 288], F32), ("delta16", [128, 256], F32), ("iotap", [128, 1], F32), ("pw2", [128, 32], F32),
        ]:
            d[nm] = K.dram(nm, sh, dt, EI)
        o = {}
        for nm, sh in [
            ("y_p", [NTOK, D]), ("y_s", [NS, D]), ("kv_p", [2, NTOK, 320]), ("kv_s", [2, NS, 320]),
            ("ret_p", [2, 4, 128, 128]), ("ret_s", [2, NS, 4, 128, 128]), ("ssm_p", [2, 8, 128, 128]),
            ("ssm_s", [2, NS, 8, 128, 128]), ("conv_p", [2, 3, 3072]), ("conv_s", [2, NS, 3, 3072]),
        ]:
            o[nm] = K.dram(nm, sh, F32, EO)
        x_a = K.dram("x_a", [NROW, D], F32, IN)
        x_b = K.dram("x_b", [NROW, D], F32, IN)
        mod_d = K.dram("mod_d", [4, 17, 6 * D], F32, IN)
        proj_d = K.dram("proj_d", [NROW + 3, ODD_IN], F32, IN)
        tb_xa = [Buf() for _ in range(NT + 1)]
        tb_xb = [Buf() for _ in range(NT + 1)]
        tb_pj = [Buf() for _ in range(NT + 2)]

        def rows(t):
            return (t * 128, 128) if t < NT else (NTOK, NS)

        cst = {}
        for nm, sh in [("ident", [128, 128]), ("ones", [128, 128]), ("triu", [128, 128]), ("mvinc", [128, 128]),
                       ("strict", [128, 128]), ("cmask", [128, 128]), ("delta16", [128, 256]), ("iotap", [128, 1]), ("pw2", [128, 32])]:
            cst[nm] = K.sb(es, "c_" + nm, sh, F32)
            K.dma("sp", cst[nm][:], d[nm][:], r=[], w=[cst[nm]])
        identb = K.sb(es, "identb", [128, 128], BF16)
        K.cp("dve", identb[:], cst["ident"][:], r=[cst["ident"]], w=[identb])
        ident = cst["ident"]
        PS = [TL(es.enter_context(nc.psum_tensor("ps%d" % i, [128, 1024], F32)), 2) for i in range(4)]
        K.PSG = PS[:4]
        PSO = [PS[2], PS[3]]
        lnG = K.sb(es, "lnG", [128, D], F32)
        lnB = K.sb(es, "lnB", [128, D], F32)
        modP = K.sb(es, "modP", [128, 3 * D], F32)

        def load_phase_consts(l, which):
            c0 = which * 3 * D
            K.dma("sp", modP[:], mod_d[l, 16:17, c0:c0 + 3 * D].partition_broadcast(128), r=[mod_d], w=[modP])
            K.dma("sp", lnG[:], d["ln_g"][l, which:which + 1, :].partition_broadcast(128), r=[], w=[lnG])
            K.dma("sp", lnB[:], d["ln_b"][l, which:which + 1, :].partition_broadcast(128), r=[], w=[lnB])
            K.ts("pool", modP[:, D:3 * D], modP[:, D:3 * D], 1.0, ALU.add, r=[modP], w=[modP])

        def load_sample_mod(l, which, dst=None):
            dst = modP if dst is None else dst
            c0 = which * 3 * D
            K.dma("sp", dst[0:NS, :], mod_d[l, 0:NS, c0:c0 + 3 * D], r=[mod_d], w=[dst])
            K.ts("pool", dst[0:NS, D:3 * D], dst[0:NS, D:3 * D], 1.0, ALU.add, r=[dst], w=[dst])

        K.modS = None

        def mod_of(t):
            return K.modS if (t == NT and K.modS is not None) else modP

        with K.scope() as p0:
            c17t = K.sb(p0, "c17t", [17, D], F32)
            scT = K.sb(p0, "scT", [128, 8, 17], BF16)
            K.dma("sp", c17t[:], d["c17"][:], r=[], w=[c17t])
            K.act(c17t[:], c17t[:], AF.Silu, r=[c17t], w=[c17t])
            pt = K.ps()
            for k in range(8):
                K.tr(pt[0:128, k * 32:k * 32 + 17], c17t[:, k * 128:(k + 1) * 128], ident[0:17, 0:17], r=[c17t, ident], w=[(pt, 0)])
            K.cp("dve", scT[:], pt[:, 0:256].rearrange("p (k c) -> p k c", c=32)[:, :, 0:17], r=[(pt, 0)], w=[scT])
            aw = [K.sb(p0, "aw%d" % i, [128, 8, 512], BF16) for i in range(2)]
            ab = [K.sb(p0, "ab%d" % i, [17, 512], F32) for i in range(2)]
            mo = [K.sb(p0, "mo%d" % i, [17, 512], F32) for i in range(2)]
            it = 0
            for l in range(DEPTH):
                for cb in range(12):
                    a, b_, m_ = aw[it % 2], ab[it % 2], mo[it % 2]
                    cs = slice(cb * 512, (cb + 1) * 512)
                    K.dma("pool", a[:], d["ada_w"][l, :, cs].rearrange("(k p) n -> p k n", p=128), r=[], w=[a])
                    K.dma("sp", b_[:], d["ada_b"][l:l + 1, cs].partition_broadcast(17), r=[], w=[b_])
                    pt = K.ps()
                    for k in range(8):
                        K.mm(pt[0:17, 0:512], scT[:, k, :], a[:, k, :], k == 0, k == 7, r=[scT, a], w=[(pt, 0)])
                    K.tt("dve", m_[:], pt[0:17, 0:512], b_[:], ALU.add, r=[(pt, 0), b_], w=[m_])
                    K.dma("sp", mod_d[l, :, cs], m_[:], r=[m_], w=[mod_d])
                    it += 1

        def load_weights_bf16(wt, src, ncols):
            nk = src.shape[0] // 128
            for k in range(nk):
                K.dma("pool", wt[:, k, 0:ncols], src[k * 128:(k + 1) * 128, :], r=[], w=[wt])

        def modulate_T(pp, xt, t, n, tmp, hm, hmT):
            md = mod_of(t)
            K.tt("dve", tmp[0:n, :], xt[0:n, :], md[0:n, D:2 * D], ALU.mult, r=[xt, md], w=[tmp])
            K.tt("pool", hm[0:n, :], tmp[0:n, :], md[0:n, 0:D], ALU.add, r=[tmp, md], w=[hm])
            to_T(hm, n, hmT)

        def to_T(hm, n, hmT):
            pt = K.ps()
            pb = pt[:].bitcast(BF16)
            for k in range(8):
                K.tr(pb[:, k * 128:k * 128 + n], hm[0:n, k * 128:(k + 1) * 128], identb[0:n, 0:n], r=[hm, identb], w=[(pt, 0)])
            K.cp("act", hmT[:, :, 0:n], pb[:, 0:1024].rearrange("p (k c) -> p k c", c=128)[:, :, 0:n], r=[(pt, 0)], w=[hmT])

        def layer_norm_out(z, n, xo, scr, st):
            K.memset("pool", st[0:n, :], 0.0, w=[st])
            K.red("dve", st[0:n, 0:1], z[0:n, :], r=[z], w=[st])
            K.ts("dve", st[0:n, 1:2], st[0:n, 0:1], -1.0 / D, ALU.mult, r=[st], w=[st])
            K.ts("dve", scr[0:n, :], z[0:n, :], st[0:n, 1:2], ALU.add, r=[z, st], w=[scr])
            K.act(xo[0:n, :], scr[0:n, :], AF.Square, r=[scr, st], w=[xo, st], acc=st[0:n, 2:3])
            K.rsqrt("dve", st[0:n, 3:4], st[0:n, 2:3], r=[st], w=[st], mult=1.0 / D, add=LN_EPS)
            K.stt("dve", xo[0:n, :], scr[0:n, :], st[0:n, 3:4], lnG[0:n, :], ALU.mult, ALU.mult, r=[scr, st, lnG], w=[xo])
            K.tt("pool", xo[0:n, :], xo[0:n, :], lnB[0:n, :], ALU.add, r=[xo, lnB], w=[xo])

        def residual_ln(py, t, n, xt, z, xo, scr, st):
            md = mod_of(t)
            K.tt("dve", z[0:n, :], py[0:n, :], md[0:n, 2 * D:3 * D], ALU.mult, r=[py, md], w=[z])
            K.stt("dve", z[0:n, :], xt[0:n, :], ALPHA, z[0:n, :], ALU.mult, ALU.add, r=[xt, z], w=[z])
            layer_norm_out(z, n, xo, scr, st)

        K.fn = dict(rows=rows, load_phase_consts=load_phase_consts, load_sample_mod=load_sample_mod, mod_of=mod_of, load_weights_bf16=load_weights_bf16,
                    modulate_T=modulate_T, to_T=to_T, residual_ln=residual_ln)
        K.ctx = dict(d=d, o=o, x_a=x_a, x_b=x_b, proj_d=proj_d, tb_xa=tb_xa, tb_xb=tb_xb, tb_pj=tb_pj, cst=cst,
                     ident=ident, identb=identb, PS=PS, PSO=PSO, NT=NT, NPG=NPG, NTOK=NTOK, NROW=NROW,
                     TOPK_P=TOPK_P, TOPK_S=TOPK_S, modP=modP, mod_d=mod_d, cfg=cfg)

        with K.scope() as pz:
            zt = K.sb(pz, "zt", [3, ODD_IN], F32)
            K.memset("pool", zt[:], 0.0, w=[zt])
            K.dma("sp", proj_d[0:3, :], zt[:], r=[zt], w=[tb_pj[NT + 1]])

        x_in = None
        for l in range(DEPTH):
            even = (l % 2 == 0)
            NIN = EVEN_IN if even else ODD_IN
            w_in_d = d["even_w_in"][l // 2] if even else d["odd_w_in"][l // 2]
            load_phase_consts(l, 0)
            with K.scope() as pa:
                wt = K.sb(pa, "w_in", [128, 8, ODD_IN], BF16)
                load_weights_bf16(wt, w_in_d, NIN)
                xts = [K.sb(pa, "a_x%d" % i, [128, D], F32) for i in range(2)]
                tmps = [K.sb(pa, "a_t%d" % i, [128, D], F32) for i in range(2)]
                hms = [K.sb(pa, "a_h%d" % i, [128, D], BF16) for i in range(2)]
                hmTs = [K.sb(pa, "a_hT%d" % i, [128, 8, 128], BF16) for i in range(2)]
                pjs = [K.sb(pa, "a_pj%d" % i, [128, ODD_IN], F32) for i in range(1)] * 2
                K.modS = K.sb(pa, "a_modS", [NS, 3 * D], F32)
                load_sample_mod(l, 0, K.modS)

                def a_s1(t):
                    r0, n = rows(t)
                    xt, tmp, hm, hmT = xts[t % 2], tmps[t % 2], hms[t % 2], hmTs[t % 2]
                    if l == 0:
                        src = d["xp"][r0:r0 + n, :] if t < NT else d["xs"][:, :]
                        K.dma("sp", xt[0:n, :], src, r=[], w=[xt])
                    else:
                        K.dma("sp", xt[0:n, :], x_a[r0:r0 + n, :], r=[tb_xa[t]], w=[xt])
                    modulate_T(pa, xt, t, n, tmp, hm, hmT)

                def a_s2(t):
                    r0, n = rows(t)
                    hmT, pj = hmTs[t % 2], pjs[0]
                    nb = (NIN + 511) // 512
                    for cb in range(nb):
                        c0 = cb * 512
                        wd = min(512, NIN - c0)
                        pt = K.ps()
                        hb = cb % 2
                        for k in range(8):
                            K.mm(pt[0:n, hb * 512:hb * 512 + wd], hmT[:, k, 0:n], wt[:, k, c0:c0 + wd], k == 0, k == 7, r=[hmT, wt], w=[(pt, hb)])
                        K.cp(K.evq(), pj[0:n, c0:c0 + wd], pt[0:n, hb * 512:hb * 512 + wd], r=[(pt, hb)], w=[pj])
                    K.dma("sp", proj_d[3 + r0:3 + r0 + n, 0:NIN], pj[0:n, 0:NIN], r=[pj], w=[tb_pj[t]])

                a_s1(0)
                for t in range(NT + 1):
                    if t + 1 <= NT:
                        a_s1(t + 1)
                    a_s2(t)
                K.modS = None
            if cfg.get("mix", "real") == "stub":
                phase_b_stub(K, l)
            elif even:
                phase_b_even(K, l)
            else:
                phase_b_odd(K, l)
            load_phase_consts(l, 1)
            with K.scope() as pf:
                wu = K.sb(pf, "w_up", [128, 8, DFF], BF16)
                wd_ = K.sb(pf, "w_dn", [128, 32, D], BF16)
                load_weights_bf16(wu, d["mlp_up"][l], DFF)
                load_weights_bf16(wd_, d["mlp_down"][l], D)
                xts = [K.sb(pf, "f_x%d" % i, [128, D], F32) for i in range(2)]
                hms = [K.sb(pf, "f_h%d" % i, [128, D], BF16) for i in range(1)] * 2
                hmTs = [K.sb(pf, "f_hT%d" % i, [128, 8, 128], BF16) for i in range(2)]
                uTs = [K.sb(pf, "f_uT%d" % i, [128, 32, 128], BF16) for i in range(1)] * 2
                rl = [K.sb(pf, "f_rl%d" % i, [128, 1024], F32) for i in range(1)] * 2
                xos = [K.sb(pf, "f_xo%d" % i, [128, D], F32) for i in range(1)] * 2
                sts = [K.sb(pf, "f_st%d" % i, [128, 4], F32) for i in range(2)]
                last = (l == DEPTH - 1)
                K.modS = K.sb(pf, "f_modS", [NS, 3 * D], F32)
                load_sample_mod(l, 1, K.modS)
                zb = K.sb(pf, "f_z", [128, D], F32)

                xts.append(K.sb(pf, "f_x2", [128, D], F32))

                def f_s1(t):
                    r0, n = rows(t)
                    xt, tmp, hm, hmT = xts[t % 3], rl[0], hms[0], hmTs[t % 2]
                    K.dma("sp", xt[0:n, :], x_b[r0:r0 + n, :], r=[tb_xb[t]], w=[xt])
                    modulate_T(pf, xt, t, n, tmp, hm, hmT)

                def f_s2a(t):
                    r0, n = rows(t)
                    hmT, uT = hmTs[t % 2], uTs[0]
                    for g in range(4):
                        pt = K.ps()
                        for j in range(8):
                            fc = g * 8 + j
                            for k in range(8):
                                K.mm(pt[:, j * 128:j * 128 + n], wu[:, k, fc * 128:(fc + 1) * 128], hmT[:, k, 0:n], k == 0, k == 7, r=[wu, hmT], w=[(pt, j // 4)])
                        r_ = rl[0]
                        pv = pt[:].rearrange("p (j c) -> p j c", c=128)[:, :, 0:n]
                        rv = r_[:].rearrange("p (j c) -> p j c", c=128)[:, :, 0:n]
                        K.act(rv, pv, AF.Relu, r=[pt], w=[r_])
                        K.tt("pool" if g % 2 else "dve", uT[:, g * 8:(g + 1) * 8, 0:n], rv, rv, ALU.mult, r=[r_], w=[uT])

                def f_s2b(t):
                    r0, n = rows(t)
                    xt, uT, xo, st = xts[t % 3], uTs[0], xos[0], sts[t % 2]
                    py = K.ps()
                    for hb in range(2):
                        for k in range(32):
                            K.mm(py[0:n, hb * 512:(hb + 1) * 512], uT[:, k, 0:n], wd_[:, k, hb * 512:(hb + 1) * 512], k == 0, k == 31, r=[uT, wd_], w=[(py, hb)])
                    residual_ln(py, t, n, xt, zb, xo, zb, st)
                    if last:
                        dst = o["y_p"][r0:r0 + n, :] if t < NT else o["y_s"][:, :]
                        K.dma("sp", dst, xo[0:n, :], r=[xo], w=[])
                    else:
                        K.dma("sp", x_a[r0:r0 + n, :], xo[0:n, :], r=[xo], w=[tb_xa[t]])

                f_s1(0)
                if NT >= 1:
                    f_s1(1)
                for t in range(NT + 1):
                    f_s2a(t)
                    if t + 2 <= NT:
                        f_s1(t + 2)
                    f_s2b(t)
                K.modS = None
        S.finish()
    return nc


def phase_b_stub(K, l):
    c = K.ctx
    f = K.fn
    f["load_phase_consts"](l, 0)
    d, NT = c["d"], c["NT"]
    even = (l % 2 == 0)
    with K.scope() as pb:
        wo = K.sb(pb, "w_out", [128, 8, D], BF16)
        f["load_weights_bf16"](wo, (d["even_w_out"] if even else d["odd_w_out"])[l // 2], D)
        xt = K.sb(pb, "b_x", [128, D], F32)
        pj = K.sb(pb, "b_pj", [128, D], F32)
        cat = K.sb(pb, "b_cat", [128, D], BF16)
        catT = K.sb(pb, "b_catT", [128, 8, 128], BF16)
        z = K.sb(pb, "b_z", [128, D], F32)
        xo = K.sb(pb, "b_xo", [128, D], F32)
        st = K.sb(pb, "b_st", [128, 4], F32)
        for t in range(NT + 1):
            r0, n = f["rows"](t)
            if l == 0:
                src = d["xp"][r0:r0 + n, :] if t < NT else d["xs"][:, :]
                K.dma("sp", xt[0:n, :], src, r=[], w=[xt])
            else:
                K.dma("sp", xt[0:n, :], c["x_a"][r0:r0 + n, :], r=[c["tb_xa"][t]], w=[xt])
            if t == NT:
                f["load_sample_mod"](l, 0)
            K.dma("sp", pj[0:n, :], c["proj_d"][3 + r0:3 + r0 + n, 0:D], r=[c["tb_pj"][t]], w=[pj])
            K.cp("dve", cat[0:n, :], pj[0:n, :], r=[pj], w=[cat])
            out_proj_ln(K, t, n, cat, catT, wo, xt, z, xo, st)


def out_proj_ln(K, t, n, cat, catT, wo, xt, z, xo, st):
    c = K.ctx
    f = K.fn
    f["to_T"](cat, n, catT)
    py = K.ps()
    for hb in range(2):
        for k in range(8):
            K.mm(py[0:n, hb * 512:(hb + 1) * 512], catT[:, k, 0:n], wo[:, k, hb * 512:(hb + 1) * 512], k == 0, k == 7, r=[catT, wo], w=[(py, hb)])
    f["residual_ln"](py, t, n, xt, z, xo, z, st)
    r0, _ = f["rows"](t)
    K.dma("sp", c["x_b"][r0:r0 + n, :], xo[0:n, :], r=[xo], w=[c["tb_xb"][t]])


def rope_ops(K, n, src, H, half, rp, c0, t1, t2, dsts, r_src):
    cosb = rp[0:n, c0:c0 + half].unsqueeze(1).unsqueeze(1).to_broadcast([n, H, 2, half])
    sinb = rp[0:n, c0 + half:c0 + 2 * half].unsqueeze(1).to_broadcast([n, H, half])
    nsinb = rp[0:n, c0 + 2 * half:c0 + 3 * half].unsqueeze(1).to_broadcast([n, H, half])
    t1v = t1[0:n, 0:H * 2 * half].rearrange("p (h a c) -> p h a c", h=H, a=2)
    t2v = t2[0:n, 0:H * 2 * half].rearrange("p (h a c) -> p h a c", h=H, a=2)
    K.tt("dve", t1v, src, cosb, ALU.mult, r=r_src + [rp], w=[t1])
    K.tt("pool", t2v[:, :, 0, :], src[:, :, 1, :], nsinb, ALU.mult, r=r_src + [rp], w=[t2])
    K.tt("pool", t2v[:, :, 1, :], src[:, :, 0, :], sinb, ALU.mult, r=r_src + [rp], w=[t2])
    for h0, h1, dap, dt_ in dsts:
        K.tt("dve", dap, t1v[:, h0:h1], t2v[:, h0:h1], ALU.add, r=[t1, t2], w=[dt_])


def phase_b_even(K, l):
    c = K.ctx
    f = K.fn
    d, o, NT, NPG = c["d"], c["o"], c["NT"], c["NPG"]
    PSO, cst, ident, identb = c["PSO"], c["cst"], c["ident"], c["identb"]
    i = l // 2
    TOPK = c["TOPK_P"]
    NIT = 15
    SC_ATT = 128 ** -0.5
    f["load_phase_consts"](l, 0)
    K.PSG = c["PS"][:2]
    with K.scope() as pb:
        sb = lambda nm, sh, dt=F32: K.sb(pb, "e_" + nm, sh, dt)
        wo = sb("wo", [128, 8, D], BF16)
        f["load_weights_bf16"](wo, d["even_w_out"][i], D)
        kc = {}
        for nm in ("dmatT", "qdecB", "cdecB", "gam1B"):
            kc[nm] = sb(nm, [128, 512])
            K.dma("sp", kc[nm][:], d[nm][:], r=[], w=[kc[nm]])
        kdecC = sb("kdecC", [128, 4])
        K.dma("sp", kdecC[:], d["kdecC"][:], r=[], w=[kdecC])
        gnB = sb("gnB", [128, 512])
        K.dma("sp", gnB[:], d["ret_gn"][i:i + 1, :].partition_broadcast(128), r=[], w=[gnB])
        Sret = sb("Sret", [128, 512])
        K.memset("pool", Sret[:], 0.0, w=[Sret])
        pj = sb("pj", [128, EVEN_IN])
        xt = sb("xt", [128, D])
        rp = sb("rp", [128, 288])
        qk_r = sb("qk_r", [128, 1024])
        aq_rb = sb("aq_rb", [128, 512], BF16)
        kvrow = sb("kvrow", [128, 320])
        iq_r = sb("iq_r", [128, 256])
        t1 = sb("t1", [128, 1024])
        t2 = sb("t2", [128, 1024])
        AT, cen, sq, sg = [sb(nm, [128, 512]) for nm in ("AT", "cen", "sq", "sg")]
        cat = sb("cat", [128, D], BF16)
        catT = sb("catT", [128, 8, 128], BF16)
        z = sb("z", [128, D])
        xo = sb("xo", [128, D])
        st = sb("st", [128, 4])
        s4 = sb("s4", [128, 16])
        aw = sb("aw", [128, 8])
        bs = sb("bs", [128, 8])
        cnt = sb("cnt", [128, NIT])
        Hh = sb("Hh", [128, 32])
        rs = sb("rs", [128, 4])

        def x_load(t, n, r0):
            if l == 0:
                src = d["xp"][r0:r0 + n, :] if t < NT else d["xs"][:, :]
                K.dma("sp", xt[0:n, :], src, r=[], w=[xt])
            else:
                K.dma("sp", xt[0:n, :], c["x_a"][r0:r0 + n, :], r=[c["tb_xa"][t]], w=[xt])

        def proj_rope(t, n, r0, rp_src, aq_dst=None):
            aq_dst = aq_rb if aq_dst is None else aq_dst
            K.dma("sp", pj[0:n, :], c["proj_d"][3 + r0:3 + r0 + n, 0:EVEN_IN], r=[c["tb_pj"][t]], w=[pj])
            K.dma("sp", rp[0:n, :], rp_src, r=[], w=[rp])
            v4 = lambda ap, H, half: ap.rearrange("p (h a c) -> p h a c", h=H, a=2)
            rope_ops(K, n, v4(pj[0:n, 0:1024], 8, 64), 8, 64, rp, 0, t1, t2,
                     [(0, 8, v4(qk_r[0:n, :], 8, 64), qk_r)], [pj])
            rope_ops(K, n, v4(pj[0:n, 2048:2688], 5, 64), 5, 64, rp, 0, t1, t2,
                     [(0, 4, v4(aq_dst[0:n, :], 4, 64), aq_dst), (4, 5, v4(kvrow[0:n, 0:128], 1, 64), kvrow)], [pj])
            rope_ops(K, n, v4(pj[0:n, 2816:3136], 5, 32), 5, 32, rp, 192, t1, t2,
                     [(0, 4, v4(iq_r[0:n, :], 4, 32), iq_r), (4, 5, v4(kvrow[0:n, 256:320], 1, 32), kvrow)], [pj])
            K.cp("act", kvrow[0:n, 128:256], pj[0:n, 2688:2816], r=[pj], w=[kvrow])
            K.act(aw[0:n, 0:4], pj[0:n, 3136:3140], AF.Abs, r=[pj], w=[aw], scale=1.0 / 16)
            K.ts("dve", aw[0:n, 4:8], pj[0:n, 3136:3140], 0.0, ALU.is_ge, r=[pj], w=[aw], s2=2.0, op1=ALU.mult)
            K.ts("dve", aw[0:n, 4:8], aw[0:n, 4:8], -1.0, ALU.add, r=[aw], w=[aw])

        def head_norm_gate(n, src, rsrc):
            pov = src.rearrange("p (h e) -> p h e", h=4)
            cv = cen[0:n, :].rearrange("p (h e) -> p h e", h=4)
            K.red("dve", s4[0:n, 0:4], pov, r=rsrc, w=[s4])
            K.ts("dve", s4[0:n, 4:8], s4[0:n, 0:4], -1.0 / 128, ALU.mult, r=[s4], w=[s4])
            K.tt("dve", cv, pov, s4[0:n, 4:8].unsqueeze(2).to_broadcast([n, 4, 128]), ALU.add, r=rsrc + [s4], w=[cen])
            K.tt("pool", sq[0:n, :], cen[0:n, :], cen[0:n, :], ALU.mult, r=[cen], w=[sq])
            K.red("dve", s4[0:n, 8:12], sq[0:n, :].rearrange("p (h e) -> p h e", h=4), r=[sq], w=[s4])
            K.rsqrt("dve", s4[0:n, 12:16], s4[0:n, 8:12], r=[s4], w=[s4], mult=1.0 / 128, add=LN_EPS)
            K.act(sg[0:n, :], pj[0:n, 1536:2048], AF.Silu, r=[pj], w=[sg])
            K.tt("pool", sg[0:n, :], sg[0:n, :], gnB[0:n, :], ALU.mult, r=[sg, gnB], w=[sg])
            K.tt("dve", cv, cv, s4[0:n, 12:16].unsqueeze(2).to_broadcast([n, 4, 128]), ALU.mult, r=[cen, s4], w=[cen])
            K.tt("dve", cat[0:n, 0:512], cen[0:n, :], sg[0:n, :], ALU.mult, r=[cen, sg], w=[cat])

        with K.scope() as pp:
            akT_all = K.sb(pp, "e_akT", [128, NT * 128], BF16, nb=NT)
            ikT_all = K.sb(pp, "e_ikT", [64, NT * 128], BF16)
            Vall = K.sb(pp, "e_Vall", [128, NT, 132], BF16, nb=NT)
            K.memset("pool", Vall[:], 1.0, w=[Vall])
            score = K.sb(pp, "e_score", [128, NT * 128], F32)
            junk = K.sb(pp, "e_junk", [128, NT * 128], F32)
            maskb = K.sb(pp, "e_maskb", [128, NT * 128], BF16)
            sbp = lambda nm, sh, dt=F32: K.sb(pp, "e_" + nm, sh, dt)
            qT, qTd, kT, kd = [sbp(nm, [128, 512]) for nm in ("qT", "qTd", "kT", "kd")]
            akb = sbp("akb", [128, 128], BF16)
            ikb = sbp("ikb", [128, 64], BF16)
            iqsb = sbp("iqsb", [128, 256], BF16)
            aqT = sbp("aqT", [128, 512], BF16)
            iqsT = sbp("iqsT", [64, 512], BF16)
            rls = [sbp("rl%d" % j, [128, 512]) for j in range(2)]
            Es = [sbp("E%d" % j, [128, 512], BF16) for j in range(2)]
            PTs = [sbp("PT%d" % j, [128, 512], BF16) for j in range(2)]
            mTss = [sbp("mTs%d" % j, [128, 128], BF16) for j in range(2)]
            def front(t):
                r0, n = t * 128, 128
                x_load(t, n, r0)
                proj_rope(t, n, r0, d["ropeP"][t])
                K.dma("sp", o["kv_p"][i, r0:r0 + n, :], kvrow[:], r=[kvrow], w=[])
                pt = K.ps()
                for h in range(8):
                    K.tr(pt[:, h * 128:(h + 1) * 128], qk_r[:, h * 128:(h + 1) * 128], ident[:], r=[qk_r, ident], w=[(pt, h // 4)])
                K.cp("act", qT[:], pt[:, 0:512], r=[(pt, 0)], w=[qT])
                K.tt("dve", qTd[:], pt[:, 0:512], kc["qdecB"][:], ALU.mult, r=[(pt, 0), kc["qdecB"]], w=[qTd])
                K.cp("act", kT[:], pt[:, 512:1024], r=[(pt, 1)], w=[kT])
                pa = K.ps()
                for h in range(4):
                    hs = slice(h * 128, (h + 1) * 128)
                    K.mm(pa[:, hs], kT[:, hs], qT[:, hs], True, True, r=[kT, qT], w=[(pa, 0)])
                K.tt("dve", AT[:], pa[:, 0:512], kc["dmatT"][:], ALU.mult, r=[(pa, 0), kc["dmatT"]], w=[AT])
                K.tt("pool", kd[:].rearrange("p (h e) -> p h e", h=4), qk_r[:, 512:1024].rearrange("p (h e) -> p h e", h=4),
                     kdecC[:].unsqueeze(2).to_broadcast([128, 4, 128]), ALU.mult, r=[qk_r, kdecC], w=[kd])
                po = K.ps()
                for h in range(4):
                    hs = slice(h * 128, (h + 1) * 128)
                    K.mm(po[:, hs], AT[:, hs], pj[:, 1024 + h * 128:1024 + (h + 1) * 128], True, False, r=[AT, pj], w=[(po, 0)])
                    K.mm(po[:, hs], qTd[:, hs], Sret[:, hs], False, True, r=[qTd, Sret], w=[(po, 0)])
                pss = po
                for h in range(4):
                    hs = slice(h * 128, (h + 1) * 128)
                    K.mm(pss[:, 512 + h * 128:512 + (h + 1) * 128], kd[:, hs], pj[:, 1024 + h * 128:1024 + (h + 1) * 128], True, True, r=[kd, pj], w=[(pss, 1)])
                K.tt("dve", Sret[:], Sret[:], kc["cdecB"][:], ALU.mult, r=[Sret, kc["cdecB"]], w=[Sret])
                K.tt("dve", Sret[:], Sret[:], pss[:, 512:1024], ALU.add, r=[Sret, (pss, 1)], w=[Sret])
                head_norm_gate(n, po[0:n, 0:512], [(po, 0)])
                if t == NT - 1:
                    for h in range(4):
                        K.dma("sp", o["ret_p"][i, h], Sret[:, h * 128:(h + 1) * 128], r=[Sret], w=[])
                K.cp("pool", akb[:], kvrow[:, 0:128], r=[kvrow], w=[akb])
                K.cp("pool", ikb[:], kvrow[:, 256:320], r=[kvrow], w=[ikb])
                K.tt("dve", iqsb[:].rearrange("p (h e) -> p h e", h=4), iq_r[:].rearrange("p (h e) -> p h e", h=4),
                     aw[:, 0:4].unsqueeze(2).to_broadcast([128, 4, 64]), ALU.mult, r=[iq_r, aw], w=[iqsb])
                pt = K.ps()
                pbb = pt[:].bitcast(BF16)
                K.tr(pbb[:, 0:128], akb[:], identb[:], r=[akb, identb], w=[(pt, 0)])
                K.tr(pbb[0:64, 128:256], ikb[:], identb[:], r=[ikb, identb], w=[(pt, 0)])
                for h in range(4):
                    K.tr(pbb[:, 256 + h * 128:256 + (h + 1) * 128], aq_rb[:, h * 128:(h + 1) * 128], identb[:], r=[aq_rb, identb], w=[(pt, 0)])
                    K.tr(pbb[0:64, 1024 + h * 128:1024 + (h + 1) * 128], iqsb[:, h * 64:(h + 1) * 64], identb[:], r=[iqsb, identb], w=[(pt, 1)])
                K.cp("act", akT_all[:, r0:r0 + 128], pbb[:, 0:128], r=[(pt, 0)], w=[(akT_all, t)])
                K.cp("act", ikT_all[:, r0:r0 + 128], pbb[0:64, 128:256], r=[(pt, 0)], w=[ikT_all])
                K.cp("act", aqT[:], pbb[:, 256:768], r=[(pt, 0)], w=[aqT])
                K.cp("act", iqsT[:], pbb[0:64, 1024:1536], r=[(pt, 1)], w=[iqsT])
                K.cp("pool", Vall[:, t, 0:128], pj[:, 2688:2816], r=[pj], w=[(Vall, t)])
                n_k = (t + 1) * 128
                for kb in range((n_k + 511) // 512):
                    k0 = kb * 512
                    wb = min(512, n_k - k0)
                    pts = [K.ps(), K.ps()]
                    for h in range(4):
                        pp, hb = pts[h // 2], h % 2
                        K.mm(pp[:, hb * 512:hb * 512 + wb], iqsT[:, h * 128:(h + 1) * 128], ikT_all[:, k0:k0 + wb], True, True, r=[iqsT, ikT_all], w=[(pp, hb)])
                        rl = rls[h % 2]
                        K.act(rl[:, 0:wb], pp[:, hb * 512:hb * 512 + wb], AF.Relu, r=[(pp, hb)], w=[rl])
                        if h == 0:
                            K.ts("dve", score[:, k0:k0 + wb], rl[:, 0:wb], aw[:, 4:5], ALU.mult, r=[rl, aw], w=[score])
                        else:
                            K.stt("dve", score[:, k0:k0 + wb], rl[:, 0:wb], aw[:, 4 + h:5 + h], score[:, k0:k0 + wb], ALU.mult, ALU.add, r=[rl, aw, score], w=[score])
                if n_k > TOPK:
                    K.red("dve", bs[:, 0:1], score[:, 0:n_k], r=[score], w=[bs], op=ALU.max)
                    K.red("dve", bs[:, 1:2], score[:, 0:n_k], r=[score], w=[bs], op=ALU.min)
                    K.tt("dve", bs[:, 2:3], bs[:, 0:1], bs[:, 1:2], ALU.subtract, r=[bs], w=[bs])
                else:
                    K.memset("dve", bs[:, 1:2], -1e29, w=[bs])
                K.tt("dve", score[:, r0:r0 + 128], score[:, r0:r0 + 128], cst["cmask"][:], ALU.add, r=[score, cst["cmask"]], w=[score])

            def bis(t):
                n_k = (t + 1) * 128
                if n_k > TOPK:
                    K.ts("dve", Hh[:, 0:NIT], cst["pw2"][:, 0:NIT], bs[:, 2:3], ALU.mult, r=[bs, cst["pw2"]], w=[Hh])
                    for it in range(NIT):
                        K.tt("dve", bs[:, 4:5], bs[:, 1:2], Hh[:, it:it + 1], ALU.add, r=[bs, Hh], w=[bs])
                        K.ts("dve", junk[:, 0:n_k], score[:, 0:n_k], bs[:, 4:5], ALU.is_ge, r=[score, bs], w=[junk, cnt], s2=0.0, op1=ALU.add, acc=cnt[:, it:it + 1])
                        K.ts("dve", bs[:, 5:6], cnt[:, it:it + 1], float(TOPK), ALU.is_ge, r=[cnt], w=[bs])
                        K.stt("dve", bs[:, 1:2], bs[:, 5:6], Hh[:, it:it + 1], bs[:, 1:2], ALU.mult, ALU.add, r=[bs, Hh], w=[bs])

            def tail(t):
                n_k = (t + 1) * 128
                K.ts("dve", maskb[:, 0:n_k], score[:, 0:n_k], bs[:, 1:2], ALU.is_ge, r=[score, bs], w=[maskb])

            def back(t):
                r0, n = t * 128, 128
                def att_front(kt):
                    ks = slice(kt * 128, (kt + 1) * 128)
                    pm = K.ps()
                    pmb = pm[:].bitcast(BF16)
                    K.tr(pmb[:, 0:128], maskb[:, ks], identb[:], r=[maskb, identb], w=[(pm, 0)])
                    K.mm(pm[:, 512:1024], akT_all[:, ks], aqT[:], True, True, r=[(akT_all, kt), aqT], w=[(pm, 1)])
                    E, PT, mTs = Es[kt % 2], PTs[kt % 2], mTss[kt % 2]
                    K.act(E[:], pm[:, 512:1024], AF.Exp, r=[(pm, 1)], w=[E], scale=SC_ATT)
                    K.cp("act", mTs[:], pmb[:, 0:128], r=[(pm, 0)], w=[mTs])
                    K.tt("pool", PT[:].rearrange("p (h q) -> p h q", h=4), E[:].rearrange("p (h q) -> p h q", h=4),
                         mTs[:].unsqueeze(1).to_broadcast([128, 4, 128]), ALU.mult, r=[E, mTs], w=[PT])

                def att_back(kt, t=t):
                    PT = PTs[kt % 2]
                    for h in range(4):
                        c0 = (h % 2) * 512
                        K.mm(PSO[h // 2][:, c0:c0 + 130], PT[:, h * 128:(h + 1) * 128], Vall[:, kt, 0:130], kt == 0, kt == t, r=[PT, (Vall, kt)], w=[(PSO[h // 2], h % 2)], dedicated=True)

                att_front(0)
                for kt in range(t + 1):
                    if kt + 1 <= t:
                        att_front(kt + 1)
                    att_back(kt)
                for h in range(4):
                    c0 = (h % 2) * 512
                    pso = PSO[h // 2]
                    K.S.op("dve", lambda e, pso=pso, c0=c0, h=h: e.reciprocal(out=rs[:, h:h + 1], in_=pso[:, c0 + 128:c0 + 129]), r=[(pso, h % 2)], w=[rs])
                    K.ts("dve", cat[:, 512 + h * 128:512 + (h + 1) * 128], pso[:, c0:c0 + 128], rs[:, h:h + 1], ALU.mult, r=[(pso, h % 2), rs], w=[cat])
                out_proj_ln(K, t, n, cat, catT, wo, xt, z, xo, st)

            xts2 = [xt, K.sb(pp, "e_xt2", [128, D], F32)]
            cats2 = [cat, K.sb(pp, "e_cat2", [128, D], BF16)]
            aqTs2 = [aqT, K.sb(pp, "e_aqT2", [128, 512], BF16)]
            PSg = c["PS"]
            xt, cat, aqT = xts2[0], cats2[0], aqTs2[0]
            K.PSG = [PSg[0]]
            front(0)
            bis(0)
            tail(0)
            for t in range(NT):
                A = []
                if t + 1 < NT:
                    xt, cat, aqT = xts2[(t + 1) % 2], cats2[(t + 1) % 2], aqTs2[(t + 1) % 2]
                    K.PSG = [PSg[0]]
                    front(t + 1)
                    K.S.begin_capture()
                    bis(t + 1)
                    A = K.S.end_capture()
                xt, cat, aqT = xts2[t % 2], cats2[t % 2], aqTs2[t % 2]
                K.PSG = [PSg[1]]
                K.S.begin_capture()
                back(t)
                Bc = K.S.end_capture()
                K.S.emit_units(A, Bc)
                if t + 1 < NT:
                    tail(t + 1)
            K.PSG = PSg[:2]
            xt, cat, aqT = xts2[0], cats2[0], aqTs2[0]
        if not c["cfg"].get("skip_sample"):
            sample_even(K, l, pb, locals())
    K.PSG = c["PS"][:4]


def bisect_thr(K, n, score, n_k, topk, bs, cnt, junk, nit):
    K.red("dve", bs[0:n, 0:1], score[0:n, 0:n_k], r=[score], w=[bs], op=ALU.max)
    K.red("dve", bs[0:n, 1:2], score[0:n, 0:n_k], r=[score], w=[bs], op=ALU.min)
    K.tt("dve", bs[0:n, 2:3], bs[0:n, 0:1], bs[0:n, 1:2], ALU.subtract, r=[bs], w=[bs])
    K.memset("pool", cnt[0:n, :], 0.0, w=[cnt])
    for it in range(nit):
        K.ts("dve", bs[0:n, 3:4], bs[0:n, 2:3], 0.5 ** (it + 1), ALU.mult, r=[bs], w=[bs])
        K.tt("dve", bs[0:n, 4:5], bs[0:n, 1:2], bs[0:n, 3:4], ALU.add, r=[bs], w=[bs])
        K.ts("dve", junk[0:n, 0:n_k], score[0:n, 0:n_k], bs[0:n, 4:5], ALU.is_ge, r=[score, bs, cnt], w=[junk, cnt], s2=0.0, op1=ALU.add, acc=cnt[0:n, it:it + 1])
        K.ts("dve", bs[0:n, 5:6], cnt[0:n, it:it + 1], float(topk), ALU.is_ge, r=[cnt], w=[bs])
        K.stt("dve", bs[0:n, 1:2], bs[0:n, 5:6], bs[0:n, 3:4], bs[0:n, 1:2], ALU.mult, ALU.add, r=[bs], w=[bs])


def sample_even(K, l, pb, L):
    c = K.ctx
    f = K.fn
    d, o, NT, NPG = c["d"], c["o"], c["NT"], c["NPG"]
    cst, ident, PSO = c["cst"], c["ident"], c["PSO"]
    i = l // 2
    TOPK = c["TOPK_S"]
    NIT = 15
    SC_ATT = 128 ** -0.5
    n, t, r0 = NS, NT, c["NTOK"]
    NK = NPG * 128
    pj, xt, qk_r, kvrow, iq_r, t1, t2 = [L[k] for k in ("pj", "xt", "qk_r", "kvrow", "iq_r", "t1", "t2")]
    AT, sq, aw, s4, cat, catT, z, xo, st, wo, kc, bs, cnt = [L[k] for k in ("AT", "sq", "aw", "s4", "cat", "catT", "z", "xo", "st", "wo", "kc", "bs", "cnt")]
    g4 = lambda ap: ap.rearrange("p (h e) -> p h e", h=4)
    with K.scope() as px:
        sb = lambda nm, sh, dt=F32: K.sb(px, "es_" + nm, sh, dt)
        aq_f = sb("aq_f", [NS, 512])
        f["load_sample_mod"](l, 0)
        L["x_load"](t, n, r0)
        L["proj_rope"](t, n, r0, d["ropeS"][:, :], aq_f)
        K.dma("sp", o["kv_s"][i, :, :], kvrow[0:n, :], r=[kvrow], w=[])
        v_ap = pj[0:n, 1024:1536]
        K.tt("dve", sq[0:n, :], qk_r[0:n, 0:512], qk_r[0:n, 512:1024], ALU.mult, r=[qk_r], w=[sq])
        K.red("dve", s4[0:n, 0:4], g4(sq[0:n, :]), r=[sq], w=[s4])
        K.ts("dve", s4[0:n, 0:4], s4[0:n, 0:4], 128 ** -0.5, ALU.mult, r=[s4], w=[s4])
        pt = K.ps()
        for h in range(4):
            K.tr(pt[:, h * 16:(h + 1) * 16], qk_r[0:n, h * 128:(h + 1) * 128], ident[0:n, 0:n], r=[qk_r, ident], w=[(pt, 0)])
        qTs = sb("qTs", [128, 64])
        K.cp("act", qTs[:], pt[:, 0:64], r=[(pt, 0)], w=[qTs])
        K.tt("dve", t1[:, 0:1024].rearrange("p (h s m) -> p h s m", h=4, s=16),
             qTs[:].rearrange("p (h s) -> p h s", h=4).unsqueeze(3).to_broadcast([128, 4, 16, 16]),
             cst["delta16"][:].rearrange("p (s m) -> p s m", s=16).unsqueeze(1).to_broadcast([128, 4, 16, 16]),
             ALU.mult, r=[qTs, cst["delta16"]], w=[t1])
        S0 = [sb("S0%d" % j, [128, 512]) for j in range(2)]
        Sn = [sb("Sn%d" % j, [128, 512]) for j in range(2)]
        km = sb("km", [NS, 512])
        for s in range(NS):
            S0s, Sns = S0[s % 2], Sn[s % 2]
            for h in range(4):
                K.dma("sp", S0s[:, h * 128:(h + 1) * 128], d["st_ret"][i, s, h], r=[], w=[S0s])
            for h in range(4):
                hs = slice(h * 128, (h + 1) * 128)
                K.mm(PSO[h // 2][0:n, (h % 2) * 512:(h % 2) * 512 + 128], t1[:, (h * 16 + s) * 16:(h * 16 + s + 1) * 16], S0s[:, hs], s == 0, s == NS - 1, r=[t1, S0s], w=[(PSO[h // 2], h % 2)])
            K.ts("dve", km[:], qk_r[0:n, 512:1024], ident[0:n, s:s + 1], ALU.mult, r=[qk_r, ident], w=[km], s2=128 ** -0.5, op1=ALU.mult)
            pss = K.ps()
            for h in range(4):
                hs = slice(h * 128, (h + 1) * 128)
                K.mm(pss[:, hs], km[:, hs], pj[0:n, 1024 + h * 128:1024 + (h + 1) * 128], True, True, r=[km, pj], w=[(pss, 0)])
            K.tt("pool", Sns[:], S0s[:], kc["gam1B"][:], ALU.mult, r=[S0s, kc["gam1B"]], w=[Sns])
            K.tt("dve", Sns[:], Sns[:], pss[:, 0:512], ALU.add, r=[Sns, (pss, 0)], w=[Sns])
            for h in range(4):
                K.dma("sp", o["ret_s"][i, s, h], Sns[:, h * 128:(h + 1) * 128], r=[Sns], w=[])
        for h in range(4):
            hs = slice(h * 128, (h + 1) * 128)
            K.tt("dve", AT[0:n, hs], PSO[h // 2][0:n, (h % 2) * 512:(h % 2) * 512 + 128], kc["gam1B"][0:n, hs], ALU.mult, r=[(PSO[h // 2], h % 2), kc["gam1B"]], w=[AT])
        K.tt("pool", g4(sq[0:n, :]), g4(v_ap), s4[0:n, 0:4].unsqueeze(2).to_broadcast([n, 4, 128]), ALU.mult, r=[pj, s4], w=[sq])
        K.tt("dve", AT[0:n, :], AT[0:n, :], sq[0:n, :], ALU.add, r=[AT, sq], w=[AT])
        L["head_norm_gate"](n, AT[0:n, :], [AT])
        iqs = sb("iqs", [NS, 256])
        K.tt("dve", g4(iqs[:]), g4(iq_r[0:n, :]), aw[0:n, 0:4].unsqueeze(2).to_broadcast([n, 4, 64]), ALU.mult, r=[iq_r, aw], w=[iqs])
        pt = K.ps()
        for h in range(4):
            K.tr(pt[:, h * 16:(h + 1) * 16], aq_f[:, h * 128:(h + 1) * 128], ident[0:n, 0:n], r=[aq_f, ident], w=[(pt, 0)])
            K.tr(pt[0:64, 512 + h * 16:512 + (h + 1) * 16], iqs[:, h * 64:(h + 1) * 64], ident[0:n, 0:n], r=[iqs, ident], w=[(pt, 1)])
        aqTa = sb("aqTa", [128, 64])
        iqTa = sb("iqTa", [64, 64])
        K.cp("act", aqTa[:], pt[:, 0:64], r=[(pt, 0)], w=[aqTa])
        K.cp("act", iqTa[:], pt[0:64, 512:576], r=[(pt, 1)], w=[iqTa])
        sgnB = sb("sgnB", [128, NS * 4])
        sg_d = K.dram("sg_d%d" % l, [1, NS * 4], F32, "Internal")
        K.dma("sp", sg_d[0:1, :].rearrange("o (s h) -> (o s) h", h=4), aw[0:n, 4:8], r=[aw], w=[sg_d])
        K.dma("sp", sgnB[:], sg_d[0:1, :].partition_broadcast(128), r=[sg_d], w=[sgnB])
        idxi = sb("idxi", [128, NS * NPG], I32)
        idxf = sb("idxf", [128, NS * NPG])
        K.dma("sp", idxi[:], d["ptab"][:, :].rearrange("(o s) j -> o (s j)", o=1).partition_broadcast(128), r=[], w=[idxi])
        K.cp("dve", idxf[:], idxi[:], r=[idxi], w=[idxf])
        K.ts("dve", idxf[:], idxf[:], 128.0, ALU.mult, r=[idxf, cst["iotap"]], w=[idxf], s2=cst["iotap"][:, 0:1], op1=ALU.add)
        if i > 0:
            K.ts("dve", idxf[:], idxf[:], float(i * c["cfg"]["NPOOL"] * 128), ALU.add, r=[idxf], w=[idxf])
        K.cp("dve", idxi[:], idxf[:], r=[idxf], w=[idxi])
        cache = d["cache_kv"]
        sc_d = K.dram("sc_d%d" % l, [NS, NK], F32, "Internal")
        os_d = K.dram("os_d%d" % l, [4, NS, 132], F32, "Internal")
        kig = [sb("kig%d" % j, [128, NPG, 320]) for j in range(2)]
        kTs = sb("kTs", [128, NK + 4])
        kiT = kTs
        scT = sb("scT", [128, NS * NPG])
        rlS = sb("rlS", [128, NPG * 4])
        scr_t = sb("scr_t", [NPG, 128])
        for s in range(NS):
            kg = kig[s % 2]
            for j in range(NPG):
                col = s * NPG + j
                K.S.dma("pool", lambda e, kg=kg, j=j, col=col: e.indirect_dma_start(
                    out=kg[:, j, :], out_offset=None, in_=cache[:, :],
                    in_offset=bass.IndirectOffsetOnAxis(ap=idxi[:, col:col + 1], axis=0)), r=[idxi], w=[kg])
            for g in range((NPG + 3) // 4):
                pt = K.ps()
                nj = min(4, NPG - g * 4)
                for jj in range(nj):
                    K.tr(pt[0:64, jj * 128:(jj + 1) * 128], kg[:, g * 4 + jj, 256:320], ident[:], r=[kg, ident], w=[(pt, 0)])
                K.cp(K.evq(), kiT[0:64, g * 512:g * 512 + nj * 128], pt[0:64, 0:nj * 128], r=[(pt, 0)], w=[kiT])
            pq = K.ps()
            for j in range(NPG):
                K.mm(pq[:, j * 4:(j + 1) * 4], kiT[0:64, j * 128:(j + 1) * 128], iqTa[:].rearrange("p (h s) -> p h s", h=4)[:, :, s], True, True, r=[kiT, iqTa], w=[(pq, 0)])
            K.act(rlS[:], pq[:, 0:NPG * 4], AF.Relu, r=[(pq, 0)], w=[rlS])
            rv = rlS[:].rearrange("p (j h) -> p j h", h=4)
            K.tt("dve", rv, rv, sgnB[:, s * 4:(s + 1) * 4].unsqueeze(1).to_broadcast([128, NPG, 4]), ALU.mult, r=[rlS, sgnB], w=[rlS])
            K.red("dve", scT[:, s * NPG:(s + 1) * NPG], rv, r=[rlS], w=[scT])
            pt2 = K.ps()
            K.tr(pt2[0:NPG, 0:128], scT[:, s * NPG:(s + 1) * NPG], ident[:], r=[scT, ident], w=[(pt2, 0)])
            K.cp("act", scr_t[:], pt2[0:NPG, 0:128], r=[(pt2, 0)], w=[scr_t])
            K.dma("sp", sc_d[s, :].rearrange("(j p) -> j p", p=128), scr_t[:], r=[scr_t], w=[sc_d])
        K.tt("dve", t2[0:n, 0:256].rearrange("p (h e) -> p h e", h=4), g4(iqs[:]), kvrow[0:n, 256:320].unsqueeze(1).to_broadcast([n, 4, 64]), ALU.mult, r=[iqs, kvrow], w=[t2])
        K.red("dve", s4[0:n, 4:8], t2[0:n, 0:256].rearrange("p (h e) -> p h e", h=4), r=[t2], w=[s4])
        K.ts("dve", s4[0:n, 4:8], s4[0:n, 4:8], 0.0, ALU.max, r=[s4], w=[s4])
        K.tt("dve", s4[0:n, 4:8], s4[0:n, 4:8], aw[0:n, 4:8], ALU.mult, r=[s4, aw], w=[s4])
        srow = sb("srow", [NS, NK + 1])
        jrow = kTs
        K.dma("sp", srow[:, 0:NK], sc_d[:, :], r=[sc_d], w=[srow])
        K.red("dve", srow[:, NK:NK + 1], s4[0:n, 4:8], r=[s4, srow], w=[srow])
        if NK + 1 > TOPK:
            bisect_thr(K, n, srow, NK + 1, TOPK, bs, cnt, jrow, NIT)
        else:
            K.memset("dve", bs[0:n, 1:2], -1e29, w=[bs])
        dth = sb("dth", [NS, NS])
        K.ts("dve", dth[:], ident[0:NS, 0:NS], bs[0:n, 1:2], ALU.mult, r=[ident, bs], w=[dth])
        pq = K.ps()
        K.mm(pq[:, 0:NS], cst["ones"][0:NS, :], dth[:], True, True, r=[cst["ones"], dth], w=[(pq, 0)])
        thrB = sb("thrB", [128, NS])
        K.cp("act", thrB[:], pq[:, 0:NS], r=[(pq, 0)], w=[thrB])
        kvg = kig
        mT = sb("mT", [128, NPG])
        Es = sb("Es", [128, NPG * 4])
        PT2 = sb("PT2", [128, NPG * 4])
        osm = sb("osm", [4, NS, 132])
        K.memset("pool", osm[:], 0.0, w=[osm])
        for s in range(NS):
            kv = kvg[s % 2]
            for j in range(NPG):
                col = s * NPG + j
                K.S.dma("pool", lambda e, kv=kv, j=j, col=col: e.indirect_dma_start(
                    out=kv[:, j, :], out_offset=None, in_=cache[:, :],
                    in_offset=bass.IndirectOffsetOnAxis(ap=idxi[:, col:col + 1], axis=0)), r=[idxi], w=[kv])
            for g in range((NPG + 3) // 4):
                pt = K.ps()
                nj = min(4, NPG - g * 4)
                for jj in range(nj):
                    K.tr(pt[:, jj * 128:(jj + 1) * 128], kv[:, g * 4 + jj, 0:128], ident[:], r=[kv, ident], w=[(pt, 0)])
                K.cp(K.evq(), kTs[:, g * 512:g * 512 + nj * 128], pt[:, 0:nj * 128], r=[(pt, 0)], w=[kTs])
            pq = K.ps()
            for j in range(NPG):
                K.mm(pq[:, j * 4:(j + 1) * 4], kTs[:, j * 128:(j + 1) * 128], aqTa[:].rearrange("p (h s) -> p h s", h=4)[:, :, s], True, True, r=[kTs, aqTa], w=[(pq, 0)])
            K.act(Es[:], pq[:, 0:NPG * 4], AF.Exp, r=[(pq, 0)], w=[Es], scale=SC_ATT)
            K.ts("dve", mT[:], scT[:, s * NPG:(s + 1) * NPG], thrB[:, s:s + 1], ALU.is_ge, r=[scT, thrB], w=[mT])
            K.tt("dve", PT2[:].rearrange("p (j h) -> p j h", h=4), Es[:].rearrange("p (j h) -> p j h", h=4),
                 mT[:].unsqueeze(2).to_broadcast([128, NPG, 4]), ALU.mult, r=[Es, mT], w=[PT2])
            po2 = K.ps()
            for j in range(NPG):
                K.mm(po2[0:4, 0:128], PT2[:, j * 4:(j + 1) * 4], kv[:, j, 128:256], j == 0, j == NPG - 1, r=[PT2, kv], w=[(po2, 0)])
            for j in range(NPG):
                K.mm(po2[0:4, 512:513], PT2[:, j * 4:(j + 1) * 4], cst["ones"][:, 0:1], j == 0, j == NPG - 1, r=[PT2, cst["ones"]], w=[(po2, 1)])
            K.cp("act", osm[:, s, 0:128], po2[0:4, 0:128], r=[(po2, 0)], w=[osm])
            K.cp("act", osm[:, s, 128:129], po2[0:4, 512:513], r=[(po2, 1)], w=[osm])
        ot = sb("ot", [NS, 4, 132])
        K.dma("sp", os_d[:, :, :], osm[:], r=[osm], w=[os_d])
        K.dma("sp", ot[:], os_d[:, :, :].rearrange("h s e -> s h e"), r=[os_d], w=[ot])
        K.tt("dve", g4(t2[0:n, 0:512]), g4(aq_f[:]), kvrow[0:n, 0:128].unsqueeze(1).to_broadcast([n, 4, 128]), ALU.mult, r=[aq_f, kvrow], w=[t2])
        K.red("dve", s4[0:n, 8:12], g4(t2[0:n, 0:512]), r=[t2], w=[s4])
        K.act(s4[0:n, 8:12], s4[0:n, 8:12], AF.Exp, r=[s4], w=[s4], scale=SC_ATT)
        K.tt("dve", bs[0:n, 6:7], srow[:, NK:NK + 1], bs[0:n, 1:2], ALU.is_ge, r=[srow, bs], w=[bs])
        K.ts("dve", s4[0:n, 8:12], s4[0:n, 8:12], bs[0:n, 6:7], ALU.mult, r=[s4, bs], w=[s4])
        K.tt("dve", g4(t1[0:n, 0:512]), kvrow[0:n, 128:256].unsqueeze(1).to_broadcast([n, 4, 128]),
             s4[0:n, 8:12].unsqueeze(2).to_broadcast([n, 4, 128]), ALU.mult, r=[kvrow, s4], w=[t1])
        K.tt("dve", g4(t1[0:n, 0:512]), g4(t1[0:n, 0:512]), ot[:, :, 0:128], ALU.add, r=[t1, ot], w=[t1])
        K.tt("dve", s4[0:n, 12:16], ot[:, :, 128], s4[0:n, 8:12], ALU.add, r=[ot, s4], w=[s4])
        K.S.op("dve", lambda e: e.reciprocal(out=s4[0:n, 12:16], in_=s4[0:n, 12:16]), r=_bufs([s4]), w=_bufs([s4]))
        K.tt("dve", g4(cat[0:n, 512:1024]), g4(t1[0:n, 0:512]), s4[0:n, 12:16].unsqueeze(2).to_broadcast([n, 4, 128]), ALU.mult, r=[t1, s4], w=[cat])
        out_proj_ln(K, t, n, cat, catT, wo, xt, z, xo, st)


def phase_b_odd(K, l):
    c = K.ctx
    f = K.fn
    d, o, NT = c["d"], c["o"], c["NT"]
    cst, ident, PS = c["cst"], c["ident"], c["PS"]
    jl = l // 2
    NTOK = c["NTOK"]
    K.PSG = PS[:4]
    f["load_phase_consts"](l, 0)
    g8 = lambda ap: ap.rearrange("p (h e) -> p h e", h=8)
    bc8 = lambda ap, n, w: ap.unsqueeze(2).to_broadcast([n, 8, w])
    with K.scope() as pb:
        sb = lambda nm, sh, dt=F32: K.sb(pb, "o_" + nm, sh, dt)
        wo = sb("wo", [128, 8, D], BF16)
        f["load_weights_bf16"](wo, d["odd_w_out"][jl], D)
        cw = sb("cw", [128, 4, 3072])
        for tap in range(4):
            K.dma("sp", cw[:, tap, :], d["odd_conv"][jl, tap:tap + 1, :].partition_broadcast(128), r=[], w=[cw])
        gnB = sb("gnB", [128, 128])
        K.dma("sp", gnB[:], d["odd_gn"][jl:jl + 1, :].partition_broadcast(128), r=[], w=[gnB])
        nea = sb("nea", [128, 8])
        dtb = sb("dtb", [128, 8])
        K.dma("sp", nea[:], d["odd_a_log"][jl:jl + 1, :].partition_broadcast(128), r=[], w=[nea])
        K.dma("sp", dtb[:], d["odd_dt_bias"][jl:jl + 1, :].partition_broadcast(128), r=[], w=[dtb])
        K.act(nea[:], nea[:], AF.Exp, r=[nea], w=[nea])
        K.ts("dve", nea[:], nea[:], -1.0, ALU.mult, r=[nea], w=[nea])
        pj = sb("pj", [128, ODD_IN])
        xt = sb("xt", [128, D])
        sh = sb("sh", [128, 3072])
        cat = sb("cat", [128, D], BF16)
        catT = sb("catT", [128, 8, 128], BF16)
        z = sb("z", [128, D])
        xo = sb("xo", [128, D])
        st = sb("st", [128, 4])
        sm = sb("sm", [128, 96])
        ob = sb("ob", [128, 1024])
        sq = sh

        def x_load(t, n, r0):
            K.dma("sp", xt[0:n, :], c["x_a"][r0:r0 + n, :], r=[c["tb_xa"][t]], w=[xt])

        def conv_gates(t, n, r0, taps):
            K.dma("sp", pj[0:n, :], c["proj_d"][3 + r0:3 + r0 + n, :], r=[c["tb_pj"][t]], w=[pj])
            pq = pj[0:n, 0:3072]
            K.tt("pool", pq, pq, cw[0:n, 3, :], ALU.mult, r=[pj, cw], w=[pj])
            for tap in range(3):
                src, deps = taps[tap]
                K.dma("sp", sh[0:n, :], src, r=deps, w=[sh])
                K.tt("pool", sh[0:n, :], sh[0:n, :], cw[0:n, tap, :], ALU.mult, r=[sh, cw], w=[sh])
                K.tt("dve", pq, pq, sh[0:n, :], ALU.add, r=[pj, sh], w=[pj])
            K.act(pj[0:n, 0:4096], pj[0:n, 0:4096], AF.Silu, r=[pj], w=[pj])
            K.tt("pool", sq[0:n, 0:2048], pj[0:n, 0:2048], pj[0:n, 0:2048], ALU.mult, r=[pj], w=[sq])
            K.red("dve", sm[0:n, 16:32], sq[0:n, 0:2048].rearrange("p (h e) -> p h e", h=16), r=[sq], w=[sm])
            K.rsqrt("dve", sm[0:n, 16:32], sm[0:n, 16:32], r=[sm], w=[sm], mult=1.0, add=1e-6)
            K.ts("dve", sm[0:n, 16:24], sm[0:n, 16:24], 128 ** -0.5, ALU.mult, r=[sm], w=[sm])
            qk = pj[0:n, 0:2048].rearrange("p (h e) -> p h e", h=16)
            K.tt("dve", qk, qk, sm[0:n, 16:32].unsqueeze(2).to_broadcast([n, 16, 128]), ALU.mult, r=[pj, sm], w=[pj])
            K.tt("dve", sm[0:n, 32:40], pj[0:n, 4096:4104], dtb[0:n, :], ALU.add, r=[pj, dtb], w=[sm])
            K.act(sm[0:n, 40:48], sm[0:n, 32:40], AF.Abs, r=[sm], w=[sm])
            K.act(sm[0:n, 40:48], sm[0:n, 40:48], AF.Exp, r=[sm], w=[sm], scale=-1.0)
            K.act(sm[0:n, 40:48], sm[0:n, 40:48], AF.Ln, r=[sm], w=[sm], bias=1.0)
            K.ts("dve", sm[0:n, 32:40], sm[0:n, 32:40], 0.0, ALU.max, r=[sm], w=[sm])
            K.tt("dve", sm[0:n, 32:40], sm[0:n, 32:40], sm[0:n, 40:48], ALU.add, r=[sm], w=[sm])
            K.tt("dve", sm[0:n, 0:8], sm[0:n, 32:40], nea[0:n, :], ALU.mult, r=[sm, nea], w=[sm])
            K.act(sm[0:n, 8:16], pj[0:n, 4104:4112], AF.Sigmoid, r=[pj], w=[sm])

        def rms_gate(n, src, rsrc):
            K.tt("pool", sq[0:n, 0:1024], src, src, ALU.mult, r=rsrc, w=[sq])
            K.red("dve", sm[0:n, 48:56], g8(sq[0:n, 0:1024]), r=[sq], w=[sm])
            K.rsqrt("dve", sm[0:n, 48:56], sm[0:n, 48:56], r=[sm], w=[sm], mult=1.0 / 128, add=LN_EPS)
            K.tt("dve", g8(sq[0:n, 0:1024]), g8(src), bc8(sm[0:n, 48:56], n, 128), ALU.mult, r=rsrc + [sm], w=[sq])
            K.tt("pool", g8(sq[0:n, 0:1024]), g8(sq[0:n, 0:1024]), gnB[0:n, :].unsqueeze(1).to_broadcast([n, 8, 128]), ALU.mult, r=[sq, gnB], w=[sq])
            K.tt("dve", cat[0:n, :], sq[0:n, 0:1024], pj[0:n, 3072:4096], ALU.mult, r=[sq, pj], w=[cat])

        with K.scope() as pp:
            G = [K.sb(pp, "o_G%d" % j, [128, 1024], F32) for j in range(15)]
            Sst = K.sb(pp, "o_S", [128, 1024], F32)
            K.memset("pool", Sst[:], 0.0, w=[Sst])
            Dg, E, ES, _, kb, kbg, vb, kg, qT, kT, kbT, Nn, Mm, aqkT, R = G
            Pa, Pb, Qa, Qb = G[0], G[1], G[2], G[3]
            u_, wT, vnew = G[4], G[9], G[10]
            identB = ident[:].unsqueeze(1).to_broadcast([128, 8, 128])
            for t in range(NT):
                r0, n = t * 128, 128
                x_load(t, n, r0)
                prevb = c["tb_pj"][t - 1] if t > 0 else c["tb_pj"][NT + 1]
                taps = [(c["proj_d"][r0 + tap:r0 + tap + 128, 0:3072], [c["tb_pj"][t], prevb]) for tap in range(3)]
                conv_gates(t, n, r0, taps)
                q_ap, k_ap, v_ap = pj[:, 0:1024], pj[:, 1024:2048], pj[:, 2048:3072]
                g_, be = sm[:, 0:8], sm[:, 8:16]
                pg = K.ps()
                K.mm(pg[:, 0:8], cst["triu"][:], g_, True, True, r=[cst["triu"], sm], w=[(pg, 0)])
                gc = sm[:, 56:64]
                K.cp("act", gc, pg[:, 0:8], r=[(pg, 0)], w=[sm])
                K.tt("pool", g8(Dg[:]), identB, bc8(gc, 128, 128), ALU.mult, r=[ident, sm], w=[Dg])
                pr = K.ps()
                K.mm(pr[:, 0:512], cst["ones"][:], Dg[:, 0:512], True, True, r=[cst["ones"], Dg], w=[(pr, 0)])
                K.mm(pr[:, 512:1024], cst["ones"][:], Dg[:, 512:1024], True, True, r=[cst["ones"], Dg], w=[(pr, 1)])
                K.tt("dve", g8(E[:]), g8(pr[:]), bc8(gc, 128, 128), ALU.subtract, r=[pr, sm], w=[E])
                K.tt("dve", g8(E[:]), g8(E[:]), cst["mvinc"][:].unsqueeze(1).to_broadcast([128, 8, 128]), ALU.min, r=[E, cst["mvinc"]], w=[E])
                K.act(E[:], E[:], AF.Exp, r=[E], w=[E])
                K.tt("pool", g8(ES[:]), g8(E[:]), cst["strict"][:].unsqueeze(1).to_broadcast([128, 8, 128]), ALU.mult, r=[E, cst["strict"]], w=[ES])
                gl = sm[:, 64:72]
                K.cp("act", gl, g8(pr[:])[:, :, 127], r=[pr], w=[sm])
                K.tt("dve", sm[:, 72:80], gl, gc, ALU.subtract, r=[sm], w=[sm])
                K.act(sm[:, 72:80], sm[:, 72:80], AF.Exp, r=[sm], w=[sm])
                K.act(sm[:, 80:88], gl, AF.Exp, r=[sm], w=[sm])
                K.act(sm[:, 88:96], gc, AF.Exp, r=[sm], w=[sm])
                K.tt("pool", g8(kb[:]), g8(k_ap), bc8(be, 128, 128), ALU.mult, r=[pj, sm], w=[kb])
                K.tt("pool", g8(kbg[:]), g8(kb[:]), bc8(sm[:, 88:96], 128, 128), ALU.mult, r=[kb, sm], w=[kbg])
                K.tt("pool", g8(vb[:]), g8(v_ap), bc8(be, 128, 128), ALU.mult, r=[pj, sm], w=[vb])
                K.tt("pool", g8(kg[:]), g8(k_ap), bc8(sm[:, 72:80], 128, 128), ALU.mult, r=[pj, sm], w=[kg])
                for src_ap, rs, dst in ((q_ap, [pj], qT), (k_ap, [pj], kT), (kb[:], [kb], kbT)):
                    pt = K.ps()
                    for h in range(8):
                        K.tr(pt[:, h * 128:(h + 1) * 128], src_ap[:, h * 128:(h + 1) * 128], ident[:], r=rs + [ident], w=[(pt, h // 4)])
                    K.cp("act", dst[:], pt[:], r=[pt], w=[dst])
                pn = K.ps()
                pa = K.ps()
                for h in range(8):
                    hs = slice(h * 128, (h + 1) * 128)
                    K.mm(pn[:, hs], kT[:, hs], kbT[:, hs], True, True, r=[kT, kbT], w=[(pn, h // 4)])
                for h in range(8):
                    hs = slice(h * 128, (h + 1) * 128)
                    K.mm(pa[:, hs], kT[:, hs], qT[:, hs], True, True, r=[kT, qT], w=[(pa, h // 4)])
                K.tt("dve", Nn[:], pn[:], ES[:], ALU.mult, r=[pn, ES], w=[Nn])
                K.tt("dve", aqkT[:], pa[:], E[:], ALU.mult, r=[pa, E], w=[aqkT])
                pm = K.ps()
                for h in range(8):
                    hs = slice(h * 128, (h + 1) * 128)
                    K.tr(pm[:, hs], Nn[:, hs], ident[:], r=[Nn, ident], w=[(pm, h // 4)])
                K.cp("act", Mm[:], pm[:], r=[pm], w=[Mm])
                K.tt("pool", g8(R[:]), identB, g8(Nn[:]), ALU.subtract, r=[ident, Nn], w=[R])
                P, Q = Mm, Nn
                Pn, Qn = [Pa, Pb], [Qa, Qb]
                for lev in range(1, 7):
                    P2, Q2 = Pn[lev % 2], Qn[lev % 2]
                    pP = K.ps()
                    for h in range(8):
                        hs = slice(h * 128, (h + 1) * 128)
                        K.mm(pP[:, hs], Q[:, hs], P[:, hs], True, True, r=[Q, P], w=[(pP, h // 4)])
                    if lev < 6:
                        pQ = K.ps()
                        for h in range(8):
                            hs = slice(h * 128, (h + 1) * 128)
                            K.mm(pQ[:, hs], P[:, hs], Q[:, hs], True, True, r=[P, Q], w=[(pQ, h // 4)])
                    K.cp("act", P2[:], pP[:], r=[pP], w=[P2])
                    if lev < 6:
                        K.cp("dve", Q2[:], pQ[:], r=[pQ], w=[Q2])
                    pR = K.ps()
                    for h in range(8):
                        hs = slice(h * 128, (h + 1) * 128)
                        K.mm(pR[:, hs], P2[:, hs], R[:, hs], True, True, r=[P2, R], w=[(pR, h // 4)])
                    K.tt("dve", R[:], R[:], pR[:], ALU.add, r=[R, pR], w=[R])
                    P, Q = P2, Q2
                pu = K.ps()
                pw = K.ps()
                for h in range(8):
                    hs = slice(h * 128, (h + 1) * 128)
                    K.mm(pu[:, hs], R[:, hs], vb[:, hs], True, True, r=[R, vb], w=[(pu, h // 4)])
                for h in range(8):
                    hs = slice(h * 128, (h + 1) * 128)
                    K.mm(pw[:, hs], kbg[:, hs], R[:, hs], True, True, r=[kbg, R], w=[(pw, h // 4)])
                K.cp("act", u_[:], pu[:], r=[pu], w=[u_])
                K.cp("act", wT[:], pw[:], r=[pw], w=[wT])
                pv = K.ps()
                for h in range(8):
                    hs = slice(h * 128, (h + 1) * 128)
                    K.mm(pv[:, hs], wT[:, hs], Sst[:, hs], True, True, r=[wT, Sst], w=[(pv, h // 4)])
                K.tt("dve", vnew[:], u_[:], pv[:], ALU.subtract, r=[u_, pv], w=[vnew])
                po1 = K.ps()
                for h in range(8):
                    hs = slice(h * 128, (h + 1) * 128)
                    K.mm(po1[:, hs], qT[:, hs], Sst[:, hs], True, True, r=[qT, Sst], w=[(po1, h // 4)])
                K.tt("dve", g8(ob[:]), g8(po1[:]), bc8(sm[:, 88:96], 128, 128), ALU.mult, r=[po1, sm], w=[ob])
                po2 = K.ps()
                for h in range(8):
                    hs = slice(h * 128, (h + 1) * 128)
                    K.mm(po2[:, hs], aqkT[:, hs], vnew[:, hs], True, True, r=[aqkT, vnew], w=[(po2, h // 4)])
                K.tt("dve", ob[:], ob[:], po2[:], ALU.add, r=[ob, po2], w=[ob])
                pS = K.ps()
                for h in range(8):
                    hs = slice(h * 128, (h + 1) * 128)
                    K.mm(pS[:, hs], kg[:, hs], vnew[:, hs], True, True, r=[kg, vnew], w=[(pS, h // 4)])
                K.tt("pool", g8(Sst[:]), g8(Sst[:]), bc8(sm[:, 80:88], 128, 128), ALU.mult, r=[Sst, sm], w=[Sst])
                K.tt("dve", Sst[:], Sst[:], pS[:], ALU.add, r=[Sst, pS], w=[Sst])
                rms_gate(n, ob[:], [ob])
                out_proj_ln(K, t, n, cat, catT, wo, xt, z, xo, st)
            for h in range(8):
                K.dma("sp", o["ssm_p"][jl, h], Sst[:, h * 128:(h + 1) * 128], r=[Sst], w=[])
            K.dma("sp", o["conv_p"][jl], c["proj_d"][NTOK:NTOK + 3, 0:3072], r=[c["tb_pj"][NT - 1]], w=[])
        if not c["cfg"].get("skip_sample"):
            sample_odd(K, l, locals())
    K.PSG = PS[:4]


def sample_odd(K, l, L):
    c = K.ctx
    f = K.fn
    d, o, NT = c["d"], c["o"], c["NT"]
    cst, ident = c["cst"], c["ident"]
    jl = l // 2
    n, t, r0 = NS, NT, c["NTOK"]
    pj, xt, sm, ob, cat, catT, z, xo, st, wo = [L[k] for k in ("pj", "xt", "sm", "ob", "cat", "catT", "z", "xo", "st", "wo")]
    g8 = lambda ap: ap.rearrange("p (h e) -> p h e", h=8)
    with K.scope() as px:
        sb = lambda nm, sh, dt=F32: K.sb(px, "os_" + nm, sh, dt)
        f["load_sample_mod"](l, 0)
        L["x_load"](t, n, r0)
        taps = [(d["st_conv"][jl, :, tap, :], []) for tap in range(3)]
        L["conv_gates"](t, n, r0, taps)
        K.dma("sp", o["conv_s"][jl, :, 0:2, :], d["st_conv"][jl, :, 1:3, :], r=[], w=[])
        K.dma("sp", o["conv_s"][jl, :, 2, :], c["proj_d"][3 + r0:3 + r0 + n, 0:3072], r=[c["tb_pj"][t]], w=[])
        K.act(sm[0:n, 32:40], sm[0:n, 0:8], AF.Exp, r=[sm], w=[sm])
        sq = L["sq"]
        K.tt("pool", sq[0:n, 0:1024], pj[0:n, 0:1024], pj[0:n, 1024:2048], ALU.mult, r=[pj], w=[sq])
        K.red("dve", sm[0:n, 40:48], g8(sq[0:n, 0:1024]), r=[sq], w=[sm])
        row_d = K.dram("gr_d%d" % l, [NS, 3072 + 24], F32, "Internal")
        orow_d = K.dram("go_d%d" % l, [NS, 1024], F32, "Internal")
        K.dma("sp", row_d[:, 0:3072], pj[0:n, 0:3072], r=[pj], w=[row_d])
        K.dma("sp", row_d[:, 3072:3080], sm[0:n, 32:40], r=[sm], w=[row_d])
        K.dma("sp", row_d[:, 3080:3088], sm[0:n, 8:16], r=[sm], w=[row_d])
        K.dma("sp", row_d[:, 3088:3096], sm[0:n, 40:48], r=[sm], w=[row_d])
        egB = sb("egB", [128, NS * 8])
        egd = K.dram("ge_d%d" % l, [1, NS * 8], F32, "Internal")
        K.dma("sp", egd[0:1, :].rearrange("o (s h) -> (o s) h", h=8), sm[0:n, 32:40], r=[sm], w=[egd])
        K.dma("sp", egB[:], egd[0:1, :].partition_broadcast(128), r=[egd], w=[egB])
        kqT = sb("kqT", [128, 8 * NS * 2])
        kq4 = kqT[:].rearrange("p (h s a) -> p h s a", h=8, a=2)
        for which, c0 in ((0, 1024), (1, 0)):
            pt = K.ps()
            for h in range(8):
                K.tr(pt[:, h * 16:(h + 1) * 16], pj[0:n, c0 + h * 128:c0 + (h + 1) * 128], ident[0:n, 0:n], r=[pj, ident], w=[(pt, 0)])
            K.cp("act", kq4[:, :, :, which], pt[:, 0:128].rearrange("p (h s) -> p h s", h=8), r=[(pt, 0)], w=[kqT])
        rows = [sb("row%d" % j, [1, 3096]) for j in range(2)]
        S0 = [sb("S0%d" % j, [128, 1024]) for j in range(2)]
        vn = sb("vn", [1, 1024])
        orow = sb("orow", [1, 1024])
        tmp = sb("tmp", [1, 1024])
        r8 = lambda ap: ap.rearrange("p (h e) -> p h e", h=8)
        b8 = lambda ap: ap.unsqueeze(2).to_broadcast([1, 8, 128])
        for s in range(NS):
            row, S0s = rows[s % 2], S0[s % 2]
            K.dma("sp", row[:], row_d[s:s + 1, :], r=[row_d], w=[row])
            for h in range(8):
                K.dma("sp", S0s[:, h * 128:(h + 1) * 128], d["st_ssm"][jl, s, h], r=[], w=[S0s])
            pk = K.ps()
            pq = K.ps()
            for h in range(8):
                hs = slice(h * 128, (h + 1) * 128)
                K.mm(pk[0:1, hs], kq4[:, h, s, 0:1], S0s[:, hs], True, True, r=[kqT, S0s], w=[(pk, h // 4)])
            for h in range(8):
                hs = slice(h * 128, (h + 1) * 128)
                K.mm(pq[0:1, hs], kq4[:, h, s, 1:2], S0s[:, hs], True, True, r=[kqT, S0s], w=[(pq, h // 4)])
            eg, be, qk = row[:, 3072:3080], row[:, 3080:3088], row[:, 3088:3096]
            K.tt("dve", r8(tmp[:]), r8(pk[0:1, :]), b8(eg), ALU.mult, r=[pk, row], w=[tmp])
            K.tt("dve", vn[:], row[:, 2048:3072], tmp[:], ALU.subtract, r=[row, tmp], w=[vn])
            K.tt("dve", r8(vn[:]), r8(vn[:]), b8(be), ALU.mult, r=[vn, row], w=[vn])
            K.tt("dve", r8(orow[:]), r8(pq[0:1, :]), b8(eg), ALU.mult, r=[pq, row], w=[orow])
            K.tt("dve", r8(tmp[:]), r8(vn[:]), b8(qk), ALU.mult, r=[vn, row], w=[tmp])
            K.tt("dve", orow[:], orow[:], tmp[:], ALU.add, r=[orow, tmp], w=[orow])
            K.dma("sp", orow_d[s:s + 1, :], orow[:], r=[orow], w=[orow_d])
            pS = K.ps()
            for h in range(8):
                hs = slice(h * 128, (h + 1) * 128)
                K.mm(pS[:, hs], row[:, 1024 + h * 128:1024 + (h + 1) * 128], vn[:, hs], True, True, r=[row, vn], w=[(pS, h // 4)])
            K.tt("pool", g8(S0s[:]), g8(S0s[:]), egB[:, s * 8:(s + 1) * 8].unsqueeze(2).to_broadcast([128, 8, 128]), ALU.mult, r=[S0s, egB], w=[S0s])
            K.tt("dve", S0s[:], S0s[:], pS[:], ALU.add, r=[S0s, pS], w=[S0s])
            for h in range(8):
                K.dma("sp", o["ssm_s"][jl, s, h], S0s[:, h * 128:(h + 1) * 128], r=[S0s], w=[])
        K.dma("sp", ob[0:n, :], orow_d[:, :], r=[orow_d], w=[ob])
        L["rms_gate"](n, ob[0:n, :], [ob])
        out_proj_ln(K, t, n, cat, catT, wo, xt, z, xo, st)


_NC_CACHE = {}


def kernel(x_prompt, x_sample, c_prompt, c_sample, cache_kv, page_table, state_ret, state_ssm, state_conv,
           ada_w, ada_b, ln_g, ln_b, mlp_up, mlp_down, even_w_in, even_w_out, ret_gn,
           odd_w_in, odd_w_out, odd_conv, odd_a_log, odd_dt_bias, odd_gn):
    f = lambda a: np.ascontiguousarray(np.asarray(a))
    B, T, _ = x_prompt.shape
    NT = T // 128
    NPG = page_table.shape[1]
    NPOOL = cache_kv.shape[1]
    cfg = dict(NT=NT, NPG=NPG, NPOOL=NPOOL, DEPTH=4, mix="real")
    key = (NT, NPG, NPOOL)
    if key not in _NC_CACHE:
        _NC_CACHE[key] = build(cfg)
    nc = _NC_CACHE[key]
    cs = make_consts(NT, NPG * cache_kv.shape[2])
    shared = dict(cache_kv=f(cache_kv).reshape(-1, 320), ada_w=f(ada_w), ada_b=f(ada_b), ln_g=f(ln_g), ln_b=f(ln_b),
                  mlp_up=f(mlp_up), mlp_down=f(mlp_down), even_w_in=f(even_w_in), even_w_out=f(even_w_out),
                  ret_gn=f(ret_gn).reshape(2, 512), odd_w_in=f(odd_w_in), odd_w_out=f(odd_w_out), odd_conv=f(odd_conv),
                  odd_a_log=f(odd_a_log), odd_dt_bias=f(odd_dt_bias), odd_gn=f(odd_gn))
    shared.update({k: f(v) for k, v in cs.items()})
    in_maps = []
    for c in range(8):
        b = c // 2
        sl = slice(c * NS, (c + 1) * NS)
        m = dict(shared)
        m.update(xp=f(x_prompt[b]), xs=f(x_sample[sl, 0]), c17=f(np.concatenate([c_sample[sl], c_prompt[b:b + 1]], 0)),
                 ptab=f(page_table[sl]), st_ret=f(state_ret[:, sl]), st_ssm=f(state_ssm[:, sl]), st_conv=f(state_conv[:, sl]))
        in_maps.append(m)
    res = run_bass_kernel_spmd(nc, in_maps, core_ids=list(range(8))).results
    ev = [res[2 * b] for b in range(B)]
    y_p = np.stack([r["y_p"] for r in ev], 0)
    y_s = np.concatenate([r["y_s"] for r in res], 0)[:, None, :]
    kv_p = np.stack([r["kv_p"] for r in ev], 1)
    kv_s = np.concatenate([r["kv_s"] for r in res], 1)[:, :, None, :]
    ret_p = np.stack([r["ret_p"] for r in ev], 1)
    ret_s = np.concatenate([r["ret_s"] for r in res], 1)
    ssm_p = np.stack([r["ssm_p"] for r in ev], 1)
    ssm_s = np.concatenate([r["ssm_s"] for r in res], 1)
    conv_p = np.stack([r["conv_p"] for r in ev], 1)
    conv_s = np.concatenate([r["conv_s"] for r in res], 1)
    return tuple(np.asarray(a, np.float32) for a in (y_p, y_s, kv_p, kv_s, ret_p, ret_s, ssm_p, ssm_s, conv_p, conv_s))
```

```python
import math
import numpy as np
from contextlib import ExitStack
import concourse.bass as bass
import concourse.mybir as mybir
from concourse.bass_utils import run_bass_kernel_spmd

F32 = mybir.dt.float32
BF16 = mybir.dt.bfloat16
I32 = mybir.dt.int32
ALU = mybir.AluOpType
AF = mybir.ActivationFunctionType
AX = mybir.AxisListType

ENGS = ("pe", "act", "dve", "pool", "sp")
NDMA = 6

D = 1024
DFF = 4096
EVEN_IN = 3140
ODD_IN = 4112
NS = 16
ALPHA = 8.0 ** 0.25
LN_EPS = 1e-5
NEG = -1e30


class Buf:
    __slots__ = ("w", "r")

    def __init__(self):
        self.w = None
        self.r = {}


class TL:
    def __init__(self, t, nb=1):
        self.t = t
        self.b = [Buf() for _ in range(nb)]

    def __getitem__(self, k):
        return self.t[k]


def _bufs(lst):
    out = []
    for x in lst:
        if isinstance(x, Buf):
            out.append(x)
        elif isinstance(x, TL):
            out.extend(x.b)
        elif isinstance(x, tuple):
            out.append(x[0].b[x[1]])
        else:
            raise TypeError(x)
    return out


class Sched:
    def __init__(self, nc, es):
        self.nc = nc
        self.ops = {e: [] for e in ENGS}
        self.cnt = {e: 0 for e in ENGS}
        self.seen = {e: {} for e in ENGS}
        self.sem = {e: es.enter_context(nc.semaphore("s_" + e)) for e in ENGS}
        self.dsem = {}
        self.dcnt = {}
        self.pend = {e: [] for e in ENGS}
        for q in ("sp", "act", "pool"):
            for j in range(NDMA):
                k = "d_%s%d" % (q, j)
                self.dsem[k] = es.enter_context(nc.semaphore(k))
            self.dcnt[q] = 0

    def barrier(self):
        tgt = {e: self.cnt[e] for e in ENGS if self.cnt[e] > 0}
        for q in ("sp", "act", "pool"):
            for j in range(NDMA):
                if self.dcnt[q] > j:
                    tgt["d_%s%d" % (q, j)] = 16 * ((self.dcnt[q] - j + NDMA - 1) // NDMA)
        for e in ENGS:
            for k, v in tgt.items():
                if self.seen[e].get(k, 0) < v:
                    self.seen[e][k] = v
                    self.pend[e].append((k, v))

    def _semobj(self, key):
        return self.sem[key] if key in self.sem else self.dsem[key]

    def _collect(self, eng, r, w):
        deps = []
        for b in r:
            if b.w is not None:
                deps.append(b.w)
        for b in w:
            if b.w is not None:
                deps.append(b.w)
            for k, v in b.r.items():
                deps.append((k, v))
        waits = []
        sn = self.seen[eng]
        for k, v in deps:
            if eng == "pe" and k == "pe":
                continue
            if sn.get(k, 0) >= v:
                continue
            sn[k] = v
            waits.append((k, v))
        return waits

    def begin_capture(self):
        self.cap = []

    def end_capture(self):
        c_, self.cap = self.cap, None
        units, cur, open_ = [], [], False
        for it in c_:
            cur.append(it)
            g = it[-1]
            if g == "open":
                open_ = True
            elif g == "close":
                open_ = False
            if not open_:
                units.append(cur)
                cur = []
        if cur:
            units.append(cur)
        return units

    def emit_units(self, A, B):
        if getattr(self, "no_interleave", False):
            A, B = A + B, []
        i = j = 0
        while i < len(A) or j < len(B):
            if j >= len(B) or (i < len(A) and i * len(B) <= j * len(A)):
                u = A[i]
                i += 1
            else:
                u = B[j]
                j += 1
            for kind, a0, a1, a2, a3, _ in u:
                if kind == "op":
                    self.op(a0, a1, a2, a3)
                else:
                    self.dma(a0, a1, a2, a3)

    def op(self, eng, fn, r=(), w=(), grp=None):
        if getattr(self, "cap", None) is not None:
            self.cap.append(("op", eng, fn, r, w, grp))
            return
        r = _bufs(r)
        w = _bufs(w)
        waits = self.pend[eng] + self._collect(eng, r, w)
        self.pend[eng] = []
        self.cnt[eng] += 1
        tok = (eng, self.cnt[eng])
        self.ops[eng].append((waits, fn, None))
        for b in r:
            if b.r.get(eng, 0) < tok[1]:
                b.r[eng] = tok[1]
        for b in w:
            b.w = tok
            b.r = {}

    def dma(self, q, fn, r=(), w=()):
        if getattr(self, "cap", None) is not None:
            self.cap.append(("dma", q, fn, r, w, None))
            return
        r = _bufs(r)
        w = _bufs(w)
        i = self.dcnt[q]
        self.dcnt[q] += 1
        key = "d_%s%d" % (q, i % NDMA)
        prev = 16 * (i // NDMA)
        waits = self.pend[q] + self._collect(q, r, w)
        self.pend[q] = []
        sn = self.seen[q]
        if prev > 0 and sn.get(key, 0) < prev:
            sn[key] = prev
            waits.append((key, prev))
        tok = (key, prev + 16)
        self.ops[q].append((waits, fn, key))
        for b in r:
            if b.r.get(key, 0) < tok[1]:
                b.r[key] = tok[1]
        for b in w:
            b.w = tok
            b.r = {}

    def finish(self):
        final = {}
        for q in ("sp", "act", "pool"):
            for j in range(NDMA):
                if self.dcnt[q] > j:
                    final["d_%s%d" % (q, j)] = 16 * ((self.dcnt[q] - j + NDMA - 1) // NDMA)
        nc = self.nc
        with nc.Block() as block:
            def replay(ename, eng):
                for waits, fn, key in self.ops[ename]:
                    for k, v in waits:
                        eng.wait_ge(self._semobj(k), v)
                    inst = fn(eng)
                    if key is None:
                        inst.then_inc(self.sem[ename], 1)
                    else:
                        inst.then_inc(self.dsem[key], 16)
                if ename == "sp":
                    for k, v in final.items():
                        eng.wait_ge(self._semobj(k), v)
                    for e in ("pe", "act", "dve", "pool"):
                        if self.cnt[e] > 0:
                            eng.wait_ge(self.sem[e], self.cnt[e])

            @block.tensor
            def _(e):
                replay("pe", e)

            @block.scalar
            def _(e):
                replay("act", e)

            @block.vector
            def _(e):
                replay("dve", e)

            @block.gpsimd
            def _(e):
                replay("pool", e)

            @block.sync
            def _(e):
                replay("sp", e)


def _rope_rows(pos):
    def tab(d):
        inv = (10000.0 ** (-np.arange(0, d, 2, dtype=np.float32) / d)).astype(np.float32)
        ang = (pos.astype(np.float32)[:, None] * inv[None, :]).astype(np.float32)
        return np.cos(ang.astype(np.float64)), np.sin(ang.astype(np.float64))
    c128, s128 = tab(128)
    c64, s64 = tab(64)
    return np.concatenate([c128, s128, -s128, c64, s64, -s64], 1).astype(np.float32)


def make_consts(NT, past_len):
    c = {}
    p = np.arange(128)
    c["ident"] = np.eye(128, dtype=np.float32)
    c["ones"] = np.ones((128, 128), np.float32)
    c["triu"] = (p[:, None] <= p[None, :]).astype(np.float32)
    c["mvinc"] = np.where(p[None, :] >= p[:, None], 0.0, -1e4).astype(np.float32)
    c["strict"] = (p[None, :] > p[:, None]).astype(np.float32)
    c["cmask"] = np.where(p[None, :] <= p[:, None], 0.0, NEG).astype(np.float32)
    lg = np.log(1.0 - 2.0 ** (-5.0 - np.arange(4, dtype=np.float64)))
    i = p.astype(np.float64)
    rel = i[None, :] - i[:, None]
    dm = np.where(rel >= 0, np.exp(lg[:, None, None] * np.maximum(rel, 0)[None]), 0.0)
    c["dmatT"] = (np.transpose(dm, (1, 0, 2)) * 128 ** -0.5).astype(np.float32).reshape(128, 512)
    qd = np.exp(lg[:, None] * (i[None, :] + 1.0))
    c["qdecB"] = np.broadcast_to(qd.reshape(1, 512), (128, 512)).astype(np.float32).copy()
    kd = np.exp(lg[None, :] * (127.0 - i[:, None])) * 128 ** -0.5
    c["kdecC"] = kd.astype(np.float32)
    cd = np.exp(lg * 128.0)
    c["cdecB"] = np.broadcast_to(np.repeat(cd, 128).reshape(1, 512), (128, 512)).astype(np.float32).copy()
    g1 = np.exp(lg)
    c["gam1B"] = np.broadcast_to(np.repeat(g1, 128).reshape(1, 512), (128, 512)).astype(np.float32).copy()
    c["ropeP"] = _rope_rows(np.arange(NT * 128)).reshape(NT, 128, 288)
    c["iotap"] = np.arange(128, dtype=np.float32).reshape(128, 1)
    c["pw2"] = np.broadcast_to((0.5 ** (np.arange(32, dtype=np.float64) + 1)).astype(np.float32).reshape(1, 32), (128, 32)).copy()
    c["ropeS"] = np.broadcast_to(_rope_rows(np.array([past_len])), (NS, 288)).copy()
    d16 = np.broadcast_to(np.eye(16, dtype=np.float32).reshape(1, 256), (128, 256)).copy()
    c["delta16"] = d16
    return c


class KB:
    def __init__(self, cfg):
        self.cfg = cfg
        self.nc = bass.Bass("TRN2", target_bir_lowering=False)
        self.es = ExitStack()
        self.rr = 0

    def scope(self):
        kb = self

        class _Sc(ExitStack):
            def __exit__(self_, *a):
                kb.S.barrier()
                return super().__exit__(*a)
        return _Sc()

    def dram(self, name, shape, dt, kind):
        return TL(self.nc.dram_tensor(name, list(shape), dt, kind=kind).ap())

    def sb(self, es, name, shape, dt, nb=1):
        self.uid = getattr(self, "uid", 0) + 1
        return TL(es.enter_context(self.nc.sbuf_tensor("%s_%d" % (name, self.uid), list(shape), dt)), nb)

    def tt(self, eng, out, a, b, op, r, w):
        self.S.op(eng, lambda e: e.tensor_tensor(out=out, in0=a, in1=b, op=op), r=r, w=w)

    def ts(self, eng, out, a, s1, op0, r, w, s2=None, op1=None, acc=None):
        if op1 is None:
            self.S.op(eng, lambda e: e.tensor_scalar(out=out, in0=a, scalar1=s1, scalar2=None, op0=op0), r=r, w=w)
        elif acc is None:
            self.S.op(eng, lambda e: e.tensor_scalar(out=out, in0=a, scalar1=s1, scalar2=s2, op0=op0, op1=op1), r=r, w=w)
        else:
            self.S.op(eng, lambda e: e.tensor_scalar(out=out, in0=a, scalar1=s1, scalar2=s2, op0=op0, op1=op1, accum_out=acc), r=r, w=w)

    def stt(self, eng, out, a, sc, b, op0, op1, r, w):
        self.S.op(eng, lambda e: e.scalar_tensor_tensor(out=out, in0=a, scalar=sc, in1=b, op0=op0, op1=op1), r=r, w=w)

    def act(self, out, in_, func, r, w, bias=None, scale=None, acc=None):
        kw = {}
        if bias is not None:
            kw["bias"] = bias
        if scale is not None:
            kw["scale"] = scale
        if acc is not None:
            kw["accum_out"] = acc
        self.S.op("act", lambda e: e.activation(out=out, in_=in_, func=func, **kw), r=r, w=w)

    def cp(self, eng, out, in_, r, w):
        if eng == "act":
            self.S.op("act", lambda e: e.copy(out=out, in_=in_), r=r, w=w)
        else:
            self.S.op(eng, lambda e: e.tensor_copy(out=out, in_=in_), r=r, w=w)

    def red(self, eng, out, in_, r, w, op=None):
        if op is None:
            self.S.op(eng, lambda e: e.reduce_sum(out=out, in_=in_, axis=AX.X), r=r, w=w)
        else:
            self.S.op(eng, lambda e: e.tensor_reduce(out=out, in_=in_, axis=AX.X, op=op), r=r, w=w)

    def mm(self, out, lhsT, rhs, start, stop, r, w, dedicated=False):
        grp = None
        if not dedicated and not (start and stop):
            grp = "open" if start else ("close" if stop else "mid")
        self.S.op("pe", lambda e: e.matmul(out, lhsT=lhsT, rhs=rhs, start=start, stop=stop), r=r, w=w, grp=grp)

    def tr(self, out, in_, ident, r, w):
        self.S.op("pe", lambda e: e.transpose(out=out, in_=in_, identity=ident), r=r, w=w)

    def dma(self, q, out, in_, r, w):
        self.S.dma(q, lambda e: e.dma_start(out=out, in_=in_), r=r, w=w)

    def memset(self, eng, ap, val, w):
        self.S.op(eng, lambda e: e.memset(ap, val), w=w)

    def evq(self):
        self.rr += 1
        return "act" if self.rr % 2 else "dve"

    def ps(self):
        self.prr = (getattr(self, "prr", -1) + 1) % len(self.PSG)
        return self.PSG[self.prr]

    def rsqrt(self, eng, out, in_, r, w, mult, add):
        self.act(out, in_, AF.Sqrt, r=r, w=w, scale=mult, bias=add)
        self.S.op("dve", lambda e: e.reciprocal(out=out, in_=out), r=w, w=w)


def build(cfg):
    NT = cfg["NT"]
    NPG = cfg["NPG"]
    NPOOL = cfg["NPOOL"]
    DEPTH = cfg["DEPTH"]
    TOPK_P = min(256, (NT * 128) // 4)
    TOPK_S = min(256, (NPG * 128 + 1) // 4)
    NTOK = NT * 128
    NROW = NTOK + NS
    K = KB(cfg)
    nc = K.nc
    es = K.es
    with es:
        S = K.S = Sched(nc, es)
        S.no_interleave = bool(cfg.get("noilv"))
        EI, EO, IN = "ExternalInput", "ExternalOutput", "Internal"
        d = {}
        for nm, sh, dt in [
            ("xp", [NTOK, D], F32), ("xs", [NS, D], F32), ("c17", [17, D], F32),
            ("cache_kv", [2 * NPOOL * 128, 320], F32), ("ptab", [NS, NPG], I32),
            ("st_ret", [2, NS, 4, 128, 128], F32), ("st_ssm", [2, NS, 8, 128, 128], F32),
            ("st_conv", [2, NS, 3, 3072], F32),
            ("ada_w", [4, D, 6 * D], F32), ("ada_b", [4, 6 * D], F32), ("ln_g", [4, 2, D], F32),
            ("ln_b", [4, 2, D], F32), ("mlp_up", [4, D, DFF], F32), ("mlp_down", [4, DFF, D], F32),
            ("even_w_in", [2, D, EVEN_IN], F32), ("even_w_out", [2, D, D], F32), ("ret_gn", [2, 512], F32),
            ("odd_w_in", [2, D, ODD_IN], F32), ("odd_w_out", [2, D, D], F32), ("odd_conv", [2, 4, 3072], F32),
            ("odd_a_log", [2, 8], F32), ("odd_dt_bias", [2, 8], F32), ("odd_gn", [2, 128], F32),
            ("ident", [128, 128], F32), ("ones", [128, 128], F32), ("triu", [128, 128], F32),
            ("mvinc", [128, 128], F32), ("strict", [128, 128], F32), ("cmask", [128, 128], F32),
            ("dmatT", [128, 512], F32), ("qdecB", [128, 512], F32), ("kdecC", [128, 4], F32),
            ("cdecB", [128, 512], F32), ("gam1B", [128, 512], F32), ("ropeP", [NT, 128, 288], F32),
            ("ropeS", [NS, 288], F32), ("delta16", [128, 256], F32), ("iotap", [128, 1], F32), ("pw2", [128, 32], F32),
        ]:
            d[nm] = K.dram(nm, sh, dt, EI)
        o = {}
        for nm, sh in [
            ("y_p", [NTOK, D]), ("y_s", [NS, D]), ("kv_p", [2, NTOK, 320]), ("kv_s", [2, NS, 320]),
            ("ret_p", [2, 4, 128, 128]), ("ret_s", [2, NS, 4, 128, 128]), ("ssm_p", [2, 8, 128, 128]),
            ("ssm_s", [2, NS, 8, 128, 128]), ("conv_p", [2, 3, 3072]), ("conv_s", [2, NS, 3, 3072]),
        ]:
            o[nm] = K.dram(nm, sh, F32, EO)
        x_a = K.dram("x_a", [NROW, D], F32, IN)
        x_b = K.dram("x_b", [NROW, D], F32, IN)
        mod_d = K.dram("mod_d", [4, 17, 6 * D], F32, IN)
        proj_d = K.dram("proj_d", [NROW + 3, ODD_IN], F32, IN)
        tb_xa = [Buf() for _ in range(NT + 1)]
        tb_xb = [Buf() for _ in range(NT + 1)]
        tb_pj = [Buf() for _ in range(NT + 2)]

        def rows(t):
            return (t * 128, 128) if t < NT else (NTOK, NS)

        cst = {}
        for nm, sh in [("ident", [128, 128]), ("ones", [128, 128]), ("triu", [128, 128]), ("mvinc", [128, 128]),
                       ("strict", [128, 128]), ("cmask", [128, 128]), ("delta16", [128, 256]), ("iotap", [128, 1]), ("pw2", [128, 32])]:
            cst[nm] = K.sb(es, "c_" + nm, sh, F32)
            K.dma("sp", cst[nm][:], d[nm][:], r=[], w=[cst[nm]])
        identb = K.sb(es, "identb", [128, 128], BF16)
        K.cp("dve", identb[:], cst["ident"][:], r=[cst["ident"]], w=[identb])
        ident = cst["ident"]
        PS = [TL(es.enter_context(nc.psum_tensor("ps%d" % i, [128, 1024], F32)), 2) for i in range(4)]
        K.PSG = PS[:4]
        PSO = [PS[2], PS[3]]
        lnG = K.sb(es, "lnG", [128, D], F32)
        lnB = K.sb(es, "lnB", [128, D], F32)
        modP = K.sb(es, "modP", [128, 3 * D], F32)

        def load_phase_consts(l, which):
            c0 = which * 3 * D
            K.dma("sp", modP[:], mod_d[l, 16:17, c0:c0 + 3 * D].partition_broadcast(128), r=[mod_d], w=[modP])
            K.dma("sp", lnG[:], d["ln_g"][l, which:which + 1, :].partition_broadcast(128), r=[], w=[lnG])
            K.dma("sp", lnB[:], d["ln_b"][l, which:which + 1, :].partition_broadcast(128), r=[], w=[lnB])
            K.ts("pool", modP[:, D:3 * D], modP[:, D:3 * D], 1.0, ALU.add, r=[modP], w=[modP])

        def load_sample_mod(l, which, dst=None):
            dst = modP if dst is None else dst
            c0 = which * 3 * D
            K.dma("sp", dst[0:NS, :], mod_d[l, 0:NS, c0:c0 + 3 * D], r=[mod_d], w=[dst])
            K.ts("pool", dst[0:NS, D:3 * D], dst[0:NS, D:3 * D], 1.0, ALU.add, r=[dst], w=[dst])

        K.modS = None

        def mod_of(t):
            return K.modS if (t == NT and K.modS is not None) else modP

        with K.scope() as p0:
            c17t = K.sb(p0, "c17t", [17, D], F32)
            scT = K.sb(p0, "scT", [128, 8, 17], BF16)
            K.dma("sp", c17t[:], d["c17"][:], r=[], w=[c17t])
            K.act(c17t[:], c17t[:], AF.Silu, r=[c17t], w=[c17t])
            pt = K.ps()
            for k in range(8):
                K.tr(pt[0:128, k * 32:k * 32 + 17], c17t[:, k * 128:(k + 1) * 128], ident[0:17, 0:17], r=[c17t, ident], w=[(pt, 0)])
            K.cp("dve", scT[:], pt[:, 0:256].rearrange("p (k c) -> p k c", c=32)[:, :, 0:17], r=[(pt, 0)], w=[scT])
            aw = [K.sb(p0, "aw%d" % i, [128, 8, 512], BF16) for i in range(2)]
            ab = [K.sb(p0, "ab%d" % i, [17, 512], F32) for i in range(2)]
            mo = [K.sb(p0, "mo%d" % i, [17, 512], F32) for i in range(2)]
            it = 0
            for l in range(DEPTH):
                for cb in range(12):
                    a, b_, m_ = aw[it % 2], ab[it % 2], mo[it % 2]
                    cs = slice(cb * 512, (cb + 1) * 512)
                    K.dma("pool", a[:], d["ada_w"][l, :, cs].rearrange("(k p) n -> p k n", p=128), r=[], w=[a])
                    K.dma("sp", b_[:], d["ada_b"][l:l + 1, cs].partition_broadcast(17), r=[], w=[b_])
                    pt = K.ps()
                    for k in range(8):
                        K.mm(pt[0:17, 0:512], scT[:, k, :], a[:, k, :], k == 0, k == 7, r=[scT, a], w=[(pt, 0)])
                    K.tt("dve", m_[:], pt[0:17, 0:512], b_[:], ALU.add, r=[(pt, 0), b_], w=[m_])
                    K.dma("sp", mod_d[l, :, cs], m_[:], r=[m_], w=[mod_d])
                    it += 1

        def load_weights_bf16(wt, src, ncols):
            nk = src.shape[0] // 128
            for k in range(nk):
                K.dma("pool", wt[:, k, 0:ncols], src[k * 128:(k + 1) * 128, :], r=[], w=[wt])

        def modulate_T(pp, xt, t, n, tmp, hm, hmT):
            md = mod_of(t)
            K.tt("dve", tmp[0:n, :], xt[0:n, :], md[0:n, D:2 * D], ALU.mult, r=[xt, md], w=[tmp])
            K.tt("pool", hm[0:n, :], tmp[0:n, :], md[0:n, 0:D], ALU.add, r=[tmp, md], w=[hm])
            to_T(hm, n, hmT)

        def to_T(hm, n, hmT):
            pt = K.ps()
            pb = pt[:].bitcast(BF16)
            for k in range(8):
                K.tr(pb[:, k * 128:k * 128 + n], hm[0:n, k * 128:(k + 1) * 128], identb[0:n, 0:n], r=[hm, identb], w=[(pt, 0)])
            K.cp("act", hmT[:, :, 0:n], pb[:, 0:1024].rearrange("p (k c) -> p k c", c=128)[:, :, 0:n], r=[(pt, 0)], w=[hmT])

        def layer_norm_out(z, n, xo, scr, st):
            K.memset("pool", st[0:n, :], 0.0, w=[st])
            K.red("dve", st[0:n, 0:1], z[0:n, :], r=[z], w=[st])
            K.ts("dve", st[0:n, 1:2], st[0:n, 0:1], -1.0 / D, ALU.mult, r=[st], w=[st])
            K.ts("dve", scr[0:n, :], z[0:n, :], st[0:n, 1:2], ALU.add, r=[z, st], w=[scr])
            K.act(xo[0:n, :], scr[0:n, :], AF.Square, r=[scr, st], w=[xo, st], acc=st[0:n, 2:3])
            K.rsqrt("dve", st[0:n, 3:4], st[0:n, 2:3], r=[st], w=[st], mult=1.0 / D, add=LN_EPS)
            K.stt("dve", xo[0:n, :], scr[0:n, :], st[0:n, 3:4], lnG[0:n, :], ALU.mult, ALU.mult, r=[scr, st, lnG], w=[xo])
            K.tt("pool", xo[0:n, :], xo[0:n, :], lnB[0:n, :], ALU.add, r=[xo, lnB], w=[xo])

        def residual_ln(py, t, n, xt, z, xo, scr, st):
            md = mod_of(t)
            K.tt("dve", z[0:n, :], py[0:n, :], md[0:n, 2 * D:3 * D], ALU.mult, r=[py, md], w=[z])
            K.stt("dve", z[0:n, :], xt[0:n, :], ALPHA, z[0:n, :], ALU.mult, ALU.add, r=[xt, z], w=[z])
            layer_norm_out(z, n, xo, scr, st)

        K.fn = dict(rows=rows, load_phase_consts=load_phase_consts, load_sample_mod=load_sample_mod, mod_of=mod_of, load_weights_bf16=load_weights_bf16,
                    modulate_T=modulate_T, to_T=to_T, residual_ln=residual_ln)
        K.ctx = dict(d=d, o=o, x_a=x_a, x_b=x_b, proj_d=proj_d, tb_xa=tb_xa, tb_xb=tb_xb, tb_pj=tb_pj, cst=cst,
                     ident=ident, identb=identb, PS=PS, PSO=PSO, NT=NT, NPG=NPG, NTOK=NTOK, NROW=NROW,
                     TOPK_P=TOPK_P, TOPK_S=TOPK_S, modP=modP, mod_d=mod_d, cfg=cfg)

        with K.scope() as pz:
            zt = K.sb(pz, "zt", [3, ODD_IN], F32)
            K.memset("pool", zt[:], 0.0, w=[zt])
            K.dma("sp", proj_d[0:3, :], zt[:], r=[zt], w=[tb_pj[NT + 1]])

        x_in = None
        for l in range(DEPTH):
            even = (l % 2 == 0)
            NIN = EVEN_IN if even else ODD_IN
            w_in_d = d["even_w_in"][l // 2] if even else d["odd_w_in"][l // 2]
            load_phase_consts(l, 0)
            with K.scope() as pa:
                wt = K.sb(pa, "w_in", [128, 8, ODD_IN], BF16)
                load_weights_bf16(wt, w_in_d, NIN)
                xts = [K.sb(pa, "a_x%d" % i, [128, D], F32) for i in range(2)]
                tmps = [K.sb(pa, "a_t%d" % i, [128, D], F32) for i in range(2)]
                hms = [K.sb(pa, "a_h%d" % i, [128, D], BF16) for i in range(2)]
                hmTs = [K.sb(pa, "a_hT%d" % i, [128, 8, 128], BF16) for i in range(2)]
                pjs = [K.sb(pa, "a_pj%d" % i, [128, ODD_IN], F32) for i in range(1)] * 2
                K.modS = K.sb(pa, "a_modS", [NS, 3 * D], F32)
                load_sample_mod(l, 0, K.modS)

                def a_s1(t):
                    r0, n = rows(t)
                    xt, tmp, hm, hmT = xts[t % 2], tmps[t % 2], hms[t % 2], hmTs[t % 2]
                    if l == 0:
                        src = d["xp"][r0:r0 + n, :] if t < NT else d["xs"][:, :]
                        K.dma("sp", xt[0:n, :], src, r=[], w=[xt])
                    else:
                        K.dma("sp", xt[0:n, :], x_a[r0:r0 + n, :], r=[tb_xa[t]], w=[xt])
                    modulate_T(pa, xt, t, n, tmp, hm, hmT)

                def a_s2(t):
                    r0, n = rows(t)
                    hmT, pj = hmTs[t % 2], pjs[0]
                    nb = (NIN + 511) // 512
                    for cb in range(nb):
                        c0 = cb * 512
                        wd = min(512, NIN - c0)
                        pt = K.ps()
                        hb = cb % 2
                        for k in range(8):
                            K.mm(pt[0:n, hb * 512:hb * 512 + wd], hmT[:, k, 0:n], wt[:, k, c0:c0 + wd], k == 0, k == 7, r=[hmT, wt], w=[(pt, hb)])
                        K.cp(K.evq(), pj[0:n, c0:c0 + wd], pt[0:n, hb * 512:hb * 512 + wd], r=[(pt, hb)], w=[pj])
                    K.dma("sp", proj_d[3 + r0:3 + r0 + n, 0:NIN], pj[0:n, 0:NIN], r=[pj], w=[tb_pj[t]])

                a_s1(0)
                for t in range(NT + 1):
                    if t + 1 <= NT:
                        a_s1(t + 1)
                    a_s2(t)
                K.modS = None
            if cfg.get("mix", "real") == "stub":
                phase_b_stub(K, l)
            elif even:
                phase_b_even(K, l)
            else:
                phase_b_odd(K, l)
            load_phase_consts(l, 1)
            with K.scope() as pf:
                wu = K.sb(pf, "w_up", [128, 8, DFF], BF16)
                wd_ = K.sb(pf, "w_dn", [128, 32, D], BF16)
                load_weights_bf16(wu, d["mlp_up"][l], DFF)
                load_weights_bf16(wd_, d["mlp_down"][l], D)
                xts = [K.sb(pf, "f_x%d" % i, [128, D], F32) for i in range(2)]
                hms = [K.sb(pf, "f_h%d" % i, [128, D], BF16) for i in range(1)] * 2
                hmTs = [K.sb(pf, "f_hT%d" % i, [128, 8, 128], BF16) for i in range(2)]
                uTs = [K.sb(pf, "f_uT%d" % i, [128, 32, 128], BF16) for i in range(1)] * 2
                rl = [K.sb(pf, "f_rl%d" % i, [128, 1024], F32) for i in range(1)] * 2
                xos = [K.sb(pf, "f_xo%d" % i, [128, D], F32) for i in range(1)] * 2
                sts = [K.sb(pf, "f_st%d" % i, [128, 4], F32) for i in range(2)]
                last = (l == DEPTH - 1)
                K.modS = K.sb(pf, "f_modS", [NS, 3 * D], F32)
                load_sample_mod(l, 1, K.modS)
                zb = K.sb(pf, "f_z", [128, D], F32)

                xts.append(K.sb(pf, "f_x2", [128, D], F32))

                def f_s1(t):
                    r0, n = rows(t)
                    xt, tmp, hm, hmT = xts[t % 3], rl[0], hms[0], hmTs[t % 2]
                    K.dma("sp", xt[0:n, :], x_b[r0:r0 + n, :], r=[tb_xb[t]], w=[xt])
                    modulate_T(pf, xt, t, n, tmp, hm, hmT)

                def f_s2a(t):
                    r0, n = rows(t)
                    hmT, uT = hmTs[t % 2], uTs[0]
                    for g in range(4):
                        pt = K.ps()
                        for j in range(8):
                            fc = g * 8 + j
                            for k in range(8):
                                K.mm(pt[:, j * 128:j * 128 + n], wu[:, k, fc * 128:(fc + 1) * 128], hmT[:, k, 0:n], k == 0, k == 7, r=[wu, hmT], w=[(pt, j // 4)])
                        r_ = rl[0]
                        pv = pt[:].rearrange("p (j c) -> p j c", c=128)[:, :, 0:n]
                        rv = r_[:].rearrange("p (j c) -> p j c", c=128)[:, :, 0:n]
                        K.act(rv, pv, AF.Relu, r=[pt], w=[r_])
                        K.tt("pool" if g % 2 else "dve", uT[:, g * 8:(g + 1) * 8, 0:n], rv, rv, ALU.mult, r=[r_], w=[uT])

                def f_s2b(t):
                    r0, n = rows(t)
                    xt, uT, xo, st = xts[t % 3], uTs[0], xos[0], sts[t % 2]
                    py = K.ps()
                    for hb in range(2):
                        for k in range(32):
                            K.mm(py[0:n, hb * 512:(hb + 1) * 512], uT[:, k, 0:n], wd_[:, k, hb * 512:(hb + 1) * 512], k == 0, k == 31, r=[uT, wd_], w=[(py, hb)])
                    residual_ln(py, t, n, xt, zb, xo, zb, st)
                    if last:
                        dst = o["y_p"][r0:r0 + n, :] if t < NT else o["y_s"][:, :]
                        K.dma("sp", dst, xo[0:n, :], r=[xo], w=[])
                    else:
                        K.dma("sp", x_a[r0:r0 + n, :], xo[0:n, :], r=[xo], w=[tb_xa[t]])

                f_s1(0)
                if NT >= 1:
                    f_s1(1)
                for t in range(NT + 1):
                    f_s2a(t)
                    if t + 2 <= NT:
                        f_s1(t + 2)
                    f_s2b(t)
                K.modS = None
        S.finish()
    return nc


def phase_b_stub(K, l):
    c = K.ctx
    f = K.fn
    f["load_phase_consts"](l, 0)
    d, NT = c["d"], c["NT"]
    even = (l % 2 == 0)
    with K.scope() as pb:
        wo = K.sb(pb, "w_out", [128, 8, D], BF16)
        f["load_weights_bf16"](wo, (d["even_w_out"] if even else d["odd_w_out"])[l // 2], D)
        xt = K.sb(pb, "b_x", [128, D], F32)
        pj = K.sb(pb, "b_pj", [128, D], F32)
        cat = K.sb(pb, "b_cat", [128, D], BF16)
        catT = K.sb(pb, "b_catT", [128, 8, 128], BF16)
        z = K.sb(pb, "b_z", [128, D], F32)
        xo = K.sb(pb, "b_xo", [128, D], F32)
        st = K.sb(pb, "b_st", [128, 4], F32)
        for t in range(NT + 1):
            r0, n = f["rows"](t)
            if l == 0:
                src = d["xp"][r0:r0 + n, :] if t < NT else d["xs"][:, :]
                K.dma("sp", xt[0:n, :], src, r=[], w=[xt])
            else:
                K.dma("sp", xt[0:n, :], c["x_a"][r0:r0 + n, :], r=[c["tb_xa"][t]], w=[xt])
            if t == NT:
                f["load_sample_mod"](l, 0)
            K.dma("sp", pj[0:n, :], c["proj_d"][3 + r0:3 + r0 + n, 0:D], r=[c["tb_pj"][t]], w=[pj])
            K.cp("dve", cat[0:n, :], pj[0:n, :], r=[pj], w=[cat])
            out_proj_ln(K, t, n, cat, catT, wo, xt, z, xo, st)


def out_proj_ln(K, t, n, cat, catT, wo, xt, z, xo, st):
    c = K.ctx
    f = K.fn
    f["to_T"](cat, n, catT)
    py = K.ps()
    for hb in range(2):
        for k in range(8):
            K.mm(py[0:n, hb * 512:(hb + 1) * 512], catT[:, k, 0:n], wo[:, k, hb * 512:(hb + 1) * 512], k == 0, k == 7, r=[catT, wo], w=[(py, hb)])
    f["residual_ln"](py, t, n, xt, z, xo, z, st)
    r0, _ = f["rows"](t)
    K.dma("sp", c["x_b"][r0:r0 + n, :], xo[0:n, :], r=[xo], w=[c["tb_xb"][t]])


def rope_ops(K, n, src, H, half, rp, c0, t1, t2, dsts, r_src):
    cosb = rp[0:n, c0:c0 + half].unsqueeze(1).unsqueeze(1).to_broadcast([n, H, 2, half])
    sinb = rp[0:n, c0 + half:c0 + 2 * half].unsqueeze(1).to_broadcast([n, H, half])
    nsinb = rp[0:n, c0 + 2 * half:c0 + 3 * half].unsqueeze(1).to_broadcast([n, H, half])
    t1v = t1[0:n, 0:H * 2 * half].rearrange("p (h a c) -> p h a c", h=H, a=2)
    t2v = t2[0:n, 0:H * 2 * half].rearrange("p (h a c) -> p h a c", h=H, a=2)
    K.tt("dve", t1v, src, cosb, ALU.mult, r=r_src + [rp], w=[t1])
    K.tt("pool", t2v[:, :, 0, :], src[:, :, 1, :], nsinb, ALU.mult, r=r_src + [rp], w=[t2])
    K.tt("pool", t2v[:, :, 1, :], src[:, :, 0, :], sinb, ALU.mult, r=r_src + [rp], w=[t2])
    for h0, h1, dap, dt_ in dsts:
        K.tt("dve", dap, t1v[:, h0:h1], t2v[:, h0:h1], ALU.add, r=[t1, t2], w=[dt_])


def phase_b_even(K, l):
    c = K.ctx
    f = K.fn
    d, o, NT, NPG = c["d"], c["o"], c["NT"], c["NPG"]
    PSO, cst, ident, identb = c["PSO"], c["cst"], c["ident"], c["identb"]
    i = l // 2
    TOPK = c["TOPK_P"]
    NIT = 15
    SC_ATT = 128 ** -0.5
    f["load_phase_consts"](l, 0)
    K.PSG = c["PS"][:2]
    with K.scope() as pb:
        sb = lambda nm, sh, dt=F32: K.sb(pb, "e_" + nm, sh, dt)
        wo = sb("wo", [128, 8, D], BF16)
        f["load_weights_bf16"](wo, d["even_w_out"][i], D)
        kc = {}
        for nm in ("dmatT", "qdecB", "cdecB", "gam1B"):
            kc[nm] = sb(nm, [128, 512])
            K.dma("sp", kc[nm][:], d[nm][:], r=[], w=[kc[nm]])
        kdecC = sb("kdecC", [128, 4])
        K.dma("sp", kdecC[:], d["kdecC"][:], r=[], w=[kdecC])
        gnB = sb("gnB", [128, 512])
        K.dma("sp", gnB[:], d["ret_gn"][i:i + 1, :].partition_broadcast(128), r=[], w=[gnB])
        Sret = sb("Sret", [128, 512])
        K.memset("pool", Sret[:], 0.0, w=[Sret])
        pj = sb("pj", [128, EVEN_IN])
        xt = sb("xt", [128, D])
        rp = sb("rp", [128, 288])
        qk_r = sb("qk_r", [128, 1024])
        aq_rb = sb("aq_rb", [128, 512], BF16)
        kvrow = sb("kvrow", [128, 320])
        iq_r = sb("iq_r", [128, 256])
        t1 = sb("t1", [128, 1024])
        t2 = sb("t2", [128, 1024])
        AT, cen, sq, sg = [sb(nm, [128, 512]) for nm in ("AT", "cen", "sq", "sg")]
        cat = sb("cat", [128, D], BF16)
        catT = sb("catT", [128, 8, 128], BF16)
        z = sb("z", [128, D])
        xo = sb("xo", [128, D])
        st = sb("st", [128, 4])
        s4 = sb("s4", [128, 16])
        aw = sb("aw", [128, 8])
        bs = sb("bs", [128, 8])
        cnt = sb("cnt", [128, NIT])
        Hh = sb("Hh", [128, 32])
        rs = sb("rs", [128, 4])

        def x_load(t, n, r0):
            if l == 0:
                src = d["xp"][r0:r0 + n, :] if t < NT else d["xs"][:, :]
                K.dma("sp", xt[0:n, :], src, r=[], w=[xt])
            else:
                K.dma("sp", xt[0:n, :], c["x_a"][r0:r0 + n, :], r=[c["tb_xa"][t]], w=[xt])

        def proj_rope(t, n, r0, rp_src, aq_dst=None):
            aq_dst = aq_rb if aq_dst is None else aq_dst
            K.dma("sp", pj[0:n, :], c["proj_d"][3 + r0:3 + r0 + n, 0:EVEN_IN], r=[c["tb_pj"][t]], w=[pj])
            K.dma("sp", rp[0:n, :], rp_src, r=[], w=[rp])
            v4 = lambda ap, H, half: ap.rearrange("p (h a c) -> p h a c", h=H, a=2)
            rope_ops(K, n, v4(pj[0:n, 0:1024], 8, 64), 8, 64, rp, 0, t1, t2,
                     [(0, 8, v4(qk_r[0:n, :], 8, 64), qk_r)], [pj])
            rope_ops(K, n, v4(pj[0:n, 2048:2688], 5, 64), 5, 64, rp, 0, t1, t2,
                     [(0, 4, v4(aq_dst[0:n, :], 4, 64), aq_dst), (4, 5, v4(kvrow[0:n, 0:128], 1, 64), kvrow)], [pj])
            rope_ops(K, n, v4(pj[0:n, 2816:3136], 5, 32), 5, 32, rp, 192, t1, t2,
                     [(0, 4, v4(iq_r[0:n, :], 4, 32), iq_r), (4, 5, v4(kvrow[0:n, 256:320], 1, 32), kvrow)], [pj])
            K.cp("act", kvrow[0:n, 128:256], pj[0:n, 2688:2816], r=[pj], w=[kvrow])
            K.act(aw[0:n, 0:4], pj[0:n, 3136:3140], AF.Abs, r=[pj], w=[aw], scale=1.0 / 16)
            K.ts("dve", aw[0:n, 4:8], pj[0:n, 3136:3140], 0.0, ALU.is_ge, r=[pj], w=[aw], s2=2.0, op1=ALU.mult)
            K.ts("dve", aw[0:n, 4:8], aw[0:n, 4:8], -1.0, ALU.add, r=[aw], w=[aw])

        def head_norm_gate(n, src, rsrc):
            pov = src.rearrange("p (h e) -> p h e", h=4)
            cv = cen[0:n, :].rearrange("p (h e) -> p h e", h=4)
            K.red("dve", s4[0:n, 0:4], pov, r=rsrc, w=[s4])
            K.ts("dve", s4[0:n, 4:8], s4[0:n, 0:4], -1.0 / 128, ALU.mult, r=[s4], w=[s4])
            K.tt("dve", cv, pov, s4[0:n, 4:8].unsqueeze(2).to_broadcast([n, 4, 128]), ALU.add, r=rsrc + [s4], w=[cen])
            K.tt("pool", sq[0:n, :], cen[0:n, :], cen[0:n, :], ALU.mult, r=[cen], w=[sq])
            K.red("dve", s4[0:n, 8:12], sq[0:n, :].rearrange("p (h e) -> p h e", h=4), r=[sq], w=[s4])
            K.rsqrt("dve", s4[0:n, 12:16], s4[0:n, 8:12], r=[s4], w=[s4], mult=1.0 / 128, add=LN_EPS)
            K.act(sg[0:n, :], pj[0:n, 1536:2048], AF.Silu, r=[pj], w=[sg])
            K.tt("pool", sg[0:n, :], sg[0:n, :], gnB[0:n, :], ALU.mult, r=[sg, gnB], w=[sg])
            K.tt("dve", cv, cv, s4[0:n, 12:16].unsqueeze(2).to_broadcast([n, 4, 128]), ALU.mult, r=[cen, s4], w=[cen])
            K.tt("dve", cat[0:n, 0:512], cen[0:n, :], sg[0:n, :], ALU.mult, r=[cen, sg], w=[cat])

        with K.scope() as pp:
            akT_all = K.sb(pp, "e_akT", [128, NT * 128], BF16, nb=NT)
            ikT_all = K.sb(pp, "e_ikT", [64, NT * 128], BF16)
            Vall = K.sb(pp, "e_Vall", [128, NT, 132], BF16, nb=NT)
            K.memset("pool", Vall[:], 1.0, w=[Vall])
            score = K.sb(pp, "e_score", [128, NT * 128], F32)
            junk = K.sb(pp, "e_junk", [128, NT * 128], F32)
            maskb = K.sb(pp, "e_maskb", [128, NT * 128], BF16)
            sbp = lambda nm, sh, dt=F32: K.sb(pp, "e_" + nm, sh, dt)
            qT, qTd, kT, kd = [sbp(nm, [128, 512]) for nm in ("qT", "qTd", "kT", "kd")]
            akb = sbp("akb", [128, 128], BF16)
            ikb = sbp("ikb", [128, 64], BF16)
            iqsb = sbp("iqsb", [128, 256], BF16)
            aqT = sbp("aqT", [128, 512], BF16)
            iqsT = sbp("iqsT", [64, 512], BF16)
            rls = [sbp("rl%d" % j, [128, 512]) for j in range(2)]
            Es = [sbp("E%d" % j, [128, 512], BF16) for j in range(2)]
            PTs = [sbp("PT%d" % j, [128, 512], BF16) for j in range(2)]
            mTss = [sbp("mTs%d" % j, [128, 128], BF16) for j in range(2)]
            def front(t):
                r0, n = t * 128, 128
                x_load(t, n, r0)
                proj_rope(t, n, r0, d["ropeP"][t])
                K.dma("sp", o["kv_p"][i, r0:r0 + n, :], kvrow[:], r=[kvrow], w=[])
                pt = K.ps()
                for h in range(8):
                    K.tr(pt[:, h * 128:(h + 1) * 128], qk_r[:, h * 128:(h + 1) * 128], ident[:], r=[qk_r, ident], w=[(pt, h // 4)])
                K.cp("act", qT[:], pt[:, 0:512], r=[(pt, 0)], w=[qT])
                K.tt("dve", qTd[:], pt[:, 0:512], kc["qdecB"][:], ALU.mult, r=[(pt, 0), kc["qdecB"]], w=[qTd])
                K.cp("act", kT[:], pt[:, 512:1024], r=[(pt, 1)], w=[kT])
                pa = K.ps()
                for h in range(4):
                    hs = slice(h * 128, (h + 1) * 128)
                    K.mm(pa[:, hs], kT[:, hs], qT[:, hs], True, True, r=[kT, qT], w=[(pa, 0)])
                K.tt("dve", AT[:], pa[:, 0:512], kc["dmatT"][:], ALU.mult, r=[(pa, 0), kc["dmatT"]], w=[AT])
                K.tt("pool", kd[:].rearrange("p (h e) -> p h e", h=4), qk_r[:, 512:1024].rearrange("p (h e) -> p h e", h=4),
                     kdecC[:].unsqueeze(2).to_broadcast([128, 4, 128]), ALU.mult, r=[qk_r, kdecC], w=[kd])
                po = K.ps()
                for h in range(4):
                    hs = slice(h * 128, (h + 1) * 128)
                    K.mm(po[:, hs], AT[:, hs], pj[:, 1024 + h * 128:1024 + (h + 1) * 128], True, False, r=[AT, pj], w=[(po, 0)])
                    K.mm(po[:, hs], qTd[:, hs], Sret[:, hs], False, True, r=[qTd, Sret], w=[(po, 0)])
                pss = po
                for h in range(4):
                    hs = slice(h * 128, (h + 1) * 128)
                    K.mm(pss[:, 512 + h * 128:512 + (h + 1) * 128], kd[:, hs], pj[:, 1024 + h * 128:1024 + (h + 1) * 128], True, True, r=[kd, pj], w=[(pss, 1)])
                K.tt("dve", Sret[:], Sret[:], kc["cdecB"][:], ALU.mult, r=[Sret, kc["cdecB"]], w=[Sret])
                K.tt("dve", Sret[:], Sret[:], pss[:, 512:1024], ALU.add, r=[Sret, (pss, 1)], w=[Sret])
                head_norm_gate(n, po[0:n, 0:512], [(po, 0)])
                if t == NT - 1:
                    for h in range(4):
                        K.dma("sp", o["ret_p"][i, h], Sret[:, h * 128:(h + 1) * 128], r=[Sret], w=[])
                K.cp("pool", akb[:], kvrow[:, 0:128], r=[kvrow], w=[akb])
                K.cp("pool", ikb[:], kvrow[:, 256:320], r=[kvrow], w=[ikb])
                K.tt("dve", iqsb[:].rearrange("p (h e) -> p h e", h=4), iq_r[:].rearrange("p (h e) -> p h e", h=4),
                     aw[:, 0:4].unsqueeze(2).to_broadcast([128, 4, 64]), ALU.mult, r=[iq_r, aw], w=[iqsb])
                pt = K.ps()
                pbb = pt[:].bitcast(BF16)
                K.tr(pbb[:, 0:128], akb[:], identb[:], r=[akb, identb], w=[(pt, 0)])
                K.tr(pbb[0:64, 128:256], ikb[:], identb[:], r=[ikb, identb], w=[(pt, 0)])
                for h in range(4):
                    K.tr(pbb[:, 256 + h * 128:256 + (h + 1) * 128], aq_rb[:, h * 128:(h + 1) * 128], identb[:], r=[aq_rb, identb], w=[(pt, 0)])
                    K.tr(pbb[0:64, 1024 + h * 128:1024 + (h + 1) * 128], iqsb[:, h * 64:(h + 1) * 64], identb[:], r=[iqsb, identb], w=[(pt, 1)])
                K.cp("act", akT_all[:, r0:r0 + 128], pbb[:, 0:128], r=[(pt, 0)], w=[(akT_all, t)])
                K.cp("act", ikT_all[:, r0:r0 + 128], pbb[0:64, 128:256], r=[(pt, 0)], w=[ikT_all])
                K.cp("act", aqT[:], pbb[:, 256:768], r=[(pt, 0)], w=[aqT])
                K.cp("act", iqsT[:], pbb[0:64, 1024:1536], r=[(pt, 1)], w=[iqsT])
                K.cp("pool", Vall[:, t, 0:128], pj[:, 2688:2816], r=[pj], w=[(Vall, t)])
                n_k = (t + 1) * 128
                for kb in range((n_k + 511) // 512):
                    k0 = kb * 512
                    wb = min(512, n_k - k0)
                    pts = [K.ps(), K.ps()]
                    for h in range(4):
                        pp, hb = pts[h // 2], h % 2
                        K.mm(pp[:, hb * 512:hb * 512 + wb], iqsT[:, h * 128:(h + 1) * 128], ikT_all[:, k0:k0 + wb], True, True, r=[iqsT, ikT_all], w=[(pp, hb)])
                        rl = rls[h % 2]
                        K.act(rl[:, 0:wb], pp[:, hb * 512:hb * 512 + wb], AF.Relu, r=[(pp, hb)], w=[rl])
                        if h == 0:
                            K.ts("dve", score[:, k0:k0 + wb], rl[:, 0:wb], aw[:, 4:5], ALU.mult, r=[rl, aw], w=[score])
                        else:
                            K.stt("dve", score[:, k0:k0 + wb], rl[:, 0:wb], aw[:, 4 + h:5 + h], score[:, k0:k0 + wb], ALU.mult, ALU.add, r=[rl, aw, score], w=[score])
                if n_k > TOPK:
                    K.red("dve", bs[:, 0:1], score[:, 0:n_k], r=[score], w=[bs], op=ALU.max)
                    K.red("dve", bs[:, 1:2], score[:, 0:n_k], r=[score], w=[bs], op=ALU.min)
                    K.tt("dve", bs[:, 2:3], bs[:, 0:1], bs[:, 1:2], ALU.subtract, r=[bs], w=[bs])
                else:
                    K.memset("dve", bs[:, 1:2], -1e29, w=[bs])
                K.tt("dve", score[:, r0:r0 + 128], score[:, r0:r0 + 128], cst["cmask"][:], ALU.add, r=[score, cst["cmask"]], w=[score])

            def bis(t):
                n_k = (t + 1) * 128
                if n_k > TOPK:
                    K.ts("dve", Hh[:, 0:NIT], cst["pw2"][:, 0:NIT], bs[:, 2:3], ALU.mult, r=[bs, cst["pw2"]], w=[Hh])
                    for it in range(NIT):
                        K.tt("dve", bs[:, 4:5], bs[:, 1:2], Hh[:, it:it + 1], ALU.add, r=[bs, Hh], w=[bs])
                        K.ts("dve", junk[:, 0:n_k], score[:, 0:n_k], bs[:, 4:5], ALU.is_ge, r=[score, bs], w=[junk, cnt], s2=0.0, op1=ALU.add, acc=cnt[:, it:it + 1])
                        K.ts("dve", bs[:, 5:6], cnt[:, it:it + 1], float(TOPK), ALU.is_ge, r=[cnt], w=[bs])
                        K.stt("dve", bs[:, 1:2], bs[:, 5:6], Hh[:, it:it + 1], bs[:, 1:2], ALU.mult, ALU.add, r=[bs, Hh], w=[bs])

            def tail(t):
                n_k = (t + 1) * 128
                K.ts("dve", maskb[:, 0:n_k], score[:, 0:n_k], bs[:, 1:2], ALU.is_ge, r=[score, bs], w=[maskb])

            def back(t):
                r0, n = t * 128, 128
                def att_front(kt):
                    ks = slice(kt * 128, (kt + 1) * 128)
                    pm = K.ps()
                    pmb = pm[:].bitcast(BF16)
                    K.tr(pmb[:, 0:128], maskb[:, ks], identb[:], r=[maskb, identb], w=[(pm, 0)])
                    K.mm(pm[:, 512:1024], akT_all[:, ks], aqT[:], True, True, r=[(akT_all, kt), aqT], w=[(pm, 1)])
                    E, PT, mTs = Es[kt % 2], PTs[kt % 2], mTss[kt % 2]
                    K.act(E[:], pm[:, 512:1024], AF.Exp, r=[(pm, 1)], w=[E], scale=SC_ATT)
                    K.cp("act", mTs[:], pmb[:, 0:128], r=[(pm, 0)], w=[mTs])
                    K.tt("pool", PT[:].rearrange("p (h q) -> p h q", h=4), E[:].rearrange("p (h q) -> p h q", h=4),
                         mTs[:].unsqueeze(1).to_broadcast([128, 4, 128]), ALU.mult, r=[E, mTs], w=[PT])

                def att_back(kt, t=t):
                    PT = PTs[kt % 2]
                    for h in range(4):
                        c0 = (h % 2) * 512
                        K.mm(PSO[h // 2][:, c0:c0 + 130], PT[:, h * 128:(h + 1) * 128], Vall[:, kt, 0:130], kt == 0, kt == t, r=[PT, (Vall, kt)], w=[(PSO[h // 2], h % 2)], dedicated=True)

                att_front(0)
                for kt in range(t + 1):
                    if kt + 1 <= t:
                        att_front(kt + 1)
                    att_back(kt)
                for h in range(4):
                    c0 = (h % 2) * 512
                    pso = PSO[h // 2]
                    K.S.op("dve", lambda e, pso=pso, c0=c0, h=h: e.reciprocal(out=rs[:, h:h + 1], in_=pso[:, c0 + 128:c0 + 129]), r=[(pso, h % 2)], w=[rs])
                    K.ts("dve", cat[:, 512 + h * 128:512 + (h + 1) * 128], pso[:, c0:c0 + 128], rs[:, h:h + 1], ALU.mult, r=[(pso, h % 2), rs], w=[cat])
                out_proj_ln(K, t, n, cat, catT, wo, xt, z, xo, st)

            xts2 = [xt, K.sb(pp, "e_xt2", [128, D], F32)]
            cats2 = [cat, K.sb(pp, "e_cat2", [128, D], BF16)]
            aqTs2 = [aqT, K.sb(pp, "e_aqT2", [128, 512], BF16)]
            PSg = c["PS"]
            xt, cat, aqT = xts2[0], cats2[0], aqTs2[0]
            K.PSG = [PSg[0]]
            front(0)
            bis(0)
            tail(0)
            for t in range(NT):
                A = []
                if t + 1 < NT:
                    xt, cat, aqT = xts2[(t + 1) % 2], cats2[(t + 1) % 2], aqTs2[(t + 1) % 2]
                    K.PSG = [PSg[0]]
                    front(t + 1)
                    K.S.begin_capture()
                    bis(t + 1)
                    A = K.S.end_capture()
                xt, cat, aqT = xts2[t % 2], cats2[t % 2], aqTs2[t % 2]
                K.PSG = [PSg[1]]
                K.S.begin_capture()
                back(t)
                Bc = K.S.end_capture()
                K.S.emit_units(A, Bc)
                if t + 1 < NT:
                    tail(t + 1)
            K.PSG = PSg[:2]
            xt, cat, aqT = xts2[0], cats2[0], aqTs2[0]
        if not c["cfg"].get("skip_sample"):
            sample_even(K, l, pb, locals())
    K.PSG = c["PS"][:4]


def bisect_thr(K, n, score, n_k, topk, bs, cnt, junk, nit):
    K.red("dve", bs[0:n, 0:1], score[0:n, 0:n_k], r=[score], w=[bs], op=ALU.max)
    K.red("dve", bs[0:n, 1:2], score[0:n, 0:n_k], r=[score], w=[bs], op=ALU.min)
    K.tt("dve", bs[0:n, 2:3], bs[0:n, 0:1], bs[0:n, 1:2], ALU.subtract, r=[bs], w=[bs])
    K.memset("pool", cnt[0:n, :], 0.0, w=[cnt])
    for it in range(nit):
        K.ts("dve", bs[0:n, 3:4], bs[0:n, 2:3], 0.5 ** (it + 1), ALU.mult, r=[bs], w=[bs])
        K.tt("dve", bs[0:n, 4:5], bs[0:n, 1:2], bs[0:n, 3:4], ALU.add, r=[bs], w=[bs])
        K.ts("dve", junk[0:n, 0:n_k], score[0:n, 0:n_k], bs[0:n, 4:5], ALU.is_ge, r=[score, bs, cnt], w=[junk, cnt], s2=0.0, op1=ALU.add, acc=cnt[0:n, it:it + 1])
        K.ts("dve", bs[0:n, 5:6], cnt[0:n, it:it + 1], float(topk), ALU.is_ge, r=[cnt], w=[bs])
        K.stt("dve", bs[0:n, 1:2], bs[0:n, 5:6], bs[0:n, 3:4], bs[0:n, 1:2], ALU.mult, ALU.add, r=[bs], w=[bs])


def sample_even(K, l, pb, L):
    c = K.ctx
    f = K.fn
    d, o, NT, NPG = c["d"], c["o"], c["NT"], c["NPG"]
    cst, ident, PSO = c["cst"], c["ident"], c["PSO"]
    i = l // 2
    TOPK = c["TOPK_S"]
    NIT = 15
    SC_ATT = 128 ** -0.5
    n, t, r0 = NS, NT, c["NTOK"]
    NK = NPG * 128
    pj, xt, qk_r, kvrow, iq_r, t1, t2 = [L[k] for k in ("pj", "xt", "qk_r", "kvrow", "iq_r", "t1", "t2")]
    AT, sq, aw, s4, cat, catT, z, xo, st, wo, kc, bs, cnt = [L[k] for k in ("AT", "sq", "aw", "s4", "cat", "catT", "z", "xo", "st", "wo", "kc", "bs", "cnt")]
    g4 = lambda ap: ap.rearrange("p (h e) -> p h e", h=4)
    with K.scope() as px:
        sb = lambda nm, sh, dt=F32: K.sb(px, "es_" + nm, sh, dt)
        aq_f = sb("aq_f", [NS, 512])
        f["load_sample_mod"](l, 0)
        L["x_load"](t, n, r0)
        L["proj_rope"](t, n, r0, d["ropeS"][:, :], aq_f)
        K.dma("sp", o["kv_s"][i, :, :], kvrow[0:n, :], r=[kvrow], w=[])
        v_ap = pj[0:n, 1024:1536]
        K.tt("dve", sq[0:n, :], qk_r[0:n, 0:512], qk_r[0:n, 512:1024], ALU.mult, r=[qk_r], w=[sq])
        K.red("dve", s4[0:n, 0:4], g4(sq[0:n, :]), r=[sq], w=[s4])
        K.ts("dve", s4[0:n, 0:4], s4[0:n, 0:4], 128 ** -0.5, ALU.mult, r=[s4], w=[s4])
        pt = K.ps()
        for h in range(4):
            K.tr(pt[:, h * 16:(h + 1) * 16], qk_r[0:n, h * 128:(h + 1) * 128], ident[0:n, 0:n], r=[qk_r, ident], w=[(pt, 0)])
        qTs = sb("qTs", [128, 64])
        K.cp("act", qTs[:], pt[:, 0:64], r=[(pt, 0)], w=[qTs])
        K.tt("dve", t1[:, 0:1024].rearrange("p (h s m) -> p h s m", h=4, s=16),
             qTs[:].rearrange("p (h s) -> p h s", h=4).unsqueeze(3).to_broadcast([128, 4, 16, 16]),
             cst["delta16"][:].rearrange("p (s m) -> p s m", s=16).unsqueeze(1).to_broadcast([128, 4, 16, 16]),
             ALU.mult, r=[qTs, cst["delta16"]], w=[t1])
        S0 = [sb("S0%d" % j, [128, 512]) for j in range(2)]
        Sn = [sb("Sn%d" % j, [128, 512]) for j in range(2)]
        km = sb("km", [NS, 512])
        for s in range(NS):
            S0s, Sns = S0[s % 2], Sn[s % 2]
            for h in range(4):
                K.dma("sp", S0s[:, h * 128:(h + 1) * 128], d["st_ret"][i, s, h], r=[], w=[S0s])
            for h in range(4):
                hs = slice(h * 128, (h + 1) * 128)
                K.mm(PSO[h // 2][0:n, (h % 2) * 512:(h % 2) * 512 + 128], t1[:, (h * 16 + s) * 16:(h * 16 + s + 1) * 16], S0s[:, hs], s == 0, s == NS - 1, r=[t1, S0s], w=[(PSO[h // 2], h % 2)])
            K.ts("dve", km[:], qk_r[0:n, 512:1024], ident[0:n, s:s + 1], ALU.mult, r=[qk_r, ident], w=[km], s2=128 ** -0.5, op1=ALU.mult)
            pss = K.ps()
            for h in range(4):
                hs = slice(h * 128, (h + 1) * 128)
                K.mm(pss[:, hs], km[:, hs], pj[0:n, 1024 + h * 128:1024 + (h + 1) * 128], True, True, r=[km, pj], w=[(pss, 0)])
            K.tt("pool", Sns[:], S0s[:], kc["gam1B"][:], ALU.mult, r=[S0s, kc["gam1B"]], w=[Sns])
            K.tt("dve", Sns[:], Sns[:], pss[:, 0:512], ALU.add, r=[Sns, (pss, 0)], w=[Sns])
            for h in range(4):
                K.dma("sp", o["ret_s"][i, s, h], Sns[:, h * 128:(h + 1) * 128], r=[Sns], w=[])
        for h in range(4):
            hs = slice(h * 128, (h + 1) * 128)
            K.tt("dve", AT[0:n, hs], PSO[h // 2][0:n, (h % 2) * 512:(h % 2) * 512 + 128], kc["gam1B"][0:n, hs], ALU.mult, r=[(PSO[h // 2], h % 2), kc["gam1B"]], w=[AT])
        K.tt("pool", g4(sq[0:n, :]), g4(v_ap), s4[0:n, 0:4].unsqueeze(2).to_broadcast([n, 4, 128]), ALU.mult, r=[pj, s4], w=[sq])
        K.tt("dve", AT[0:n, :], AT[0:n, :], sq[0:n, :], ALU.add, r=[AT, sq], w=[AT])
        L["head_norm_gate"](n, AT[0:n, :], [AT])
        iqs = sb("iqs", [NS, 256])
        K.tt("dve", g4(iqs[:]), g4(iq_r[0:n, :]), aw[0:n, 0:4].unsqueeze(2).to_broadcast([n, 4, 64]), ALU.mult, r=[iq_r, aw], w=[iqs])
        pt = K.ps()
        for h in range(4):
            K.tr(pt[:, h * 16:(h + 1) * 16], aq_f[:, h * 128:(h + 1) * 128], ident[0:n, 0:n], r=[aq_f, ident], w=[(pt, 0)])
            K.tr(pt[0:64, 512 + h * 16:512 + (h + 1) * 16], iqs[:, h * 64:(h + 1) * 64], ident[0:n, 0:n], r=[iqs, ident], w=[(pt, 1)])
        aqTa = sb("aqTa", [128, 64])
        iqTa = sb("iqTa", [64, 64])
        K.cp("act", aqTa[:], pt[:, 0:64], r=[(pt, 0)], w=[aqTa])
        K.cp("act", iqTa[:], pt[0:64, 512:576], r=[(pt, 1)], w=[iqTa])
        sgnB = sb("sgnB", [128, NS * 4])
        sg_d = K.dram("sg_d%d" % l, [1, NS * 4], F32, "Internal")
        K.dma("sp", sg_d[0:1, :].rearrange("o (s h) -> (o s) h", h=4), aw[0:n, 4:8], r=[aw], w=[sg_d])
        K.dma("sp", sgnB[:], sg_d[0:1, :].partition_broadcast(128), r=[sg_d], w=[sgnB])
        idxi = sb("idxi", [128, NS * NPG], I32)
        idxf = sb("idxf", [128, NS * NPG])
        K.dma("sp", idxi[:], d["ptab"][:, :].rearrange("(o s) j -> o (s j)", o=1).partition_broadcast(128), r=[], w=[idxi])
        K.cp("dve", idxf[:], idxi[:], r=[idxi], w=[idxf])
        K.ts("dve", idxf[:], idxf[:], 128.0, ALU.mult, r=[idxf, cst["iotap"]], w=[idxf], s2=cst["iotap"][:, 0:1], op1=ALU.add)
        if i > 0:
            K.ts("dve", idxf[:], idxf[:], float(i * c["cfg"]["NPOOL"] * 128), ALU.add, r=[idxf], w=[idxf])
        K.cp("dve", idxi[:], idxf[:], r=[idxf], w=[idxi])
        cache = d["cache_kv"]
        sc_d = K.dram("sc_d%d" % l, [NS, NK], F32, "Internal")
        os_d = K.dram("os_d%d" % l, [4, NS, 132], F32, "Internal")
        kig = [sb("kig%d" % j, [128, NPG, 320]) for j in range(2)]
        kTs = sb("kTs", [128, NK + 4])
        kiT = kTs
        scT = sb("scT", [128, NS * NPG])
        rlS = sb("rlS", [128, NPG * 4])
        scr_t = sb("scr_t", [NPG, 128])
        for s in range(NS):
            kg = kig[s % 2]
            for j in range(NPG):
                col = s * NPG + j
                K.S.dma("pool", lambda e, kg=kg, j=j, col=col: e.indirect_dma_start(
                    out=kg[:, j, :], out_offset=None, in_=cache[:, :],
                    in_offset=bass.IndirectOffsetOnAxis(ap=idxi[:, col:col + 1], axis=0)), r=[idxi], w=[kg])
            for g in range((NPG + 3) // 4):
                pt = K.ps()
                nj = min(4, NPG - g * 4)
                for jj in range(nj):
                    K.tr(pt[0:64, jj * 128:(jj + 1) * 128], kg[:, g * 4 + jj, 256:320], ident[:], r=[kg, ident], w=[(pt, 0)])
                K.cp(K.evq(), kiT[0:64, g * 512:g * 512 + nj * 128], pt[0:64, 0:nj * 128], r=[(pt, 0)], w=[kiT])
            pq = K.ps()
            for j in range(NPG):
                K.mm(pq[:, j * 4:(j + 1) * 4], kiT[0:64, j * 128:(j + 1) * 128], iqTa[:].rearrange("p (h s) -> p h s", h=4)[:, :, s], True, True, r=[kiT, iqTa], w=[(pq, 0)])
            K.act(rlS[:], pq[:, 0:NPG * 4], AF.Relu, r=[(pq, 0)], w=[rlS])
            rv = rlS[:].rearrange("p (j h) -> p j h", h=4)
            K.tt("dve", rv, rv, sgnB[:, s * 4:(s + 1) * 4].unsqueeze(1).to_broadcast([128, NPG, 4]), ALU.mult, r=[rlS, sgnB], w=[rlS])
            K.red("dve", scT[:, s * NPG:(s + 1) * NPG], rv, r=[rlS], w=[scT])
            pt2 = K.ps()
            K.tr(pt2[0:NPG, 0:128], scT[:, s * NPG:(s + 1) * NPG], ident[:], r=[scT, ident], w=[(pt2, 0)])
            K.cp("act", scr_t[:], pt2[0:NPG, 0:128], r=[(pt2, 0)], w=[scr_t])
            K.dma("sp", sc_d[s, :].rearrange("(j p) -> j p", p=128), scr_t[:], r=[scr_t], w=[sc_d])
        K.tt("dve", t2[0:n, 0:256].rearrange("p (h e) -> p h e", h=4), g4(iqs[:]), kvrow[0:n, 256:320].unsqueeze(1).to_broadcast([n, 4, 64]), ALU.mult, r=[iqs, kvrow], w=[t2])
        K.red("dve", s4[0:n, 4:8], t2[0:n, 0:256].rearrange("p (h e) -> p h e", h=4), r=[t2], w=[s4])
        K.ts("dve", s4[0:n, 4:8], s4[0:n, 4:8], 0.0, ALU.max, r=[s4], w=[s4])
        K.tt("dve", s4[0:n, 4:8], s4[0:n, 4:8], aw[0:n, 4:8], ALU.mult, r=[s4, aw], w=[s4])
        srow = sb("srow", [NS, NK + 1])
        jrow = kTs
        K.dma("sp", srow[:, 0:NK], sc_d[:, :], r=[sc_d], w=[srow])
        K.red("dve", srow[:, NK:NK + 1], s4[0:n, 4:8], r=[s4, srow], w=[srow])
        if NK + 1 > TOPK:
            bisect_thr(K, n, srow, NK + 1, TOPK, bs, cnt, jrow, NIT)
        else:
            K.memset("dve", bs[0:n, 1:2], -1e29, w=[bs])
        dth = sb("dth", [NS, NS])
        K.ts("dve", dth[:], ident[0:NS, 0:NS], bs[0:n, 1:2], ALU.mult, r=[ident, bs], w=[dth])
        pq = K.ps()
        K.mm(pq[:, 0:NS], cst["ones"][0:NS, :], dth[:], True, True, r=[cst["ones"], dth], w=[(pq, 0)])
        thrB = sb("thrB", [128, NS])
        K.cp("act", thrB[:], pq[:, 0:NS], r=[(pq, 0)], w=[thrB])
        kvg = kig
        mT = sb("mT", [128, NPG])
        Es = sb("Es", [128, NPG * 4])
        PT2 = sb("PT2", [128, NPG * 4])
        osm = sb("osm", [4, NS, 132])
        K.memset("pool", osm[:], 0.0, w=[osm])
        for s in range(NS):
            kv = kvg[s % 2]
            for j in range(NPG):
                col = s * NPG + j
                K.S.dma("pool", lambda e, kv=kv, j=j, col=col: e.indirect_dma_start(
                    out=kv[:, j, :], out_offset=None, in_=cache[:, :],
                    in_offset=bass.IndirectOffsetOnAxis(ap=idxi[:, col:col + 1], axis=0)), r=[idxi], w=[kv])
            for g in range((NPG + 3) // 4):
                pt = K.ps()
                nj = min(4, NPG - g * 4)
                for jj in range(nj):
                    K.tr(pt[:, jj * 128:(jj + 1) * 128], kv[:, g * 4 + jj, 0:128], ident[:], r=[kv, ident], w=[(pt, 0)])
                K.cp(K.evq(), kTs[:, g * 512:g * 512 + nj * 128], pt[:, 0:nj * 128], r=[(pt, 0)], w=[kTs])
            pq = K.ps()
            for j in range(NPG):
                K.mm(pq[:, j * 4:(j + 1) * 4], kTs[:, j * 128:(j + 1) * 128], aqTa[:].rearrange("p (h s) -> p h s", h=4)[:, :, s], True, True, r=[kTs, aqTa], w=[(pq, 0)])
            K.act(Es[:], pq[:, 0:NPG * 4], AF.Exp, r=[(pq, 0)], w=[Es], scale=SC_ATT)
            K.ts("dve", mT[:], scT[:, s * NPG:(s + 1) * NPG], thrB[:, s:s + 1], ALU.is_ge, r=[scT, thrB], w=[mT])
            K.tt("dve", PT2[:].rearrange("p (j h) -> p j h", h=4), Es[:].rearrange("p (j h) -> p j h", h=4),
                 mT[:].unsqueeze(2).to_broadcast([128, NPG, 4]), ALU.mult, r=[Es, mT], w=[PT2])
            po2 = K.ps()
            for j in range(NPG):
                K.mm(po2[0:4, 0:128], PT2[:, j * 4:(j + 1) * 4], kv[:, j, 128:256], j == 0, j == NPG - 1, r=[PT2, kv], w=[(po2, 0)])
            for j in range(NPG):
                K.mm(po2[0:4, 512:513], PT2[:, j * 4:(j + 1) * 4], cst["ones"][:, 0:1], j == 0, j == NPG - 1, r=[PT2, cst["ones"]], w=[(po2, 1)])
            K.cp("act", osm[:, s, 0:128], po2[0:4, 0:128], r=[(po2, 0)], w=[osm])
            K.cp("act", osm[:, s, 128:129], po2[0:4, 512:513], r=[(po2, 1)], w=[osm])
        ot = sb("ot", [NS, 4, 132])
        K.dma("sp", os_d[:, :, :], osm[:], r=[osm], w=[os_d])
        K.dma("sp", ot[:], os_d[:, :, :].rearrange("h s e -> s h e"), r=[os_d], w=[ot])
        K.tt("dve", g4(t2[0:n, 0:512]), g4(aq_f[:]), kvrow[0:n, 0:128].unsqueeze(1).to_broadcast([n, 4, 128]), ALU.mult, r=[aq_f, kvrow], w=[t2])
        K.red("dve", s4[0:n, 8:12], g4(t2[0:n, 0:512]), r=[t2], w=[s4])
        K.act(s4[0:n, 8:12], s4[0:n, 8:12], AF.Exp, r=[s4], w=[s4], scale=SC_ATT)
        K.tt("dve", bs[0:n, 6:7], srow[:, NK:NK + 1], bs[0:n, 1:2], ALU.is_ge, r=[srow, bs], w=[bs])
        K.ts("dve", s4[0:n, 8:12], s4[0:n, 8:12], bs[0:n, 6:7], ALU.mult, r=[s4, bs], w=[s4])
        K.tt("dve", g4(t1[0:n, 0:512]), kvrow[0:n, 128:256].unsqueeze(1).to_broadcast([n, 4, 128]),
             s4[0:n, 8:12].unsqueeze(2).to_broadcast([n, 4, 128]), ALU.mult, r=[kvrow, s4], w=[t1])
        K.tt("dve", g4(t1[0:n, 0:512]), g4(t1[0:n, 0:512]), ot[:, :, 0:128], ALU.add, r=[t1, ot], w=[t1])
        K.tt("dve", s4[0:n, 12:16], ot[:, :, 128], s4[0:n, 8:12], ALU.add, r=[ot, s4], w=[s4])
        K.S.op("dve", lambda e: e.reciprocal(out=s4[0:n, 12:16], in_=s4[0:n, 12:16]), r=_bufs([s4]), w=_bufs([s4]))
        K.tt("dve", g4(cat[0:n, 512:1024]), g4(t1[0:n, 0:512]), s4[0:n, 12:16].unsqueeze(2).to_broadcast([n, 4, 128]), ALU.mult, r=[t1, s4], w=[cat])
        out_proj_ln(K, t, n, cat, catT, wo, xt, z, xo, st)


def phase_b_odd(K, l):
    c = K.ctx
    f = K.fn
    d, o, NT = c["d"], c["o"], c["NT"]
    cst, ident, PS = c["cst"], c["ident"], c["PS"]
    jl = l // 2
    NTOK = c["NTOK"]
    K.PSG = PS[:4]
    f["load_phase_consts"](l, 0)
    g8 = lambda ap: ap.rearrange("p (h e) -> p h e", h=8)
    bc8 = lambda ap, n, w: ap.unsqueeze(2).to_broadcast([n, 8, w])
    with K.scope() as pb:
        sb = lambda nm, sh, dt=F32: K.sb(pb, "o_" + nm, sh, dt)
        wo = sb("wo", [128, 8, D], BF16)
        f["load_weights_bf16"](wo, d["odd_w_out"][jl], D)
        cw = sb("cw", [128, 4, 3072])
        for tap in range(4):
            K.dma("sp", cw[:, tap, :], d["odd_conv"][jl, tap:tap + 1, :].partition_broadcast(128), r=[], w=[cw])
        gnB = sb("gnB", [128, 128])
        K.dma("sp", gnB[:], d["odd_gn"][jl:jl + 1, :].partition_broadcast(128), r=[], w=[gnB])
        nea = sb("nea", [128, 8])
        dtb = sb("dtb", [128, 8])
        K.dma("sp", nea[:], d["odd_a_log"][jl:jl + 1, :].partition_broadcast(128), r=[], w=[nea])
        K.dma("sp", dtb[:], d["odd_dt_bias"][jl:jl + 1, :].partition_broadcast(128), r=[], w=[dtb])
        K.act(nea[:], nea[:], AF.Exp, r=[nea], w=[nea])
        K.ts("dve", nea[:], nea[:], -1.0, ALU.mult, r=[nea], w=[nea])
        pj = sb("pj", [128, ODD_IN])
        xt = sb("xt", [128, D])
        sh = sb("sh", [128, 3072])
        cat = sb("cat", [128, D], BF16)
        catT = sb("catT", [128, 8, 128], BF16)
        z = sb("z", [128, D])
        xo = sb("xo", [128, D])
        st = sb("st", [128, 4])
        sm = sb("sm", [128, 96])
        ob = sb("ob", [128, 1024])
        sq = sh

        def x_load(t, n, r0):
            K.dma("sp", xt[0:n, :], c["x_a"][r0:r0 + n, :], r=[c["tb_xa"][t]], w=[xt])

        def conv_gates(t, n, r0, taps):
            K.dma("sp", pj[0:n, :], c["proj_d"][3 + r0:3 + r0 + n, :], r=[c["tb_pj"][t]], w=[pj])
            pq = pj[0:n, 0:3072]
            K.tt("pool", pq, pq, cw[0:n, 3, :], ALU.mult, r=[pj, cw], w=[pj])
            for tap in range(3):
                src, deps = taps[tap]
                K.dma("sp", sh[0:n, :], src, r=deps, w=[sh])
                K.tt("pool", sh[0:n, :], sh[0:n, :], cw[0:n, tap, :], ALU.mult, r=[sh, cw], w=[sh])
                K.tt("dve", pq, pq, sh[0:n, :], ALU.add, r=[pj, sh], w=[pj])
            K.act(pj[0:n, 0:4096], pj[0:n, 0:4096], AF.Silu, r=[pj], w=[pj])
            K.tt("pool", sq[0:n, 0:2048], pj[0:n, 0:2048], pj[0:n, 0:2048], ALU.mult, r=[pj], w=[sq])
            K.red("dve", sm[0:n, 16:32], sq[0:n, 0:2048].rearrange("p (h e) -> p h e", h=16), r=[sq], w=[sm])
            K.rsqrt("dve", sm[0:n, 16:32], sm[0:n, 16:32], r=[sm], w=[sm], mult=1.0, add=1e-6)
            K.ts("dve", sm[0:n, 16:24], sm[0:n, 16:24], 128 ** -0.5, ALU.mult, r=[sm], w=[sm])
            qk = pj[0:n, 0:2048].rearrange("p (h e) -> p h e", h=16)
            K.tt("dve", qk, qk, sm[0:n, 16:32].unsqueeze(2).to_broadcast([n, 16, 128]), ALU.mult, r=[pj, sm], w=[pj])
            K.tt("dve", sm[0:n, 32:40], pj[0:n, 4096:4104], dtb[0:n, :], ALU.add, r=[pj, dtb], w=[sm])
            K.act(sm[0:n, 40:48], sm[0:n, 32:40], AF.Abs, r=[sm], w=[sm])
            K.act(sm[0:n, 40:48], sm[0:n, 40:48], AF.Exp, r=[sm], w=[sm], scale=-1.0)
            K.act(sm[0:n, 40:48], sm[0:n, 40:48], AF.Ln, r=[sm], w=[sm], bias=1.0)
            K.ts("dve", sm[0:n, 32:40], sm[0:n, 32:40], 0.0, ALU.max, r=[sm], w=[sm])
            K.tt("dve", sm[0:n, 32:40], sm[0:n, 32:40], sm[0:n, 40:48], ALU.add, r=[sm], w=[sm])
            K.tt("dve", sm[0:n, 0:8], sm[0:n, 32:40], nea[0:n, :], ALU.mult, r=[sm, nea], w=[sm])
            K.act(sm[0:n, 8:16], pj[0:n, 4104:4112], AF.Sigmoid, r=[pj], w=[sm])

        def rms_gate(n, src, rsrc):
            K.tt("pool", sq[0:n, 0:1024], src, src, ALU.mult, r=rsrc, w=[sq])
            K.red("dve", sm[0:n, 48:56], g8(sq[0:n, 0:1024]), r=[sq], w=[sm])
            K.rsqrt("dve", sm[0:n, 48:56], sm[0:n, 48:56], r=[sm], w=[sm], mult=1.0 / 128, add=LN_EPS)
            K.tt("dve", g8(sq[0:n, 0:1024]), g8(src), bc8(sm[0:n, 48:56], n, 128), ALU.mult, r=rsrc + [sm], w=[sq])
            K.tt("pool", g8(sq[0:n, 0:1024]), g8(sq[0:n, 0:1024]), gnB[0:n, :].unsqueeze(1).to_broadcast([n, 8, 128]), ALU.mult, r=[sq, gnB], w=[sq])
            K.tt("dve", cat[0:n, :], sq[0:n, 0:1024], pj[0:n, 3072:4096], ALU.mult, r=[sq, pj], w=[cat])

        with K.scope() as pp:
            G = [K.sb(pp, "o_G%d" % j, [128, 1024], F32) for j in range(15)]
            Sst = K.sb(pp, "o_S", [128, 1024], F32)
            K.memset("pool", Sst[:], 0.0, w=[Sst])
            Dg, E, ES, _, kb, kbg, vb, kg, qT, kT, kbT, Nn, Mm, aqkT, R = G
            Pa, Pb, Qa, Qb = G[0], G[1], G[2], G[3]
            u_, wT, vnew = G[4], G[9], G[10]
            identB = ident[:].unsqueeze(1).to_broadcast([128, 8, 128])
            for t in range(NT):
                r0, n = t * 128, 128
                x_load(t, n, r0)
                prevb = c["tb_pj"][t - 1] if t > 0 else c["tb_pj"][NT + 1]
                taps = [(c["proj_d"][r0 + tap:r0 + tap + 128, 0:3072], [c["tb_pj"][t], prevb]) for tap in range(3)]
                conv_gates(t, n, r0, taps)
                q_ap, k_ap, v_ap = pj[:, 0:1024], pj[:, 1024:2048], pj[:, 2048:3072]
                g_, be = sm[:, 0:8], sm[:, 8:16]
                pg = K.ps()
                K.mm(pg[:, 0:8], cst["triu"][:], g_, True, True, r=[cst["triu"], sm], w=[(pg, 0)])
                gc = sm[:, 56:64]
                K.cp("act", gc, pg[:, 0:8], r=[(pg, 0)], w=[sm])
                K.tt("pool", g8(Dg[:]), identB, bc8(gc, 128, 128), ALU.mult, r=[ident, sm], w=[Dg])
                pr = K.ps()
                K.mm(pr[:, 0:512], cst["ones"][:], Dg[:, 0:512], True, True, r=[cst["ones"], Dg], w=[(pr, 0)])
                K.mm(pr[:, 512:1024], cst["ones"][:], Dg[:, 512:1024], True, True, r=[cst["ones"], Dg], w=[(pr, 1)])
                K.tt("dve", g8(E[:]), g8(pr[:]), bc8(gc, 128, 128), ALU.subtract, r=[pr, sm], w=[E])
                K.tt("dve", g8(E[:]), g8(E[:]), cst["mvinc"][:].unsqueeze(1).to_broadcast([128, 8, 128]), ALU.min, r=[E, cst["mvinc"]], w=[E])
                K.act(E[:], E[:], AF.Exp, r=[E], w=[E])
                K.tt("pool", g8(ES[:]), g8(E[:]), cst["strict"][:].unsqueeze(1).to_broadcast([128, 8, 128]), ALU.mult, r=[E, cst["strict"]], w=[ES])
                gl = sm[:, 64:72]
                K.cp("act", gl, g8(pr[:])[:, :, 127], r=[pr], w=[sm])
                K.tt("dve", sm[:, 72:80], gl, gc, ALU.subtract, r=[sm], w=[sm])
                K.act(sm[:, 72:80], sm[:, 72:80], AF.Exp, r=[sm], w=[sm])
                K.act(sm[:, 80:88], gl, AF.Exp, r=[sm], w=[sm])
                K.act(sm[:, 88:96], gc, AF.Exp, r=[sm], w=[sm])
                K.tt("pool", g8(kb[:]), g8(k_ap), bc8(be, 128, 128), ALU.mult, r=[pj, sm], w=[kb])
                K.tt("pool", g8(kbg[:]), g8(kb[:]), bc8(sm[:, 88:96], 128, 128), ALU.mult, r=[kb, sm], w=[kbg])
                K.tt("pool", g8(vb[:]), g8(v_ap), bc8(be, 128, 128), ALU.mult, r=[pj, sm], w=[vb])
                K.tt("pool", g8(kg[:]), g8(k_ap), bc8(sm[:, 72:80], 128, 128), ALU.mult, r=[pj, sm], w=[kg])
                for src_ap, rs, dst in ((q_ap, [pj], qT), (k_ap, [pj], kT), (kb[:], [kb], kbT)):
                    pt = K.ps()
                    for h in range(8):
                        K.tr(pt[:, h * 128:(h + 1) * 128], src_ap[:, h * 128:(h + 1) * 128], ident[:], r=rs + [ident], w=[(pt, h // 4)])
                    K.cp("act", dst[:], pt[:], r=[pt], w=[dst])
                pn = K.ps()
                pa = K.ps()
                for h in range(8):
                    hs = slice(h * 128, (h + 1) * 128)
                    K.mm(pn[:, hs], kT[:, hs], kbT[:, hs], True, True, r=[kT, kbT], w=[(pn, h // 4)])
                for h in range(8):
                    hs = slice(h * 128, (h + 1) * 128)
                    K.mm(pa[:, hs], kT[:, hs], qT[:, hs], True, True, r=[kT, qT], w=[(pa, h // 4)])
                K.tt("dve", Nn[:], pn[:], ES[:], ALU.mult, r=[pn, ES], w=[Nn])
                K.tt("dve", aqkT[:], pa[:], E[:], ALU.mult, r=[pa, E], w=[aqkT])
                pm = K.ps()
                for h in range(8):
                    hs = slice(h * 128, (h + 1) * 128)
                    K.tr(pm[:, hs], Nn[:, hs], ident[:], r=[Nn, ident], w=[(pm, h // 4)])
                K.cp("act", Mm[:], pm[:], r=[pm], w=[Mm])
                K.tt("pool", g8(R[:]), identB, g8(Nn[:]), ALU.subtract, r=[ident, Nn], w=[R])
                P, Q = Mm, Nn
                Pn, Qn = [Pa, Pb], [Qa, Qb]
                for lev in range(1, 7):
                    P2, Q2 = Pn[lev % 2], Qn[lev % 2]
                    pP = K.ps()
                    for h in range(8):
                        hs = slice(h * 128, (h + 1) * 128)
                        K.mm(pP[:, hs], Q[:, hs], P[:, hs], True, True, r=[Q, P], w=[(pP, h // 4)])
                    if lev < 6:
                        pQ = K.ps()
                        for h in range(8):
                            hs = slice(h * 128, (h + 1) * 128)
                            K.mm(pQ[:, hs], P[:, hs], Q[:, hs], True, True, r=[P, Q], w=[(pQ, h // 4)])
                    K.cp("act", P2[:], pP[:], r=[pP], w=[P2])
                    if lev < 6:
                        K.cp("dve", Q2[:], pQ[:], r=[pQ], w=[Q2])
                    pR = K.ps()
                    for h in range(8):
                        hs = slice(h * 128, (h + 1) * 128)
                        K.mm(pR[:, hs], P2[:, hs], R[:, hs], True, True, r=[P2, R], w=[(pR, h // 4)])
                    K.tt("dve", R[:], R[:], pR[:], ALU.add, r=[R, pR], w=[R])
                    P, Q = P2, Q2
                pu = K.ps()
                pw = K.ps()
                for h in range(8):
                    hs = slice(h * 128, (h + 1) * 128)
                    K.mm(pu[:, hs], R[:, hs], vb[:, hs], True, True, r=[R, vb], w=[(pu, h // 4)])
                for h in range(8):
                    hs = slice(h * 128, (h + 1) * 128)
                    K.mm(pw[:, hs], kbg[:, hs], R[:, hs], True, True, r=[kbg, R], w=[(pw, h // 4)])
                K.cp("act", u_[:], pu[:], r=[pu], w=[u_])
                K.cp("act", wT[:], pw[:], r=[pw], w=[wT])
                pv = K.ps()
                for h in range(8):
                    hs = slice(h * 128, (h + 1) * 128)
                    K.mm(pv[:, hs], wT[:, hs], Sst[:, hs], True, True, r=[wT, Sst], w=[(pv, h // 4)])
                K.tt("dve", vnew[:], u_[:], pv[:], ALU.subtract, r=[u_, pv], w=[vnew])
                po1 = K.ps()
                for h in range(8):
                    hs = slice(h * 128, (h + 1) * 128)
                    K.mm(po1[:, hs], qT[:, hs], Sst[:, hs], True, True, r=[qT, Sst], w=[(po1, h // 4)])
                K.tt("dve", g8(ob[:]), g8(po1[:]), bc8(sm[:, 88:96], 128, 128), ALU.mult, r=[po1, sm], w=[ob])
                po2 = K.ps()
                for h in range(8):
                    hs = slice(h * 128, (h + 1) * 128)
                    K.mm(po2[:, hs], aqkT[:, hs], vnew[:, hs], True, True, r=[aqkT, vnew], w=[(po2, h // 4)])
                K.tt("dve", ob[:], ob[:], po2[:], ALU.add, r=[ob, po2], w=[ob])
                pS = K.ps()
                for h in range(8):
                    hs = slice(h * 128, (h + 1) * 128)
                    K.mm(pS[:, hs], kg[:, hs], vnew[:, hs], True, True, r=[kg, vnew], w=[(pS, h // 4)])
                K.tt("pool", g8(Sst[:]), g8(Sst[:]), bc8(sm[:, 80:88], 128, 128), ALU.mult, r=[Sst, sm], w=[Sst])
                K.tt("dve", Sst[:], Sst[:], pS[:], ALU.add, r=[Sst, pS], w=[Sst])
                rms_gate(n, ob[:], [ob])
                out_proj_ln(K, t, n, cat, catT, wo, xt, z, xo, st)
            for h in range(8):
                K.dma("sp", o["ssm_p"][jl, h], Sst[:, h * 128:(h + 1) * 128], r=[Sst], w=[])
            K.dma("sp", o["conv_p"][jl], c["proj_d"][NTOK:NTOK + 3, 0:3072], r=[c["tb_pj"][NT - 1]], w=[])
        if not c["cfg"].get("skip_sample"):
            sample_odd(K, l, locals())
    K.PSG = PS[:4]


def sample_odd(K, l, L):
    c = K.ctx
    f = K.fn
    d, o, NT = c["d"], c["o"], c["NT"]
    cst, ident = c["cst"], c["ident"]
    jl = l // 2
    n, t, r0 = NS, NT, c["NTOK"]
    pj, xt, sm, ob, cat, catT, z, xo, st, wo = [L[k] for k in ("pj", "xt", "sm", "ob", "cat", "catT", "z", "xo", "st", "wo")]
    g8 = lambda ap: ap.rearrange("p (h e) -> p h e", h=8)
    with K.scope() as px:
        sb = lambda nm, sh, dt=F32: K.sb(px, "os_" + nm, sh, dt)
        f["load_sample_mod"](l, 0)
        L["x_load"](t, n, r0)
        taps = [(d["st_conv"][jl, :, tap, :], []) for tap in range(3)]
        L["conv_gates"](t, n, r0, taps)
        K.dma("sp", o["conv_s"][jl, :, 0:2, :], d["st_conv"][jl, :, 1:3, :], r=[], w=[])
        K.dma("sp", o["conv_s"][jl, :, 2, :], c["proj_d"][3 + r0:3 + r0 + n, 0:3072], r=[c["tb_pj"][t]], w=[])
        K.act(sm[0:n, 32:40], sm[0:n, 0:8], AF.Exp, r=[sm], w=[sm])
        sq = L["sq"]
        K.tt("pool", sq[0:n, 0:1024], pj[0:n, 0:1024], pj[0:n, 1024:2048], ALU.mult, r=[pj], w=[sq])
        K.red("dve", sm[0:n, 40:48], g8(sq[0:n, 0:1024]), r=[sq], w=[sm])
        row_d = K.dram("gr_d%d" % l, [NS, 3072 + 24], F32, "Internal")
        orow_d = K.dram("go_d%d" % l, [NS, 1024], F32, "Internal")
        K.dma("sp", row_d[:, 0:3072], pj[0:n, 0:3072], r=[pj], w=[row_d])
        K.dma("sp", row_d[:, 3072:3080], sm[0:n, 32:40], r=[sm], w=[row_d])
        K.dma("sp", row_d[:, 3080:3088], sm[0:n, 8:16], r=[sm], w=[row_d])
        K.dma("sp", row_d[:, 3088:3096], sm[0:n, 40:48], r=[sm], w=[row_d])
        egB = sb("egB", [128, NS * 8])
        egd = K.dram("ge_d%d" % l, [1, NS * 8], F32, "Internal")
        K.dma("sp", egd[0:1, :].rearrange("o (s h) -> (o s) h", h=8), sm[0:n, 32:40], r=[sm], w=[egd])
        K.dma("sp", egB[:], egd[0:1, :].partition_broadcast(128), r=[egd], w=[egB])
        kqT = sb("kqT", [128, 8 * NS * 2])
        kq4 = kqT[:].rearrange("p (h s a) -> p h s a", h=8, a=2)
        for which, c0 in ((0, 1024), (1, 0)):
            pt = K.ps()
            for h in range(8):
                K.tr(pt[:, h * 16:(h + 1) * 16], pj[0:n, c0 + h * 128:c0 + (h + 1) * 128], ident[0:n, 0:n], r=[pj, ident], w=[(pt, 0)])
            K.cp("act", kq4[:, :, :, which], pt[:, 0:128].rearrange("p (h s) -> p h s", h=8), r=[(pt, 0)], w=[kqT])
        rows = [sb("row%d" % j, [1, 3096]) for j in range(2)]
        S0 = [sb("S0%d" % j, [128, 1024]) for j in range(2)]
        vn = sb("vn", [1, 1024])
        orow = sb("orow", [1, 1024])
        tmp = sb("tmp", [1, 1024])
        r8 = lambda ap: ap.rearrange("p (h e) -> p h e", h=8)
        b8 = lambda ap: ap.unsqueeze(2).to_broadcast([1, 8, 128])
        for s in range(NS):
            row, S0s = rows[s % 2], S0[s % 2]
            K.dma("sp", row[:], row_d[s:s + 1, :], r=[row_d], w=[row])
            for h in range(8):
                K.dma("sp", S0s[:, h * 128:(h + 1) * 128], d["st_ssm"][jl, s, h], r=[], w=[S0s])
            pk = K.ps()
            pq = K.ps()
            for h in range(8):
                hs = slice(h * 128, (h + 1) * 128)
                K.mm(pk[0:1, hs], kq4[:, h, s, 0:1], S0s[:, hs], True, True, r=[kqT, S0s], w=[(pk, h // 4)])
            for h in range(8):
                hs = slice(h * 128, (h + 1) * 128)
                K.mm(pq[0:1, hs], kq4[:, h, s, 1:2], S0s[:, hs], True, True, r=[kqT, S0s], w=[(pq, h // 4)])
            eg, be, qk = row[:, 3072:3080], row[:, 3080:3088], row[:, 3088:3096]
            K.tt("dve", r8(tmp[:]), r8(pk[0:1, :]), b8(eg), ALU.mult, r=[pk, row], w=[tmp])
            K.tt("dve", vn[:], row[:, 2048:3072], tmp[:], ALU.subtract, r=[row, tmp], w=[vn])
            K.tt("dve", r8(vn[:]), r8(vn[:]), b8(be), ALU.mult, r=[vn, row], w=[vn])
            K.tt("dve", r8(orow[:]), r8(pq[0:1, :]), b8(eg), ALU.mult, r=[pq, row], w=[orow])
            K.tt("dve", r8(tmp[:]), r8(vn[:]), b8(qk), ALU.mult, r=[vn, row], w=[tmp])
            K.tt("dve", orow[:], orow[:], tmp[:], ALU.add, r=[orow, tmp], w=[orow])
            K.dma("sp", orow_d[s:s + 1, :], orow[:], r=[orow], w=[orow_d])
            pS = K.ps()
            for h in range(8):
                hs = slice(h * 128, (h + 1) * 128)
                K.mm(pS[:, hs], row[:, 1024 + h * 128:1024 + (h + 1) * 128], vn[:, hs], True, True, r=[row, vn], w=[(pS, h // 4)])
            K.tt("pool", g8(S0s[:]), g8(S0s[:]), egB[:, s * 8:(s + 1) * 8].unsqueeze(2).to_broadcast([128, 8, 128]), ALU.mult, r=[S0s, egB], w=[S0s])
            K.tt("dve", S0s[:], S0s[:], pS[:], ALU.add, r=[S0s, pS], w=[S0s])
            for h in range(8):
                K.dma("sp", o["ssm_s"][jl, s, h], S0s[:, h * 128:(h + 1) * 128], r=[S0s], w=[])
        K.dma("sp", ob[0:n, :], orow_d[:, :], r=[orow_d], w=[ob])
        L["rms_gate"](n, ob[0:n, :], [ob])
        out_proj_ln(K, t, n, cat, catT, wo, xt, z, xo, st)


_NC_CACHE = {}


def kernel(x_prompt, x_sample, c_prompt, c_sample, cache_kv, page_table, state_ret, state_ssm, state_conv,
           ada_w, ada_b, ln_g, ln_b, mlp_up, mlp_down, even_w_in, even_w_out, ret_gn,
           odd_w_in, odd_w_out, odd_conv, odd_a_log, odd_dt_bias, odd_gn):
    f = lambda a: np.ascontiguousarray(np.asarray(a))
    B, T, _ = x_prompt.shape
    NT = T // 128
    NPG = page_table.shape[1]
    NPOOL = cache_kv.shape[1]
    cfg = dict(NT=NT, NPG=NPG, NPOOL=NPOOL, DEPTH=4, mix="real")
    key = (NT, NPG, NPOOL)
    if key not in _NC_CACHE:
        _NC_CACHE[key] = build(cfg)
    nc = _NC_CACHE[key]
    cs = make_consts(NT, NPG * cache_kv.shape[2])
    shared = dict(cache_kv=f(cache_kv).reshape(-1, 320), ada_w=f(ada_w), ada_b=f(ada_b), ln_g=f(ln_g), ln_b=f(ln_b),
                  mlp_up=f(mlp_up), mlp_down=f(mlp_down), even_w_in=f(even_w_in), even_w_out=f(even_w_out),
                  ret_gn=f(ret_gn).reshape(2, 512), odd_w_in=f(odd_w_in), odd_w_out=f(odd_w_out), odd_conv=f(odd_conv),
                  odd_a_log=f(odd_a_log), odd_dt_bias=f(odd_dt_bias), odd_gn=f(odd_gn))
    shared.update({k: f(v) for k, v in cs.items()})
    in_maps = []
    for c in range(8):
        b = c // 2
        sl = slice(c * NS, (c + 1) * NS)
        m = dict(shared)
        m.update(xp=f(x_prompt[b]), xs=f(x_sample[sl, 0]), c17=f(np.concatenate([c_sample[sl], c_prompt[b:b + 1]], 0)),
                 ptab=f(page_table[sl]), st_ret=f(state_ret[:, sl]), st_ssm=f(state_ssm[:, sl]), st_conv=f(state_conv[:, sl]))
        in_maps.append(m)
    res = run_bass_kernel_spmd(nc, in_maps, core_ids=list(range(8))).results
    ev = [res[2 * b] for b in range(B)]
    y_p = np.stack([r["y_p"] for r in ev], 0)
    y_s = np.concatenate([r["y_s"] for r in res], 0)[:, None, :]
    kv_p = np.stack([r["kv_p"] for r in ev], 1)
    kv_s = np.concatenate([r["kv_s"] for r in res], 1)[:, :, None, :]
    ret_p = np.stack([r["ret_p"] for r in ev], 1)
    ret_s = np.concatenate([r["ret_s"] for r in res], 1)
    ssm_p = np.stack([r["ssm_p"] for r in ev], 1)
    ssm_s = np.concatenate([r["ssm_s"] for r in res], 1)
    conv_p = np.stack([r["conv_p"] for r in ev], 1)
    conv_s = np.concatenate([r["conv_s"] for r in res], 1)
    return tuple(np.asarray(a, np.float32) for a in (y_p, y_s, kv_p, kv_s, ret_p, ret_s, ssm_p, ssm_s, conv_p, conv_s))
```
